# Optimizing a Trainium2 kernel written in Bass

```python
import math, functools
import jax, jax.numpy as jnp
from jax import lax
import numpy as np

D_MODEL = 1024
BATCH = 8
SEQ = 2048
DEPTH = 1
DEC_BATCH = 128
DEC_SEQ = 4
PAST_LEN = 8192
PAGE_SIZE = 128

D_SSM = 512
SSM_GROUP = 16
N_GROUPS = D_SSM // SSM_GROUP
STATE_DIM = 64
N_HEADS = 8
N_KV = 2
HEAD_DIM = 64
Q_GROUP = N_HEADS // N_KV
D_ATTN = N_HEADS * HEAD_DIM
KV_DIM = N_KV * HEAD_DIM
WINDOW = 128
BLOCK = WINDOW
D_FF = 2816
CONV_W = 3
LN_EPS = 1e-5
ALPHA = (2 * DEPTH) ** 0.25
BETA = (8 * DEPTH) ** -0.25
D_IN = D_SSM + D_ATTN + 2 * KV_DIM + 2 * D_MODEL
SPLITS = [D_SSM, D_SSM + D_ATTN, D_SSM + D_ATTN + KV_DIM, D_SSM + D_ATTN + 2 * KV_DIM, D_SSM + D_ATTN + 2 * KV_DIM + D_MODEL]

kernel_name = 'hybrid_s5_swa_sink_convglu_deepnorm_step'


def layer_norm(x, g, b):
    xf = x.astype(jnp.float32)
    mu = jnp.mean(xf, -1, keepdims=True)
    var = jnp.mean(jnp.square(xf - mu), -1, keepdims=True)
    return ((xf - mu) * lax.rsqrt(var + LN_EPS) * g + b).astype(x.dtype)


def sink_attend(q, k, v, mask, sinks):
    s = jnp.einsum('...qkgd,...skd->...kgqs', q, k).astype(jnp.float32) * (HEAD_DIM ** -0.5)
    s = jnp.where(mask, s, -jnp.inf)
    sink = sinks.astype(jnp.float32).reshape(N_KV, Q_GROUP, 1, 1)
    m = jnp.maximum(jnp.max(s, -1, keepdims=True), sink)
    p = jnp.exp(s - m)
    p = (p / (jnp.sum(p, -1, keepdims=True) + jnp.exp(sink - m))).astype(v.dtype)
    return jnp.einsum('...kgqs,...skd->...qkgd', p, v)


def window_attn_prompt(q, k, v, sinks):
    B, L = q.shape[:2]
    nb = L // BLOCK
    qb = q.reshape(B, nb, BLOCK, N_KV, Q_GROUP, HEAD_DIM)

    def band(t):
        tb = t.reshape(B, nb, BLOCK, N_KV, HEAD_DIM)
        prev = jnp.pad(tb, ((0, 0), (1, 0), (0, 0), (0, 0), (0, 0)))[:, :-1]
        return jnp.concatenate([prev, tb], axis=2)

    kb, vb = band(k), band(v)
    qi = jnp.arange(BLOCK)[:, None]
    sj = jnp.arange(2 * BLOCK)[None, :]
    rel = qi + BLOCK - sj
    local = (rel >= 0) & (rel < WINDOW)
    has_prev = (jnp.arange(nb) > 0)[:, None, None] | (sj >= BLOCK)[None]
    mask = (local[None] & has_prev)[:, None, None]
    o = sink_attend(qb, kb, vb, mask, sinks)
    return o.reshape(B, L, D_ATTN), k[:, -WINDOW:], v[:, -WINDOW:]


def window_attn_sample(q, k, v, sinks, k_cache, v_cache):
    B, T = q.shape[:2]
    W = k_cache.shape[1]
    kk = jnp.concatenate([k_cache.astype(k.dtype), k], axis=1)
    vv = jnp.concatenate([v_cache.astype(v.dtype), v], axis=1)
    qpos = jnp.arange(T)[:, None]
    kpos = jnp.arange(W + T)[None, :] - W
    mask = (kpos <= qpos) & (qpos - kpos < WINDOW)
    o = sink_attend(q, kk, vv, mask, sinks)
    return o.reshape(B, T, D_ATTN), kk[:, -W:], vv[:, -W:]


def s5_scan(u, h0_re, h0_im, lam_re, lam_im, log_dt, b_re, b_im, c_re, c_im, d_skip):
    f32 = jnp.float32
    Bsz, L, _ = u.shape
    uf = u.astype(f32)
    lr, li = lam_re.astype(f32), lam_im.astype(f32)
    dt = jnp.exp(log_dt.astype(f32))[:, None]
    mag = jnp.exp(lr * dt)
    ang = li * dt
    ab_re, ab_im = mag * jnp.cos(ang), mag * jnp.sin(ang)
    den = lr * lr + li * li
    nr = ab_re - 1.0
    coef_re = (nr * lr + ab_im * li) / den
    coef_im = (ab_im * lr - nr * li) / den
    br, bi = b_re.astype(f32), b_im.astype(f32)
    bb_re = coef_re[..., None] * br - coef_im[..., None] * bi
    bb_im = coef_re[..., None] * bi + coef_im[..., None] * br
    ug = uf.reshape(Bsz, L, N_GROUPS, SSM_GROUP)
    x_re = jnp.einsum('blgc,gpc->blgp', ug, bb_re)
    x_im = jnp.einsum('blgc,gpc->blgp', ug, bb_im)
    h_re, h_im = h0_re.astype(f32), h0_im.astype(f32)
    x_re = x_re.at[:, 0].add(ab_re * h_re - ab_im * h_im)
    x_im = x_im.at[:, 0].add(ab_re * h_im + ab_im * h_re)
    a_re = jnp.broadcast_to(ab_re, x_re.shape)
    a_im = jnp.broadcast_to(ab_im, x_im.shape)

    def combine(l, r):
        ar1, ai1, br1, bi1 = l
        ar2, ai2, br2, bi2 = r
        return (ar1 * ar2 - ai1 * ai2, ar1 * ai2 + ai1 * ar2,
                ar2 * br1 - ai2 * bi1 + br2, ar2 * bi1 + ai2 * br1 + bi2)

    _, _, s_re, s_im = lax.associative_scan(combine, (a_re, a_im, x_re, x_im), axis=1)
    y = (jnp.einsum('blgp,gcp->blgc', s_re, c_re.astype(f32))
         - jnp.einsum('blgp,gcp->blgc', s_im, c_im.astype(f32)))
    y = y.reshape(Bsz, L, D_SSM) + d_skip.astype(f32) * uf
    return y.astype(u.dtype), s_re[:, -1], s_im[:, -1]


def conv_glu_ffn(x, conv_state, w_up, conv_w, conv_b, w_down):
    L = x.shape[1]
    a, g = jnp.split(x @ w_up, 2, axis=-1)
    ext = jnp.concatenate([conv_state.astype(a.dtype), a], axis=1)
    conv = conv_b
    for j in range(CONV_W):
        conv = conv + ext[:, j:j + L] * conv_w[j]
    h = jax.nn.gelu(conv) * g
    return h @ w_down, ext[:, -(CONV_W - 1):]


def hybrid_layer(x, attn_fn, h_re, h_im, conv_state, lp):
    Bsz, L, _ = x.shape
    u, q, k, v, g_s, g_a = jnp.split(x @ lp['w_in'], SPLITS, axis=-1)
    y, h_re_new, h_im_new = s5_scan(u, h_re, h_im, lp['lam_re'], lp['lam_im'], lp['log_dt'],
                                    lp['b_re'], lp['b_im'], lp['c_re'], lp['c_im'], lp['d'])
    ya, yb = jnp.split(jax.nn.gelu(y) @ lp['w_glu'], 2, axis=-1)
    branch_s = ya * jax.nn.sigmoid(yb)
    q = q.reshape(Bsz, L, N_KV, Q_GROUP, HEAD_DIM)
    k = k.reshape(Bsz, L, N_KV, HEAD_DIM)
    v = v.reshape(Bsz, L, N_KV, HEAD_DIM)
    o, k_new, v_new = attn_fn(q, k, v, lp['sinks'])
    branch_a = o @ lp['w_attn_br']
    merged = jax.nn.sigmoid(g_s) * branch_s + jax.nn.sigmoid(g_a) * branch_a
    x = layer_norm(ALPHA * x + merged @ lp['w_o'], lp['ln1_g'], lp['ln1_b'])
    f, conv_new = conv_glu_ffn(x, conv_state, lp['w_up'], lp['conv_w'], lp['conv_b'], lp['w_down'])
    x = layer_norm(ALPHA * x + f, lp['ln2_g'], lp['ln2_b'])
    return x, k_new, v_new, h_re_new, h_im_new, conv_new


def setup_inputs(seed: int = 0) -> dict:
    key = jax.random.key(seed)
    ks = jax.random.split(key, 32)
    f32 = jnp.float32

    def nrm(k, shape, scale):
        return jax.random.normal(k, shape, f32) * scale

    win = min(WINDOW, PAST_LEN)
    pidx = jnp.arange(STATE_DIM, dtype=f32)
    return {
        'x_prompt': nrm(ks[0], (BATCH, SEQ, D_MODEL), 1.0),
        'x_sample': nrm(ks[1], (DEC_BATCH, DEC_SEQ, D_MODEL), 1.0),
        'cache_k_win': nrm(ks[2], (DEPTH, DEC_BATCH, win, N_KV, HEAD_DIM), 1.0),
        'cache_v_win': nrm(ks[3], (DEPTH, DEC_BATCH, win, N_KV, HEAD_DIM), 1.0),
        'state_ssm_re': nrm(ks[4], (DEPTH, DEC_BATCH, N_GROUPS, STATE_DIM), 0.1),
        'state_ssm_im': nrm(ks[5], (DEPTH, DEC_BATCH, N_GROUPS, STATE_DIM), 0.1),
        'state_ffn_conv': nrm(ks[6], (DEPTH, DEC_BATCH, CONV_W - 1, D_FF), 1.0),
        'w_in': nrm(ks[7], (DEPTH, D_MODEL, D_IN), D_MODEL ** -0.5),
        'ssm_lam_re': -0.5 * jnp.exp(nrm(ks[8], (DEPTH, N_GROUPS, STATE_DIM), 0.05)),
        'ssm_lam_im': math.pi * pidx + nrm(ks[9], (DEPTH, N_GROUPS, STATE_DIM), 0.05),
        'ssm_log_dt': jax.random.uniform(ks[10], (DEPTH, N_GROUPS), f32, math.log(1e-3), math.log(1e-1)),
        'ssm_b_re': nrm(ks[11], (DEPTH, N_GROUPS, STATE_DIM, SSM_GROUP), (2 * SSM_GROUP) ** -0.5),
        'ssm_b_im': nrm(ks[12], (DEPTH, N_GROUPS, STATE_DIM, SSM_GROUP), (2 * SSM_GROUP) ** -0.5),
        'ssm_c_re': nrm(ks[13], (DEPTH, N_GROUPS, SSM_GROUP, STATE_DIM), (2 * STATE_DIM) ** -0.5),
        'ssm_c_im': nrm(ks[14], (DEPTH, N_GROUPS, SSM_GROUP, STATE_DIM), (2 * STATE_DIM) ** -0.5),
        'ssm_d': nrm(ks[15], (DEPTH, D_SSM), 1.0),
        'w_glu': nrm(ks[16], (DEPTH, D_SSM, 2 * D_MODEL), D_SSM ** -0.5),
        'attn_sinks': nrm(ks[17], (DEPTH, N_HEADS), 0.5),
        'w_attn_br': nrm(ks[18], (DEPTH, D_ATTN, D_MODEL), D_ATTN ** -0.5),
        'w_o': nrm(ks[19], (DEPTH, D_MODEL, D_MODEL), BETA * D_MODEL ** -0.5),
        'ln1_g': 1.0 + nrm(ks[20], (DEPTH, D_MODEL), 0.02),
        'ln1_b': nrm(ks[21], (DEPTH, D_MODEL), 0.02),
        'w_up': nrm(ks[22], (DEPTH, D_MODEL, 2 * D_FF), D_MODEL ** -0.5),
        'conv_w': nrm(ks[23], (DEPTH, CONV_W, D_FF), CONV_W ** -0.5),
        'conv_b': nrm(ks[24], (DEPTH, D_FF), 0.02),
        'w_down': nrm(ks[25], (DEPTH, D_FF, D_MODEL), BETA * D_FF ** -0.5),
        'ln2_g': 1.0 + nrm(ks[26], (DEPTH, D_MODEL), 0.02),
        'ln2_b': nrm(ks[27], (DEPTH, D_MODEL), 0.02),
    }


def reference(x_prompt, x_sample, cache_k_win, cache_v_win, state_ssm_re, state_ssm_im, state_ffn_conv,
              w_in, ssm_lam_re, ssm_lam_im, ssm_log_dt, ssm_b_re, ssm_b_im, ssm_c_re, ssm_c_im, ssm_d,
              w_glu, attn_sinks, w_attn_br, w_o, ln1_g, ln1_b, w_up, conv_w, conv_b, w_down, ln2_g, ln2_b):
    yp, ys = x_prompt, x_sample
    kp_l, vp_l, hrp_l, hip_l, cp_l = [], [], [], [], []
    ks_l, vs_l, hrs_l, his_l, cs_l = [], [], [], [], []
    for i in range(DEPTH):
        lp = {'w_in': w_in[i], 'lam_re': ssm_lam_re[i], 'lam_im': ssm_lam_im[i], 'log_dt': ssm_log_dt[i],
              'b_re': ssm_b_re[i], 'b_im': ssm_b_im[i], 'c_re': ssm_c_re[i], 'c_im': ssm_c_im[i], 'd': ssm_d[i],
              'w_glu': w_glu[i], 'sinks': attn_sinks[i], 'w_attn_br': w_attn_br[i], 'w_o': w_o[i],
              'ln1_g': ln1_g[i], 'ln1_b': ln1_b[i], 'w_up': w_up[i], 'conv_w': conv_w[i], 'conv_b': conv_b[i],
              'w_down': w_down[i], 'ln2_g': ln2_g[i], 'ln2_b': ln2_b[i]}
        h0 = jnp.zeros((yp.shape[0], N_GROUPS, STATE_DIM), jnp.float32)
        c0 = jnp.zeros((yp.shape[0], CONV_W - 1, D_FF), yp.dtype)
        yp, kp, vp, hrp, hip, cp = hybrid_layer(yp, window_attn_prompt, h0, h0, c0, lp)
        attn_s = functools.partial(window_attn_sample, k_cache=cache_k_win[i], v_cache=cache_v_win[i])
        ys, kss, vss, hrs, his, css = hybrid_layer(ys, attn_s, state_ssm_re[i], state_ssm_im[i],
                                                  state_ffn_conv[i], lp)
        kp_l.append(kp); vp_l.append(vp); hrp_l.append(hrp); hip_l.append(hip); cp_l.append(cp)
        ks_l.append(kss); vs_l.append(vss); hrs_l.append(hrs); his_l.append(his); cs_l.append(css)
    return (yp, ys,
            jnp.stack(kp_l), jnp.stack(vp_l), jnp.stack(ks_l), jnp.stack(vs_l),
            jnp.stack(hrp_l), jnp.stack(hip_l), jnp.stack(hrs_l), jnp.stack(his_l),
            jnp.stack(cp_l), jnp.stack(cs_l))
```

```python
import math
from contextlib import ExitStack

import numpy as np
import concourse.bass as bass
import concourse.mybir as mybir
from concourse.bass_utils import run_bass_kernel_spmd

F32 = mybir.dt.float32
BF16 = mybir.dt.bfloat16
I32 = mybir.dt.int32
AF = mybir.ActivationFunctionType
ALU = mybir.AluOpType

D = 1024
L = 2048
NSEQ = 16
TS = 4
NSAMP = NSEQ * TS
NTOK = L + NSAMP
G = 32
P = 64
CH = 16
T = 8
DFF = 2816
NF = DFF // 128
DIN = 3328
ALPHA = 2.0 ** 0.25
LN_EPS = 1e-5
TWO_PI = 2.0 * math.pi
C1 = 6.28125
C2 = TWO_PI - C1
MAGIC = 12582912.0
NEG = -30000.0
BLOCKS = [(0, 512), (512, 512), (1024, 512), (1536, 512), (2048, 64)]
DEBUG = {}
TAPS = set()
SKIP = set()


def psplit(ap, c):
    (pstep, npart), (estep, n) = ap.ap
    return bass.AP(tensor=ap.tensor, offset=ap.offset, ap=[[pstep, c], [pstep * c, npart // c], [estep, n]])


class Buf:
    __slots__ = ("t", "wr", "rd", "name", "psum")

    def __init__(self, t, name):
        self.t = t
        self.wr = []
        self.rd = {}
        self.name = name
        self.psum = False

    def __getitem__(self, k):
        return self.t[k]


class Eng:
    def __init__(self, name, eng, sem, dma_sems=None):
        self.name = name
        self.eng = eng
        self.sem = sem
        self.count = 0
        self.waited = {}
        self.pool = [[s, 0] for s in (dma_sems or [])]
        self.pidx = 0

    def wait_ev(self, ev, force=False):
        if ev is None:
            return
        sem, val = ev
        if sem is self.sem and not force and self.name == "pe":
            return
        k = id(sem)
        if self.waited.get(k, 0) >= val:
            return
        self.eng.wait_ge(sem, val)
        self.waited[k] = val

    def deps(self, reads, writes, par=False):
        for b in reads:
            for ev in b.wr:
                self.wait_ev(ev)
            if b.psum:
                for ev in b.rd.values():
                    self.wait_ev(ev)
        for b in writes:
            if not par:
                for ev in b.wr:
                    self.wait_ev(ev)
            for ev in b.rd.values():
                self.wait_ev(ev)

    def record(self, ev, reads, writes, par=False):
        k = id(ev[0])
        for b in reads:
            b.rd[k] = ev
        for b in writes:
            if par:
                b.wr = [e for e in b.wr if id(e[0]) != k] + [ev]
            else:
                b.wr = [ev]
            b.rd = {}

    def op(self, method, reads, writes, signal=True, **kw):
        self.deps(reads, writes)
        ins = getattr(self.eng, method)(**kw)
        if signal:
            self.count += 1
            ins.then_inc(self.sem, 1)
            ev = (self.sem, self.count)
        else:
            ev = (self.sem, self.count + 1)
        self.record(ev, reads, writes)
        return ev

    def dma(self, out, in_, reads, writes, par=False, **kw):
        self.deps(reads, writes, par)
        slot = self.pool[self.pidx % len(self.pool)]
        self.pidx += 1
        sem, cnt = slot
        if cnt > 0:
            self.wait_ev((sem, cnt))
        self.eng.dma_start(out=out, in_=in_, **kw).then_inc(sem, 16)
        slot[1] = cnt + 16
        ev = (sem, cnt + 16)
        self.record(ev, reads, writes, par)
        return ev


class KB:
    def __init__(self):
        self.nc = bass.Bass("TRN2", target_bir_lowering=False)
        self.out_events = []

    def dram_in(self, name, shape, dt=F32):
        return self.nc.dram_tensor(name, list(shape), dt, kind="ExternalInput").ap()

    def dram_out(self, name, shape, dt=F32):
        return self.nc.dram_tensor(name, list(shape), dt, kind="ExternalOutput").ap()

    def sb(self, es, name, shape, dt=F32):
        return Buf(es.enter_context(self.nc.sbuf_tensor("sb_" + name, list(shape), dt)), name)

    def barrier(self):
        engs = [self.pe, self.act, self.dve, self.pool, self.sp]
        evs = [(e.sem, e.count) for e in engs if e.sem is not None and e.count > 0]
        for e in engs:
            for s, c in e.pool:
                if c > 0:
                    evs.append((s, c))
        for e in engs:
            for ev in evs:
                e.wait_ev(ev, force=True)

    def tap(self, name, buf, ap=None):
        if name not in TAPS:
            return
        ap = buf[:] if ap is None else ap
        shp = list(ap.shape)
        o = self.dram_out("tap_" + name, shp, ap.dtype)
        self.out_events.append(self.sp.dma(o, ap, [buf], []))

    def next_ps(self):
        b = self.psb[self.psi % 8]
        self.psi += 1
        return b

    def V(self, method, reads, writes, **kw):
        return self.dve.op(method, reads, writes, **kw)

    def A(self, reads, writes, **kw):
        return self.act.op("activation", reads, writes, **kw)

    def Pl(self, method, reads, writes, **kw):
        return self.pool.op(method, reads, writes, **kw)

    def mm(self, ps, out, lhsT, rhs, reads, start=True, stop=True, signal=None):
        if signal is None:
            signal = stop
        return self.pe.op("matmul", reads, [ps], signal=signal, out=out, lhsT=lhsT, rhs=rhs,
                          start=start, stop=stop)

    def tr(self, ps, out, in_, ident, reads, signal=True):
        return self.pe.op("transpose", reads, [ps], signal=signal, out=out, in_=in_, identity=ident)

    def sin_rr(self, out_b, out_ap, in_b, in_ap, shift, tmp_bs, tmp_aps):
        (ty, tn, tr_) = tmp_aps
        (by, bn, br) = tmp_bs
        V = self.V
        if shift != 0.0:
            V("tensor_scalar", [in_b], [by], out=ty, in0=in_ap, scalar1=float(shift), scalar2=None, op0=ALU.add)
            yb, y = by, ty
        else:
            yb, y = in_b, in_ap
        V("tensor_scalar", [yb], [bn], out=tn, in0=y, scalar1=1.0 / TWO_PI, scalar2=MAGIC, op0=ALU.mult, op1=ALU.add)
        V("tensor_scalar", [bn], [bn], out=tn, in0=tn, scalar1=MAGIC, scalar2=None, op0=ALU.subtract)
        V("scalar_tensor_tensor", [bn, yb], [br], out=tr_, in0=tn, scalar=-C1, in1=y, op0=ALU.mult, op1=ALU.add)
        V("scalar_tensor_tensor", [bn, br], [br], out=tr_, in0=tn, scalar=-C2, in1=tr_, op0=ALU.mult, op1=ALU.add)
        V("tensor_scalar", [br], [br], out=tr_, in0=tr_, scalar1=-math.pi, scalar2=math.pi, op0=ALU.max, op1=ALU.min)
        self.A([br], [out_b], out=out_ap, in_=tr_, func=AF.Sin)

    def build(self):
        nc = self.nc
        self.ins = {}
        I = self.ins
        I["x"] = self.dram_in("x", [NTOK, D])
        I["ck"] = self.dram_in("ck", [NSEQ, 128, 128])
        I["cv"] = self.dram_in("cv", [NSEQ, 128, 128])
        I["sre"] = self.dram_in("sre", [NSEQ * G, P])
        I["sim"] = self.dram_in("sim", [NSEQ * G, P])
        I["sconv"] = self.dram_in("sconv", [NSEQ * 2, DFF])
        I["w_in"] = self.dram_in("w_in", [D, DIN])
        I["lam_re"] = self.dram_in("lam_re", [G, P])
        I["lam_im"] = self.dram_in("lam_im", [G, P])
        I["log_dt"] = self.dram_in("log_dt", [1, G])
        I["b_re"] = self.dram_in("b_re", [G, P, CH])
        I["b_im"] = self.dram_in("b_im", [G, P, CH])
        I["c_re"] = self.dram_in("c_re", [G * CH, P])
        I["c_im"] = self.dram_in("c_im", [G * CH, P])
        I["d"] = self.dram_in("d", [G, CH])
        I["w_glu"] = self.dram_in("w_glu", [512, 2048])
        I["sinks"] = self.dram_in("sinks", [1, 8])
        I["w_attn"] = self.dram_in("w_attn", [512, D])
        I["w_o"] = self.dram_in("w_o", [D, D])
        I["ln1_g"] = self.dram_in("ln1_g", [1, D])
        I["ln1_b"] = self.dram_in("ln1_b", [1, D])
        I["w_up"] = self.dram_in("w_up", [D, 2 * DFF])
        I["conv_w"] = self.dram_in("conv_w", [3, DFF])
        I["conv_b"] = self.dram_in("conv_b", [1, DFF])
        I["w_down"] = self.dram_in("w_down", [DFF, D])
        I["ln2_g"] = self.dram_in("ln2_g", [1, D])
        I["ln2_b"] = self.dram_in("ln2_b", [1, D])
        self.outs = {}
        O = self.outs
        O["y"] = self.dram_out("y", [NTOK, D])
        O["kwp"] = self.dram_out("kwp", [128, 128])
        O["vwp"] = self.dram_out("vwp", [128, 128])
        O["kws"] = self.dram_out("kws", [NSEQ, 128, 128])
        O["vws"] = self.dram_out("vws", [NSEQ, 128, 128])
        O["hrp"] = self.dram_out("hrp", [G, P])
        O["hip"] = self.dram_out("hip", [G, P])
        O["hrs"] = self.dram_out("hrs", [NSEQ * G, P])
        O["his"] = self.dram_out("his", [NSEQ * G, P])
        O["cp"] = self.dram_out("cp", [2, DFF])
        O["cs"] = self.dram_out("cs", [NSEQ * 2, DFF])
        self.x1d = nc.dram_tensor("x1d", [NTOK, D], F32, kind="Internal").ap()
        for k, (shp, dt_) in DEBUG.items():
            O[k] = self.dram_out(k, shp, dt_)

        with ExitStack() as es0:
            E = es0.enter_context
            sems = [E(nc.semaphore("s%d" % i)) for i in range(4)]
            dsem_sp = [E(nc.semaphore("dsp%d" % i)) for i in range(40)]
            dsem_pl = [E(nc.semaphore("dpl%d" % i)) for i in range(40)]
            self.psb = [Buf(E(nc.psum_tensor("ps%d" % i, [128, 512], F32)), "ps%d" % i) for i in range(8)]
            self.psi = 0
            for b_ in self.psb:
                b_.psum = True
            block = E(nc.Block())
            self.pe = Eng("pe", nc.tensor, sems[0])
            self.act = Eng("act", nc.scalar, sems[1])
            self.dve = Eng("dve", nc.vector, sems[2])
            self.pool = Eng("pool", nc.gpsimd, sems[3], dsem_pl)
            self.sp = Eng("sp", nc.sync, None, dsem_sp)
            self.ident = self.sb(es0, "ident", [128, 128], BF16)
            self.identf = self.sb(es0, "identf", [128, 128], F32)
            self.mhalf = self.sb(es0, "mhalf", [128, 1], F32)
            self.consts()
            with ExitStack() as esSA:
                self.pass_s()
                self.barrier()
                if "A" not in SKIP:
                    self.pass_a()
            self.barrier()
            if "dbg_x1" in DEBUG:
                self.out_events.append(self.sp.dma(O["dbg_x1"], self.x1d, [], []))
            if "B" not in SKIP:
                self.pass_b()
            for ev in self.out_events:
                self.sp.wait_ev(ev)
            self.barrier()
        return nc

    def consts(self):
        Pl = self.Pl
        Pl("memset", [], [self.identf], ap=self.identf[:], constant=0.0)
        Pl("memset", [], [self.mhalf], ap=self.mhalf[:], constant=-0.5)
        self.barrier()
        Pl("affine_select", [self.identf], [self.identf], out=self.identf[:], in_=self.identf[:], pattern=[[-1, 128]],
           compare_op=ALU.not_equal, fill=1.0, base=0, channel_multiplier=1)
        Pl("tensor_copy", [self.identf], [self.ident], out=self.ident[:], in_=self.identf[:])

    def pass_s(self):
        nc = self.nc
        I, O = self.ins, self.outs
        V, A, Pl = self.V, self.A, self.Pl
        sp, pool = self.sp, self.pool
        with ExitStack() as es:
            sb = lambda n, s, d=F32: self.sb(es, n, s, d)
            Toep = sb("Toep", [128, G, 128], BF16)
            Wend = sb("Wend", [128, G, 128], BF16)
            Wend_sw = sb("Wend_sw", [128, G, 128], BF16)
            Cpow = sb("Cpow", [128, G, 128], BF16)
            Cpow_sw = sb("Cpow_sw", [128, G, 128], BF16)
            CT = sb("CT", [128, G, 64])
            ST = sb("ST", [128, G, 64])
            RHOT = sb("RHOT", [128, G, 64])
            RHO1 = sb("RHO1", [128, G])
            AR4 = sb("AR4", [128, G]); AI4 = sb("AI4", [128, G]); ARm4 = sb("ARm4", [128, G]); AIm4 = sb("AIm4", [128, G])
            Drep = sb("Drep", [128, G])
            w_u = sb("w_u", [128, 8, 512], BF16)
            SGN = sb("SGN", [128, 1]); NSGN = sb("NSGN", [128, 1])
            Hin = sb("Hin", [128, G])
            self._w_u = w_u
            NC_ = L // T
            U = sb("s_U", [128, G, NC_], BF16)
            Us = sb("s_Us", [128, G, NSEQ], BF16)
            Pl("memset", [], [Us], ap=Us[:], constant=0.0)
            self._U, self._Us = U, Us
            with ExitStack() as esp:
                gens = [self.ssm_prep(esp, Toep, Wend, Wend_sw, Cpow, Cpow_sw, CT, ST, RHOT, RHO1, AR4, AI4, ARm4, AIm4,
                                      Drep, SGN, NSGN, Hin), self.ssm_stage1(esp, w_u)]
                first = True
                while gens:
                    for g_ in list(gens):
                        try:
                            next(g_)
                        except StopIteration:
                            gens.remove(g_)
                        if first:
                            first = False
                ev1 = sp.dma(self.scrU.rearrange("j p n -> p j n"), self._uT[:], [self._uT], [])
                ev2 = sp.dma(self.scrUs.rearrange("j p n -> p j n"), self._uTs[:], [self._uTs], [])
                sp.wait_ev(ev1); sp.wait_ev(ev2)
                for g in range(G):
                    j, gl = divmod(g, 8)
                    src = bass.AP(tensor=self.scrU.tensor, offset=(j * 128 + gl * 16) * L, ap=[[NC_, T], [L, CH], [1, NC_]])
                    sp.dma(U[:, g, :], src, [], [U], par=True)
                    srcs = bass.AP(tensor=self.scrUs.tensor, offset=(j * 128 + gl * 16) * NSAMP, ap=[[NSEQ, TS], [NSAMP, CH], [1, NSEQ]])
                    sp.dma(Us[64:128, g, :], srcs, [], [Us], par=True)
                for gh in range(self._prep_nparts):
                    self._prep_part2(gh)
                self.ssm_taps(Toep, Wend, Wend_sw, Cpow, Cpow_sw)
                self.barrier()
            self.ssm_main(es, Toep, Wend, Wend_sw, Cpow, Cpow_sw, CT, ST, RHOT, RHO1, AR4, AI4, ARm4, AIm4,
                          Drep, w_u, Hin)

    def ssm_prep(self, es, Toep, Wend, Wend_sw, Cpow, Cpow_sw, CT, ST, RHOT, RHO1, AR4, AI4, ARm4, AIm4, Drep,
                 SGN, NSGN, Hin):
        I = self.ins
        V, A, Pl = self.V, self.A, self.Pl
        sp = self.sp
        sb = lambda n, s, d=F32: self.sb(es, n, s, d)
        lamre2 = sb("lamre2", [128, G]); lamim2 = sb("lamim2", [128, G]); logdt2 = sb("logdt2", [128, G])
        BA = sb("BA", [128, G, CH]); BB = sb("BB", [128, G, CH])
        Ct_re = sb("Ct_re", [128, 4, 128]); Ct_im = sb("Ct_im", [128, 4, 128])
        CRE2 = sb("CRE2", [128, G, CH]); CIM2 = sb("CIM2", [128, G, CH])
        def _loads():
            w_u = self._w_u
            wv = I["w_in"].rearrange("(k p) n -> p k n", p=128)
            for k in range(8):
                self.pool.dma(w_u[:, k, :], wv[:, k, 0:512], [], [w_u], par=True)
            for h in range(2):
                sl = slice(64 * h, 64 * h + 64)
                sp.dma(lamre2[sl, :], I["lam_re"].rearrange("g p -> p g"), [], [lamre2], allow_slow_non_contiguous=True, par=True)
                sp.dma(lamim2[sl, :], I["lam_im"].rearrange("g p -> p g"), [], [lamim2], allow_slow_non_contiguous=True, par=True)
            sp.dma(logdt2[:], I["log_dt"][0:1, :].partition_broadcast(128), [], [logdt2])
            sp.dma(BA[0:64, :, :], I["b_re"].rearrange("g p c -> p g c"), [], [BA], par=True)
            sp.dma(BA[64:128, :, :], I["b_im"].rearrange("g p c -> p g c"), [], [BA], par=True)
            sp.dma(BB[0:64, :, :], I["b_im"].rearrange("g p c -> p g c"), [], [BB], par=True)
            sp.dma(BB[64:128, :, :], I["b_re"].rearrange("g p c -> p g c"), [], [BB], par=True)
            for j in range(4):
                for h in range(2):
                    sp.dma(Ct_re[:, j, 64 * h:64 * h + 64], I["c_re"][128 * j:128 * j + 128, :], [], [Ct_re], par=True)
                    sp.dma(Ct_im[:, j, 64 * h:64 * h + 64], I["c_im"][128 * j:128 * j + 128, :], [], [Ct_im], par=True)
            for r in range(T):
                sp.dma(Drep[16 * r:16 * r + 16, :], I["d"].rearrange("g c -> c g"), [], [Drep], allow_slow_non_contiguous=True, par=True)
        self.issue_prep_loads = _loads
        Pl("memset", [], [SGN], ap=SGN[0:64, :], constant=1.0)
        Pl("memset", [], [SGN], ap=SGN[64:128, :], constant=-1.0)
        Pl("memset", [], [NSGN], ap=NSGN[0:64, :], constant=-1.0)
        Pl("memset", [], [NSGN], ap=NSGN[64:128, :], constant=1.0)
        Pl("memset", [], [Hin], ap=Hin[:], constant=0.0)
        JVi = sb("JVi", [128, 64], I32); JV = sb("JV", [128, 64])
        KVi = sb("KVi", [128, 32], I32); KV = sb("KV", [128, 32])
        Pl("iota", [], [JVi], out=JVi[:], pattern=[[1, 64]], base=1, channel_multiplier=0)
        Pl("iota", [], [KVi], out=KVi[:, 0:16], pattern=[[1, 16]], base=-7, channel_multiplier=0)
        Pl("iota", [], [KVi], out=KVi[:, 16:32], pattern=[[-1, 16]], base=8, channel_multiplier=0)
        ones = sb("ones", [128, 128]); tmask = sb("tmask", [128, 128])
        Pl("memset", [], [ones], ap=ones[:], constant=1.0)
        self.barrier()
        self.issue_prep_loads()
        Pl("tensor_copy", [JVi], [JV], out=JV[:], in_=JVi[:])
        Pl("tensor_copy", [KVi], [KV], out=KV[:], in_=KVi[:])
        for (Ct, Cdst) in ((Ct_re, CRE2), (Ct_im, CIM2)):
            for j in range(4):
                ps = self.next_ps()
                self.tr(ps, ps[:, 0:128], Ct[:, j, :], self.identf[:], [Ct, self.identf])
                A([ps], [Cdst], out=Cdst[:, 8 * j:8 * j + 8, :], in_=ps[:, 0:128].rearrange("p (g c) -> p g c", c=CH), func=AF.Copy)
                yield
        dt = sb("dt", [128, G]); ell = sb("ell", [128, G]); th = sb("th", [128, G])
        den = sb("den", [128, G]); t0 = sb("t0", [128, G]); wr = sb("wr", [128, G]); wi = sb("wi", [128, G])
        A([logdt2], [dt], out=dt[:], in_=logdt2[:], func=AF.Exp)
        yield
        V("tensor_tensor", [lamre2, dt], [ell], out=ell[:], in0=lamre2[:], in1=dt[:], op=ALU.mult)
        V("tensor_tensor", [lamim2, dt], [th], out=th[:], in0=lamim2[:], in1=dt[:], op=ALU.mult)
        V("tensor_tensor", [lamre2], [den], out=den[:], in0=lamre2[:], in1=lamre2[:], op=ALU.mult)
        V("tensor_tensor", [lamim2], [t0], out=t0[:], in0=lamim2[:], in1=lamim2[:], op=ALU.mult)
        V("tensor_tensor", [den, t0], [den], out=den[:], in0=den[:], in1=t0[:], op=ALU.add)
        V("reciprocal", [den], [den], out=den[:], in_=den[:])
        V("tensor_tensor", [lamre2, den], [wr], out=wr[:], in0=lamre2[:], in1=den[:], op=ALU.mult)
        V("scalar_tensor_tensor", [lamim2, den], [wi], out=wi[:], in0=lamim2[:], scalar=-1.0, in1=den[:], op0=ALU.mult, op1=ALU.mult)
        kvals = list(range(-7, 9)) + list(range(8, -8, -1))
        ELLK = sb("ELLK", [128, G, 32]); ANGK = sb("ANGK", [128, G, 32])
        V("tensor_tensor", [ell, KV], [ELLK], out=ELLK[:], in0=ell[:].unsqueeze(2).to_broadcast([128, G, 32]),
          in1=KV[:].unsqueeze(1).to_broadcast([128, G, 32]), op=ALU.mult)
        V("tensor_tensor", [th, KV], [ANGK], out=ANGK[:], in0=th[:].unsqueeze(2).to_broadcast([128, G, 32]),
          in1=KV[:].unsqueeze(1).to_broadcast([128, G, 32]), op=ALU.mult)
        MAGK = ELLK; RR = sb("RR", [128, G, 32]); II = sb("II", [128, G, 32])
        ty = sb("ty", [128, G * 32]); tn = sb("tn", [128, G * 32]); trr = sb("trr", [128, G * 32])
        f1 = lambda b, n=G * 32: b[:, 0:n]
        fl = lambda b: b[:].rearrange("p g k -> p (g k)")
        A([ELLK], [MAGK], out=fl(MAGK), in_=fl(ELLK), func=AF.Exp)
        yield
        self.sin_rr(II, fl(II), ANGK, fl(ANGK), 0.0, (ty, tn, trr), (f1(ty), f1(tn), f1(trr)))
        yield
        self.sin_rr(RR, fl(RR), ANGK, fl(ANGK), math.pi / 2, (ty, tn, trr), (f1(ty), f1(tn), f1(trr)))
        yield
        V("tensor_tensor", [RR, MAGK], [RR], out=fl(RR), in0=fl(RR), in1=fl(MAGK), op=ALU.mult)
        V("tensor_tensor", [II, MAGK], [II], out=fl(II), in0=fl(II), in1=fl(MAGK), op=ALU.mult)
        self.tap("RR", RR); self.tap("II", II); self.tap("th", th); self.tap("ell", ell); self.tap("CRE2", CRE2); self.tap("BA", BA)
        V("tensor_copy", [RR], [AR4], out=AR4[:], in_=RR[:, :, 11])
        V("tensor_scalar", [II, NSGN], [AI4], out=AI4[:], in0=II[:, :, 11], scalar1=NSGN[:, 0:1], scalar2=None, op0=ALU.mult)
        V("tensor_copy", [RR], [ARm4], out=ARm4[:], in_=RR[:, :, 3])
        V("tensor_scalar", [II, NSGN], [AIm4], out=AIm4[:], in0=II[:, :, 3], scalar1=NSGN[:, 0:1], scalar2=None, op0=ALU.mult)
        ph = sb("ph", [128, G]); ANGJ = sb("ANGJ", [128, 16, 64])
        V("tensor_scalar", [th], [ph], out=ph[:], in0=th[:], scalar1=8.0, scalar2=None, op0=ALU.mult)
        V("tensor_scalar", [ph], [t0], out=t0[:], in0=ph[:], scalar1=1.0 / TWO_PI, scalar2=MAGIC, op0=ALU.mult, op1=ALU.add)
        V("tensor_scalar", [t0], [t0], out=t0[:], in0=t0[:], scalar1=MAGIC, scalar2=None, op0=ALU.subtract)
        V("scalar_tensor_tensor", [t0, ph], [ph], out=ph[:], in0=t0[:], scalar=-C1, in1=ph[:], op0=ALU.mult, op1=ALU.add)
        V("scalar_tensor_tensor", [t0, ph], [ph], out=ph[:], in0=t0[:], scalar=-C2, in1=ph[:], op0=ALU.mult, op1=ALU.add)
        f2 = lambda a: a.rearrange("p g j -> p (g j)")
        for gh in range(2):
            hs_ = slice(16 * gh, 16 * gh + 16)
            V("tensor_tensor", [ph, JV], [ANGJ], out=ANGJ[:], in0=ph[:, hs_].unsqueeze(2).to_broadcast([128, 16, 64]),
              in1=JV[:].unsqueeze(1).to_broadcast([128, 16, 64]), op=ALU.mult)
            self.sin_rr(CT, f2(CT[:, hs_, :]), ANGJ, f2(ANGJ[:]), math.pi / 2, (ty, tn, trr), (ty[:], tn[:], trr[:]))
            yield
            self.sin_rr(ST, f2(ST[:, hs_, :]), ANGJ, f2(ANGJ[:]), 0.0, (ty, tn, trr), (ty[:], tn[:], trr[:]))
            yield
        V("tensor_scalar", [ST, SGN], [ST], out=f2(ST[:]), in0=f2(ST[:]), scalar1=SGN[:, 0:1], scalar2=None, op0=ALU.mult)
        A([ell], [RHO1], out=RHO1[:], in_=ell[:], func=AF.Exp, scale=8.0)
        yield
        V("tensor_copy", [RHO1], [RHOT], out=RHOT[:], in_=RHO1[:].unsqueeze(2).to_broadcast([128, G, 64]))
        V("tensor_scalar", [RHOT], [RHOT], out=RHOT[:, :, 0], in0=RHOT[:, :, 0], scalar1=0.0, scalar2=None, op0=ALU.mult)
        self.tap("CT", CT); self.tap("ST", ST); self.tap("RHOT", RHOT); self.tap("ph", ph); self.tap("JV", JV); self.tap("ANGJ", ANGJ)
        NRR = sb("NRR", [128, G, 16]); NII = sb("NII", [128, G, 16])
        V("tensor_scalar", [RR], [NRR], out=NRR[:], in0=RR[:, :, 0:16], scalar1=-1.0, scalar2=None, op0=ALU.mult)
        V("tensor_scalar", [II], [NII], out=NII[:], in0=II[:, :, 0:16], scalar1=-1.0, scalar2=None, op0=ALU.mult)
        X = [sb("X%d" % i, [128, G, 16]) for i in range(4)]
        lo, hi = slice(0, 64), slice(64, 128)
        srcs = [(RR, NII), (NII, NRR), (NII, RR), (NRR, NII)]
        for i in range(4):
            bl, bh = srcs[i]
            V("tensor_copy", [bl], [X[i]], out=X[i][lo, :, :], in_=bl[lo, :, 0:16])
            V("tensor_copy", [bh], [X[i]], out=X[i][hi, :, :], in_=bh[hi, :, 0:16])
        DR = sb("DR", [128, G, 16]); DI = sb("DI", [128, G, 16])
        V("tensor_tensor", [RR], [DR], out=DR[:, :, 1:16], in0=RR[:, :, 16:31], in1=RR[:, :, 17:32], op=ALU.subtract)
        V("tensor_tensor", [II], [DI], out=DI[:, :, 1:16], in0=II[:, :, 16:31], in1=II[:, :, 17:32], op=ALU.subtract)
        QWR = sb("QWR", [128, G, 16]); QWI = sb("QWI", [128, G, 16]); q1 = sb("q1", [128, G, 16])
        wrb = wr[:].unsqueeze(2).to_broadcast([128, G, 15]); wib = wi[:].unsqueeze(2).to_broadcast([128, G, 15])
        s15 = (slice(None), slice(None), slice(1, 16))
        V("tensor_tensor", [DR, wr], [QWR], out=QWR[s15], in0=DR[s15], in1=wrb, op=ALU.mult)
        V("tensor_tensor", [DI, wi], [q1], out=q1[s15], in0=DI[s15], in1=wib, op=ALU.mult)
        V("tensor_tensor", [QWR, q1], [QWR], out=QWR[s15], in0=QWR[s15], in1=q1[s15], op=ALU.subtract)
        V("tensor_tensor", [DR, wi], [QWI], out=QWI[s15], in0=DR[s15], in1=wib, op=ALU.mult)
        V("tensor_tensor", [DI, wr], [q1], out=q1[s15], in0=DI[s15], in1=wrb, op=ALU.mult)
        V("tensor_tensor", [QWI, q1], [QWI], out=QWI[s15], in0=QWI[s15], in1=q1[s15], op=ALU.add)
        V("tensor_scalar", [QWI, NSGN], [QWI], out=QWI[s15], in0=QWI[s15], scalar1=NSGN[:, 0:1], scalar2=None, op0=ALU.mult)
        Pl("affine_select", [ones], [tmask], out=tmask[:].rearrange("p (r c) -> p r c", c=CH),
           in_=ones[:].rearrange("p (r c) -> p r c", c=CH), pattern=[[16, 8], [0, 16]], compare_op=ALU.is_ge, fill=0.0,
           base=15, channel_multiplier=-1)
        GH = 8
        class _View:
            def __init__(self, buf):
                self.buf = buf
        tA, tB, tC, tD = ty, tn, trr, ANGJ
        v4t = {id(ty): lambda: ty[:, 0:1024].rearrange("p (g r c) -> p g r c", g=GH, r=T),
               id(tn): lambda: tn[:, 0:1024].rearrange("p (g r c) -> p g r c", g=GH, r=T),
               id(trr): lambda: trr[:, 0:1024].rearrange("p (g r c) -> p g r c", g=GH, r=T),
               id(ANGJ): lambda: ANGJ[:].rearrange("p a (b c) -> p (a b) c", c=CH).rearrange("p (g r) c -> p g r c", r=T)}
        Cp0 = sb("Cp0", [128, GH, 128]); Bp0 = sb("Bp0", [128, GH, 128]); Bp7 = sb("Bp7", [128, GH, 128])
        v4 = lambda a: a.rearrange("p g (r c) -> p g r c", c=CH)
        def part2(gh):
            hs_ = slice(GH * gh, GH * gh + GH)

            def cmat(outb, out4, Xa, Xb, ki0, E=V, ta=tA, tb=tB):
                c_re_b = CRE2[:, hs_, :].unsqueeze(2).to_broadcast([128, GH, T, CH])
                c_im_b = CIM2[:, hs_, :].unsqueeze(2).to_broadcast([128, GH, T, CH])
                xa = Xa[:, hs_, ki0:ki0 + T].unsqueeze(3).to_broadcast([128, GH, T, CH])
                xb = Xb[:, hs_, ki0:ki0 + T].unsqueeze(3).to_broadcast([128, GH, T, CH])
                E("tensor_tensor", [CRE2, Xa], [ta], out=v4t[id(ta)](), in0=c_re_b, in1=xa, op=ALU.mult)
                E("tensor_tensor", [CIM2, Xb], [tb], out=v4t[id(tb)](), in0=c_im_b, in1=xb, op=ALU.mult)
                E("tensor_tensor", [ta, tb], [outb], out=out4, in0=v4t[id(ta)](), in1=v4t[id(tb)](), op=ALU.add)

            def bmat(outb, i0, E=V, ta=tA, tb=tB):
                ba = BA[:, hs_, :].unsqueeze(2).to_broadcast([128, GH, T, CH])
                bb = BB[:, hs_, :].unsqueeze(2).to_broadcast([128, GH, T, CH])
                qa = QWR[:, hs_, i0:i0 + T].unsqueeze(3).to_broadcast([128, GH, T, CH])
                qb = QWI[:, hs_, i0:i0 + T].unsqueeze(3).to_broadcast([128, GH, T, CH])
                E("tensor_tensor", [BA, QWR], [ta], out=v4t[id(ta)](), in0=ba, in1=qa, op=ALU.mult)
                E("tensor_tensor", [BB, QWI], [tb], out=v4t[id(tb)](), in0=bb, in1=qb, op=ALU.mult)
                E("tensor_tensor", [ta, tb], [outb], out=v4(outb[:]), in0=v4t[id(ta)](), in1=v4t[id(tb)](), op=ALU.add)

            bmat(Bp7, 1, Pl, tC, tD)
            cmat(Cp0, v4(Cp0[:]), X[0], X[1], 7)
            bmat(Bp0, 8)
            cmat(Cpow, v4(Cpow[:, hs_, :]), X[0], X[1], 8)
            cmat(Cpow_sw, v4(Cpow_sw[:, hs_, :]), X[2], X[3], 8)
            for gl in range(GH):
                g = GH * gh + gl
                ps = self.next_ps()
                self.mm(ps, ps[:, 0:128], Bp0[:, gl, :], Cp0[:, gl, :], [Bp0, Cp0])
                V("tensor_tensor", [ps, tmask], [Toep], out=Toep[:, g, :], in0=ps[:, 0:128], in1=tmask[:], op=ALU.mult)
                ps2 = self.next_ps()
                self.tr(ps2, ps2[:, 0:128], Bp7[:, gl, :], self.identf[:], [Bp7, self.identf])
                A([ps2], [Wend], out=Wend[:, g, :], in_=ps2[:, 0:128], func=AF.Copy)
                A([ps2], [Wend_sw], out=Wend_sw[:, g, 0:64], in_=ps2[:, 64:128], func=AF.Copy)
                A([ps2], [Wend_sw], out=Wend_sw[:, g, 64:128], in_=ps2[:, 0:64], func=AF.Copy)

        self._prep_part2 = part2
        self._prep_nparts = G // GH

    def ssm_taps(self, Toep, Wend, Wend_sw, Cpow, Cpow_sw):
        self.tap("Toep", Toep); self.tap("Wend", Wend); self.tap("Wend_sw", Wend_sw); self.tap("Cpow", Cpow); self.tap("Cpow_sw", Cpow_sw)

    def ssm_stage1(self, es, w_u):
        nc = self.nc
        I = self.ins
        A, pool, sp = self.A, self.pool, self.sp
        sb = lambda n, s, d=F32: self.sb(es, n, s, d)
        NC = L // T
        self.scrU = nc.dram_tensor("scrU", [4, 128, L], BF16, kind="Internal").ap()
        self.scrUs = nc.dram_tensor("scrUs", [4, 128, NSAMP], BF16, kind="Internal").ap()
        x_bf = sb("s_xbf", [128, 4, D], BF16)
        xT = sb("s_xT", [128, 8, 512], BF16)
        uT = sb("s_uT", [128, 4, L], BF16)
        uTs = sb("s_uTs", [128, 4, NSAMP], BF16)
        xall = I["x"]
        nblk = len(BLOCKS)
        self.load_tokens(xall, 0, x_bf)
        for bi in range(nblk):
            t0_, nt = BLOCKS[bi]
            samp = (bi == nblk - 1)
            self.transposes(x_bf, xT, nt)
            if bi + 1 < nblk:
                self.load_tokens(xall, bi + 1, x_bf)
            yield
            for j in range(4):
                ps = self.next_ps()
                for k in range(8):
                    self.mm(ps, ps[:, 0:nt], w_u[:, k, 128 * j:128 * j + 128], xT[:, k, 0:nt], [w_u, xT],
                            start=(k == 0), stop=(k == 7))
                if not samp:
                    A([ps], [uT], out=uT[:, j, :].rearrange("p (r c) -> p r c", c=NC)[:, :, 64 * bi:64 * bi + 64],
                      in_=ps[:, 0:512].rearrange("p (c r) -> p r c", r=T), func=AF.Copy)
                else:
                    A([ps], [uTs], out=uTs[:, j, :].rearrange("p (t s) -> p t s", s=NSEQ),
                      in_=ps[:, 0:64].rearrange("p (s t) -> p t s", t=TS), func=AF.Copy)
                yield
        self._uT, self._uTs = uT, uTs

    def ssm_main(self, es, Toep, Wend, Wend_sw, Cpow, Cpow_sw, CT, ST, RHOT, RHO1, AR4, AI4, ARm4, AIm4, Drep, w_u, Hin):
        nc = self.nc
        I, O = self.ins, self.outs
        V, A, Pl = self.V, self.A, self.Pl
        sp, pool = self.sp, self.pool
        sb = lambda n, s, d=F32: self.sb(es, n, s, d)
        NC = L // T
        scrG = nc.dram_tensor("scrG", [128, G, NC], BF16, kind="Internal").ap()
        scrGs = nc.dram_tensor("scrGs", [64, G, NSEQ], BF16, kind="Internal").ap()
        U, Us = self._U, self._Us
        Gall = sb("s_G", [128, G, NC], BF16)
        V1e = [sb("s_V1e%d" % i, [128, 8, 65], BF16) for i in range(2)]
        V2e = [sb("s_V2e%d" % i, [128, 8, 65], BF16) for i in range(2)]
        T1 = sb("s_T1", [128, 8, 64]); T2 = sb("s_T2", [128, 8, 64]); Z = sb("s_Z", [128, 8, 64])
        W1 = sb("s_W1", [128, 8]); W2 = sb("s_W2", [128, 8]); W3 = sb("s_W3", [128, 8])
        Hnew = sb("s_Hnew", [128, G])
        tmpY = sb("s_tmpY", [128, 8, 64])
        for i in range(2):
            Pl("memset", [], [V2e[i]], ap=V2e[i][:], constant=0.0)
        hs = sb("s_hs", [128, 4, 128]); hs_sw = sb("s_hs_sw", [128, 4, 128])
        H0 = sb("s_H0", [128, NSEQ, G]); H0sw = sb("s_H0sw", [128, NSEQ, G])
        self.barrier()
        for j in range(4):
            sp.dma(hs[:, j, 0:64], I["sre"][128 * j:128 * j + 128, :], [], [hs], par=True)
            sp.dma(hs[:, j, 64:128], I["sim"][128 * j:128 * j + 128, :], [], [hs], par=True)
            sp.dma(hs_sw[:, j, 0:64], I["sim"][128 * j:128 * j + 128, :], [], [hs_sw], par=True)
            sp.dma(hs_sw[:, j, 64:128], I["sre"][128 * j:128 * j + 128, :], [], [hs_sw], par=True)

        nblk = len(BLOCKS)
        scrU, scrUs = self.scrU, self.scrUs
        self.tap("U0", U)
        s3 = lambda b: b[:, 0:512].rearrange("p (g c) -> p g c", c=64)
        slots = [dict(T1=T1, T2=T2, Z=Z, W1=W1, W2=W2, W3=W3, tmpY=tmpY, V1=V1e[0], V2=V2e[0]),
                 dict(T1=sb("s_T1b", [128, 8, 64]), T2=sb("s_T2b", [128, 8, 64]), Z=sb("s_Zb", [128, 8, 64]),
                      W1=sb("s_W1b", [128, 8]), W2=sb("s_W2b", [128, 8]), W3=sb("s_W3b", [128, 8]),
                      tmpY=sb("s_tmpYb", [128, 8, 64]), V1=V1e[1], V2=V2e[1])]

        def lvl_b(bi, gs, B):
            T1, T2, Z, W1, W2, W3, tmpY, V1, V2 = (B[k] for k in ("T1", "T2", "Z", "W1", "W2", "W3", "tmpY", "V1", "V2"))
            csl = slice(64 * bi, 64 * bi + 64)
            g0 = 8 * gs
            gsl = slice(g0, g0 + 8)
            psS = self.next_ps(); psW = self.next_ps()
            for gl in range(8):
                g = g0 + gl
                self.mm(psS, psS[:, 64 * gl:64 * gl + 64], Wend[:, g, :], U[:, g, csl], [Wend, U], signal=(gl == 7))
            for gl in range(8):
                g = g0 + gl
                self.mm(psW, psW[:, 64 * gl:64 * gl + 64], Wend_sw[:, g, :], U[:, g, csl], [Wend_sw, U], signal=(gl == 7))
            yield
            V("tensor_tensor", [psS, CT], [T1], out=T1[:], in0=s3(psS), in1=CT[:, gsl, :], op=ALU.mult)
            V("tensor_tensor", [psW, ST], [T2], out=T2[:], in0=s3(psW), in1=ST[:, gsl, :], op=ALU.mult)
            V("tensor_tensor", [RHO1, Hin], [W1], out=W1[:], in0=RHO1[:, gsl], in1=Hin[:, gsl], op=ALU.mult)
            yield
            Pl("tensor_tensor", [T1, T2], [T1], out=T1[:], in0=T1[:], in1=T2[:], op=ALU.add)
            yield
            V("tensor_tensor", [T1, W1], [T1], out=T1[:, :, 0], in0=T1[:, :, 0], in1=W1[:], op=ALU.add)
            V("tensor_tensor_scan", [RHOT, T1], [Z], out=Z[:].rearrange("p g c -> p (g c)"),
              data0=RHOT[:, gsl, :].rearrange("p g c -> p (g c)"), data1=T1[:].rearrange("p g c -> p (g c)"),
              initial=0.0, op0=ALU.mult, op1=ALU.add)
            yield
            Pl("tensor_tensor", [Z, CT], [V1], out=V1[:, :, 1:65], in0=Z[:], in1=CT[:, gsl, :], op=ALU.mult)
            Pl("tensor_tensor", [Z, ST], [V2], out=V2[:, :, 1:65], in0=Z[:], in1=ST[:, gsl, :], op=ALU.mult)
            V("tensor_copy", [Hin], [V1], out=V1[:, :, 0], in_=Hin[:, gsl])
            Pl("tensor_tensor", [U, Drep], [tmpY], out=tmpY[:], in0=U[:, gsl, csl],
               in1=Drep[:, gsl].unsqueeze(2).to_broadcast([128, 8, 64]), op=ALU.mult)
            yield
            V("tensor_tensor", [Z, CT], [W1], out=W1[:], in0=Z[:, :, 63], in1=CT[:, gsl, 63], op=ALU.mult)
            V("tensor_tensor", [Z, ST], [W2], out=W2[:], in0=Z[:, :, 63], in1=ST[:, gsl, 63], op=ALU.mult)
            yield
            V("tensor_copy", [W2], [W3], out=W3[0:64, :], in_=W2[64:128, :])
            V("tensor_copy", [W2], [W3], out=W3[64:128, :], in_=W2[0:64, :])
            yield
            V("tensor_tensor", [W1, W3], [Hnew], out=Hnew[:, gsl], in0=W1[:], in1=W3[:], op=ALU.add)
            psY = self.next_ps()
            for gl in range(8):
                g = g0 + gl
                o_ = psY[:, 64 * gl:64 * gl + 64]
                self.mm(psY, o_, Toep[:, g, :], U[:, g, csl], [Toep, U], start=True, stop=False)
                self.mm(psY, o_, Cpow[:, g, :], V1[:, gl, 0:64], [Cpow, V1], start=False, stop=False)
                self.mm(psY, o_, Cpow_sw[:, g, :], V2[:, gl, 0:64], [Cpow_sw, V2], start=False, stop=True, signal=(gl == 7))
            yield
            V("tensor_tensor", [tmpY, psY], [tmpY], out=tmpY[:], in0=tmpY[:], in1=s3(psY), op=ALU.add)
            yield
            A([tmpY], [Gall], out=Gall[:, gsl, csl], in_=tmpY[:], func=AF.Gelu_apprx_tanh)

        for bi in range(nblk - 1):
            for pair in ((0, 1), (2, 3)):
                gens = [lvl_b(bi, pair[0], slots[0]), lvl_b(bi, pair[1], slots[1])]
                while gens:
                    for g_ in list(gens):
                        try:
                            next(g_)
                        except StopIteration:
                            gens.remove(g_)
            V("tensor_copy", [Hnew], [Hin], out=Hin[:], in_=Hnew[:])
        ps = self.next_ps()
        self.tr(ps, ps[0:G, 0:128], Hin[:, :], self.identf[:], [Hin, self.identf])
        hp = sb("s_hp", [G, 128])
        A([ps], [hp], out=hp[:], in_=ps[0:G, 0:128], func=AF.Copy)
        self.out_events.append(sp.dma(O["hrp"][:, :], hp[:, 0:64], [hp], []))
        self.out_events.append(sp.dma(O["hip"][:, :], hp[:, 64:128], [hp], []))
        sp.dma(scrG, Gall[:], [Gall], [])
        for (src_, dst) in ((hs, H0), (hs_sw, H0sw)):
            for j in range(4):
                ps = self.next_ps()
                self.tr(ps, ps[:, 0:128], src_[:, j, :], self.identf[:], [src_, self.identf])
                A([ps], [dst], out=dst[:, 4 * j:4 * j + 4, :], in_=ps[:, 0:128].rearrange("p (s g) -> p s g", g=G), func=AF.Copy)
        Hm = sb("s_Hm", [128, NSEQ, G]); Hp = sb("s_Hp", [128, NSEQ, G]); tq = sb("s_tq", [128, NSEQ, G])
        Hm_bf = sb("s_Hmbf", [128, G, NSEQ], BF16)
        bc = lambda b: b[:].unsqueeze(1).to_broadcast([128, NSEQ, G])
        V("tensor_tensor", [H0, ARm4], [Hm], out=Hm[:], in0=H0[:], in1=bc(ARm4), op=ALU.mult)
        V("tensor_tensor", [H0sw, AIm4], [tq], out=tq[:], in0=H0sw[:], in1=bc(AIm4), op=ALU.mult)
        V("tensor_tensor", [Hm, tq], [Hm_bf], out=Hm_bf[:].rearrange("p g s -> p s g"), in0=Hm[:], in1=tq[:], op=ALU.add)
        V("tensor_tensor", [H0, AR4], [Hp], out=Hp[:], in0=H0[:], in1=bc(AR4), op=ALU.mult)
        V("tensor_tensor", [H0sw, AI4], [tq], out=tq[:], in0=H0sw[:], in1=bc(AI4), op=ALU.mult)
        V("tensor_tensor", [Hp, tq], [Hp], out=Hp[:], in0=Hp[:], in1=tq[:], op=ALU.add)
        Hout = sb("s_Hout", [128, NSEQ, G])
        Gs = sb("s_Gs", [128, G, NSEQ], BF16)
        tmps = sb("s_tmps", [128, G, NSEQ])
        psS = self.next_ps(); psY = self.next_ps()
        for g in range(G):
            self.mm(psS, psS[:, NSEQ * g:NSEQ * g + NSEQ], Wend[:, g, :], Us[:, g, :], [Wend, Us], signal=(g == G - 1))
        V("tensor_tensor", [psS, Hp], [Hout], out=Hout[:], in0=psS[:, 0:512].rearrange("p (g s) -> p s g", s=NSEQ),
          in1=Hp[:], op=ALU.add)
        for g in range(G):
            o_ = psY[:, NSEQ * g:NSEQ * g + NSEQ]
            self.mm(psY, o_, Toep[:, g, :], Us[:, g, :], [Toep, Us], start=True, stop=False)
            self.mm(psY, o_, Cpow[:, g, :], Hm_bf[:, g, :], [Cpow, Hm_bf], start=False, stop=True, signal=(g == G - 1))
        V("tensor_tensor", [Us, Drep], [tmps], out=tmps[:], in0=Us[:], in1=Drep[:].unsqueeze(2).to_broadcast([128, G, NSEQ]), op=ALU.mult)
        V("tensor_tensor", [tmps, psY], [tmps], out=tmps[:], in0=tmps[:], in1=psY[:, 0:512].rearrange("p (g s) -> p g s", s=NSEQ), op=ALU.add)
        A([tmps], [Gs], out=Gs[:], in_=tmps[:], func=AF.Gelu_apprx_tanh)
        sp.dma(scrGs, Gs[64:128, :, :], [Gs], [])
        ho = sb("s_ho", [128, 4, 128])
        for j in range(4):
            ps = self.next_ps()
            self.tr(ps, ps[:, 0:128], Hout[:, 4 * j:4 * j + 4, :].rearrange("p s g -> p (s g)"), self.identf[:], [Hout, self.identf])
            A([ps], [ho], out=ho[:, j, :], in_=ps[:, 0:128], func=AF.Copy)
            self.out_events.append(sp.dma(O["hrs"][128 * j:128 * j + 128, :], ho[:, j, 0:64], [ho], []))
            self.out_events.append(sp.dma(O["his"][128 * j:128 * j + 128, :], ho[:, j, 64:128], [ho], []))
        self.scrG, self.scrGs = scrG, scrGs

    def load_tokens(self, src, bi, dst):
        t0_, nt = BLOCKS[bi]
        ntile = (nt + 127) // 128
        tp = min(nt, 128)
        for i in range(ntile):
            self.pool.dma(dst[0:tp, i, :], src[t0_ + 128 * i:t0_ + 128 * i + tp, :], [], [dst], par=True)

    def transposes(self, xb, xt, nt):
        ntile = (nt + 127) // 128
        tp = min(nt, 128)
        for k in range(8):
            ps = self.next_ps()
            pv = ps[:].bitcast(BF16)
            for i in range(ntile):
                self.tr(ps, pv[:, 128 * i:128 * i + tp], xb[0:tp, i, 128 * k:128 * k + 128], self.ident[0:tp, 0:tp],
                        [xb, self.ident], signal=(i == ntile - 1))
            self.A([ps], [xt], out=xt[:, k, 0:nt], in_=pv[:, 0:nt], func=AF.Copy)

    def halves(self, buf):
        return (Buf(buf.t, buf.name + "_lo"), Buf(buf.t, buf.name + "_hi"))

    def layer_norm(self, ps_pair, x_tok, r, xh, g_bc, b_bc, st6, mv, sd, tp):
        V, A, Pl = self.V, self.A, self.Pl
        cs = [slice(0, 512), slice(512, D)]
        for h in range(2):
            V("scalar_tensor_tensor", [x_tok, ps_pair[h]], [r[h]], out=r[h][0:tp, cs[h]],
              in0=x_tok[0:tp, cs[h]], scalar=ALPHA, in1=ps_pair[h][0:tp, :], op0=ALU.mult, op1=ALU.add)
        for h in range(2):
            V("bn_stats", [r[h]], [st6], out=st6[0:tp, h, :], in_=r[h][0:tp, cs[h]])
        V("bn_aggr", [st6], [mv], out=mv[0:tp, :], in_=st6[0:tp, :, :].rearrange("p a b -> p (a b)"))
        V("tensor_scalar", [mv], [sd], out=sd[0:tp, 0:1], in0=mv[0:tp, 1:2], scalar1=LN_EPS, scalar2=None, op0=ALU.add)
        Pl("tensor_tensor", [sd, self.mhalf], [sd], out=sd[0:tp, 1:2], in0=sd[0:tp, 0:1], in1=self.mhalf[0:tp, 0:1], op=ALU.pow)
        V("scalar_tensor_tensor", [mv, sd], [sd], out=sd[0:tp, 2:3], in0=mv[0:tp, 0:1], scalar=-1.0, in1=sd[0:tp, 1:2],
          op0=ALU.mult, op1=ALU.mult)
        V("tensor_scalar", [r[0], sd], [xh[0]], out=xh[0][0:tp, cs[0]], in0=r[0][0:tp, cs[0]], scalar1=sd[0:tp, 1:2], scalar2=sd[0:tp, 2:3], op0=ALU.mult, op1=ALU.add)
        Pl("tensor_scalar", [r[1], sd], [xh[1]], out=xh[1][0:tp, cs[1]], in0=r[1][0:tp, cs[1]], scalar1=sd[0:tp, 1:2], scalar2=sd[0:tp, 2:3], op0=ALU.mult, op1=ALU.add)
        Pl("tensor_tensor", [xh[1], g_bc], [xh[1]], out=xh[1][0:tp, cs[1]], in0=xh[1][0:tp, cs[1]], in1=g_bc[0:tp, cs[1]], op=ALU.mult)
        V("tensor_tensor", [xh[0], g_bc], [xh[0]], out=xh[0][0:tp, cs[0]], in0=xh[0][0:tp, cs[0]], in1=g_bc[0:tp, cs[0]], op=ALU.mult)
        Pl("tensor_tensor", [xh[1], b_bc], [xh[1]], out=xh[1][0:tp, cs[1]], in0=xh[1][0:tp, cs[1]], in1=b_bc[0:tp, cs[1]], op=ALU.add)
        V("tensor_tensor", [xh[0], b_bc], [xh[0]], out=xh[0][0:tp, cs[0]], in0=xh[0][0:tp, cs[0]], in1=b_bc[0:tp, cs[0]], op=ALU.add)

    def pass_a(self):
        nc = self.nc
        I, O = self.ins, self.outs
        V, A, Pl = self.V, self.A, self.Pl
        sp, pool = self.sp, self.pool
        NC = L // T
        with ExitStack() as es:
            sb = lambda n, s, d=F32: self.sb(es, n, s, d)
            gT = sb("gT", [128, 4, NTOK], BF16)
            w_a = sb("w_a", [128, 8, 2816], BF16)
            w_glu = sb("w_glu", [128, 4, 2048], BF16)
            w_att = sb("w_att", [128, 4, D], BF16)
            w_o = sb("w_o", [128, 8, D], BF16)
            g_bc = sb("g1_bc", [128, D]); b_bc = sb("b1_bc", [128, D])
            sp.dma(g_bc[:], I["ln1_g"][0:1, :].partition_broadcast(128), [], [g_bc])
            sp.dma(b_bc[:], I["ln1_b"][0:1, :].partition_broadcast(128), [], [b_bc])
            maskD = sb("maskD", [128, 512], BF16); maskP = sb("maskP", [128, 512], BF16)
            maskC = sb("maskC", [128, 256], BF16); maskN = sb("maskN", [64, 256], BF16)
            es8 = sb("es8", [128, 8]); ES = sb("ES", [128, 2, 4, 128])
            vext = sb("vext", [128, 5, 2, 128], BF16)
            vc_ext = sb("vc_ext", [128, NSEQ, 2, 128], BF16)
            es_m = ExitStack()
            zer = self.sb(es_m, "zer", [128, 512]); mtmp = self.sb(es_m, "mtmp", [128, 512]); one_t = self.sb(es_m, "one_t", [128, 256])
            Pl("memset", [], [one_t], ap=one_t[:], constant=1.0)
            Pl("memset", [], [zer], ap=zer[:], constant=0.0)
            Pl("memset", [], [vext], ap=vext[:], constant=1.0)
            Pl("memset", [], [vc_ext], ap=vc_ext[:], constant=1.0)
            self.barrier()
            sp.dma(es8[:], I["sinks"][0:1, :].partition_broadcast(128), [], [es8])
            A([es8], [es8], out=es8[:], in_=es8[:], func=AF.Exp)
            V("tensor_copy", [es8], [ES], out=ES[:].rearrange("p a h q -> p (a h) q"), in_=es8[:].unsqueeze(2).to_broadcast([128, 8, 128]))
            z3 = zer[:].rearrange("p (h q) -> p h q", q=128)
            Pl("affine_select", [zer], [mtmp], out=mtmp[:].rearrange("p (h q) -> p h q", q=128), in_=z3, pattern=[[0, 4], [1, 128]],
               compare_op=ALU.is_ge, fill=NEG, base=0, channel_multiplier=-1)
            Pl("tensor_copy", [mtmp], [maskD], out=maskD[:], in_=mtmp[:])
            Pl("affine_select", [zer], [mtmp], out=mtmp[:].rearrange("p (h q) -> p h q", q=128), in_=z3, pattern=[[0, 4], [-1, 128]],
               compare_op=ALU.is_ge, fill=NEG, base=-1, channel_multiplier=1)
            Pl("tensor_copy", [mtmp], [maskP], out=maskP[:], in_=mtmp[:])
            Pl("affine_select", [one_t], [mtmp], out=mtmp[:, 0:256].rearrange("p (s h t) -> p s h t", h=4, t=TS),
               in_=one_t[:, 0:256].rearrange("p (s h t) -> p s h t", h=4, t=TS), pattern=[[0, NSEQ], [0, 4], [-1, TS]],
               compare_op=ALU.is_ge, fill=0.0, base=-1, channel_multiplier=1)
            Pl("tensor_copy", [mtmp], [maskC], out=maskC[:], in_=mtmp[:, 0:256])
            Pl("affine_select", [one_t], [mtmp], out=mtmp[0:64, 0:256].rearrange("p (h s t) -> p h s t", s=NSEQ, t=TS),
               in_=one_t[0:64, 0:256].rearrange("p (h s t) -> p h s t", s=NSEQ, t=TS), pattern=[[0, 4], [-4, NSEQ], [0, TS]],
               compare_op=ALU.is_ge, fill=0.0, base=0, channel_multiplier=1)
            Pl("affine_select", [mtmp], [mtmp], out=mtmp[0:64, 0:256].rearrange("p (h s t) -> p h s t", s=NSEQ, t=TS),
               in_=mtmp[0:64, 0:256].rearrange("p (h s t) -> p h s t", s=NSEQ, t=TS), pattern=[[0, 4], [4, NSEQ], [1, TS]],
               compare_op=ALU.is_ge, fill=0.0, base=0, channel_multiplier=-1)
            Pl("tensor_copy", [mtmp], [maskN], out=maskN[:], in_=mtmp[0:64, 0:256])
            self.barrier()
            es_m.close()
            x_bf = [sb("a_xbf", [128, 4, D], BF16)] * 2
            xT = sb("a_xT", [128, 8, 512], BF16)
            qT = sb("a_qT", [128, 4, 512], BF16)
            kT = sb("a_kT", [128, 640], BF16)
            kvf = sb("a_kvf", [128, 256])
            PT = [sb("a_PT%d" % i, [128, 512], BF16) for i in range(4)]
            oT = sb("a_oT", [128, 4, 512], BF16)
            den = sb("a_den", [64, 512]); rec = sb("a_rec", [64, 512])
            dens = [den, rec]
            osc = sb("a_osc", [128, 256]); osum = sb("a_osum", [128, 256])
            sig = [sb("a_sig%d" % i, [128, 512]) for i in range(2)]
            gsb = [sb("a_gs%d" % i, [128, 512], BF16) for i in range(2)]
            gab = [sb("a_ga%d" % i, [128, 512], BF16) for i in range(2)]
            bsb = [sb("a_bs%d" % i, [128, 512], BF16) for i in range(2)]
            t1 = sb("a_t1", [128, 512]); t2 = sb("a_t2", [128, 512])
            mT = sb("a_mT", [128, 8, 512], BF16)
            x_tok = [sb("a_xtok", [128, D])] * 2
            rr = [self.halves(sb("a_r", [128, D]))] * 2
            xh = [self.halves(sb("a_xh", [128, D]))] * 2
            st6 = sb("a_st6", [128, 2, 6]); mv = sb("a_mv", [128, 2]); sd = sb("a_sd", [128, 3])
            ckb = sb("a_ckb", [128, NSEQ, 128], BF16); kcT = sb("a_kcT", [128, NSEQ, 128], BF16)
            if "A_d2d" not in SKIP:
                self.out_events.append(sp.dma(O["kws"][:, 0:124, :], I["ck"][:, 4:128, :], [], []))
                self.out_events.append(sp.dma(O["vws"][:, 0:124, :], I["cv"][:, 4:128, :], [], []))

            xall = I["x"]
            nblk = len(BLOCKS)
            self.load_tokens(xall, 0, x_bf[0])
            wv = I["w_in"].rearrange("(k p) n -> p k n", p=128)
            for k in range(8 if "A_w" not in SKIP else 0):
                pool.dma(w_a[:, k, 0:768], wv[:, k, 512:1280], [], [w_a], par=True)
            wg = I["w_glu"].rearrange("(k p) n -> p k n", p=128)
            for k in range(4 if "A_w" not in SKIP else 0):
                for c in range(2):
                    pool.dma(w_glu[:, k, 1024 * c:1024 * c + 1024], wg[:, k, 1024 * c:1024 * c + 1024], [], [w_glu], par=True)
            wa = I["w_attn"].rearrange("(k p) n -> p k n", p=128)
            for k in range(4 if "A_w" not in SKIP else 0):
                pool.dma(w_att[:, k, :], wa[:, k, :], [], [w_att], par=True)
            for k in range(8 if "A_w" not in SKIP else 0):
                for c in range(2):
                    pool.dma(w_a[:, k, 768 + 1024 * c:1792 + 1024 * c], wv[:, k, 1280 + 1024 * c:2304 + 1024 * c], [], [w_a], par=True)
            wo = I["w_o"].rearrange("(k p) n -> p k n", p=128)
            for k in range(8 if "A_w" not in SKIP else 0):
                pool.dma(w_o[:, k, :], wo[:, k, :], [], [w_o], par=True)
            for g in range(G):
                j, gl = divmod(g, 8)
                src = bass.AP(tensor=self.scrG.tensor, offset=g * NC, ap=[[G * NC, CH], [CH * G * NC, T], [1, NC]])
                sp.dma(gT[16 * gl:16 * gl + 16, j, 0:L].rearrange("c (r n) -> c r n", n=NC), src, [], [gT], par=True)
                srcs = bass.AP(tensor=self.scrGs.tensor, offset=g * NSEQ, ap=[[G * NSEQ, CH], [CH * G * NSEQ, TS], [1, NSEQ]])
                sp.dma(gT[16 * gl:16 * gl + 16, j, L:L + NSAMP].rearrange("c (r n) -> c r n", n=NSEQ), srcs, [], [gT], par=True)
            if "A_cache" not in SKIP:
                pool.dma(ckb[:], I["ck"].rearrange("s w d -> w s d"), [], [ckb])
            for a_ in range(2 if "A_cache" not in SKIP else 0):
                pool.dma(vc_ext[:, :, a_, 0:64], I["cv"][:, :, 64 * a_:64 * a_ + 64].rearrange("s w d -> w s d"), [], [vc_ext])
            def blk(bi):
                if "A_blk" in SKIP or ("A_samp" in SKIP and bi == nblk - 1) or ("A_prompt" in SKIP and bi < nblk - 1):
                    return
                t0_, nt = BLOCKS[bi]
                ntile = (nt + 127) // 128
                tp = min(nt, 128)
                samp = (bi == nblk - 1)
                xb = x_bf[bi % 2]
                self.transposes(xb, xT, nt)
                if bi + 1 < nblk:
                    self.load_tokens(xall, bi + 1, x_bf[(bi + 1) % 2])
                if "A_proj" in SKIP:
                    return
                for j in range(4):
                    ps = self.next_ps()
                    for k in range(8):
                        self.mm(ps, ps[:, 0:nt], w_a[:, k, 128 * j:128 * j + 128], xT[:, k, 0:nt], [w_a, xT], start=(k == 0), stop=(k == 7))
                    A([ps], [qT], out=qT[:, j, 0:nt], in_=ps[:, 0:nt], func=AF.Copy)
                ps = self.next_ps()
                for k in range(8):
                    self.mm(ps, ps[:, 0:nt], w_a[:, k, 512:640], xT[:, k, 0:nt], [w_a, xT], start=(k == 0), stop=(k == 7))
                A([ps], [kT], out=kT[:, 128:128 + nt], in_=ps[:, 0:nt], func=AF.Copy)
                for i in range(ntile if "A_kvt" not in SKIP else 0):
                    ps = self.next_ps()
                    for k in range(8):
                        self.mm(ps, ps[0:tp, 0:256], xT[:, k, 128 * i:128 * i + tp], w_a[:, k, 512:768], [w_a, xT], start=(k == 0), stop=(k == 7))
                    if "A_kv_act" not in SKIP:
                        A([ps], [vext], out=vext[0:tp, 1 + i, :, 0:64], in_=ps[0:tp, 128:256].rearrange("p (a d) -> p a d", a=2), func=AF.Copy)
                    if ((bi == nblk - 2 and i == ntile - 1) or samp) and "A_kv_out" not in SKIP:
                        A([ps], [kvf], out=kvf[0:tp, :], in_=ps[0:tp, 0:256], func=AF.Copy)
                        if samp:
                            self.out_events.append(sp.dma(O["kws"][:, 124:128, :], kvf[0:NSAMP, 0:128], [kvf], []))
                            self.out_events.append(sp.dma(O["vws"][:, 124:128, :], kvf[0:NSAMP, 128:256], [kvf], []))
                        elif "A_kv_dma" not in SKIP:
                            self.out_events.append(sp.dma(O["kwp"][:, :], kvf[:, 0:128], [kvf], []))
                            self.out_events.append(sp.dma(O["vwp"][:, :], kvf[:, 128:256], [kvf], []))
                if "A_attn" in SKIP:
                    pass
                elif not samp:
                    def attn_unit(i, kv, slot):
                        qs = slice(128 * i, 128 * i + 128)
                        hp_ = slice(64 * kv, 64 * kv + 64)
                        has_prev = not (bi == 0 and i == 0)
                        rhs_q = qT[hp_, :, qs]
                        den_ = dens[slot]
                        PTd, PTp = PT[2 * slot], PT[2 * slot + 1]
                        psD = self.next_ps()
                        self.mm(psD, psD[:, :], self.ident[:], maskD[:], [self.ident, maskD], start=True, stop=False)
                        self.mm(psD, psD[:, :].rearrange("p (h q) -> p h q", q=128), kT[hp_, 128 + 128 * i:256 + 128 * i], rhs_q, [kT, qT], start=False, stop=True)
                        if has_prev:
                            psP = self.next_ps()
                            self.mm(psP, psP[:, :], self.ident[:], maskP[:], [self.ident, maskP], start=True, stop=False)
                            self.mm(psP, psP[:, :].rearrange("p (h q) -> p h q", q=128), kT[hp_, 128 * i:128 + 128 * i], rhs_q, [kT, qT], start=False, stop=True)
                        yield
                        A([psD], [PTd], out=PTd[:], in_=psD[:, :], func=AF.Exp, scale=0.125)
                        if has_prev:
                            A([psP], [PTp], out=PTp[:], in_=psP[:, :], func=AF.Exp, scale=0.125)
                        yield
                        psO = self.next_ps()
                        self.mm(psO, psO[:, :], vext[:, 1 + i, kv, :], PTd[:], [vext, PTd], start=True, stop=not has_prev)
                        if has_prev:
                            self.mm(psO, psO[:, :], vext[:, i, kv, :], PTp[:], [vext, PTp], start=False, stop=True)
                        yield
                        V("tensor_tensor", [psO, ES], [den_], out=den_[:], in0=psO[64:128, :], in1=ES[64:128, kv, :, :].rearrange("p h q -> p (h q)"), op=ALU.add)
                        yield
                        A([den_], [den_], out=den_[:], in_=den_[:], func=AF.Ln)
                        A([den_], [den_], out=den_[:], in_=den_[:], func=AF.Exp, scale=-1.0)
                        yield
                        V("tensor_tensor", [psO, den_], [oT], out=oT[hp_, :, qs], in0=psO[0:64, :].rearrange("p (h q) -> p h q", q=128),
                          in1=den_[:].rearrange("p (h q) -> p h q", q=128), op=ALU.mult)

                    units = [(i, kv) for i in range(ntile) for kv in range(2)]
                    for u0 in range(0, len(units), 2):
                        gens = [attn_unit(units[u0][0], units[u0][1], 0), attn_unit(units[u0 + 1][0], units[u0 + 1][1], 1)]
                        while gens:
                            for g_ in list(gens):
                                try:
                                    next(g_)
                                except StopIteration:
                                    gens.remove(g_)
                else:
                    for s_ in range(NSEQ):
                        ps = self.next_ps()
                        pv = ps[:].bitcast(BF16)
                        self.tr(ps, pv[:, 0:128], ckb[:, s_, :], self.ident[:], [ckb, self.ident])
                        A([ps], [kcT], out=kcT[:, s_, :], in_=pv[:, 0:128], func=AF.Copy)
                    for kv in range(2):
                        hp_ = slice(64 * kv, 64 * kv + 64)
                        psC = self.next_ps()
                        for s_ in range(NSEQ):
                            self.mm(psC, psC[:, 16 * s_:16 * s_ + 16].rearrange("p (h t) -> p h t", t=TS), kcT[hp_, s_, :], qT[hp_, :, TS * s_:TS * s_ + TS], [kcT, qT], start=True, stop=True, signal=(s_ == NSEQ - 1))
                        PTc = PT[0]
                        A([psC], [PTc], out=PTc[:, 0:256], in_=psC[:, 0:256], func=AF.Exp, scale=0.125)
                        V("tensor_tensor", [PTc, maskC], [PTc], out=PTc[:, 0:256], in0=PTc[:, 0:256], in1=maskC[:], op=ALU.mult)
                        psN = self.next_ps()
                        self.mm(psN, psN[0:64, 0:256].rearrange("p (h q) -> p h q", q=NSAMP), kT[hp_, 128:128 + NSAMP], qT[hp_, :, 0:NSAMP], [kT, qT], start=True, stop=True)
                        PTn = PT[1]
                        A([psN], [PTn], out=PTn[0:64, 0:256], in_=psN[0:64, 0:256], func=AF.Exp, scale=0.125)
                        V("tensor_tensor", [PTn, maskN], [PTn], out=PTn[0:64, 0:256], in0=PTn[0:64, 0:256], in1=maskN[:], op=ALU.mult)
                        psOc = self.next_ps()
                        for s_ in range(NSEQ):
                            self.mm(psOc, psOc[:, 16 * s_:16 * s_ + 16], vc_ext[:, s_, kv, :], PTc[:, 16 * s_:16 * s_ + 16], [vc_ext, PTc], start=True, stop=True, signal=(s_ == NSEQ - 1))
                        psOn = self.next_ps()
                        self.mm(psOn, psOn[:, 0:256], vext[0:64, 1, kv, :], PTn[0:64, 0:256], [vext, PTn], start=True, stop=True)
                        A([psOc], [osc], out=osc[:, 0:256], in_=psOc[:, 0:256], func=AF.Copy)
                        V("tensor_tensor", [psOn, osc], [osum], out=osum[:, 0:256].rearrange("p (h s t) -> p h s t", s=NSEQ, t=TS),
                          in0=psOn[:, 0:256].rearrange("p (h s t) -> p h s t", s=NSEQ, t=TS),
                          in1=osc[:, 0:256].rearrange("p (s h t) -> p h s t", h=4, t=TS), op=ALU.add)
                        V("tensor_tensor", [osum, ES], [den], out=den[:, 0:256].rearrange("p (h q) -> p h q", q=NSAMP), in0=osum[64:128, 0:256].rearrange("p (h q) -> p h q", q=NSAMP),
                          in1=ES[64:128, kv, :, 0:NSAMP], op=ALU.add)
                        A([den], [den], out=den[:, 0:256], in_=den[:, 0:256], func=AF.Ln)
                        A([den], [rec], out=rec[:, 0:256], in_=den[:, 0:256], func=AF.Exp, scale=-1.0)
                        V("tensor_tensor", [osum, rec], [oT], out=oT[hp_, :, 0:NSAMP], in0=osum[0:64, 0:256].rearrange("p (h q) -> p h q", q=NSAMP),
                          in1=rec[:, 0:256].rearrange("p (h q) -> p h q", q=NSAMP), op=ALU.mult)
                if not samp and bi + 1 < nblk - 1 and "A_carry" not in SKIP:
                    A([kT], [kT], out=kT[:, 0:128], in_=kT[:, 512:640], func=AF.Copy)
                    A([vext], [vext], out=vext[:, 0, :, 0:64], in_=vext[:, 4, :, 0:64], func=AF.Copy)
                yield
                for jf in range(8 if "A_merge" not in SKIP else 0):
                    psA = self.next_ps(); psB = self.next_ps()
                    for (psx, c0) in ((psA, 128 * jf), (psB, 1024 + 128 * jf)):
                        for k in range(4):
                            if not samp:
                                rhs = gT[:, k, 0:L].rearrange("p (r c) -> p r c", c=NC)[:, :, 64 * bi:64 * bi + 64]
                                o_ = psx[:, :].rearrange("p (r c) -> p r c", c=64)
                            else:
                                rhs = gT[:, k, L:L + NSAMP]
                                o_ = psx[:, 0:NSAMP]
                            self.mm(psx, o_, w_glu[:, k, c0:c0 + 128], rhs, [w_glu, gT], start=(k == 0), stop=(k == 3))
                    sg = sig[jf % 2]
                    bs = bsb[jf % 2]
                    A([psB], [sg], out=sg[:, 0:nt], in_=psB[:, 0:nt], func=AF.Sigmoid)
                    if not samp:
                        V("tensor_tensor", [psA, sg], [bs], out=bs[:, :].rearrange("p (c r) -> p r c", r=T),
                          in0=psA[:, :].rearrange("p (r c) -> p r c", c=64), in1=sg[:].rearrange("p (r c) -> p r c", c=64), op=ALU.mult)
                    else:
                        V("tensor_tensor", [psA, sg], [bs], out=bs[:, 0:NSAMP].rearrange("p (s t) -> p t s", t=TS),
                          in0=psA[:, 0:NSAMP].rearrange("p (t s) -> p t s", s=NSEQ), in1=sg[:, 0:NSAMP].rearrange("p (t s) -> p t s", s=NSEQ), op=ALU.mult)
                    psBA = self.next_ps()
                    for k in range(4):
                        self.mm(psBA, psBA[:, 0:nt], w_att[:, k, 128 * jf:128 * jf + 128], oT[:, k, 0:nt], [w_att, oT], start=(k == 0), stop=(k == 3))
                    psGS = self.next_ps()
                    for k in range(8):
                        self.mm(psGS, psGS[:, 0:nt], w_a[:, k, 768 + 128 * jf:896 + 128 * jf], xT[:, k, 0:nt], [w_a, xT], start=(k == 0), stop=(k == 7))
                    psGA = self.next_ps()
                    for k in range(8):
                        self.mm(psGA, psGA[:, 0:nt], w_a[:, k, 1792 + 128 * jf:1920 + 128 * jf], xT[:, k, 0:nt], [w_a, xT], start=(k == 0), stop=(k == 7))
                    gs_, ga_ = gsb[jf % 2], gab[jf % 2]
                    A([psGS], [gs_], out=gs_[:, 0:nt], in_=psGS[:, 0:nt], func=AF.Sigmoid)
                    A([psGA], [ga_], out=ga_[:, 0:nt], in_=psGA[:, 0:nt], func=AF.Sigmoid)
                    Pl("tensor_tensor", [gs_, bs], [t1], out=t1[:, 0:nt], in0=gs_[:, 0:nt], in1=bs[:, 0:nt], op=ALU.mult)
                    V("tensor_tensor", [ga_, psBA], [t2], out=t2[:, 0:nt], in0=ga_[:, 0:nt], in1=psBA[:, 0:nt], op=ALU.mult)
                    V("tensor_tensor", [t1, t2], [mT], out=mT[:, jf, 0:nt], in0=t1[:, 0:nt], in1=t2[:, 0:nt], op=ALU.add)
                yield
                for i in range(ntile if "A_ln" not in SKIP else 0):
                    tsl = slice(t0_ + 128 * i, t0_ + 128 * i + tp)
                    xt_ = x_tok[i % 2]; r_ = rr[i % 2]; xh_ = xh[i % 2]
                    sp.dma(xt_[0:tp, :], xall[tsl, :], [], [xt_])
                    pp = [self.next_ps(), self.next_ps()]
                    for h in range(2):
                        for k in range(8):
                            self.mm(pp[h], pp[h][0:tp, :], mT[:, k, 128 * i:128 * i + tp], w_o[:, k, 512 * h:512 * h + 512], [mT, w_o], start=(k == 0), stop=(k == 7))
                    self.layer_norm(pp, xt_, r_, xh_, g_bc, b_bc, st6, mv, sd, tp)
                    sp.dma(self.x1d[tsl, :], xh_[0][0:tp, :], [xh_[0], xh_[1]], [])

            def finish(g_):
                for _ in g_:
                    pass

            gens = [blk(bi) for bi in range(nblk)]
            next(gens[0], None)
            next(gens[0], None)
            for b_ in range(1, nblk):
                next(gens[b_], None)
                finish(gens[b_ - 1])
                next(gens[b_], None)
            finish(gens[nblk - 1])

    def pass_b(self):
        nc = self.nc
        I, O = self.ins, self.outs
        V, A, Pl = self.V, self.A, self.Pl
        sp, pool = self.sp, self.pool
        with ExitStack() as es:
            sb = lambda n, s, d=F32: self.sb(es, n, s, d)
            w_up = sb("w_up", [128, 8, 2 * DFF], BF16)
            w_dn = sb("w_dn", [128, NF, D], BF16)
            g_bc = sb("g2_bc", [128, D]); b_bc = sb("b2_bc", [128, D])
            sp.dma(g_bc[:], I["ln2_g"][0:1, :].partition_broadcast(128), [], [g_bc])
            sp.dma(b_bc[:], I["ln2_b"][0:1, :].partition_broadcast(128), [], [b_bc])
            cw = sb("cw", [128, NF, 3]); cb = sb("cb", [128, NF])
            for j in range(3):
                sp.dma(cw[:, :, j], I["conv_w"][j:j + 1, :].rearrange("o (f p) -> p (o f)", p=128), [], [cw], allow_slow_non_contiguous=True)
            sp.dma(cb[:], I["conv_b"][0:1, :].rearrange("o (f p) -> p (o f)", p=128), [], [cb], allow_slow_non_contiguous=True)
            a_carry = sb("a_carry", [128, NF, 2])
            Pl("memset", [], [a_carry], ap=a_carry[:], constant=0.0)
            self.barrier()
            scT = sb("scT", [128, NF, 2 * NSEQ]); csT = sb("csT", [128, NF, 2 * NSEQ])
            stg = [sb("b_stg%d" % i, [32, 512]) for i in range(2)]
            for c in range(6):
                w_ = min(512, DFF - 512 * c)
                st_ = stg[c % 2]
                sp.dma(st_[:, 0:w_], I["sconv"][:, 512 * c:512 * c + w_], [], [st_])
                for q in range(w_ // 128):
                    f = 4 * c + q
                    ps = self.next_ps()
                    self.tr(ps, ps[:, 0:32], st_[:, 128 * q:128 * q + 128], self.identf[0:32, 0:32], [st_, self.identf])
                    A([ps], [scT], out=scT[:, f, :], in_=ps[:, 0:32], func=AF.Copy)
            x_bf = sb("b_xbf", [128, 4, D], BF16)
            xT = sb("b_xT", [128, 8, 512], BF16)
            a_ext = [sb("b_aext", [128, 514])] * 2
            c1 = [sb("b_c1%d" % i, [128, 512]) for i in range(2)]
            ge = [sb("b_ge%d" % i, [128, 512], BF16) for i in range(2)]
            hT = sb("b_hT", [128, NF, 512], BF16)
            x_tok = [sb("b_xtok", [128, D])] * 2
            rr = self.halves(sb("b_r", [128, D])); xh = rr
            st6 = sb("b_st6", [128, 2, 6]); mv = sb("b_mv", [128, 2]); sd = sb("b_sd", [128, 3])
            nblk = len(BLOCKS)
            self.load_tokens(self.x1d, 0, x_bf)
            wu = I["w_up"].rearrange("(k p) n -> p k n", p=128)
            for c in (0, 2, 1, 3):
                for k in range(8):
                    pool.dma(w_up[:, k, 1408 * c:1408 * c + 1408], wu[:, k, 1408 * c:1408 * c + 1408], [], [w_up], par=True)
            wd = I["w_down"].rearrange("(k p) n -> p k n", p=128)
            for k in range(NF):
                pool.dma(w_dn[:, k, :], wd[:, k, :], [], [w_dn], par=True)
            for bi in range(nblk):
                t0_, nt = BLOCKS[bi]
                ntile = (nt + 127) // 128
                tp = min(nt, 128)
                samp = (bi == nblk - 1)
                self.transposes(x_bf, xT, nt)
                if bi + 1 < nblk:
                    self.load_tokens(self.x1d, bi + 1, x_bf)
                GF = 2
                for f0 in range(0, NF, GF):
                    fs = list(range(f0, min(NF, f0 + GF)))
                    pA, pG = {}, {}
                    for f in fs:
                        pA[f] = self.next_ps(); pG[f] = self.next_ps()
                        for (psx, c0) in ((pA[f], 128 * f), (pG[f], DFF + 128 * f)):
                            for k in range(8):
                                self.mm(psx, psx[:, 0:nt], w_up[:, k, c0:c0 + 128], xT[:, k, 0:nt], [w_up, xT], start=(k == 0), stop=(k == 7))
                    if not samp:
                        for f in fs:
                            c_ = c1[f % 2]
                            V("tensor_scalar", [a_carry, cw, cb], [c_], out=c_[:, 0:2], in0=a_carry[:, f, :], scalar1=cw[:, f, 0:1], scalar2=cb[:, f:f + 1], op0=ALU.mult, op1=ALU.add)
                        for f in fs:
                            c_ = c1[f % 2]
                            V("tensor_scalar", [pA[f], cw, cb], [c_], out=c_[:, 2:nt], in0=pA[f][:, 0:nt - 2], scalar1=cw[:, f, 0:1], scalar2=cb[:, f:f + 1], op0=ALU.mult, op1=ALU.add)
                        for f in fs:
                            c_ = c1[f % 2]
                            V("scalar_tensor_tensor", [a_carry, cw, c_], [c_], out=c_[:, 0:1], in0=a_carry[:, f, 1:2], scalar=cw[:, f, 1:2], in1=c_[:, 0:1], op0=ALU.mult, op1=ALU.add)
                        for f in fs:
                            c_ = c1[f % 2]
                            V("scalar_tensor_tensor", [pA[f], cw, c_], [c_], out=c_[:, 1:nt], in0=pA[f][:, 0:nt - 1], scalar=cw[:, f, 1:2], in1=c_[:, 1:nt], op0=ALU.mult, op1=ALU.add)
                        for f in fs:
                            c_ = c1[f % 2]
                            V("scalar_tensor_tensor", [pA[f], cw, c_], [c_], out=c_[:, 0:nt], in0=pA[f][:, 0:nt], scalar=cw[:, f, 2:3], in1=c_[:, 0:nt], op0=ALU.mult, op1=ALU.add)
                        for f in fs:
                            A([pA[f]], [a_carry], out=a_carry[:, f, :], in_=pA[f][:, nt - 2:nt], func=AF.Copy)
                        for f in fs:
                            A([c1[f % 2]], [ge[f % 2]], out=ge[f % 2][:, 0:nt], in_=c1[f % 2][:, 0:nt], func=AF.Gelu_apprx_tanh)
                        for f in fs:
                            V("tensor_tensor", [ge[f % 2], pG[f]], [hT], out=hT[:, f, 0:nt], in0=ge[f % 2][:, 0:nt], in1=pG[f][:, 0:nt], op=ALU.mult)
                    else:
                        for f in fs:
                            psA, psG = pA[f], pG[f]
                            ae = a_ext[0]; c_ = c1[f % 2]; g_ = ge[f % 2]
                            a3 = ae[:, 0:6 * NSEQ].rearrange("p (s j) -> p s j", j=6)
                            c3 = c_[:, 0:NSAMP].rearrange("p (s t) -> p s t", t=TS)
                            A([scT], [ae], out=a3[:, :, 0:2], in_=scT[:, f, :].rearrange("p (s j) -> p s j", j=2), func=AF.Copy)
                            A([psA], [ae], out=a3[:, :, 2:6], in_=psA[:, 0:NSAMP].rearrange("p (s t) -> p s t", t=TS), func=AF.Copy)
                            A([ae], [csT], out=csT[:, f, :].rearrange("p (s j) -> p s j", j=2), in_=a3[:, :, 4:6], func=AF.Copy)
                            V("tensor_scalar", [ae, cw, cb], [c_], out=c3, in0=a3[:, :, 0:4], scalar1=cw[:, f, 0:1], scalar2=cb[:, f:f + 1], op0=ALU.mult, op1=ALU.add)
                            V("scalar_tensor_tensor", [ae, cw, c_], [c_], out=c3, in0=a3[:, :, 1:5], scalar=cw[:, f, 1:2], in1=c3, op0=ALU.mult, op1=ALU.add)
                            V("scalar_tensor_tensor", [ae, cw, c_], [c_], out=c3, in0=a3[:, :, 2:6], scalar=cw[:, f, 2:3], in1=c3, op0=ALU.mult, op1=ALU.add)
                            A([c_], [g_], out=g_[:, 0:nt], in_=c_[:, 0:nt], func=AF.Gelu_apprx_tanh)
                            V("tensor_tensor", [g_, psG], [hT], out=hT[:, f, 0:nt], in0=g_[:, 0:nt], in1=psG[:, 0:nt], op=ALU.mult)
                for i in range(ntile):
                    tsl = slice(t0_ + 128 * i, t0_ + 128 * i + tp)
                    xt_ = x_tok[i % 2]
                    sp.dma(xt_[0:tp, :], self.x1d[tsl, :], [], [xt_])
                    pp = [self.next_ps(), self.next_ps()]
                    for h in range(2):
                        for f in range(NF):
                            self.mm(pp[h], pp[h][0:tp, :], hT[:, f, 128 * i:128 * i + tp], w_dn[:, f, 512 * h:512 * h + 512], [hT, w_dn], start=(f == 0), stop=(f == NF - 1))
                    self.layer_norm(pp, xt_, rr, xh, g_bc, b_bc, st6, mv, sd, tp)
                    self.out_events.append(sp.dma(O["y"][tsl, :], xh[0][0:tp, :], [xh[0], xh[1]], []))
            for c in range(6):
                w_ = min(512, DFF - 512 * c)
                nq = w_ // 128
                ps = self.next_ps(); ps2 = self.next_ps()
                for q in range(nq):
                    f = 4 * c + q
                    self.tr(ps, ps[0:2, 128 * q:128 * q + 128], a_carry[:, f, :], self.identf[:], [a_carry, self.identf], signal=(q == nq - 1))
                for q in range(nq):
                    f = 4 * c + q
                    self.tr(ps2, ps2[0:32, 128 * q:128 * q + 128], csT[:, f, :], self.identf[:], [csT, self.identf], signal=(q == nq - 1))
                s0, s1 = stg[0], stg[1]
                A([ps], [s0], out=s0[0:2, 0:w_], in_=ps[0:2, 0:w_], func=AF.Copy)
                A([ps2], [s1], out=s1[0:32, 0:w_], in_=ps2[0:32, 0:w_], func=AF.Copy)
                self.out_events.append(sp.dma(O["cp"][:, 512 * c:512 * c + w_], s0[0:2, 0:w_], [s0], []))
                self.out_events.append(sp.dma(O["cs"][:, 512 * c:512 * c + w_], s1[0:32, 0:w_], [s1], []))


def _host_inputs(inp):
    f = lambda a: np.ascontiguousarray(np.asarray(a, dtype=np.float32))
    w_in = f(inp["w_in"][0])
    qcols = np.concatenate([np.r_[512 + 64 * j:512 + 64 * j + 64, 512 + 64 * (4 + j):512 + 64 * (4 + j) + 64] for j in range(4)])
    perm = np.r_[0:512, qcols, 1024:DIN]
    w_in = np.ascontiguousarray(w_in[:, perm])
    w_attn = f(inp["w_attn_br"][0])
    rows = np.concatenate([np.r_[64 * j:64 * j + 64, 64 * (4 + j):64 * (4 + j) + 64] for j in range(4)])
    w_attn = np.ascontiguousarray(w_attn[rows, :])
    shared = {
        "w_in": w_in, "lam_re": f(inp["ssm_lam_re"][0]), "lam_im": f(inp["ssm_lam_im"][0]),
        "log_dt": f(inp["ssm_log_dt"][0]).reshape(1, G), "b_re": f(inp["ssm_b_re"][0]), "b_im": f(inp["ssm_b_im"][0]),
        "c_re": f(inp["ssm_c_re"][0]).reshape(G * CH, P), "c_im": f(inp["ssm_c_im"][0]).reshape(G * CH, P),
        "d": f(inp["ssm_d"][0]).reshape(G, CH), "w_glu": f(inp["w_glu"][0]), "sinks": f(inp["attn_sinks"][0]).reshape(1, 8),
        "w_attn": w_attn, "w_o": f(inp["w_o"][0]), "ln1_g": f(inp["ln1_g"][0]).reshape(1, D), "ln1_b": f(inp["ln1_b"][0]).reshape(1, D),
        "w_up": f(inp["w_up"][0]), "conv_w": f(inp["conv_w"][0]), "conv_b": f(inp["conv_b"][0]).reshape(1, DFF),
        "w_down": f(inp["w_down"][0]), "ln2_g": f(inp["ln2_g"][0]).reshape(1, D), "ln2_b": f(inp["ln2_b"][0]).reshape(1, D),
    }
    maps = []
    for c in range(8):
        s = slice(NSEQ * c, NSEQ * c + NSEQ)
        m = dict(shared)
        m["x"] = np.ascontiguousarray(np.concatenate([f(inp["x_prompt"][c]), f(inp["x_sample"][s]).reshape(NSAMP, D)], 0))
        m["ck"] = f(inp["cache_k_win"][0, s]).reshape(NSEQ, 128, 128)
        m["cv"] = f(inp["cache_v_win"][0, s]).reshape(NSEQ, 128, 128)
        m["sre"] = f(inp["state_ssm_re"][0, s]).reshape(NSEQ * G, P)
        m["sim"] = f(inp["state_ssm_im"][0, s]).reshape(NSEQ * G, P)
        m["sconv"] = f(inp["state_ffn_conv"][0, s]).reshape(NSEQ * 2, DFF)
        maps.append(m)
    return maps


_NC_CACHE = {}


def _run(inp):
    if "nc" not in _NC_CACHE:
        _NC_CACHE["nc"] = KB().build()
    nc = _NC_CACHE["nc"]
    maps = _host_inputs(inp)
    res = run_bass_kernel_spmd(nc, maps, core_ids=list(range(8)))
    return res.results


def kernel(**inp):
    rs = _run(inp)
    cat = lambda k: np.stack([np.asarray(r[k]) for r in rs], 0)
    y = cat("y")
    yp = y[:, :L, :]
    ys = y[:, L:, :].reshape(8 * NSEQ, TS, D)
    kwp = cat("kwp").reshape(1, 8, 128, 2, 64)
    vwp = cat("vwp").reshape(1, 8, 128, 2, 64)
    kws = cat("kws").reshape(1, 8 * NSEQ, 128, 2, 64)
    vws = cat("vws").reshape(1, 8 * NSEQ, 128, 2, 64)
    hrp = cat("hrp").reshape(1, 8, G, P)
    hip = cat("hip").reshape(1, 8, G, P)
    hrs = cat("hrs").reshape(1, 8 * NSEQ, G, P)
    his = cat("his").reshape(1, 8 * NSEQ, G, P)
    cp = cat("cp").reshape(1, 8, 2, DFF)
    cs = cat("cs").reshape(1, 8 * NSEQ, 2, DFF)
    return (np.ascontiguousarray(yp), np.ascontiguousarray(ys), kwp, vwp, kws, vws, hrp, hip, hrs, his, cp, cs)
```

```python
import math
from contextlib import ExitStack

import numpy as np
import concourse.bass as bass
import concourse.mybir as mybir
from concourse.bass_utils import run_bass_kernel_spmd

F32 = mybir.dt.float32
BF16 = mybir.dt.bfloat16
I32 = mybir.dt.int32
AF = mybir.ActivationFunctionType
ALU = mybir.AluOpType

D = 1024
L = 2048
NSEQ = 16
TS = 4
NSAMP = NSEQ * TS
NTOK = L + NSAMP
G = 32
P = 64
CH = 16
T = 8
DFF = 2816
NF = DFF // 128
DIN = 3328
ALPHA = 2.0 ** 0.25
LN_EPS = 1e-5
TWO_PI = 2.0 * math.pi
C1 = 6.28125
C2 = TWO_PI - C1
MAGIC = 12582912.0
NEG = -30000.0
BLOCKS = [(0, 512), (512, 512), (1024, 512), (1536, 512), (2048, 64)]
DEBUG = {}
TAPS = set()
SKIP = set()


def psplit(ap, c):
    (pstep, npart), (estep, n) = ap.ap
    return bass.AP(tensor=ap.tensor, offset=ap.offset, ap=[[pstep, c], [pstep * c, npart // c], [estep, n]])


class Buf:
    __slots__ = ("t", "wr", "rd", "name", "psum")

    def __init__(self, t, name):
        self.t = t
        self.wr = []
        self.rd = {}
        self.name = name
        self.psum = False

    def __getitem__(self, k):
        return self.t[k]


class Eng:
    def __init__(self, name, eng, sem, dma_sems=None):
        self.name = name
        self.eng = eng
        self.sem = sem
        self.count = 0
        self.waited = {}
        self.pool = [[s, 0] for s in (dma_sems or [])]
        self.pidx = 0

    def wait_ev(self, ev, force=False):
        if ev is None:
            return
        sem, val = ev
        if sem is self.sem and not force and self.name == "pe":
            return
        k = id(sem)
        if self.waited.get(k, 0) >= val:
            return
        self.eng.wait_ge(sem, val)
        self.waited[k] = val

    def deps(self, reads, writes, par=False):
        for b in reads:
            for ev in b.wr:
                self.wait_ev(ev)
            if b.psum:
                for ev in b.rd.values():
                    self.wait_ev(ev)
        for b in writes:
            if not par:
                for ev in b.wr:
                    self.wait_ev(ev)
            for ev in b.rd.values():
                self.wait_ev(ev)

    def record(self, ev, reads, writes, par=False):
        k = id(ev[0])
        for b in reads:
            b.rd[k] = ev
        for b in writes:
            if par:
                b.wr = [e for e in b.wr if id(e[0]) != k] + [ev]
            else:
                b.wr = [ev]
            b.rd = {}

    def op(self, method, reads, writes, signal=True, **kw):
        self.deps(reads, writes)
        ins = getattr(self.eng, method)(**kw)
        if signal:
            self.count += 1
            ins.then_inc(self.sem, 1)
            ev = (self.sem, self.count)
        else:
            ev = (self.sem, self.count + 1)
        self.record(ev, reads, writes)
        return ev

    def dma(self, out, in_, reads, writes, par=False, **kw):
        self.deps(reads, writes, par)
        slot = self.pool[self.pidx % len(self.pool)]
        self.pidx += 1
        sem, cnt = slot
        if cnt > 0:
            self.wait_ev((sem, cnt))
        self.eng.dma_start(out=out, in_=in_, **kw).then_inc(sem, 16)
        slot[1] = cnt + 16
        ev = (sem, cnt + 16)
        self.record(ev, reads, writes, par)
        return ev


class KB:
    def __init__(self):
        self.nc = bass.Bass("TRN2", target_bir_lowering=False)
        self.out_events = []

    def dram_in(self, name, shape, dt=F32):
        return self.nc.dram_tensor(name, list(shape), dt, kind="ExternalInput").ap()

    def dram_out(self, name, shape, dt=F32):
        return self.nc.dram_tensor(name, list(shape), dt, kind="ExternalOutput").ap()

    def sb(self, es, name, shape, dt=F32):
        return Buf(es.enter_context(self.nc.sbuf_tensor("sb_" + name, list(shape), dt)), name)

    def barrier(self):
        engs = [self.pe, self.act, self.dve, self.pool, self.sp]
        evs = [(e.sem, e.count) for e in engs if e.sem is not None and e.count > 0]
        for e in engs:
            for s, c in e.pool:
                if c > 0:
                    evs.append((s, c))
        for e in engs:
            for ev in evs:
                e.wait_ev(ev, force=True)

    def tap(self, name, buf, ap=None):
        if name not in TAPS:
            return
        ap = buf[:] if ap is None else ap
        shp = list(ap.shape)
        o = self.dram_out("tap_" + name, shp, ap.dtype)
        self.out_events.append(self.sp.dma(o, ap, [buf], []))

    def next_ps(self):
        b = self.psb[self.psi % 8]
        self.psi += 1
        return b

    def V(self, method, reads, writes, **kw):
        return self.dve.op(method, reads, writes, **kw)

    def A(self, reads, writes, **kw):
        return self.act.op("activation", reads, writes, **kw)

    def Pl(self, method, reads, writes, **kw):
        return self.pool.op(method, reads, writes, **kw)

    def mm(self, ps, out, lhsT, rhs, reads, start=True, stop=True, signal=None):
        if signal is None:
            signal = stop
        return self.pe.op("matmul", reads, [ps], signal=signal, out=out, lhsT=lhsT, rhs=rhs,
                          start=start, stop=stop)

    def tr(self, ps, out, in_, ident, reads, signal=True):
        return self.pe.op("transpose", reads, [ps], signal=signal, out=out, in_=in_, identity=ident)

    def sin_rr(self, out_b, out_ap, in_b, in_ap, shift, tmp_bs, tmp_aps):
        (ty, tn, tr_) = tmp_aps
        (by, bn, br) = tmp_bs
        V = self.V
        if shift != 0.0:
            V("tensor_scalar", [in_b], [by], out=ty, in0=in_ap, scalar1=float(shift), scalar2=None, op0=ALU.add)
            yb, y = by, ty
        else:
            yb, y = in_b, in_ap
        V("tensor_scalar", [yb], [bn], out=tn, in0=y, scalar1=1.0 / TWO_PI, scalar2=MAGIC, op0=ALU.mult, op1=ALU.add)
        V("tensor_scalar", [bn], [bn], out=tn, in0=tn, scalar1=MAGIC, scalar2=None, op0=ALU.subtract)
        V("scalar_tensor_tensor", [bn, yb], [br], out=tr_, in0=tn, scalar=-C1, in1=y, op0=ALU.mult, op1=ALU.add)
        V("scalar_tensor_tensor", [bn, br], [br], out=tr_, in0=tn, scalar=-C2, in1=tr_, op0=ALU.mult, op1=ALU.add)
        V("tensor_scalar", [br], [br], out=tr_, in0=tr_, scalar1=-math.pi, scalar2=math.pi, op0=ALU.max, op1=ALU.min)
        self.A([br], [out_b], out=out_ap, in_=tr_, func=AF.Sin)

    def build(self):
        nc = self.nc
        self.ins = {}
        I = self.ins
        I["x"] = self.dram_in("x", [NTOK, D])
        I["ck"] = self.dram_in("ck", [NSEQ, 128, 128])
        I["cv"] = self.dram_in("cv", [NSEQ, 128, 128])
        I["sre"] = self.dram_in("sre", [NSEQ * G, P])
        I["sim"] = self.dram_in("sim", [NSEQ * G, P])
        I["sconv"] = self.dram_in("sconv", [NSEQ * 2, DFF])
        I["w_in"] = self.dram_in("w_in", [D, DIN])
        I["lam_re"] = self.dram_in("lam_re", [G, P])
        I["lam_im"] = self.dram_in("lam_im", [G, P])
        I["log_dt"] = self.dram_in("log_dt", [1, G])
        I["b_re"] = self.dram_in("b_re", [G, P, CH])
        I["b_im"] = self.dram_in("b_im", [G, P, CH])
        I["c_re"] = self.dram_in("c_re", [G * CH, P])
        I["c_im"] = self.dram_in("c_im", [G * CH, P])
        I["d"] = self.dram_in("d", [G, CH])
        I["w_glu"] = self.dram_in("w_glu", [512, 2048])
        I["sinks"] = self.dram_in("sinks", [1, 8])
        I["w_attn"] = self.dram_in("w_attn", [512, D])
        I["w_o"] = self.dram_in("w_o", [D, D])
        I["ln1_g"] = self.dram_in("ln1_g", [1, D])
        I["ln1_b"] = self.dram_in("ln1_b", [1, D])
        I["w_up"] = self.dram_in("w_up", [D, 2 * DFF])
        I["conv_w"] = self.dram_in("conv_w", [3, DFF])
        I["conv_b"] = self.dram_in("conv_b", [1, DFF])
        I["w_down"] = self.dram_in("w_down", [DFF, D])
        I["ln2_g"] = self.dram_in("ln2_g", [1, D])
        I["ln2_b"] = self.dram_in("ln2_b", [1, D])
        self.outs = {}
        O = self.outs
        O["y"] = self.dram_out("y", [NTOK, D])
        O["kwp"] = self.dram_out("kwp", [128, 128])
        O["vwp"] = self.dram_out("vwp", [128, 128])
        O["kws"] = self.dram_out("kws", [NSEQ, 128, 128])
        O["vws"] = self.dram_out("vws", [NSEQ, 128, 128])
        O["hrp"] = self.dram_out("hrp", [G, P])
        O["hip"] = self.dram_out("hip", [G, P])
        O["hrs"] = self.dram_out("hrs", [NSEQ * G, P])
        O["his"] = self.dram_out("his", [NSEQ * G, P])
        O["cp"] = self.dram_out("cp", [2, DFF])
        O["cs"] = self.dram_out("cs", [NSEQ * 2, DFF])
        self.x1d = nc.dram_tensor("x1d", [NTOK, D], F32, kind="Internal").ap()
        for k, (shp, dt_) in DEBUG.items():
            O[k] = self.dram_out(k, shp, dt_)

        with ExitStack() as es0:
            E = es0.enter_context
            sems = [E(nc.semaphore("s%d" % i)) for i in range(4)]
            dsem_sp = [E(nc.semaphore("dsp%d" % i)) for i in range(40)]
            dsem_pl = [E(nc.semaphore("dpl%d" % i)) for i in range(40)]
            self.psb = [Buf(E(nc.psum_tensor("ps%d" % i, [128, 512], F32)), "ps%d" % i) for i in range(8)]
            self.psi = 0
            for b_ in self.psb:
                b_.psum = True
            block = E(nc.Block())
            self.pe = Eng("pe", nc.tensor, sems[0])
            self.act = Eng("act", nc.scalar, sems[1])
            self.dve = Eng("dve", nc.vector, sems[2])
            self.pool = Eng("pool", nc.gpsimd, sems[3], dsem_pl)
            self.sp = Eng("sp", nc.sync, None, dsem_sp)
            self.ident = self.sb(es0, "ident", [128, 128], BF16)
            self.identf = self.sb(es0, "identf", [128, 128], F32)
            self.mhalf = self.sb(es0, "mhalf", [128, 1], F32)
            self.consts()
            with ExitStack() as esSA:
                self.pass_s()
                self.barrier()
                if "A" not in SKIP:
                    self.pass_a()
            self.barrier()
            if "dbg_x1" in DEBUG:
                self.out_events.append(self.sp.dma(O["dbg_x1"], self.x1d, [], []))
            if "B" not in SKIP:
                self.pass_b()
            for ev in self.out_events:
                self.sp.wait_ev(ev)
            self.barrier()
        return nc

    def consts(self):
        Pl = self.Pl
        Pl("memset", [], [self.identf], ap=self.identf[:], constant=0.0)
        Pl("memset", [], [self.mhalf], ap=self.mhalf[:], constant=-0.5)
        self.barrier()
        Pl("affine_select", [self.identf], [self.identf], out=self.identf[:], in_=self.identf[:], pattern=[[-1, 128]],
           compare_op=ALU.not_equal, fill=1.0, base=0, channel_multiplier=1)
        Pl("tensor_copy", [self.identf], [self.ident], out=self.ident[:], in_=self.identf[:])

    def pass_s(self):
        nc = self.nc
        I, O = self.ins, self.outs
        V, A, Pl = self.V, self.A, self.Pl
        sp, pool = self.sp, self.pool
        with ExitStack() as es:
            sb = lambda n, s, d=F32: self.sb(es, n, s, d)
            Toep = sb("Toep", [128, G, 128], BF16)
            Wend = sb("Wend", [128, G, 128], BF16)
            Wend_sw = sb("Wend_sw", [128, G, 128], BF16)
            Cpow = sb("Cpow", [128, G, 128], BF16)
            Cpow_sw = sb("Cpow_sw", [128, G, 128], BF16)
            CT = sb("CT", [128, G, 64])
            ST = sb("ST", [128, G, 64])
            RHOT = sb("RHOT", [128, G, 64])
            RHO1 = sb("RHO1", [128, G])
            AR4 = sb("AR4", [128, G]); AI4 = sb("AI4", [128, G]); ARm4 = sb("ARm4", [128, G]); AIm4 = sb("AIm4", [128, G])
            Drep = sb("Drep", [128, G])
            w_u = sb("w_u", [128, 8, 512], BF16)
            SGN = sb("SGN", [128, 1]); NSGN = sb("NSGN", [128, 1])
            Hin = sb("Hin", [128, G])
            self._w_u = w_u
            NC_ = L // T
            U = sb("s_U", [128, G, NC_], BF16)
            Us = sb("s_Us", [128, G, NSEQ], BF16)
            Pl("memset", [], [Us], ap=Us[:], constant=0.0)
            self._U, self._Us = U, Us
            with ExitStack() as esp:
                gens = [self.ssm_prep(esp, Toep, Wend, Wend_sw, Cpow, Cpow_sw, CT, ST, RHOT, RHO1, AR4, AI4, ARm4, AIm4,
                                      Drep, SGN, NSGN, Hin), self.ssm_stage1(esp, w_u)]
                first = True
                while gens:
                    for g_ in list(gens):
                        try:
                            next(g_)
                        except StopIteration:
                            gens.remove(g_)
                        if first:
                            first = False
                ev1 = sp.dma(self.scrU.rearrange("j p n -> p j n"), self._uT[:], [self._uT], [])
                ev2 = sp.dma(self.scrUs.rearrange("j p n -> p j n"), self._uTs[:], [self._uTs], [])
                sp.wait_ev(ev1); sp.wait_ev(ev2)
                for g in range(G):
                    j, gl = divmod(g, 8)
                    src = bass.AP(tensor=self.scrU.tensor, offset=(j * 128 + gl * 16) * L, ap=[[NC_, T], [L, CH], [1, NC_]])
                    sp.dma(U[:, g, :], src, [], [U], par=True)
                    srcs = bass.AP(tensor=self.scrUs.tensor, offset=(j * 128 + gl * 16) * NSAMP, ap=[[NSEQ, TS], [NSAMP, CH], [1, NSEQ]])
                    sp.dma(Us[64:128, g, :], srcs, [], [Us], par=True)
                for gh in range(self._prep_nparts):
                    self._prep_part2(gh)
                self.ssm_taps(Toep, Wend, Wend_sw, Cpow, Cpow_sw)
                self.barrier()
            self.ssm_main(es, Toep, Wend, Wend_sw, Cpow, Cpow_sw, CT, ST, RHOT, RHO1, AR4, AI4, ARm4, AIm4,
                          Drep, w_u, Hin)

    def ssm_prep(self, es, Toep, Wend, Wend_sw, Cpow, Cpow_sw, CT, ST, RHOT, RHO1, AR4, AI4, ARm4, AIm4, Drep,
                 SGN, NSGN, Hin):
        I = self.ins
        V, A, Pl = self.V, self.A, self.Pl
        sp = self.sp
        sb = lambda n, s, d=F32: self.sb(es, n, s, d)
        lamre2 = sb("lamre2", [128, G]); lamim2 = sb("lamim2", [128, G]); logdt2 = sb("logdt2", [128, G])
        BA = sb("BA", [128, G, CH]); BB = sb("BB", [128, G, CH])
        Ct_re = sb("Ct_re", [128, 4, 128]); Ct_im = sb("Ct_im", [128, 4, 128])
        CRE2 = sb("CRE2", [128, G, CH]); CIM2 = sb("CIM2", [128, G, CH])
        lt_re = sb("lt_re", [G, 128]); lt_im = sb("lt_im", [G, 128]); lt_d = sb("lt_d", [G, 128])
        def _loads():
            w_u = self._w_u
            wv = I["w_in"].rearrange("(k p) n -> p k n", p=128)
            for k in range(8):
                self.pool.dma(w_u[:, k, :], wv[:, k, 0:512], [], [w_u], par=True)
            for h in range(2):
                sl = slice(64 * h, 64 * h + 64)
                sp.dma(lt_re[:, sl], I["lam_re"][:, :], [], [lt_re], par=True)
                sp.dma(lt_im[:, sl], I["lam_im"][:, :], [], [lt_im], par=True)
            sp.dma(logdt2[:], I["log_dt"][0:1, :].partition_broadcast(128), [], [logdt2])
            sp.dma(BA[0:64, :, :], I["b_re"].rearrange("g p c -> p g c"), [], [BA], par=True)
            sp.dma(BA[64:128, :, :], I["b_im"].rearrange("g p c -> p g c"), [], [BA], par=True)
            sp.dma(BB[0:64, :, :], I["b_im"].rearrange("g p c -> p g c"), [], [BB], par=True)
            sp.dma(BB[64:128, :, :], I["b_re"].rearrange("g p c -> p g c"), [], [BB], par=True)
            for j in range(4):
                for h in range(2):
                    sp.dma(Ct_re[:, j, 64 * h:64 * h + 64], I["c_re"][128 * j:128 * j + 128, :], [], [Ct_re], par=True)
                    sp.dma(Ct_im[:, j, 64 * h:64 * h + 64], I["c_im"][128 * j:128 * j + 128, :], [], [Ct_im], par=True)
            for r in range(T):
                sp.dma(lt_d[:, 16 * r:16 * r + 16], I["d"][:, :], [], [lt_d], par=True)
        self.issue_prep_loads = _loads
        Pl("memset", [], [SGN], ap=SGN[0:64, :], constant=1.0)
        Pl("memset", [], [SGN], ap=SGN[64:128, :], constant=-1.0)
        Pl("memset", [], [NSGN], ap=NSGN[0:64, :], constant=-1.0)
        Pl("memset", [], [NSGN], ap=NSGN[64:128, :], constant=1.0)
        Pl("memset", [], [Hin], ap=Hin[:], constant=0.0)
        JVi = sb("JVi", [128, 64], I32); JV = sb("JV", [128, 64])
        KVi = sb("KVi", [128, 32], I32); KV = sb("KV", [128, 32])
        Pl("iota", [], [JVi], out=JVi[:], pattern=[[1, 64]], base=1, channel_multiplier=0)
        Pl("iota", [], [KVi], out=KVi[:, 0:16], pattern=[[1, 16]], base=-7, channel_multiplier=0)
        Pl("iota", [], [KVi], out=KVi[:, 16:32], pattern=[[-1, 16]], base=8, channel_multiplier=0)
        ones = sb("ones", [128, 128]); tmask = sb("tmask", [128, 128])
        Pl("memset", [], [ones], ap=ones[:], constant=1.0)
        self.barrier()
        self.issue_prep_loads()
        for (lt_, dst_) in ((lt_re, lamre2), (lt_im, lamim2), (lt_d, Drep)):
            ps = self.next_ps()
            self.tr(ps, ps[:, 0:G], lt_[0:G, :], self.identf[0:G, 0:G], [lt_, self.identf])
            A([ps], [dst_], out=dst_[:], in_=ps[:, 0:G], func=AF.Copy)
        Pl("tensor_copy", [JVi], [JV], out=JV[:], in_=JVi[:])
        Pl("tensor_copy", [KVi], [KV], out=KV[:], in_=KVi[:])
        for (Ct, Cdst) in ((Ct_re, CRE2), (Ct_im, CIM2)):
            for j in range(4):
                ps = self.next_ps()
                self.tr(ps, ps[:, 0:128], Ct[:, j, :], self.identf[:], [Ct, self.identf])
                A([ps], [Cdst], out=Cdst[:, 8 * j:8 * j + 8, :], in_=ps[:, 0:128].rearrange("p (g c) -> p g c", c=CH), func=AF.Copy)
                yield
        dt = sb("dt", [128, G]); ell = sb("ell", [128, G]); th = sb("th", [128, G])
        den = sb("den", [128, G]); t0 = sb("t0", [128, G]); wr = sb("wr", [128, G]); wi = sb("wi", [128, G])
        A([logdt2], [dt], out=dt[:], in_=logdt2[:], func=AF.Exp)
        yield
        V("tensor_tensor", [lamre2, dt], [ell], out=ell[:], in0=lamre2[:], in1=dt[:], op=ALU.mult)
        V("tensor_tensor", [lamim2, dt], [th], out=th[:], in0=lamim2[:], in1=dt[:], op=ALU.mult)
        V("tensor_tensor", [lamre2], [den], out=den[:], in0=lamre2[:], in1=lamre2[:], op=ALU.mult)
        V("tensor_tensor", [lamim2], [t0], out=t0[:], in0=lamim2[:], in1=lamim2[:], op=ALU.mult)
        V("tensor_tensor", [den, t0], [den], out=den[:], in0=den[:], in1=t0[:], op=ALU.add)
        V("reciprocal", [den], [den], out=den[:], in_=den[:])
        V("tensor_tensor", [lamre2, den], [wr], out=wr[:], in0=lamre2[:], in1=den[:], op=ALU.mult)
        V("scalar_tensor_tensor", [lamim2, den], [wi], out=wi[:], in0=lamim2[:], scalar=-1.0, in1=den[:], op0=ALU.mult, op1=ALU.mult)
        kvals = list(range(-7, 9)) + list(range(8, -8, -1))
        ELLK = sb("ELLK", [128, G, 32]); ANGK = sb("ANGK", [128, G, 32])
        V("tensor_tensor", [ell, KV], [ELLK], out=ELLK[:], in0=ell[:].unsqueeze(2).to_broadcast([128, G, 32]),
          in1=KV[:].unsqueeze(1).to_broadcast([128, G, 32]), op=ALU.mult)
        V("tensor_tensor", [th, KV], [ANGK], out=ANGK[:], in0=th[:].unsqueeze(2).to_broadcast([128, G, 32]),
          in1=KV[:].unsqueeze(1).to_broadcast([128, G, 32]), op=ALU.mult)
        MAGK = ELLK; RR = sb("RR", [128, G, 32]); II = sb("II", [128, G, 32])
        ty = sb("ty", [128, G * 32]); tn = sb("tn", [128, G * 32]); trr = sb("trr", [128, G * 32])
        f1 = lambda b, n=G * 32: b[:, 0:n]
        fl = lambda b: b[:].rearrange("p g k -> p (g k)")
        A([ELLK], [MAGK], out=fl(MAGK), in_=fl(ELLK), func=AF.Exp)
        yield
        self.sin_rr(II, fl(II), ANGK, fl(ANGK), 0.0, (ty, tn, trr), (f1(ty), f1(tn), f1(trr)))
        yield
        self.sin_rr(RR, fl(RR), ANGK, fl(ANGK), math.pi / 2, (ty, tn, trr), (f1(ty), f1(tn), f1(trr)))
        yield
        V("tensor_tensor", [RR, MAGK], [RR], out=fl(RR), in0=fl(RR), in1=fl(MAGK), op=ALU.mult)
        V("tensor_tensor", [II, MAGK], [II], out=fl(II), in0=fl(II), in1=fl(MAGK), op=ALU.mult)
        self.tap("RR", RR); self.tap("II", II); self.tap("th", th); self.tap("ell", ell); self.tap("CRE2", CRE2); self.tap("BA", BA)
        V("tensor_copy", [RR], [AR4], out=AR4[:], in_=RR[:, :, 11])
        V("tensor_scalar", [II, NSGN], [AI4], out=AI4[:], in0=II[:, :, 11], scalar1=NSGN[:, 0:1], scalar2=None, op0=ALU.mult)
        V("tensor_copy", [RR], [ARm4], out=ARm4[:], in_=RR[:, :, 3])
        V("tensor_scalar", [II, NSGN], [AIm4], out=AIm4[:], in0=II[:, :, 3], scalar1=NSGN[:, 0:1], scalar2=None, op0=ALU.mult)
        ph = sb("ph", [128, G]); ANGJ = sb("ANGJ", [128, 16, 64])
        V("tensor_scalar", [th], [ph], out=ph[:], in0=th[:], scalar1=8.0, scalar2=None, op0=ALU.mult)
        V("tensor_scalar", [ph], [t0], out=t0[:], in0=ph[:], scalar1=1.0 / TWO_PI, scalar2=MAGIC, op0=ALU.mult, op1=ALU.add)
        V("tensor_scalar", [t0], [t0], out=t0[:], in0=t0[:], scalar1=MAGIC, scalar2=None, op0=ALU.subtract)
        V("scalar_tensor_tensor", [t0, ph], [ph], out=ph[:], in0=t0[:], scalar=-C1, in1=ph[:], op0=ALU.mult, op1=ALU.add)
        V("scalar_tensor_tensor", [t0, ph], [ph], out=ph[:], in0=t0[:], scalar=-C2, in1=ph[:], op0=ALU.mult, op1=ALU.add)
        f2 = lambda a: a.rearrange("p g j -> p (g j)")
        for gh in range(2):
            hs_ = slice(16 * gh, 16 * gh + 16)
            V("tensor_tensor", [ph, JV], [ANGJ], out=ANGJ[:], in0=ph[:, hs_].unsqueeze(2).to_broadcast([128, 16, 64]),
              in1=JV[:].unsqueeze(1).to_broadcast([128, 16, 64]), op=ALU.mult)
            self.sin_rr(CT, f2(CT[:, hs_, :]), ANGJ, f2(ANGJ[:]), math.pi / 2, (ty, tn, trr), (ty[:], tn[:], trr[:]))
            yield
            self.sin_rr(ST, f2(ST[:, hs_, :]), ANGJ, f2(ANGJ[:]), 0.0, (ty, tn, trr), (ty[:], tn[:], trr[:]))
            yield
        V("tensor_scalar", [ST, SGN], [ST], out=f2(ST[:]), in0=f2(ST[:]), scalar1=SGN[:, 0:1], scalar2=None, op0=ALU.mult)
        A([ell], [RHO1], out=RHO1[:], in_=ell[:], func=AF.Exp, scale=8.0)
        yield
        V("tensor_copy", [RHO1], [RHOT], out=RHOT[:], in_=RHO1[:].unsqueeze(2).to_broadcast([128, G, 64]))
        V("tensor_scalar", [RHOT], [RHOT], out=RHOT[:, :, 0], in0=RHOT[:, :, 0], scalar1=0.0, scalar2=None, op0=ALU.mult)
        self.tap("CT", CT); self.tap("ST", ST); self.tap("RHOT", RHOT); self.tap("ph", ph); self.tap("JV", JV); self.tap("ANGJ", ANGJ)
        NRR = sb("NRR", [128, G, 16]); NII = sb("NII", [128, G, 16])
        V("tensor_scalar", [RR], [NRR], out=NRR[:], in0=RR[:, :, 0:16], scalar1=-1.0, scalar2=None, op0=ALU.mult)
        V("tensor_scalar", [II], [NII], out=NII[:], in0=II[:, :, 0:16], scalar1=-1.0, scalar2=None, op0=ALU.mult)
        X = [sb("X%d" % i, [128, G, 16]) for i in range(4)]
        lo, hi = slice(0, 64), slice(64, 128)
        srcs = [(RR, NII), (NII, NRR), (NII, RR), (NRR, NII)]
        for i in range(4):
            bl, bh = srcs[i]
            V("tensor_copy", [bl], [X[i]], out=X[i][lo, :, :], in_=bl[lo, :, 0:16])
            V("tensor_copy", [bh], [X[i]], out=X[i][hi, :, :], in_=bh[hi, :, 0:16])
        DR = sb("DR", [128, G, 16]); DI = sb("DI", [128, G, 16])
        V("tensor_tensor", [RR], [DR], out=DR[:, :, 1:16], in0=RR[:, :, 16:31], in1=RR[:, :, 17:32], op=ALU.subtract)
        V("tensor_tensor", [II], [DI], out=DI[:, :, 1:16], in0=II[:, :, 16:31], in1=II[:, :, 17:32], op=ALU.subtract)
        QWR = sb("QWR", [128, G, 16]); QWI = sb("QWI", [128, G, 16]); q1 = sb("q1", [128, G, 16])
        wrb = wr[:].unsqueeze(2).to_broadcast([128, G, 15]); wib = wi[:].unsqueeze(2).to_broadcast([128, G, 15])
        s15 = (slice(None), slice(None), slice(1, 16))
        V("tensor_tensor", [DR, wr], [QWR], out=QWR[s15], in0=DR[s15], in1=wrb, op=ALU.mult)
        V("tensor_tensor", [DI, wi], [q1], out=q1[s15], in0=DI[s15], in1=wib, op=ALU.mult)
        V("tensor_tensor", [QWR, q1], [QWR], out=QWR[s15], in0=QWR[s15], in1=q1[s15], op=ALU.subtract)
        V("tensor_tensor", [DR, wi], [QWI], out=QWI[s15], in0=DR[s15], in1=wib, op=ALU.mult)
        V("tensor_tensor", [DI, wr], [q1], out=q1[s15], in0=DI[s15], in1=wrb, op=ALU.mult)
        V("tensor_tensor", [QWI, q1], [QWI], out=QWI[s15], in0=QWI[s15], in1=q1[s15], op=ALU.add)
        V("tensor_scalar", [QWI, NSGN], [QWI], out=QWI[s15], in0=QWI[s15], scalar1=NSGN[:, 0:1], scalar2=None, op0=ALU.mult)
        Pl("affine_select", [ones], [tmask], out=tmask[:].rearrange("p (r c) -> p r c", c=CH),
           in_=ones[:].rearrange("p (r c) -> p r c", c=CH), pattern=[[16, 8], [0, 16]], compare_op=ALU.is_ge, fill=0.0,
           base=15, channel_multiplier=-1)
        GH = 8
        class _View:
            def __init__(self, buf):
                self.buf = buf
        tA, tB, tC, tD = ty, tn, trr, ANGJ
        v4t = {id(ty): lambda: ty[:, 0:1024].rearrange("p (g r c) -> p g r c", g=GH, r=T),
               id(tn): lambda: tn[:, 0:1024].rearrange("p (g r c) -> p g r c", g=GH, r=T),
               id(trr): lambda: trr[:, 0:1024].rearrange("p (g r c) -> p g r c", g=GH, r=T),
               id(ANGJ): lambda: ANGJ[:].rearrange("p a (b c) -> p (a b) c", c=CH).rearrange("p (g r) c -> p g r c", r=T)}
        Cp0 = sb("Cp0", [128, GH, 128]); Bp0 = sb("Bp0", [128, GH, 128]); Bp7 = sb("Bp7", [128, GH, 128])
        v4 = lambda a: a.rearrange("p g (r c) -> p g r c", c=CH)
        def part2(gh):
            hs_ = slice(GH * gh, GH * gh + GH)

            def cmat(outb, out4, Xa, Xb, ki0, E=V, ta=tA, tb=tB):
                c_re_b = CRE2[:, hs_, :].unsqueeze(2).to_broadcast([128, GH, T, CH])
                c_im_b = CIM2[:, hs_, :].unsqueeze(2).to_broadcast([128, GH, T, CH])
                xa = Xa[:, hs_, ki0:ki0 + T].unsqueeze(3).to_broadcast([128, GH, T, CH])
                xb = Xb[:, hs_, ki0:ki0 + T].unsqueeze(3).to_broadcast([128, GH, T, CH])
                E("tensor_tensor", [CRE2, Xa], [ta], out=v4t[id(ta)](), in0=c_re_b, in1=xa, op=ALU.mult)
                E("tensor_tensor", [CIM2, Xb], [tb], out=v4t[id(tb)](), in0=c_im_b, in1=xb, op=ALU.mult)
                E("tensor_tensor", [ta, tb], [outb], out=out4, in0=v4t[id(ta)](), in1=v4t[id(tb)](), op=ALU.add)

            def bmat(outb, i0, E=V, ta=tA, tb=tB):
                ba = BA[:, hs_, :].unsqueeze(2).to_broadcast([128, GH, T, CH])
                bb = BB[:, hs_, :].unsqueeze(2).to_broadcast([128, GH, T, CH])
                qa = QWR[:, hs_, i0:i0 + T].unsqueeze(3).to_broadcast([128, GH, T, CH])
                qb = QWI[:, hs_, i0:i0 + T].unsqueeze(3).to_broadcast([128, GH, T, CH])
                E("tensor_tensor", [BA, QWR], [ta], out=v4t[id(ta)](), in0=ba, in1=qa, op=ALU.mult)
                E("tensor_tensor", [BB, QWI], [tb], out=v4t[id(tb)](), in0=bb, in1=qb, op=ALU.mult)
                E("tensor_tensor", [ta, tb], [outb], out=v4(outb[:]), in0=v4t[id(ta)](), in1=v4t[id(tb)](), op=ALU.add)

            bmat(Bp7, 1, Pl, tC, tD)
            cmat(Cp0, v4(Cp0[:]), X[0], X[1], 7)
            bmat(Bp0, 8)
            cmat(Cpow, v4(Cpow[:, hs_, :]), X[0], X[1], 8)
            cmat(Cpow_sw, v4(Cpow_sw[:, hs_, :]), X[2], X[3], 8)
            for gl in range(GH):
                g = GH * gh + gl
                ps = self.next_ps()
                self.mm(ps, ps[:, 0:128], Bp0[:, gl, :], Cp0[:, gl, :], [Bp0, Cp0])
                V("tensor_tensor", [ps, tmask], [Toep], out=Toep[:, g, :], in0=ps[:, 0:128], in1=tmask[:], op=ALU.mult)
                ps2 = self.next_ps()
                self.tr(ps2, ps2[:, 0:128], Bp7[:, gl, :], self.identf[:], [Bp7, self.identf])
                A([ps2], [Wend], out=Wend[:, g, :], in_=ps2[:, 0:128], func=AF.Copy)
                A([ps2], [Wend_sw], out=Wend_sw[:, g, 0:64], in_=ps2[:, 64:128], func=AF.Copy)
                A([ps2], [Wend_sw], out=Wend_sw[:, g, 64:128], in_=ps2[:, 0:64], func=AF.Copy)

        self._prep_part2 = part2
        self._prep_nparts = G // GH

    def ssm_taps(self, Toep, Wend, Wend_sw, Cpow, Cpow_sw):
        self.tap("Toep", Toep); self.tap("Wend", Wend); self.tap("Wend_sw", Wend_sw); self.tap("Cpow", Cpow); self.tap("Cpow_sw", Cpow_sw)

    def ssm_stage1(self, es, w_u):
        nc = self.nc
        I = self.ins
        A, pool, sp = self.A, self.pool, self.sp
        sb = lambda n, s, d=F32: self.sb(es, n, s, d)
        NC = L // T
        self.scrU = nc.dram_tensor("scrU", [4, 128, L], BF16, kind="Internal").ap()
        self.scrUs = nc.dram_tensor("scrUs", [4, 128, NSAMP], BF16, kind="Internal").ap()
        x_bf = sb("s_xbf", [128, 4, D], BF16)
        xT = sb("s_xT", [128, 8, 512], BF16)
        uT = sb("s_uT", [128, 4, L], BF16)
        uTs = sb("s_uTs", [128, 4, NSAMP], BF16)
        xall = I["x"]
        nblk = len(BLOCKS)
        self.load_tokens(xall, 0, x_bf)
        for bi in range(nblk):
            t0_, nt = BLOCKS[bi]
            samp = (bi == nblk - 1)
            self.transposes(x_bf, xT, nt)
            if bi + 1 < nblk:
                self.load_tokens(xall, bi + 1, x_bf)
            yield
            for j in range(4):
                ps = self.next_ps()
                for k in range(8):
                    self.mm(ps, ps[:, 0:nt], w_u[:, k, 128 * j:128 * j + 128], xT[:, k, 0:nt], [w_u, xT],
                            start=(k == 0), stop=(k == 7))
                if not samp:
                    A([ps], [uT], out=uT[:, j, :].rearrange("p (r c) -> p r c", c=NC)[:, :, 64 * bi:64 * bi + 64],
                      in_=ps[:, 0:512].rearrange("p (c r) -> p r c", r=T), func=AF.Copy)
                else:
                    A([ps], [uTs], out=uTs[:, j, :].rearrange("p (t s) -> p t s", s=NSEQ),
                      in_=ps[:, 0:64].rearrange("p (s t) -> p t s", t=TS), func=AF.Copy)
                yield
        self._uT, self._uTs = uT, uTs

    def ssm_main(self, es, Toep, Wend, Wend_sw, Cpow, Cpow_sw, CT, ST, RHOT, RHO1, AR4, AI4, ARm4, AIm4, Drep, w_u, Hin):
        nc = self.nc
        I, O = self.ins, self.outs
        V, A, Pl = self.V, self.A, self.Pl
        sp, pool = self.sp, self.pool
        sb = lambda n, s, d=F32: self.sb(es, n, s, d)
        NC = L // T
        scrG = nc.dram_tensor("scrG", [128, G, NC], BF16, kind="Internal").ap()
        scrGs = nc.dram_tensor("scrGs", [64, G, NSEQ], BF16, kind="Internal").ap()
        U, Us = self._U, self._Us
        Gall = sb("s_G", [128, G, NC], BF16)
        V1e = [sb("s_V1e%d" % i, [128, 8, 65], BF16) for i in range(2)]
        V2e = [sb("s_V2e%d" % i, [128, 8, 65], BF16) for i in range(2)]
        T1 = sb("s_T1", [128, 8, 64]); T2 = sb("s_T2", [128, 8, 64]); Z = sb("s_Z", [128, 8, 64])
        W1 = sb("s_W1", [128, 8]); W2 = sb("s_W2", [128, 8]); W3 = sb("s_W3", [128, 8])
        Hnew = sb("s_Hnew", [128, G])
        tmpY = sb("s_tmpY", [128, 8, 64])
        for i in range(2):
            Pl("memset", [], [V2e[i]], ap=V2e[i][:], constant=0.0)
        hs = sb("s_hs", [128, 4, 128]); hs_sw = sb("s_hs_sw", [128, 4, 128])
        H0 = sb("s_H0", [128, NSEQ, G]); H0sw = sb("s_H0sw", [128, NSEQ, G])
        self.barrier()
        for j in range(4):
            sp.dma(hs[:, j, 0:64], I["sre"][128 * j:128 * j + 128, :], [], [hs], par=True)
            sp.dma(hs[:, j, 64:128], I["sim"][128 * j:128 * j + 128, :], [], [hs], par=True)
            sp.dma(hs_sw[:, j, 0:64], I["sim"][128 * j:128 * j + 128, :], [], [hs_sw], par=True)
            sp.dma(hs_sw[:, j, 64:128], I["sre"][128 * j:128 * j + 128, :], [], [hs_sw], par=True)

        nblk = len(BLOCKS)
        scrU, scrUs = self.scrU, self.scrUs
        self.tap("U0", U)
        s3 = lambda b: b[:, 0:512].rearrange("p (g c) -> p g c", c=64)
        slots = [dict(T1=T1, T2=T2, Z=Z, W1=W1, W2=W2, W3=W3, tmpY=tmpY, V1=V1e[0], V2=V2e[0]),
                 dict(T1=sb("s_T1b", [128, 8, 64]), T2=sb("s_T2b", [128, 8, 64]), Z=sb("s_Zb", [128, 8, 64]),
                      W1=sb("s_W1b", [128, 8]), W2=sb("s_W2b", [128, 8]), W3=sb("s_W3b", [128, 8]),
                      tmpY=sb("s_tmpYb", [128, 8, 64]), V1=V1e[1], V2=V2e[1])]

        def lvl_b(bi, gs, B):
            T1, T2, Z, W1, W2, W3, tmpY, V1, V2 = (B[k] for k in ("T1", "T2", "Z", "W1", "W2", "W3", "tmpY", "V1", "V2"))
            csl = slice(64 * bi, 64 * bi + 64)
            g0 = 8 * gs
            gsl = slice(g0, g0 + 8)
            psS = self.next_ps(); psW = self.next_ps()
            for gl in range(8):
                g = g0 + gl
                self.mm(psS, psS[:, 64 * gl:64 * gl + 64], Wend[:, g, :], U[:, g, csl], [Wend, U], signal=(gl == 7))
            for gl in range(8):
                g = g0 + gl
                self.mm(psW, psW[:, 64 * gl:64 * gl + 64], Wend_sw[:, g, :], U[:, g, csl], [Wend_sw, U], signal=(gl == 7))
            yield
            V("tensor_tensor", [psS, CT], [T1], out=T1[:], in0=s3(psS), in1=CT[:, gsl, :], op=ALU.mult)
            V("tensor_tensor", [psW, ST], [T2], out=T2[:], in0=s3(psW), in1=ST[:, gsl, :], op=ALU.mult)
            V("tensor_tensor", [RHO1, Hin], [W1], out=W1[:], in0=RHO1[:, gsl], in1=Hin[:, gsl], op=ALU.mult)
            yield
            Pl("tensor_tensor", [T1, T2], [T1], out=T1[:], in0=T1[:], in1=T2[:], op=ALU.add)
            yield
            V("tensor_tensor", [T1, W1], [T1], out=T1[:, :, 0], in0=T1[:, :, 0], in1=W1[:], op=ALU.add)
            V("tensor_tensor_scan", [RHOT, T1], [Z], out=Z[:].rearrange("p g c -> p (g c)"),
              data0=RHOT[:, gsl, :].rearrange("p g c -> p (g c)"), data1=T1[:].rearrange("p g c -> p (g c)"),
              initial=0.0, op0=ALU.mult, op1=ALU.add)
            yield
            V("tensor_tensor", [Z, CT], [V1], out=V1[:, :, 1:65], in0=Z[:], in1=CT[:, gsl, :], op=ALU.mult)
            Pl("tensor_tensor", [Z, ST], [V2], out=V2[:, :, 1:65], in0=Z[:], in1=ST[:, gsl, :], op=ALU.mult)
            V("tensor_copy", [Hin], [V1], out=V1[:, :, 0], in_=Hin[:, gsl])
            Pl("tensor_tensor", [U, Drep], [tmpY], out=tmpY[:], in0=U[:, gsl, csl],
               in1=Drep[:, gsl].unsqueeze(2).to_broadcast([128, 8, 64]), op=ALU.mult)
            yield
            V("tensor_tensor", [Z, CT], [W1], out=W1[:], in0=Z[:, :, 63], in1=CT[:, gsl, 63], op=ALU.mult)
            V("tensor_tensor", [Z, ST], [W2], out=W2[:], in0=Z[:, :, 63], in1=ST[:, gsl, 63], op=ALU.mult)
            yield
            V("tensor_copy", [W2], [W3], out=W3[0:64, :], in_=W2[64:128, :])
            V("tensor_copy", [W2], [W3], out=W3[64:128, :], in_=W2[0:64, :])
            yield
            V("tensor_tensor", [W1, W3], [Hnew], out=Hnew[:, gsl], in0=W1[:], in1=W3[:], op=ALU.add)
            psY = self.next_ps()
            for gl in range(8):
                g = g0 + gl
                o_ = psY[:, 64 * gl:64 * gl + 64]
                self.mm(psY, o_, Toep[:, g, :], U[:, g, csl], [Toep, U], start=True, stop=False)
                self.mm(psY, o_, Cpow[:, g, :], V1[:, gl, 0:64], [Cpow, V1], start=False, stop=False)
                self.mm(psY, o_, Cpow_sw[:, g, :], V2[:, gl, 0:64], [Cpow_sw, V2], start=False, stop=True, signal=(gl == 7))
            yield
            V("tensor_tensor", [tmpY, psY], [tmpY], out=tmpY[:], in0=tmpY[:], in1=s3(psY), op=ALU.add)
            yield
            A([tmpY], [Gall], out=Gall[:, gsl, csl], in_=tmpY[:], func=AF.Gelu_apprx_tanh)

        for bi in range(nblk - 1):
            for pair in ((0, 1), (2, 3)):
                gens = [lvl_b(bi, pair[0], slots[0]), lvl_b(bi, pair[1], slots[1])]
                while gens:
                    for g_ in list(gens):
                        try:
                            next(g_)
                        except StopIteration:
                            gens.remove(g_)
            V("tensor_copy", [Hnew], [Hin], out=Hin[:], in_=Hnew[:])
        ps = self.next_ps()
        self.tr(ps, ps[0:G, 0:128], Hin[:, :], self.identf[:], [Hin, self.identf])
        hp = sb("s_hp", [G, 128])
        A([ps], [hp], out=hp[:], in_=ps[0:G, 0:128], func=AF.Copy)
        self.out_events.append(sp.dma(O["hrp"][:, :], hp[:, 0:64], [hp], []))
        self.out_events.append(sp.dma(O["hip"][:, :], hp[:, 64:128], [hp], []))
        sp.dma(scrG, Gall[:], [Gall], [])
        for (src_, dst) in ((hs, H0), (hs_sw, H0sw)):
            for j in range(4):
                ps = self.next_ps()
                self.tr(ps, ps[:, 0:128], src_[:, j, :], self.identf[:], [src_, self.identf])
                A([ps], [dst], out=dst[:, 4 * j:4 * j + 4, :], in_=ps[:, 0:128].rearrange("p (s g) -> p s g", g=G), func=AF.Copy)
        Hm = sb("s_Hm", [128, NSEQ, G]); Hp = sb("s_Hp", [128, NSEQ, G]); tq = sb("s_tq", [128, NSEQ, G])
        Hm_bf = sb("s_Hmbf", [128, G, NSEQ], BF16)
        bc = lambda b: b[:].unsqueeze(1).to_broadcast([128, NSEQ, G])
        V("tensor_tensor", [H0, ARm4], [Hm], out=Hm[:], in0=H0[:], in1=bc(ARm4), op=ALU.mult)
        V("tensor_tensor", [H0sw, AIm4], [tq], out=tq[:], in0=H0sw[:], in1=bc(AIm4), op=ALU.mult)
        V("tensor_tensor", [Hm, tq], [Hm_bf], out=Hm_bf[:].rearrange("p g s -> p s g"), in0=Hm[:], in1=tq[:], op=ALU.add)
        V("tensor_tensor", [H0, AR4], [Hp], out=Hp[:], in0=H0[:], in1=bc(AR4), op=ALU.mult)
        V("tensor_tensor", [H0sw, AI4], [tq], out=tq[:], in0=H0sw[:], in1=bc(AI4), op=ALU.mult)
        V("tensor_tensor", [Hp, tq], [Hp], out=Hp[:], in0=Hp[:], in1=tq[:], op=ALU.add)
        Hout = sb("s_Hout", [128, NSEQ, G])
        Gs = sb("s_Gs", [128, G, NSEQ], BF16)
        tmps = sb("s_tmps", [128, G, NSEQ])
        psS = self.next_ps(); psY = self.next_ps()
        for g in range(G):
            self.mm(psS, psS[:, NSEQ * g:NSEQ * g + NSEQ], Wend[:, g, :], Us[:, g, :], [Wend, Us], signal=(g == G - 1))
        V("tensor_tensor", [psS, Hp], [Hout], out=Hout[:], in0=psS[:, 0:512].rearrange("p (g s) -> p s g", s=NSEQ),
          in1=Hp[:], op=ALU.add)
        for g in range(G):
            o_ = psY[:, NSEQ * g:NSEQ * g + NSEQ]
            self.mm(psY, o_, Toep[:, g, :], Us[:, g, :], [Toep, Us], start=True, stop=False)
            self.mm(psY, o_, Cpow[:, g, :], Hm_bf[:, g, :], [Cpow, Hm_bf], start=False, stop=True, signal=(g == G - 1))
        V("tensor_tensor", [Us, Drep], [tmps], out=tmps[:], in0=Us[:], in1=Drep[:].unsqueeze(2).to_broadcast([128, G, NSEQ]), op=ALU.mult)
        V("tensor_tensor", [tmps, psY], [tmps], out=tmps[:], in0=tmps[:], in1=psY[:, 0:512].rearrange("p (g s) -> p g s", s=NSEQ), op=ALU.add)
        A([tmps], [Gs], out=Gs[:], in_=tmps[:], func=AF.Gelu_apprx_tanh)
        sp.dma(scrGs, Gs[64:128, :, :], [Gs], [])
        ho = sb("s_ho", [128, 4, 128])
        for j in range(4):
            ps = self.next_ps()
            self.tr(ps, ps[:, 0:128], Hout[:, 4 * j:4 * j + 4, :].rearrange("p s g -> p (s g)"), self.identf[:], [Hout, self.identf])
            A([ps], [ho], out=ho[:, j, :], in_=ps[:, 0:128], func=AF.Copy)
            self.out_events.append(sp.dma(O["hrs"][128 * j:128 * j + 128, :], ho[:, j, 0:64], [ho], []))
            self.out_events.append(sp.dma(O["his"][128 * j:128 * j + 128, :], ho[:, j, 64:128], [ho], []))
        self.scrG, self.scrGs = scrG, scrGs

    def load_tokens(self, src, bi, dst):
        t0_, nt = BLOCKS[bi]
        ntile = (nt + 127) // 128
        tp = min(nt, 128)
        for i in range(ntile):
            self.pool.dma(dst[0:tp, i, :], src[t0_ + 128 * i:t0_ + 128 * i + tp, :], [], [dst], par=True)

    def transposes(self, xb, xt, nt):
        ntile = (nt + 127) // 128
        tp = min(nt, 128)
        for k in range(8):
            ps = self.next_ps()
            pv = ps[:].bitcast(BF16)
            for i in range(ntile):
                self.tr(ps, pv[:, 128 * i:128 * i + tp], xb[0:tp, i, 128 * k:128 * k + 128], self.ident[0:tp, 0:tp],
                        [xb, self.ident], signal=(i == ntile - 1))
            self.A([ps], [xt], out=xt[:, k, 0:nt], in_=pv[:, 0:nt], func=AF.Copy)

    def halves(self, buf):
        return (Buf(buf.t, buf.name + "_lo"), Buf(buf.t, buf.name + "_hi"))

    def layer_norm(self, ps_pair, x_tok, r, xh, g_bc, b_bc, st6, mv, sd, tp):
        V, A, Pl = self.V, self.A, self.Pl
        cs = [slice(0, 512), slice(512, D)]
        for h in range(2):
            V("scalar_tensor_tensor", [x_tok, ps_pair[h]], [r[h]], out=r[h][0:tp, cs[h]],
              in0=x_tok[0:tp, cs[h]], scalar=ALPHA, in1=ps_pair[h][0:tp, :], op0=ALU.mult, op1=ALU.add)
        for h in range(2):
            V("bn_stats", [r[h]], [st6], out=st6[0:tp, h, :], in_=r[h][0:tp, cs[h]])
        V("bn_aggr", [st6], [mv], out=mv[0:tp, :], in_=st6[0:tp, :, :].rearrange("p a b -> p (a b)"))
        V("tensor_scalar", [mv], [sd], out=sd[0:tp, 0:1], in0=mv[0:tp, 1:2], scalar1=LN_EPS, scalar2=None, op0=ALU.add)
        Pl("tensor_tensor", [sd, self.mhalf], [sd], out=sd[0:tp, 1:2], in0=sd[0:tp, 0:1], in1=self.mhalf[0:tp, 0:1], op=ALU.pow)
        V("scalar_tensor_tensor", [mv, sd], [sd], out=sd[0:tp, 2:3], in0=mv[0:tp, 0:1], scalar=-1.0, in1=sd[0:tp, 1:2],
          op0=ALU.mult, op1=ALU.mult)
        V("tensor_scalar", [r[0], sd], [xh[0]], out=xh[0][0:tp, cs[0]], in0=r[0][0:tp, cs[0]], scalar1=sd[0:tp, 1:2], scalar2=sd[0:tp, 2:3], op0=ALU.mult, op1=ALU.add)
        Pl("tensor_scalar", [r[1], sd], [xh[1]], out=xh[1][0:tp, cs[1]], in0=r[1][0:tp, cs[1]], scalar1=sd[0:tp, 1:2], scalar2=sd[0:tp, 2:3], op0=ALU.mult, op1=ALU.add)
        Pl("tensor_tensor", [xh[1], g_bc], [xh[1]], out=xh[1][0:tp, cs[1]], in0=xh[1][0:tp, cs[1]], in1=g_bc[0:tp, cs[1]], op=ALU.mult)
        V("tensor_tensor", [xh[0], g_bc], [xh[0]], out=xh[0][0:tp, cs[0]], in0=xh[0][0:tp, cs[0]], in1=g_bc[0:tp, cs[0]], op=ALU.mult)
        Pl("tensor_tensor", [xh[1], b_bc], [xh[1]], out=xh[1][0:tp, cs[1]], in0=xh[1][0:tp, cs[1]], in1=b_bc[0:tp, cs[1]], op=ALU.add)
        V("tensor_tensor", [xh[0], b_bc], [xh[0]], out=xh[0][0:tp, cs[0]], in0=xh[0][0:tp, cs[0]], in1=b_bc[0:tp, cs[0]], op=ALU.add)

    def pass_a(self):
        nc = self.nc
        I, O = self.ins, self.outs
        V, A, Pl = self.V, self.A, self.Pl
        sp, pool = self.sp, self.pool
        NC = L // T
        with ExitStack() as es:
            sb = lambda n, s, d=F32: self.sb(es, n, s, d)
            gT = sb("gT", [128, 4, NTOK], BF16)
            w_a = sb("w_a", [128, 8, 2816], BF16)
            w_glu = sb("w_glu", [128, 4, 2048], BF16)
            w_att = sb("w_att", [128, 4, D], BF16)
            w_o = sb("w_o", [128, 8, D], BF16)
            g_bc = sb("g1_bc", [128, D]); b_bc = sb("b1_bc", [128, D])
            sp.dma(g_bc[:], I["ln1_g"][0:1, :].partition_broadcast(128), [], [g_bc])
            sp.dma(b_bc[:], I["ln1_b"][0:1, :].partition_broadcast(128), [], [b_bc])
            maskD = sb("maskD", [128, 512], BF16); maskP = sb("maskP", [128, 512], BF16)
            maskC = sb("maskC", [128, 256], BF16); maskN = sb("maskN", [64, 256], BF16)
            es8 = sb("es8", [128, 8]); ES = sb("ES", [128, 2, 4, 128])
            vext = sb("vext", [128, 5, 2, 128], BF16)
            vc_ext = sb("vc_ext", [128, NSEQ, 2, 128], BF16)
            es_m = ExitStack()
            zer = self.sb(es_m, "zer", [128, 512]); mtmp = self.sb(es_m, "mtmp", [128, 512]); one_t = self.sb(es_m, "one_t", [128, 256])
            Pl("memset", [], [one_t], ap=one_t[:], constant=1.0)
            Pl("memset", [], [zer], ap=zer[:], constant=0.0)
            Pl("memset", [], [vext], ap=vext[:], constant=1.0)
            Pl("memset", [], [vc_ext], ap=vc_ext[:], constant=1.0)
            self.barrier()
            sp.dma(es8[:], I["sinks"][0:1, :].partition_broadcast(128), [], [es8])
            A([es8], [es8], out=es8[:], in_=es8[:], func=AF.Exp)
            V("tensor_copy", [es8], [ES], out=ES[:].rearrange("p a h q -> p (a h) q"), in_=es8[:].unsqueeze(2).to_broadcast([128, 8, 128]))
            z3 = zer[:].rearrange("p (h q) -> p h q", q=128)
            Pl("affine_select", [zer], [mtmp], out=mtmp[:].rearrange("p (h q) -> p h q", q=128), in_=z3, pattern=[[0, 4], [1, 128]],
               compare_op=ALU.is_ge, fill=NEG, base=0, channel_multiplier=-1)
            Pl("tensor_copy", [mtmp], [maskD], out=maskD[:], in_=mtmp[:])
            Pl("affine_select", [zer], [mtmp], out=mtmp[:].rearrange("p (h q) -> p h q", q=128), in_=z3, pattern=[[0, 4], [-1, 128]],
               compare_op=ALU.is_ge, fill=NEG, base=-1, channel_multiplier=1)
            Pl("tensor_copy", [mtmp], [maskP], out=maskP[:], in_=mtmp[:])
            Pl("affine_select", [one_t], [mtmp], out=mtmp[:, 0:256].rearrange("p (s h t) -> p s h t", h=4, t=TS),
               in_=one_t[:, 0:256].rearrange("p (s h t) -> p s h t", h=4, t=TS), pattern=[[0, NSEQ], [0, 4], [-1, TS]],
               compare_op=ALU.is_ge, fill=0.0, base=-1, channel_multiplier=1)
            Pl("tensor_copy", [mtmp], [maskC], out=maskC[:], in_=mtmp[:, 0:256])
            Pl("affine_select", [one_t], [mtmp], out=mtmp[0:64, 0:256].rearrange("p (h s t) -> p h s t", s=NSEQ, t=TS),
               in_=one_t[0:64, 0:256].rearrange("p (h s t) -> p h s t", s=NSEQ, t=TS), pattern=[[0, 4], [-4, NSEQ], [0, TS]],
               compare_op=ALU.is_ge, fill=0.0, base=0, channel_multiplier=1)
            Pl("affine_select", [mtmp], [mtmp], out=mtmp[0:64, 0:256].rearrange("p (h s t) -> p h s t", s=NSEQ, t=TS),
               in_=mtmp[0:64, 0:256].rearrange("p (h s t) -> p h s t", s=NSEQ, t=TS), pattern=[[0, 4], [4, NSEQ], [1, TS]],
               compare_op=ALU.is_ge, fill=0.0, base=0, channel_multiplier=-1)
            Pl("tensor_copy", [mtmp], [maskN], out=maskN[:], in_=mtmp[0:64, 0:256])
            self.barrier()
            es_m.close()
            x_bf = [sb("a_xbf", [128, 4, D], BF16)] * 2
            xT = sb("a_xT", [128, 8, 512], BF16)
            qT = sb("a_qT", [128, 4, 512], BF16)
            kT = sb("a_kT", [128, 640], BF16)
            kvf = sb("a_kvf", [128, 256])
            PT = [sb("a_PT%d" % i, [128, 512], BF16) for i in range(4)]
            oT = sb("a_oT", [128, 4, 512], BF16)
            den = sb("a_den", [64, 512]); rec = sb("a_rec", [64, 512])
            dens = [den, rec]
            osc = sb("a_osc", [128, 256]); osum = sb("a_osum", [128, 256])
            sig = [sb("a_sig%d" % i, [128, 512]) for i in range(2)]
            gsb = [sb("a_gs%d" % i, [128, 512], BF16) for i in range(2)]
            gab = [sb("a_ga%d" % i, [128, 512], BF16) for i in range(2)]
            bsb = [sb("a_bs%d" % i, [128, 512], BF16) for i in range(2)]
            t1 = sb("a_t1", [128, 512]); t2 = sb("a_t2", [128, 512])
            mT = sb("a_mT", [128, 8, 512], BF16)
            x_tok = [sb("a_xtok", [128, D])] * 2
            rr = [self.halves(sb("a_r", [128, D]))] * 2
            xh = [self.halves(sb("a_xh", [128, D]))] * 2
            st6 = sb("a_st6", [128, 2, 6]); mv = sb("a_mv", [128, 2]); sd = sb("a_sd", [128, 3])
            ckb = sb("a_ckb", [128, NSEQ, 128], BF16); kcT = sb("a_kcT", [128, NSEQ, 128], BF16)
            if "A_d2d" not in SKIP:
                self.out_events.append(sp.dma(O["kws"][:, 0:124, :], I["ck"][:, 4:128, :], [], []))
                self.out_events.append(sp.dma(O["vws"][:, 0:124, :], I["cv"][:, 4:128, :], [], []))

            xall = I["x"]
            nblk = len(BLOCKS)
            self.load_tokens(xall, 0, x_bf[0])
            wv = I["w_in"].rearrange("(k p) n -> p k n", p=128)
            for k in range(8 if "A_w" not in SKIP else 0):
                pool.dma(w_a[:, k, 0:768], wv[:, k, 512:1280], [], [w_a], par=True)
            wg = I["w_glu"].rearrange("(k p) n -> p k n", p=128)
            for k in range(4 if "A_w" not in SKIP else 0):
                for c in range(2):
                    pool.dma(w_glu[:, k, 1024 * c:1024 * c + 1024], wg[:, k, 1024 * c:1024 * c + 1024], [], [w_glu], par=True)
            wa = I["w_attn"].rearrange("(k p) n -> p k n", p=128)
            for k in range(4 if "A_w" not in SKIP else 0):
                pool.dma(w_att[:, k, :], wa[:, k, :], [], [w_att], par=True)
            for k in range(8 if "A_w" not in SKIP else 0):
                for c in range(2):
                    pool.dma(w_a[:, k, 768 + 1024 * c:1792 + 1024 * c], wv[:, k, 1280 + 1024 * c:2304 + 1024 * c], [], [w_a], par=True)
            wo = I["w_o"].rearrange("(k p) n -> p k n", p=128)
            for k in range(8 if "A_w" not in SKIP else 0):
                pool.dma(w_o[:, k, :], wo[:, k, :], [], [w_o], par=True)
            for g in range(G):
                j, gl = divmod(g, 8)
                src = bass.AP(tensor=self.scrG.tensor, offset=g * NC, ap=[[G * NC, CH], [CH * G * NC, T], [1, NC]])
                sp.dma(gT[16 * gl:16 * gl + 16, j, 0:L].rearrange("c (r n) -> c r n", n=NC), src, [], [gT], par=True)
                srcs = bass.AP(tensor=self.scrGs.tensor, offset=g * NSEQ, ap=[[G * NSEQ, CH], [CH * G * NSEQ, TS], [1, NSEQ]])
                sp.dma(gT[16 * gl:16 * gl + 16, j, L:L + NSAMP].rearrange("c (r n) -> c r n", n=NSEQ), srcs, [], [gT], par=True)
            if "A_cache" not in SKIP:
                pool.dma(ckb[:], I["ck"].rearrange("s w d -> w s d"), [], [ckb])
            for a_ in range(2 if "A_cache" not in SKIP else 0):
                pool.dma(vc_ext[:, :, a_, 0:64], I["cv"][:, :, 64 * a_:64 * a_ + 64].rearrange("s w d -> w s d"), [], [vc_ext])
            def blk(bi):
                if "A_blk" in SKIP or ("A_samp" in SKIP and bi == nblk - 1) or ("A_prompt" in SKIP and bi < nblk - 1):
                    return
                t0_, nt = BLOCKS[bi]
                ntile = (nt + 127) // 128
                tp = min(nt, 128)
                samp = (bi == nblk - 1)
                xb = x_bf[bi % 2]
                self.transposes(xb, xT, nt)
                if bi + 1 < nblk:
                    self.load_tokens(xall, bi + 1, x_bf[(bi + 1) % 2])
                if "A_proj" in SKIP:
                    return
                for j in range(4):
                    ps = self.next_ps()
                    for k in range(8):
                        self.mm(ps, ps[:, 0:nt], w_a[:, k, 128 * j:128 * j + 128], xT[:, k, 0:nt], [w_a, xT], start=(k == 0), stop=(k == 7))
                    A([ps], [qT], out=qT[:, j, 0:nt], in_=ps[:, 0:nt], func=AF.Copy)
                ps = self.next_ps()
                for k in range(8):
                    self.mm(ps, ps[:, 0:nt], w_a[:, k, 512:640], xT[:, k, 0:nt], [w_a, xT], start=(k == 0), stop=(k == 7))
                A([ps], [kT], out=kT[:, 128:128 + nt], in_=ps[:, 0:nt], func=AF.Copy)
                for i in range(ntile if "A_kvt" not in SKIP else 0):
                    ps = self.next_ps()
                    for k in range(8):
                        self.mm(ps, ps[0:tp, 0:256], xT[:, k, 128 * i:128 * i + tp], w_a[:, k, 512:768], [w_a, xT], start=(k == 0), stop=(k == 7))
                    if "A_kv_act" not in SKIP:
                        A([ps], [vext], out=vext[0:tp, 1 + i, :, 0:64], in_=ps[0:tp, 128:256].rearrange("p (a d) -> p a d", a=2), func=AF.Copy)
                    if ((bi == nblk - 2 and i == ntile - 1) or samp) and "A_kv_out" not in SKIP:
                        A([ps], [kvf], out=kvf[0:tp, :], in_=ps[0:tp, 0:256], func=AF.Copy)
                        if samp:
                            self.out_events.append(sp.dma(O["kws"][:, 124:128, :], kvf[0:NSAMP, 0:128], [kvf], []))
                            self.out_events.append(sp.dma(O["vws"][:, 124:128, :], kvf[0:NSAMP, 128:256], [kvf], []))
                        elif "A_kv_dma" not in SKIP:
                            self.out_events.append(sp.dma(O["kwp"][:, :], kvf[:, 0:128], [kvf], []))
                            self.out_events.append(sp.dma(O["vwp"][:, :], kvf[:, 128:256], [kvf], []))
                if "A_attn" in SKIP:
                    pass
                elif not samp:
                    def attn_unit(i, kv, slot):
                        qs = slice(128 * i, 128 * i + 128)
                        hp_ = slice(64 * kv, 64 * kv + 64)
                        has_prev = not (bi == 0 and i == 0)
                        rhs_q = qT[hp_, :, qs]
                        den_ = dens[slot]
                        PTd, PTp = PT[2 * slot], PT[2 * slot + 1]
                        psD = self.next_ps()
                        self.mm(psD, psD[:, :], self.ident[:], maskD[:], [self.ident, maskD], start=True, stop=False)
                        self.mm(psD, psD[:, :].rearrange("p (h q) -> p h q", q=128), kT[hp_, 128 + 128 * i:256 + 128 * i], rhs_q, [kT, qT], start=False, stop=True)
                        if has_prev:
                            psP = self.next_ps()
                            self.mm(psP, psP[:, :], self.ident[:], maskP[:], [self.ident, maskP], start=True, stop=False)
                            self.mm(psP, psP[:, :].rearrange("p (h q) -> p h q", q=128), kT[hp_, 128 * i:128 + 128 * i], rhs_q, [kT, qT], start=False, stop=True)
                        yield
                        A([psD], [PTd], out=PTd[:], in_=psD[:, :], func=AF.Exp, scale=0.125)
                        if has_prev:
                            A([psP], [PTp], out=PTp[:], in_=psP[:, :], func=AF.Exp, scale=0.125)
                        yield
                        psO = self.next_ps()
                        self.mm(psO, psO[:, :], vext[:, 1 + i, kv, :], PTd[:], [vext, PTd], start=True, stop=not has_prev)
                        if has_prev:
                            self.mm(psO, psO[:, :], vext[:, i, kv, :], PTp[:], [vext, PTp], start=False, stop=True)
                        yield
                        V("tensor_tensor", [psO, ES], [den_], out=den_[:], in0=psO[64:128, :], in1=ES[64:128, kv, :, :].rearrange("p h q -> p (h q)"), op=ALU.add)
                        yield
                        A([den_], [den_], out=den_[:], in_=den_[:], func=AF.Ln)
                        A([den_], [den_], out=den_[:], in_=den_[:], func=AF.Exp, scale=-1.0)
                        yield
                        V("tensor_tensor", [psO, den_], [oT], out=oT[hp_, :, qs], in0=psO[0:64, :].rearrange("p (h q) -> p h q", q=128),
                          in1=den_[:].rearrange("p (h q) -> p h q", q=128), op=ALU.mult)

                    units = [(i, kv) for i in range(ntile) for kv in range(2)]
                    for u0 in range(0, len(units), 2):
                        gens = [attn_unit(units[u0][0], units[u0][1], 0), attn_unit(units[u0 + 1][0], units[u0 + 1][1], 1)]
                        while gens:
                            for g_ in list(gens):
                                try:
                                    next(g_)
                                except StopIteration:
                                    gens.remove(g_)
                else:
                    for s_ in range(NSEQ):
                        ps = self.next_ps()
                        pv = ps[:].bitcast(BF16)
                        self.tr(ps, pv[:, 0:128], ckb[:, s_, :], self.ident[:], [ckb, self.ident])
                        A([ps], [kcT], out=kcT[:, s_, :], in_=pv[:, 0:128], func=AF.Copy)
                    for kv in range(2):
                        hp_ = slice(64 * kv, 64 * kv + 64)
                        psC = self.next_ps()
                        for s_ in range(NSEQ):
                            self.mm(psC, psC[:, 16 * s_:16 * s_ + 16].rearrange("p (h t) -> p h t", t=TS), kcT[hp_, s_, :], qT[hp_, :, TS * s_:TS * s_ + TS], [kcT, qT], start=True, stop=True, signal=(s_ == NSEQ - 1))
                        PTc = PT[0]
                        A([psC], [PTc], out=PTc[:, 0:256], in_=psC[:, 0:256], func=AF.Exp, scale=0.125)
                        V("tensor_tensor", [PTc, maskC], [PTc], out=PTc[:, 0:256], in0=PTc[:, 0:256], in1=maskC[:], op=ALU.mult)
                        psN = self.next_ps()
                        self.mm(psN, psN[0:64, 0:256].rearrange("p (h q) -> p h q", q=NSAMP), kT[hp_, 128:128 + NSAMP], qT[hp_, :, 0:NSAMP], [kT, qT], start=True, stop=True)
                        PTn = PT[1]
                        A([psN], [PTn], out=PTn[0:64, 0:256], in_=psN[0:64, 0:256], func=AF.Exp, scale=0.125)
                        V("tensor_tensor", [PTn, maskN], [PTn], out=PTn[0:64, 0:256], in0=PTn[0:64, 0:256], in1=maskN[:], op=ALU.mult)
                        psOc = self.next_ps()
                        for s_ in range(NSEQ):
                            self.mm(psOc, psOc[:, 16 * s_:16 * s_ + 16], vc_ext[:, s_, kv, :], PTc[:, 16 * s_:16 * s_ + 16], [vc_ext, PTc], start=True, stop=True, signal=(s_ == NSEQ - 1))
                        psOn = self.next_ps()
                        self.mm(psOn, psOn[:, 0:256], vext[0:64, 1, kv, :], PTn[0:64, 0:256], [vext, PTn], start=True, stop=True)
                        A([psOc], [osc], out=osc[:, 0:256], in_=psOc[:, 0:256], func=AF.Copy)
                        V("tensor_tensor", [psOn, osc], [osum], out=osum[:, 0:256].rearrange("p (h s t) -> p h s t", s=NSEQ, t=TS),
                          in0=psOn[:, 0:256].rearrange("p (h s t) -> p h s t", s=NSEQ, t=TS),
                          in1=osc[:, 0:256].rearrange("p (s h t) -> p h s t", h=4, t=TS), op=ALU.add)
                        V("tensor_tensor", [osum, ES], [den], out=den[:, 0:256].rearrange("p (h q) -> p h q", q=NSAMP), in0=osum[64:128, 0:256].rearrange("p (h q) -> p h q", q=NSAMP),
                          in1=ES[64:128, kv, :, 0:NSAMP], op=ALU.add)
                        A([den], [den], out=den[:, 0:256], in_=den[:, 0:256], func=AF.Ln)
                        A([den], [rec], out=rec[:, 0:256], in_=den[:, 0:256], func=AF.Exp, scale=-1.0)
                        V("tensor_tensor", [osum, rec], [oT], out=oT[hp_, :, 0:NSAMP], in0=osum[0:64, 0:256].rearrange("p (h q) -> p h q", q=NSAMP),
                          in1=rec[:, 0:256].rearrange("p (h q) -> p h q", q=NSAMP), op=ALU.mult)
                if not samp and bi + 1 < nblk - 1 and "A_carry" not in SKIP:
                    A([kT], [kT], out=kT[:, 0:128], in_=kT[:, 512:640], func=AF.Copy)
                    A([vext], [vext], out=vext[:, 0, :, 0:64], in_=vext[:, 4, :, 0:64], func=AF.Copy)
                yield
                for jf in range(8 if "A_merge" not in SKIP else 0):
                    psA = self.next_ps(); psB = self.next_ps()
                    for (psx, c0) in ((psA, 128 * jf), (psB, 1024 + 128 * jf)):
                        for k in range(4):
                            if not samp:
                                rhs = gT[:, k, 0:L].rearrange("p (r c) -> p r c", c=NC)[:, :, 64 * bi:64 * bi + 64]
                                o_ = psx[:, :].rearrange("p (r c) -> p r c", c=64)
                            else:
                                rhs = gT[:, k, L:L + NSAMP]
                                o_ = psx[:, 0:NSAMP]
                            self.mm(psx, o_, w_glu[:, k, c0:c0 + 128], rhs, [w_glu, gT], start=(k == 0), stop=(k == 3))
                    sg = sig[jf % 2]
                    bs = bsb[jf % 2]
                    A([psB], [sg], out=sg[:, 0:nt], in_=psB[:, 0:nt], func=AF.Sigmoid)
                    if not samp:
                        V("tensor_tensor", [psA, sg], [bs], out=bs[:, :].rearrange("p (c r) -> p r c", r=T),
                          in0=psA[:, :].rearrange("p (r c) -> p r c", c=64), in1=sg[:].rearrange("p (r c) -> p r c", c=64), op=ALU.mult)
                    else:
                        V("tensor_tensor", [psA, sg], [bs], out=bs[:, 0:NSAMP].rearrange("p (s t) -> p t s", t=TS),
                          in0=psA[:, 0:NSAMP].rearrange("p (t s) -> p t s", s=NSEQ), in1=sg[:, 0:NSAMP].rearrange("p (t s) -> p t s", s=NSEQ), op=ALU.mult)
                    psBA = self.next_ps()
                    for k in range(4):
                        self.mm(psBA, psBA[:, 0:nt], w_att[:, k, 128 * jf:128 * jf + 128], oT[:, k, 0:nt], [w_att, oT], start=(k == 0), stop=(k == 3))
                    psGS = self.next_ps()
                    for k in range(8):
                        self.mm(psGS, psGS[:, 0:nt], w_a[:, k, 768 + 128 * jf:896 + 128 * jf], xT[:, k, 0:nt], [w_a, xT], start=(k == 0), stop=(k == 7))
                    psGA = self.next_ps()
                    for k in range(8):
                        self.mm(psGA, psGA[:, 0:nt], w_a[:, k, 1792 + 128 * jf:1920 + 128 * jf], xT[:, k, 0:nt], [w_a, xT], start=(k == 0), stop=(k == 7))
                    gs_, ga_ = gsb[jf % 2], gab[jf % 2]
                    A([psGS], [gs_], out=gs_[:, 0:nt], in_=psGS[:, 0:nt], func=AF.Sigmoid)
                    A([psGA], [ga_], out=ga_[:, 0:nt], in_=psGA[:, 0:nt], func=AF.Sigmoid)
                    Pl("tensor_tensor", [gs_, bs], [t1], out=t1[:, 0:nt], in0=gs_[:, 0:nt], in1=bs[:, 0:nt], op=ALU.mult)
                    V("tensor_tensor", [ga_, psBA], [t2], out=t2[:, 0:nt], in0=ga_[:, 0:nt], in1=psBA[:, 0:nt], op=ALU.mult)
                    V("tensor_tensor", [t1, t2], [mT], out=mT[:, jf, 0:nt], in0=t1[:, 0:nt], in1=t2[:, 0:nt], op=ALU.add)
                yield
                for i in range(ntile if "A_ln" not in SKIP else 0):
                    tsl = slice(t0_ + 128 * i, t0_ + 128 * i + tp)
                    xt_ = x_tok[i % 2]; r_ = rr[i % 2]; xh_ = xh[i % 2]
                    sp.dma(xt_[0:tp, :], xall[tsl, :], [], [xt_])
                    pp = [self.next_ps(), self.next_ps()]
                    for h in range(2):
                        for k in range(8):
                            self.mm(pp[h], pp[h][0:tp, :], mT[:, k, 128 * i:128 * i + tp], w_o[:, k, 512 * h:512 * h + 512], [mT, w_o], start=(k == 0), stop=(k == 7))
                    self.layer_norm(pp, xt_, r_, xh_, g_bc, b_bc, st6, mv, sd, tp)
                    sp.dma(self.x1d[tsl, :], xh_[0][0:tp, :], [xh_[0], xh_[1]], [])

            def finish(g_):
                for _ in g_:
                    pass

            gens = [blk(bi) for bi in range(nblk)]
            next(gens[0], None)
            next(gens[0], None)
            for b_ in range(1, nblk):
                next(gens[b_], None)
                finish(gens[b_ - 1])
                next(gens[b_], None)
            finish(gens[nblk - 1])

    def pass_b(self):
        nc = self.nc
        I, O = self.ins, self.outs
        V, A, Pl = self.V, self.A, self.Pl
        sp, pool = self.sp, self.pool
        with ExitStack() as es:
            sb = lambda n, s, d=F32: self.sb(es, n, s, d)
            w_up = sb("w_up", [128, 8, 2 * DFF], BF16)
            w_dn = sb("w_dn", [128, NF, D], BF16)
            g_bc = sb("g2_bc", [128, D]); b_bc = sb("b2_bc", [128, D])
            sp.dma(g_bc[:], I["ln2_g"][0:1, :].partition_broadcast(128), [], [g_bc])
            sp.dma(b_bc[:], I["ln2_b"][0:1, :].partition_broadcast(128), [], [b_bc])
            cw = sb("cw", [128, NF, 3]); cb = sb("cb", [128, NF])
            for j in range(3):
                sp.dma(cw[:, :, j], I["conv_w"][j:j + 1, :].rearrange("o (f p) -> p (o f)", p=128), [], [cw], allow_slow_non_contiguous=True)
            sp.dma(cb[:], I["conv_b"][0:1, :].rearrange("o (f p) -> p (o f)", p=128), [], [cb], allow_slow_non_contiguous=True)
            a_carry = sb("a_carry", [128, NF, 2])
            Pl("memset", [], [a_carry], ap=a_carry[:], constant=0.0)
            self.barrier()
            scT = sb("scT", [128, NF, 2 * NSEQ]); csT = sb("csT", [128, NF, 2 * NSEQ])
            stg = [sb("b_stg%d" % i, [32, 512]) for i in range(2)]
            for c in range(6):
                w_ = min(512, DFF - 512 * c)
                st_ = stg[c % 2]
                sp.dma(st_[:, 0:w_], I["sconv"][:, 512 * c:512 * c + w_], [], [st_])
                for q in range(w_ // 128):
                    f = 4 * c + q
                    ps = self.next_ps()
                    self.tr(ps, ps[:, 0:32], st_[:, 128 * q:128 * q + 128], self.identf[0:32, 0:32], [st_, self.identf])
                    A([ps], [scT], out=scT[:, f, :], in_=ps[:, 0:32], func=AF.Copy)
            x_bf = sb("b_xbf", [128, 4, D], BF16)
            xT = sb("b_xT", [128, 8, 512], BF16)
            a_ext = [sb("b_aext", [128, 514])] * 2
            c1 = [sb("b_c1%d" % i, [128, 512]) for i in range(2)]
            ge = [sb("b_ge%d" % i, [128, 512], BF16) for i in range(2)]
            hT = sb("b_hT", [128, NF, 512], BF16)
            x_tok = [sb("b_xtok", [128, D])] * 2
            rr = self.halves(sb("b_r", [128, D])); xh = rr
            st6 = sb("b_st6", [128, 2, 6]); mv = sb("b_mv", [128, 2]); sd = sb("b_sd", [128, 3])
            nblk = len(BLOCKS)
            self.load_tokens(self.x1d, 0, x_bf)
            wu = I["w_up"].rearrange("(k p) n -> p k n", p=128)
            for c in (0, 2, 1, 3):
                for k in range(8):
                    pool.dma(w_up[:, k, 1408 * c:1408 * c + 1408], wu[:, k, 1408 * c:1408 * c + 1408], [], [w_up], par=True)
            wd = I["w_down"].rearrange("(k p) n -> p k n", p=128)
            for k in range(NF):
                pool.dma(w_dn[:, k, :], wd[:, k, :], [], [w_dn], par=True)
            for bi in range(nblk):
                t0_, nt = BLOCKS[bi]
                ntile = (nt + 127) // 128
                tp = min(nt, 128)
                samp = (bi == nblk - 1)
                self.transposes(x_bf, xT, nt)
                if bi + 1 < nblk:
                    self.load_tokens(self.x1d, bi + 1, x_bf)
                GF = 2
                for f0 in range(0, NF, GF):
                    fs = list(range(f0, min(NF, f0 + GF)))
                    pA, pG = {}, {}
                    for f in fs:
                        pA[f] = self.next_ps(); pG[f] = self.next_ps()
                        for (psx, c0) in ((pA[f], 128 * f), (pG[f], DFF + 128 * f)):
                            for k in range(8):
                                self.mm(psx, psx[:, 0:nt], w_up[:, k, c0:c0 + 128], xT[:, k, 0:nt], [w_up, xT], start=(k == 0), stop=(k == 7))
                    if not samp:
                        for f in fs:
                            c_ = c1[f % 2]
                            V("tensor_scalar", [a_carry, cw, cb], [c_], out=c_[:, 0:2], in0=a_carry[:, f, :], scalar1=cw[:, f, 0:1], scalar2=cb[:, f:f + 1], op0=ALU.mult, op1=ALU.add)
                        for f in fs:
                            c_ = c1[f % 2]
                            V("tensor_scalar", [pA[f], cw, cb], [c_], out=c_[:, 2:nt], in0=pA[f][:, 0:nt - 2], scalar1=cw[:, f, 0:1], scalar2=cb[:, f:f + 1], op0=ALU.mult, op1=ALU.add)
                        for f in fs:
                            c_ = c1[f % 2]
                            V("scalar_tensor_tensor", [a_carry, cw, c_], [c_], out=c_[:, 0:1], in0=a_carry[:, f, 1:2], scalar=cw[:, f, 1:2], in1=c_[:, 0:1], op0=ALU.mult, op1=ALU.add)
                        for f in fs:
                            c_ = c1[f % 2]
                            V("scalar_tensor_tensor", [pA[f], cw, c_], [c_], out=c_[:, 1:nt], in0=pA[f][:, 0:nt - 1], scalar=cw[:, f, 1:2], in1=c_[:, 1:nt], op0=ALU.mult, op1=ALU.add)
                        for f in fs:
                            c_ = c1[f % 2]
                            V("scalar_tensor_tensor", [pA[f], cw, c_], [c_], out=c_[:, 0:nt], in0=pA[f][:, 0:nt], scalar=cw[:, f, 2:3], in1=c_[:, 0:nt], op0=ALU.mult, op1=ALU.add)
                        for f in fs:
                            A([pA[f]], [a_carry], out=a_carry[:, f, :], in_=pA[f][:, nt - 2:nt], func=AF.Copy)
                        for f in fs:
                            A([c1[f % 2]], [ge[f % 2]], out=ge[f % 2][:, 0:nt], in_=c1[f % 2][:, 0:nt], func=AF.Gelu_apprx_tanh)
                        for f in fs:
                            V("tensor_tensor", [ge[f % 2], pG[f]], [hT], out=hT[:, f, 0:nt], in0=ge[f % 2][:, 0:nt], in1=pG[f][:, 0:nt], op=ALU.mult)
                    else:
                        for f in fs:
                            psA, psG = pA[f], pG[f]
                            ae = a_ext[0]; c_ = c1[f % 2]; g_ = ge[f % 2]
                            a3 = ae[:, 0:6 * NSEQ].rearrange("p (s j) -> p s j", j=6)
                            c3 = c_[:, 0:NSAMP].rearrange("p (s t) -> p s t", t=TS)
                            A([scT], [ae], out=a3[:, :, 0:2], in_=scT[:, f, :].rearrange("p (s j) -> p s j", j=2), func=AF.Copy)
                            A([psA], [ae], out=a3[:, :, 2:6], in_=psA[:, 0:NSAMP].rearrange("p (s t) -> p s t", t=TS), func=AF.Copy)
                            A([ae], [csT], out=csT[:, f, :].rearrange("p (s j) -> p s j", j=2), in_=a3[:, :, 4:6], func=AF.Copy)
                            V("tensor_scalar", [ae, cw, cb], [c_], out=c3, in0=a3[:, :, 0:4], scalar1=cw[:, f, 0:1], scalar2=cb[:, f:f + 1], op0=ALU.mult, op1=ALU.add)
                            V("scalar_tensor_tensor", [ae, cw, c_], [c_], out=c3, in0=a3[:, :, 1:5], scalar=cw[:, f, 1:2], in1=c3, op0=ALU.mult, op1=ALU.add)
                            V("scalar_tensor_tensor", [ae, cw, c_], [c_], out=c3, in0=a3[:, :, 2:6], scalar=cw[:, f, 2:3], in1=c3, op0=ALU.mult, op1=ALU.add)
                            A([c_], [g_], out=g_[:, 0:nt], in_=c_[:, 0:nt], func=AF.Gelu_apprx_tanh)
                            V("tensor_tensor", [g_, psG], [hT], out=hT[:, f, 0:nt], in0=g_[:, 0:nt], in1=psG[:, 0:nt], op=ALU.mult)
                for i in range(ntile):
                    tsl = slice(t0_ + 128 * i, t0_ + 128 * i + tp)
                    xt_ = x_tok[i % 2]
                    sp.dma(xt_[0:tp, :], self.x1d[tsl, :], [], [xt_])
                    pp = [self.next_ps(), self.next_ps()]
                    for h in range(2):
                        for f in range(NF):
                            self.mm(pp[h], pp[h][0:tp, :], hT[:, f, 128 * i:128 * i + tp], w_dn[:, f, 512 * h:512 * h + 512], [hT, w_dn], start=(f == 0), stop=(f == NF - 1))
                    self.layer_norm(pp, xt_, rr, xh, g_bc, b_bc, st6, mv, sd, tp)
                    self.out_events.append(sp.dma(O["y"][tsl, :], xh[0][0:tp, :], [xh[0], xh[1]], []))
            for c in range(6):
                w_ = min(512, DFF - 512 * c)
                nq = w_ // 128
                ps = self.next_ps(); ps2 = self.next_ps()
                for q in range(nq):
                    f = 4 * c + q
                    self.tr(ps, ps[0:2, 128 * q:128 * q + 128], a_carry[:, f, :], self.identf[:], [a_carry, self.identf], signal=(q == nq - 1))
                for q in range(nq):
                    f = 4 * c + q
                    self.tr(ps2, ps2[0:32, 128 * q:128 * q + 128], csT[:, f, :], self.identf[:], [csT, self.identf], signal=(q == nq - 1))
                s0, s1 = stg[0], stg[1]
                A([ps], [s0], out=s0[0:2, 0:w_], in_=ps[0:2, 0:w_], func=AF.Copy)
                A([ps2], [s1], out=s1[0:32, 0:w_], in_=ps2[0:32, 0:w_], func=AF.Copy)
                self.out_events.append(sp.dma(O["cp"][:, 512 * c:512 * c + w_], s0[0:2, 0:w_], [s0], []))
                self.out_events.append(sp.dma(O["cs"][:, 512 * c:512 * c + w_], s1[0:32, 0:w_], [s1], []))


def _host_inputs(inp):
    f = lambda a: np.ascontiguousarray(np.asarray(a, dtype=np.float32))
    w_in = f(inp["w_in"][0])
    qcols = np.concatenate([np.r_[512 + 64 * j:512 + 64 * j + 64, 512 + 64 * (4 + j):512 + 64 * (4 + j) + 64] for j in range(4)])
    perm = np.r_[0:512, qcols, 1024:DIN]
    w_in = np.ascontiguousarray(w_in[:, perm])
    w_attn = f(inp["w_attn_br"][0])
    rows = np.concatenate([np.r_[64 * j:64 * j + 64, 64 * (4 + j):64 * (4 + j) + 64] for j in range(4)])
    w_attn = np.ascontiguousarray(w_attn[rows, :])
    shared = {
        "w_in": w_in, "lam_re": f(inp["ssm_lam_re"][0]), "lam_im": f(inp["ssm_lam_im"][0]),
        "log_dt": f(inp["ssm_log_dt"][0]).reshape(1, G), "b_re": f(inp["ssm_b_re"][0]), "b_im": f(inp["ssm_b_im"][0]),
        "c_re": f(inp["ssm_c_re"][0]).reshape(G * CH, P), "c_im": f(inp["ssm_c_im"][0]).reshape(G * CH, P),
        "d": f(inp["ssm_d"][0]).reshape(G, CH), "w_glu": f(inp["w_glu"][0]), "sinks": f(inp["attn_sinks"][0]).reshape(1, 8),
        "w_attn": w_attn, "w_o": f(inp["w_o"][0]), "ln1_g": f(inp["ln1_g"][0]).reshape(1, D), "ln1_b": f(inp["ln1_b"][0]).reshape(1, D),
        "w_up": f(inp["w_up"][0]), "conv_w": f(inp["conv_w"][0]), "conv_b": f(inp["conv_b"][0]).reshape(1, DFF),
        "w_down": f(inp["w_down"][0]), "ln2_g": f(inp["ln2_g"][0]).reshape(1, D), "ln2_b": f(inp["ln2_b"][0]).reshape(1, D),
    }
    maps = []
    for c in range(8):
        s = slice(NSEQ * c, NSEQ * c + NSEQ)
        m = dict(shared)
        m["x"] = np.ascontiguousarray(np.concatenate([f(inp["x_prompt"][c]), f(inp["x_sample"][s]).reshape(NSAMP, D)], 0))
        m["ck"] = f(inp["cache_k_win"][0, s]).reshape(NSEQ, 128, 128)
        m["cv"] = f(inp["cache_v_win"][0, s]).reshape(NSEQ, 128, 128)
        m["sre"] = f(inp["state_ssm_re"][0, s]).reshape(NSEQ * G, P)
        m["sim"] = f(inp["state_ssm_im"][0, s]).reshape(NSEQ * G, P)
        m["sconv"] = f(inp["state_ffn_conv"][0, s]).reshape(NSEQ * 2, DFF)
        maps.append(m)
    return maps


_NC_CACHE = {}


def _run(inp):
    if "nc" not in _NC_CACHE:
        _NC_CACHE["nc"] = KB().build()
    nc = _NC_CACHE["nc"]
    maps = _host_inputs(inp)
    res = run_bass_kernel_spmd(nc, maps, core_ids=list(range(8)))
    return res.results


def kernel(**inp):
    rs = _run(inp)
    cat = lambda k: np.stack([np.asarray(r[k]) for r in rs], 0)
    y = cat("y")
    yp = y[:, :L, :]
    ys = y[:, L:, :].reshape(8 * NSEQ, TS, D)
    kwp = cat("kwp").reshape(1, 8, 128, 2, 64)
    vwp = cat("vwp").reshape(1, 8, 128, 2, 64)
    kws = cat("kws").reshape(1, 8 * NSEQ, 128, 2, 64)
    vws = cat("vws").reshape(1, 8 * NSEQ, 128, 2, 64)
    hrp = cat("hrp").reshape(1, 8, G, P)
    hip = cat("hip").reshape(1, 8, G, P)
    hrs = cat("hrs").reshape(1, 8 * NSEQ, G, P)
    his = cat("his").reshape(1, 8 * NSEQ, G, P)
    cp = cat("cp").reshape(1, 8, 2, DFF)
    cs = cat("cs").reshape(1, 8 * NSEQ, 2, DFF)
    return (np.ascontiguousarray(yp), np.ascontiguousarray(ys), kwp, vwp, kws, vws, hrp, hip, hrs, his, cp, cs)
```

```python
import math
from contextlib import ExitStack

import numpy as np
import concourse.bass as bass
import concourse.mybir as mybir
from concourse.bass_utils import run_bass_kernel_spmd

F32 = mybir.dt.float32
BF16 = mybir.dt.bfloat16
I32 = mybir.dt.int32
AF = mybir.ActivationFunctionType
ALU = mybir.AluOpType

D = 1024
L = 2048
NSEQ = 16
TS = 4
NSAMP = NSEQ * TS
NTOK = L + NSAMP
G = 32
P = 64
CH = 16
T = 8
DFF = 2816
NF = DFF // 128
DIN = 3328
ALPHA = 2.0 ** 0.25
LN_EPS = 1e-5
TWO_PI = 2.0 * math.pi
C1 = 6.28125
C2 = TWO_PI - C1
MAGIC = 12582912.0
NEG = -30000.0
BLOCKS = [(0, 512), (512, 512), (1024, 512), (1536, 512), (2048, 64)]
DEBUG = {}
TAPS = set()
SKIP = set()


def psplit(ap, c):
    (pstep, npart), (estep, n) = ap.ap
    return bass.AP(tensor=ap.tensor, offset=ap.offset, ap=[[pstep, c], [pstep * c, npart // c], [estep, n]])


class Buf:
    __slots__ = ("t", "wr", "rd", "name", "psum")

    def __init__(self, t, name):
        self.t = t
        self.wr = []
        self.rd = {}
        self.name = name
        self.psum = False

    def __getitem__(self, k):
        return self.t[k]


class Eng:
    def __init__(self, name, eng, sem, dma_sems=None):
        self.name = name
        self.eng = eng
        self.sem = sem
        self.count = 0
        self.waited = {}
        self.pool = [[s, 0] for s in (dma_sems or [])]
        self.pidx = 0

    def wait_ev(self, ev, force=False):
        if ev is None:
            return
        sem, val = ev
        if sem is self.sem and not force and self.name == "pe":
            return
        k = id(sem)
        if self.waited.get(k, 0) >= val:
            return
        self.eng.wait_ge(sem, val)
        self.waited[k] = val

    def deps(self, reads, writes, par=False):
        for b in reads:
            for ev in b.wr:
                self.wait_ev(ev)
            if b.psum:
                for ev in b.rd.values():
                    self.wait_ev(ev)
        for b in writes:
            if not par:
                for ev in b.wr:
                    self.wait_ev(ev)
            for ev in b.rd.values():
                self.wait_ev(ev)

    def record(self, ev, reads, writes, par=False):
        k = id(ev[0])
        for b in reads:
            b.rd[k] = ev
        for b in writes:
            if par:
                b.wr = [e for e in b.wr if id(e[0]) != k] + [ev]
            else:
                b.wr = [ev]
            b.rd = {}

    def op(self, method, reads, writes, signal=True, **kw):
        self.deps(reads, writes)
        ins = getattr(self.eng, method)(**kw)
        if signal:
            self.count += 1
            ins.then_inc(self.sem, 1)
            ev = (self.sem, self.count)
        else:
            ev = (self.sem, self.count + 1)
        self.record(ev, reads, writes)
        return ev

    def dma(self, out, in_, reads, writes, par=False, **kw):
        self.deps(reads, writes, par)
        slot = self.pool[self.pidx % len(self.pool)]
        self.pidx += 1
        sem, cnt = slot
        if cnt > 0:
            self.wait_ev((sem, cnt))
        self.eng.dma_start(out=out, in_=in_, **kw).then_inc(sem, 16)
        slot[1] = cnt + 16
        ev = (sem, cnt + 16)
        self.record(ev, reads, writes, par)
        return ev


class KB:
    def __init__(self):
        self.nc = bass.Bass("TRN2", target_bir_lowering=False)
        self.out_events = []

    def dram_in(self, name, shape, dt=F32):
        return self.nc.dram_tensor(name, list(shape), dt, kind="ExternalInput").ap()

    def dram_out(self, name, shape, dt=F32):
        return self.nc.dram_tensor(name, list(shape), dt, kind="ExternalOutput").ap()

    def sb(self, es, name, shape, dt=F32):
        return Buf(es.enter_context(self.nc.sbuf_tensor("sb_" + name, list(shape), dt)), name)

    def barrier(self):
        engs = [self.pe, self.act, self.dve, self.pool, self.sp]
        evs = [(e.sem, e.count) for e in engs if e.sem is not None and e.count > 0]
        for e in engs:
            for s, c in e.pool:
                if c > 0:
                    evs.append((s, c))
        for e in engs:
            for ev in evs:
                e.wait_ev(ev, force=True)

    def tap(self, name, buf, ap=None):
        if name not in TAPS:
            return
        ap = buf[:] if ap is None else ap
        shp = list(ap.shape)
        o = self.dram_out("tap_" + name, shp, ap.dtype)
        self.out_events.append(self.sp.dma(o, ap, [buf], []))

    def next_ps(self):
        b = self.psb[self.psi % 8]
        self.psi += 1
        return b

    def V(self, method, reads, writes, **kw):
        return self.dve.op(method, reads, writes, **kw)

    def A(self, reads, writes, **kw):
        return self.act.op("activation", reads, writes, **kw)

    def Pl(self, method, reads, writes, **kw):
        return self.pool.op(method, reads, writes, **kw)

    def mm(self, ps, out, lhsT, rhs, reads, start=True, stop=True, signal=None):
        if signal is None:
            signal = stop
        return self.pe.op("matmul", reads, [ps], signal=signal, out=out, lhsT=lhsT, rhs=rhs,
                          start=start, stop=stop)

    def tr(self, ps, out, in_, ident, reads, signal=True):
        return self.pe.op("transpose", reads, [ps], signal=signal, out=out, in_=in_, identity=ident)

    def sin_rr(self, out_b, out_ap, in_b, in_ap, shift, tmp_bs, tmp_aps):
        (ty, tn, tr_) = tmp_aps
        (by, bn, br) = tmp_bs
        V = self.V
        if shift != 0.0:
            V("tensor_scalar", [in_b], [by], out=ty, in0=in_ap, scalar1=float(shift), scalar2=None, op0=ALU.add)
            yb, y = by, ty
        else:
            yb, y = in_b, in_ap
        V("tensor_scalar", [yb], [bn], out=tn, in0=y, scalar1=1.0 / TWO_PI, scalar2=MAGIC, op0=ALU.mult, op1=ALU.add)
        V("tensor_scalar", [bn], [bn], out=tn, in0=tn, scalar1=MAGIC, scalar2=None, op0=ALU.subtract)
        V("scalar_tensor_tensor", [bn, yb], [br], out=tr_, in0=tn, scalar=-C1, in1=y, op0=ALU.mult, op1=ALU.add)
        V("scalar_tensor_tensor", [bn, br], [br], out=tr_, in0=tn, scalar=-C2, in1=tr_, op0=ALU.mult, op1=ALU.add)
        V("tensor_scalar", [br], [br], out=tr_, in0=tr_, scalar1=-math.pi, scalar2=math.pi, op0=ALU.max, op1=ALU.min)
        self.A([br], [out_b], out=out_ap, in_=tr_, func=AF.Sin)

    def build(self):
        nc = self.nc
        self.ins = {}
        I = self.ins
        I["x"] = self.dram_in("x", [NTOK, D])
        I["ck"] = self.dram_in("ck", [NSEQ, 128, 128])
        I["cv"] = self.dram_in("cv", [NSEQ, 128, 128])
        I["sre"] = self.dram_in("sre", [NSEQ * G, P])
        I["sim"] = self.dram_in("sim", [NSEQ * G, P])
        I["sconv"] = self.dram_in("sconv", [NSEQ * 2, DFF])
        I["w_in"] = self.dram_in("w_in", [D, DIN])
        I["lam_re"] = self.dram_in("lam_re", [G, P])
        I["lam_im"] = self.dram_in("lam_im", [G, P])
        I["log_dt"] = self.dram_in("log_dt", [1, G])
        I["b_re"] = self.dram_in("b_re", [G, P, CH])
        I["b_im"] = self.dram_in("b_im", [G, P, CH])
        I["c_re"] = self.dram_in("c_re", [G * CH, P])
        I["c_im"] = self.dram_in("c_im", [G * CH, P])
        I["d"] = self.dram_in("d", [G, CH])
        I["w_glu"] = self.dram_in("w_glu", [512, 2048])
        I["sinks"] = self.dram_in("sinks", [1, 8])
        I["w_attn"] = self.dram_in("w_attn", [512, D])
        I["w_o"] = self.dram_in("w_o", [D, D])
        I["ln1_g"] = self.dram_in("ln1_g", [1, D])
        I["ln1_b"] = self.dram_in("ln1_b", [1, D])
        I["w_up"] = self.dram_in("w_up", [D, 2 * DFF])
        I["conv_w"] = self.dram_in("conv_w", [3, DFF])
        I["conv_b"] = self.dram_in("conv_b", [1, DFF])
        I["w_down"] = self.dram_in("w_down", [DFF, D])
        I["ln2_g"] = self.dram_in("ln2_g", [1, D])
        I["ln2_b"] = self.dram_in("ln2_b", [1, D])
        self.outs = {}
        O = self.outs
        O["y"] = self.dram_out("y", [NTOK, D])
        O["kwp"] = self.dram_out("kwp", [128, 128])
        O["vwp"] = self.dram_out("vwp", [128, 128])
        O["kws"] = self.dram_out("kws", [NSEQ, 128, 128])
        O["vws"] = self.dram_out("vws", [NSEQ, 128, 128])
        O["hrp"] = self.dram_out("hrp", [G, P])
        O["hip"] = self.dram_out("hip", [G, P])
        O["hrs"] = self.dram_out("hrs", [NSEQ * G, P])
        O["his"] = self.dram_out("his", [NSEQ * G, P])
        O["cp"] = self.dram_out("cp", [2, DFF])
        O["cs"] = self.dram_out("cs", [NSEQ * 2, DFF])
        self.x1d = nc.dram_tensor("x1d", [NTOK, D], F32, kind="Internal").ap()
        for k, (shp, dt_) in DEBUG.items():
            O[k] = self.dram_out(k, shp, dt_)

        with ExitStack() as es0:
            E = es0.enter_context
            sems = [E(nc.semaphore("s%d" % i)) for i in range(4)]
            dsem_sp = [E(nc.semaphore("dsp%d" % i)) for i in range(40)]
            dsem_pl = [E(nc.semaphore("dpl%d" % i)) for i in range(40)]
            self.psb = [Buf(E(nc.psum_tensor("ps%d" % i, [128, 512], F32)), "ps%d" % i) for i in range(8)]
            self.psi = 0
            for b_ in self.psb:
                b_.psum = True
            block = E(nc.Block())
            self.pe = Eng("pe", nc.tensor, sems[0])
            self.act = Eng("act", nc.scalar, sems[1])
            self.dve = Eng("dve", nc.vector, sems[2])
            self.pool = Eng("pool", nc.gpsimd, sems[3], dsem_pl)
            self.sp = Eng("sp", nc.sync, None, dsem_sp)
            self.ident = self.sb(es0, "ident", [128, 128], BF16)
            self.identf = self.sb(es0, "identf", [128, 128], F32)
            self.mhalf = self.sb(es0, "mhalf", [128, 1], F32)
            self.consts()
            with ExitStack() as esSA:
                self.pass_s()
                self.barrier()
                if "A" not in SKIP:
                    self.pass_a()
            self.barrier()
            if "dbg_x1" in DEBUG:
                self.out_events.append(self.sp.dma(O["dbg_x1"], self.x1d, [], []))
            if "B" not in SKIP:
                self.pass_b()
            for ev in self.out_events:
                self.sp.wait_ev(ev)
            self.barrier()
        return nc

    def consts(self):
        Pl = self.Pl
        Pl("memset", [], [self.identf], ap=self.identf[:], constant=0.0)
        Pl("memset", [], [self.mhalf], ap=self.mhalf[:], constant=-0.5)
        self.barrier()
        Pl("affine_select", [self.identf], [self.identf], out=self.identf[:], in_=self.identf[:], pattern=[[-1, 128]],
           compare_op=ALU.not_equal, fill=1.0, base=0, channel_multiplier=1)
        Pl("tensor_copy", [self.identf], [self.ident], out=self.ident[:], in_=self.identf[:])

    def pass_s(self):
        nc = self.nc
        I, O = self.ins, self.outs
        V, A, Pl = self.V, self.A, self.Pl
        sp, pool = self.sp, self.pool
        with ExitStack() as es:
            sb = lambda n, s, d=F32: self.sb(es, n, s, d)
            Toep = sb("Toep", [128, G, 128], BF16)
            Wend = sb("Wend", [128, G, 128], BF16)
            Wend_sw = sb("Wend_sw", [128, G, 128], BF16)
            Cpow = sb("Cpow", [128, G, 128], BF16)
            Cpow_sw = sb("Cpow_sw", [128, G, 128], BF16)
            CT = sb("CT", [128, G, 64])
            ST = sb("ST", [128, G, 64])
            RHOT = sb("RHOT", [128, G, 64])
            RHO1 = sb("RHO1", [128, G])
            AR4 = sb("AR4", [128, G]); AI4 = sb("AI4", [128, G]); ARm4 = sb("ARm4", [128, G]); AIm4 = sb("AIm4", [128, G])
            Drep = sb("Drep", [128, G])
            w_u = sb("w_u", [128, 8, 512], BF16)
            SGN = sb("SGN", [128, 1]); NSGN = sb("NSGN", [128, 1])
            Hin = sb("Hin", [128, G])
            self._w_u = w_u
            NC_ = L // T
            U = sb("s_U", [128, G, NC_], BF16)
            Us = sb("s_Us", [128, G, NSEQ], BF16)
            Pl("memset", [], [Us], ap=Us[:], constant=0.0)
            self._U, self._Us = U, Us
            with ExitStack() as esp:
                gens = [self.ssm_prep(esp, Toep, Wend, Wend_sw, Cpow, Cpow_sw, CT, ST, RHOT, RHO1, AR4, AI4, ARm4, AIm4,
                                      Drep, SGN, NSGN, Hin), self.ssm_stage1(esp, w_u)]
                first = True
                while gens:
                    for g_ in list(gens):
                        try:
                            next(g_)
                        except StopIteration:
                            gens.remove(g_)
                        if first:
                            first = False
                ev1 = sp.dma(self.scrU.rearrange("j p n -> p j n"), self._uT[:], [self._uT], [])
                ev2 = sp.dma(self.scrUs.rearrange("j p n -> p j n"), self._uTs[:], [self._uTs], [])
                sp.wait_ev(ev1); sp.wait_ev(ev2)
                for g in range(G):
                    j, gl = divmod(g, 8)
                    src = bass.AP(tensor=self.scrU.tensor, offset=(j * 128 + gl * 16) * L, ap=[[NC_, T], [L, CH], [1, NC_]])
                    sp.dma(U[:, g, :], src, [], [U], par=True)
                    srcs = bass.AP(tensor=self.scrUs.tensor, offset=(j * 128 + gl * 16) * NSAMP, ap=[[NSEQ, TS], [NSAMP, CH], [1, NSEQ]])
                    sp.dma(Us[64:128, g, :], srcs, [], [Us], par=True)
                for gh in range(self._prep_nparts):
                    self._prep_part2(gh)
                self.ssm_taps(Toep, Wend, Wend_sw, Cpow, Cpow_sw)
                self.barrier()
            self.ssm_main(es, Toep, Wend, Wend_sw, Cpow, Cpow_sw, CT, ST, RHOT, RHO1, AR4, AI4, ARm4, AIm4,
                          Drep, w_u, Hin)

    def ssm_prep(self, es, Toep, Wend, Wend_sw, Cpow, Cpow_sw, CT, ST, RHOT, RHO1, AR4, AI4, ARm4, AIm4, Drep,
                 SGN, NSGN, Hin):
        I = self.ins
        V, A, Pl = self.V, self.A, self.Pl
        sp = self.sp
        sb = lambda n, s, d=F32: self.sb(es, n, s, d)
        lamre2 = sb("lamre2", [128, G]); lamim2 = sb("lamim2", [128, G]); logdt2 = sb("logdt2", [128, G])
        BA = sb("BA", [128, G, CH]); BB = sb("BB", [128, G, CH])
        Ct_re = sb("Ct_re", [128, 4, 128]); Ct_im = sb("Ct_im", [128, 4, 128])
        CRE2 = sb("CRE2", [128, G, CH]); CIM2 = sb("CIM2", [128, G, CH])
        lt_re = sb("lt_re", [G, 128]); lt_im = sb("lt_im", [G, 128]); lt_d = sb("lt_d", [G, 128])
        def _loads():
            w_u = self._w_u
            wv = I["w_in"].rearrange("(k p) n -> p k n", p=128)
            for k in range(8):
                self.pool.dma(w_u[:, k, :], wv[:, k, 0:512], [], [w_u], par=True)
            for h in range(2):
                sl = slice(64 * h, 64 * h + 64)
                sp.dma(lt_re[:, sl], I["lam_re"][:, :], [], [lt_re], par=True)
                sp.dma(lt_im[:, sl], I["lam_im"][:, :], [], [lt_im], par=True)
            for r in range(T):
                sp.dma(lt_d[:, 16 * r:16 * r + 16], I["d"][:, :], [], [lt_d], par=True)
            sp.dma(logdt2[:], I["log_dt"][0:1, :].partition_broadcast(128), [], [logdt2])
            sp.dma(BA[0:64, :, :], I["b_re"].rearrange("g p c -> p g c"), [], [BA], par=True)
            sp.dma(BA[64:128, :, :], I["b_im"].rearrange("g p c -> p g c"), [], [BA], par=True)
            sp.dma(BB[0:64, :, :], I["b_im"].rearrange("g p c -> p g c"), [], [BB], par=True)
            sp.dma(BB[64:128, :, :], I["b_re"].rearrange("g p c -> p g c"), [], [BB], par=True)
            for j in range(4):
                for h in range(2):
                    sp.dma(Ct_re[:, j, 64 * h:64 * h + 64], I["c_re"][128 * j:128 * j + 128, :], [], [Ct_re], par=True)
                    sp.dma(Ct_im[:, j, 64 * h:64 * h + 64], I["c_im"][128 * j:128 * j + 128, :], [], [Ct_im], par=True)
        self.issue_prep_loads = _loads
        Pl("memset", [], [SGN], ap=SGN[0:64, :], constant=1.0)
        Pl("memset", [], [SGN], ap=SGN[64:128, :], constant=-1.0)
        Pl("memset", [], [NSGN], ap=NSGN[0:64, :], constant=-1.0)
        Pl("memset", [], [NSGN], ap=NSGN[64:128, :], constant=1.0)
        Pl("memset", [], [Hin], ap=Hin[:], constant=0.0)
        JVi = sb("JVi", [128, 64], I32); JV = sb("JV", [128, 64])
        KVi = sb("KVi", [128, 32], I32); KV = sb("KV", [128, 32])
        Pl("iota", [], [JVi], out=JVi[:], pattern=[[1, 64]], base=1, channel_multiplier=0)
        Pl("iota", [], [KVi], out=KVi[:, 0:16], pattern=[[1, 16]], base=-7, channel_multiplier=0)
        Pl("iota", [], [KVi], out=KVi[:, 16:32], pattern=[[-1, 16]], base=8, channel_multiplier=0)
        ones = sb("ones", [128, 128]); tmask = sb("tmask", [128, 128])
        Pl("memset", [], [ones], ap=ones[:], constant=1.0)
        self.barrier()
        self.issue_prep_loads()
        for (lt_, dst_) in ((lt_re, lamre2), (lt_im, lamim2), (lt_d, Drep)):
            ps = self.next_ps()
            self.tr(ps, ps[:, 0:G], lt_[0:G, :], self.identf[0:G, 0:G], [lt_, self.identf])
            A([ps], [dst_], out=dst_[:], in_=ps[:, 0:G], func=AF.Copy)
        Pl("tensor_copy", [JVi], [JV], out=JV[:], in_=JVi[:])
        Pl("tensor_copy", [KVi], [KV], out=KV[:], in_=KVi[:])
        for (Ct, Cdst) in ((Ct_re, CRE2), (Ct_im, CIM2)):
            for j in range(4):
                ps = self.next_ps()
                self.tr(ps, ps[:, 0:128], Ct[:, j, :], self.identf[:], [Ct, self.identf])
                A([ps], [Cdst], out=Cdst[:, 8 * j:8 * j + 8, :], in_=ps[:, 0:128].rearrange("p (g c) -> p g c", c=CH), func=AF.Copy)
                yield
        dt = sb("dt", [128, G]); ell = sb("ell", [128, G]); th = sb("th", [128, G])
        den = sb("den", [128, G]); t0 = sb("t0", [128, G]); wr = sb("wr", [128, G]); wi = sb("wi", [128, G])
        A([logdt2], [dt], out=dt[:], in_=logdt2[:], func=AF.Exp)
        yield
        V("tensor_tensor", [lamre2, dt], [ell], out=ell[:], in0=lamre2[:], in1=dt[:], op=ALU.mult)
        V("tensor_tensor", [lamim2, dt], [th], out=th[:], in0=lamim2[:], in1=dt[:], op=ALU.mult)
        V("tensor_tensor", [lamre2], [den], out=den[:], in0=lamre2[:], in1=lamre2[:], op=ALU.mult)
        V("tensor_tensor", [lamim2], [t0], out=t0[:], in0=lamim2[:], in1=lamim2[:], op=ALU.mult)
        V("tensor_tensor", [den, t0], [den], out=den[:], in0=den[:], in1=t0[:], op=ALU.add)
        V("reciprocal", [den], [den], out=den[:], in_=den[:])
        V("tensor_tensor", [lamre2, den], [wr], out=wr[:], in0=lamre2[:], in1=den[:], op=ALU.mult)
        V("scalar_tensor_tensor", [lamim2, den], [wi], out=wi[:], in0=lamim2[:], scalar=-1.0, in1=den[:], op0=ALU.mult, op1=ALU.mult)
        kvals = list(range(-7, 9)) + list(range(8, -8, -1))
        ELLK = sb("ELLK", [128, G, 32]); ANGK = sb("ANGK", [128, G, 32])
        V("tensor_tensor", [ell, KV], [ELLK], out=ELLK[:], in0=ell[:].unsqueeze(2).to_broadcast([128, G, 32]),
          in1=KV[:].unsqueeze(1).to_broadcast([128, G, 32]), op=ALU.mult)
        V("tensor_tensor", [th, KV], [ANGK], out=ANGK[:], in0=th[:].unsqueeze(2).to_broadcast([128, G, 32]),
          in1=KV[:].unsqueeze(1).to_broadcast([128, G, 32]), op=ALU.mult)
        MAGK = ELLK; RR = sb("RR", [128, G, 32]); II = sb("II", [128, G, 32])
        ty = sb("ty", [128, G * 32]); tn = sb("tn", [128, G * 32]); trr = sb("trr", [128, G * 32])
        f1 = lambda b, n=G * 32: b[:, 0:n]
        fl = lambda b: b[:].rearrange("p g k -> p (g k)")
        A([ELLK], [MAGK], out=fl(MAGK), in_=fl(ELLK), func=AF.Exp)
        yield
        self.sin_rr(II, fl(II), ANGK, fl(ANGK), 0.0, (ty, tn, trr), (f1(ty), f1(tn), f1(trr)))
        yield
        self.sin_rr(RR, fl(RR), ANGK, fl(ANGK), math.pi / 2, (ty, tn, trr), (f1(ty), f1(tn), f1(trr)))
        yield
        V("tensor_tensor", [RR, MAGK], [RR], out=fl(RR), in0=fl(RR), in1=fl(MAGK), op=ALU.mult)
        V("tensor_tensor", [II, MAGK], [II], out=fl(II), in0=fl(II), in1=fl(MAGK), op=ALU.mult)
        self.tap("RR", RR); self.tap("II", II); self.tap("th", th); self.tap("ell", ell); self.tap("CRE2", CRE2); self.tap("BA", BA)
        V("tensor_copy", [RR], [AR4], out=AR4[:], in_=RR[:, :, 11])
        V("tensor_scalar", [II, NSGN], [AI4], out=AI4[:], in0=II[:, :, 11], scalar1=NSGN[:, 0:1], scalar2=None, op0=ALU.mult)
        V("tensor_copy", [RR], [ARm4], out=ARm4[:], in_=RR[:, :, 3])
        V("tensor_scalar", [II, NSGN], [AIm4], out=AIm4[:], in0=II[:, :, 3], scalar1=NSGN[:, 0:1], scalar2=None, op0=ALU.mult)
        ph = sb("ph", [128, G]); ANGJ = sb("ANGJ", [128, 16, 64])
        V("tensor_scalar", [th], [ph], out=ph[:], in0=th[:], scalar1=8.0, scalar2=None, op0=ALU.mult)
        V("tensor_scalar", [ph], [t0], out=t0[:], in0=ph[:], scalar1=1.0 / TWO_PI, scalar2=MAGIC, op0=ALU.mult, op1=ALU.add)
        V("tensor_scalar", [t0], [t0], out=t0[:], in0=t0[:], scalar1=MAGIC, scalar2=None, op0=ALU.subtract)
        V("scalar_tensor_tensor", [t0, ph], [ph], out=ph[:], in0=t0[:], scalar=-C1, in1=ph[:], op0=ALU.mult, op1=ALU.add)
        V("scalar_tensor_tensor", [t0, ph], [ph], out=ph[:], in0=t0[:], scalar=-C2, in1=ph[:], op0=ALU.mult, op1=ALU.add)
        f2 = lambda a: a.rearrange("p g j -> p (g j)")
        for gh in range(2):
            hs_ = slice(16 * gh, 16 * gh + 16)
            V("tensor_tensor", [ph, JV], [ANGJ], out=ANGJ[:], in0=ph[:, hs_].unsqueeze(2).to_broadcast([128, 16, 64]),
              in1=JV[:].unsqueeze(1).to_broadcast([128, 16, 64]), op=ALU.mult)
            self.sin_rr(CT, f2(CT[:, hs_, :]), ANGJ, f2(ANGJ[:]), math.pi / 2, (ty, tn, trr), (ty[:], tn[:], trr[:]))
            yield
            self.sin_rr(ST, f2(ST[:, hs_, :]), ANGJ, f2(ANGJ[:]), 0.0, (ty, tn, trr), (ty[:], tn[:], trr[:]))
            yield
        V("tensor_scalar", [ST, SGN], [ST], out=f2(ST[:]), in0=f2(ST[:]), scalar1=SGN[:, 0:1], scalar2=None, op0=ALU.mult)
        A([ell], [RHO1], out=RHO1[:], in_=ell[:], func=AF.Exp, scale=8.0)
        yield
        V("tensor_copy", [RHO1], [RHOT], out=RHOT[:], in_=RHO1[:].unsqueeze(2).to_broadcast([128, G, 64]))
        V("tensor_scalar", [RHOT], [RHOT], out=RHOT[:, :, 0], in0=RHOT[:, :, 0], scalar1=0.0, scalar2=None, op0=ALU.mult)
        self.tap("CT", CT); self.tap("ST", ST); self.tap("RHOT", RHOT); self.tap("ph", ph); self.tap("JV", JV); self.tap("ANGJ", ANGJ)
        NRR = sb("NRR", [128, G, 16]); NII = sb("NII", [128, G, 16])
        V("tensor_scalar", [RR], [NRR], out=NRR[:], in0=RR[:, :, 0:16], scalar1=-1.0, scalar2=None, op0=ALU.mult)
        V("tensor_scalar", [II], [NII], out=NII[:], in0=II[:, :, 0:16], scalar1=-1.0, scalar2=None, op0=ALU.mult)
        X = [sb("X%d" % i, [128, G, 16]) for i in range(4)]
        lo, hi = slice(0, 64), slice(64, 128)
        srcs = [(RR, NII), (NII, NRR), (NII, RR), (NRR, NII)]
        for i in range(4):
            bl, bh = srcs[i]
            V("tensor_copy", [bl], [X[i]], out=X[i][lo, :, :], in_=bl[lo, :, 0:16])
            V("tensor_copy", [bh], [X[i]], out=X[i][hi, :, :], in_=bh[hi, :, 0:16])
        DR = sb("DR", [128, G, 16]); DI = sb("DI", [128, G, 16])
        V("tensor_tensor", [RR], [DR], out=DR[:, :, 1:16], in0=RR[:, :, 16:31], in1=RR[:, :, 17:32], op=ALU.subtract)
        V("tensor_tensor", [II], [DI], out=DI[:, :, 1:16], in0=II[:, :, 16:31], in1=II[:, :, 17:32], op=ALU.subtract)
        QWR = sb("QWR", [128, G, 16]); QWI = sb("QWI", [128, G, 16]); q1 = sb("q1", [128, G, 16])
        wrb = wr[:].unsqueeze(2).to_broadcast([128, G, 15]); wib = wi[:].unsqueeze(2).to_broadcast([128, G, 15])
        s15 = (slice(None), slice(None), slice(1, 16))
        V("tensor_tensor", [DR, wr], [QWR], out=QWR[s15], in0=DR[s15], in1=wrb, op=ALU.mult)
        V("tensor_tensor", [DI, wi], [q1], out=q1[s15], in0=DI[s15], in1=wib, op=ALU.mult)
        V("tensor_tensor", [QWR, q1], [QWR], out=QWR[s15], in0=QWR[s15], in1=q1[s15], op=ALU.subtract)
        V("tensor_tensor", [DR, wi], [QWI], out=QWI[s15], in0=DR[s15], in1=wib, op=ALU.mult)
        V("tensor_tensor", [DI, wr], [q1], out=q1[s15], in0=DI[s15], in1=wrb, op=ALU.mult)
        V("tensor_tensor", [QWI, q1], [QWI], out=QWI[s15], in0=QWI[s15], in1=q1[s15], op=ALU.add)
        V("tensor_scalar", [QWI, NSGN], [QWI], out=QWI[s15], in0=QWI[s15], scalar1=NSGN[:, 0:1], scalar2=None, op0=ALU.mult)
        Pl("affine_select", [ones], [tmask], out=tmask[:].rearrange("p (r c) -> p r c", c=CH),
           in_=ones[:].rearrange("p (r c) -> p r c", c=CH), pattern=[[16, 8], [0, 16]], compare_op=ALU.is_ge, fill=0.0,
           base=15, channel_multiplier=-1)
        GH = 8
        class _View:
            def __init__(self, buf):
                self.buf = buf
        tA, tB, tC, tD = ty, tn, trr, ANGJ
        v4t = {id(ty): lambda: ty[:, 0:1024].rearrange("p (g r c) -> p g r c", g=GH, r=T),
               id(tn): lambda: tn[:, 0:1024].rearrange("p (g r c) -> p g r c", g=GH, r=T),
               id(trr): lambda: trr[:, 0:1024].rearrange("p (g r c) -> p g r c", g=GH, r=T),
               id(ANGJ): lambda: ANGJ[:].rearrange("p a (b c) -> p (a b) c", c=CH).rearrange("p (g r) c -> p g r c", r=T)}
        Cp0 = sb("Cp0", [128, GH, 128]); Bp0 = sb("Bp0", [128, GH, 128]); Bp7 = sb("Bp7", [128, GH, 128])
        v4 = lambda a: a.rearrange("p g (r c) -> p g r c", c=CH)
        def part2(gh):
            hs_ = slice(GH * gh, GH * gh + GH)

            def cmat(outb, out4, Xa, Xb, ki0, E=V, ta=tA, tb=tB):
                c_re_b = CRE2[:, hs_, :].unsqueeze(2).to_broadcast([128, GH, T, CH])
                c_im_b = CIM2[:, hs_, :].unsqueeze(2).to_broadcast([128, GH, T, CH])
                xa = Xa[:, hs_, ki0:ki0 + T].unsqueeze(3).to_broadcast([128, GH, T, CH])
                xb = Xb[:, hs_, ki0:ki0 + T].unsqueeze(3).to_broadcast([128, GH, T, CH])
                E("tensor_tensor", [CRE2, Xa], [ta], out=v4t[id(ta)](), in0=c_re_b, in1=xa, op=ALU.mult)
                E("tensor_tensor", [CIM2, Xb], [tb], out=v4t[id(tb)](), in0=c_im_b, in1=xb, op=ALU.mult)
                E("tensor_tensor", [ta, tb], [outb], out=out4, in0=v4t[id(ta)](), in1=v4t[id(tb)](), op=ALU.add)

            def bmat(outb, i0, E=V, ta=tA, tb=tB):
                ba = BA[:, hs_, :].unsqueeze(2).to_broadcast([128, GH, T, CH])
                bb = BB[:, hs_, :].unsqueeze(2).to_broadcast([128, GH, T, CH])
                qa = QWR[:, hs_, i0:i0 + T].unsqueeze(3).to_broadcast([128, GH, T, CH])
                qb = QWI[:, hs_, i0:i0 + T].unsqueeze(3).to_broadcast([128, GH, T, CH])
                E("tensor_tensor", [BA, QWR], [ta], out=v4t[id(ta)](), in0=ba, in1=qa, op=ALU.mult)
                E("tensor_tensor", [BB, QWI], [tb], out=v4t[id(tb)](), in0=bb, in1=qb, op=ALU.mult)
                E("tensor_tensor", [ta, tb], [outb], out=v4(outb[:]), in0=v4t[id(ta)](), in1=v4t[id(tb)](), op=ALU.add)

            bmat(Bp7, 1, Pl, tC, tD)
            cmat(Cp0, v4(Cp0[:]), X[0], X[1], 7)
            bmat(Bp0, 8)
            cmat(Cpow, v4(Cpow[:, hs_, :]), X[0], X[1], 8)
            cmat(Cpow_sw, v4(Cpow_sw[:, hs_, :]), X[2], X[3], 8)
            for gl in range(GH):
                g = GH * gh + gl
                ps = self.next_ps()
                self.mm(ps, ps[:, 0:128], Bp0[:, gl, :], Cp0[:, gl, :], [Bp0, Cp0])
                V("tensor_tensor", [ps, tmask], [Toep], out=Toep[:, g, :], in0=ps[:, 0:128], in1=tmask[:], op=ALU.mult)
                ps2 = self.next_ps()
                self.tr(ps2, ps2[:, 0:128], Bp7[:, gl, :], self.identf[:], [Bp7, self.identf])
                A([ps2], [Wend], out=Wend[:, g, :], in_=ps2[:, 0:128], func=AF.Copy)
                A([ps2], [Wend_sw], out=Wend_sw[:, g, 0:64], in_=ps2[:, 64:128], func=AF.Copy)
                A([ps2], [Wend_sw], out=Wend_sw[:, g, 64:128], in_=ps2[:, 0:64], func=AF.Copy)

        self._prep_part2 = part2
        self._prep_nparts = G // GH

    def ssm_taps(self, Toep, Wend, Wend_sw, Cpow, Cpow_sw):
        self.tap("Toep", Toep); self.tap("Wend", Wend); self.tap("Wend_sw", Wend_sw); self.tap("Cpow", Cpow); self.tap("Cpow_sw", Cpow_sw)

    def ssm_stage1(self, es, w_u):
        nc = self.nc
        I = self.ins
        A, pool, sp = self.A, self.pool, self.sp
        sb = lambda n, s, d=F32: self.sb(es, n, s, d)
        NC = L // T
        self.scrU = nc.dram_tensor("scrU", [4, 128, L], BF16, kind="Internal").ap()
        self.scrUs = nc.dram_tensor("scrUs", [4, 128, NSAMP], BF16, kind="Internal").ap()
        x_bf = sb("s_xbf", [128, 4, D], BF16)
        xT = sb("s_xT", [128, 8, 512], BF16)
        uT = sb("s_uT", [128, 4, L], BF16)
        uTs = sb("s_uTs", [128, 4, NSAMP], BF16)
        xall = I["x"]
        nblk = len(BLOCKS)
        self.load_tokens(xall, 0, x_bf)
        for bi in range(nblk):
            t0_, nt = BLOCKS[bi]
            samp = (bi == nblk - 1)
            self.transposes(x_bf, xT, nt)
            if bi + 1 < nblk:
                self.load_tokens(xall, bi + 1, x_bf)
            yield
            for j in range(4):
                ps = self.next_ps()
                for k in range(8):
                    self.mm(ps, ps[:, 0:nt], w_u[:, k, 128 * j:128 * j + 128], xT[:, k, 0:nt], [w_u, xT],
                            start=(k == 0), stop=(k == 7))
                if not samp:
                    A([ps], [uT], out=uT[:, j, :].rearrange("p (r c) -> p r c", c=NC)[:, :, 64 * bi:64 * bi + 64],
                      in_=ps[:, 0:512].rearrange("p (c r) -> p r c", r=T), func=AF.Copy)
                else:
                    A([ps], [uTs], out=uTs[:, j, :].rearrange("p (t s) -> p t s", s=NSEQ),
                      in_=ps[:, 0:64].rearrange("p (s t) -> p t s", t=TS), func=AF.Copy)
                yield
        self._uT, self._uTs = uT, uTs

    def ssm_main(self, es, Toep, Wend, Wend_sw, Cpow, Cpow_sw, CT, ST, RHOT, RHO1, AR4, AI4, ARm4, AIm4, Drep, w_u, Hin):
        nc = self.nc
        I, O = self.ins, self.outs
        V, A, Pl = self.V, self.A, self.Pl
        sp, pool = self.sp, self.pool
        sb = lambda n, s, d=F32: self.sb(es, n, s, d)
        NC = L // T
        scrG = nc.dram_tensor("scrG", [128, G, NC], BF16, kind="Internal").ap()
        scrGs = nc.dram_tensor("scrGs", [64, G, NSEQ], BF16, kind="Internal").ap()
        U, Us = self._U, self._Us
        Gall = sb("s_G", [128, G, NC], BF16)
        V1e = [sb("s_V1e%d" % i, [128, 8, 65], BF16) for i in range(2)]
        V2e = [sb("s_V2e%d" % i, [128, 8, 65], BF16) for i in range(2)]
        T1 = sb("s_T1", [128, 8, 64]); T2 = sb("s_T2", [128, 8, 64]); Z = sb("s_Z", [128, 8, 64])
        W1 = sb("s_W1", [128, 8]); W2 = sb("s_W2", [128, 8]); W3 = sb("s_W3", [128, 8])
        Hnew = sb("s_Hnew", [128, G])
        tmpY = sb("s_tmpY", [128, 8, 64])
        for i in range(2):
            Pl("memset", [], [V2e[i]], ap=V2e[i][:], constant=0.0)
        hs = sb("s_hs", [128, 4, 128]); hs_sw = sb("s_hs_sw", [128, 4, 128])
        H0 = sb("s_H0", [128, NSEQ, G]); H0sw = sb("s_H0sw", [128, NSEQ, G])
        self.barrier()
        for j in range(4):
            sp.dma(hs[:, j, 0:64], I["sre"][128 * j:128 * j + 128, :], [], [hs], par=True)
            sp.dma(hs[:, j, 64:128], I["sim"][128 * j:128 * j + 128, :], [], [hs], par=True)
            sp.dma(hs_sw[:, j, 0:64], I["sim"][128 * j:128 * j + 128, :], [], [hs_sw], par=True)
            sp.dma(hs_sw[:, j, 64:128], I["sre"][128 * j:128 * j + 128, :], [], [hs_sw], par=True)

        nblk = len(BLOCKS)
        scrU, scrUs = self.scrU, self.scrUs
        self.tap("U0", U)
        s3 = lambda b: b[:, 0:512].rearrange("p (g c) -> p g c", c=64)
        slots = [dict(T1=T1, T2=T2, Z=Z, W1=W1, W2=W2, W3=W3, tmpY=tmpY, V1=V1e[0], V2=V2e[0]),
                 dict(T1=sb("s_T1b", [128, 8, 64]), T2=sb("s_T2b", [128, 8, 64]), Z=sb("s_Zb", [128, 8, 64]),
                      W1=sb("s_W1b", [128, 8]), W2=sb("s_W2b", [128, 8]), W3=sb("s_W3b", [128, 8]),
                      tmpY=sb("s_tmpYb", [128, 8, 64]), V1=V1e[1], V2=V2e[1])]

        def lvl_b(bi, gs, B):
            T1, T2, Z, W1, W2, W3, tmpY, V1, V2 = (B[k] for k in ("T1", "T2", "Z", "W1", "W2", "W3", "tmpY", "V1", "V2"))
            csl = slice(64 * bi, 64 * bi + 64)
            g0 = 8 * gs
            gsl = slice(g0, g0 + 8)
            psS = self.next_ps(); psW = self.next_ps()
            for gl in range(8):
                g = g0 + gl
                self.mm(psS, psS[:, 64 * gl:64 * gl + 64], Wend[:, g, :], U[:, g, csl], [Wend, U], signal=(gl == 7))
            for gl in range(8):
                g = g0 + gl
                self.mm(psW, psW[:, 64 * gl:64 * gl + 64], Wend_sw[:, g, :], U[:, g, csl], [Wend_sw, U], signal=(gl == 7))
            yield
            V("tensor_tensor", [psS, CT], [T1], out=T1[:], in0=s3(psS), in1=CT[:, gsl, :], op=ALU.mult)
            V("tensor_tensor", [psW, ST], [T2], out=T2[:], in0=s3(psW), in1=ST[:, gsl, :], op=ALU.mult)
            V("tensor_tensor", [RHO1, Hin], [W1], out=W1[:], in0=RHO1[:, gsl], in1=Hin[:, gsl], op=ALU.mult)
            yield
            Pl("tensor_tensor", [T1, T2], [T1], out=T1[:], in0=T1[:], in1=T2[:], op=ALU.add)
            yield
            V("tensor_tensor", [T1, W1], [T1], out=T1[:, :, 0], in0=T1[:, :, 0], in1=W1[:], op=ALU.add)
            V("tensor_tensor_scan", [RHOT, T1], [Z], out=Z[:].rearrange("p g c -> p (g c)"),
              data0=RHOT[:, gsl, :].rearrange("p g c -> p (g c)"), data1=T1[:].rearrange("p g c -> p (g c)"),
              initial=0.0, op0=ALU.mult, op1=ALU.add)
            yield
            V("tensor_tensor", [Z, CT], [V1], out=V1[:, :, 1:65], in0=Z[:], in1=CT[:, gsl, :], op=ALU.mult)
            Pl("tensor_tensor", [Z, ST], [V2], out=V2[:, :, 1:65], in0=Z[:], in1=ST[:, gsl, :], op=ALU.mult)
            V("tensor_copy", [Hin], [V1], out=V1[:, :, 0], in_=Hin[:, gsl])
            Pl("tensor_tensor", [U, Drep], [tmpY], out=tmpY[:], in0=U[:, gsl, csl],
               in1=Drep[:, gsl].unsqueeze(2).to_broadcast([128, 8, 64]), op=ALU.mult)
            yield
            V("tensor_tensor", [Z, CT], [W1], out=W1[:], in0=Z[:, :, 63], in1=CT[:, gsl, 63], op=ALU.mult)
            V("tensor_tensor", [Z, ST], [W2], out=W2[:], in0=Z[:, :, 63], in1=ST[:, gsl, 63], op=ALU.mult)
            yield
            V("tensor_copy", [W2], [W3], out=W3[0:64, :], in_=W2[64:128, :])
            V("tensor_copy", [W2], [W3], out=W3[64:128, :], in_=W2[0:64, :])
            yield
            V("tensor_tensor", [W1, W3], [Hnew], out=Hnew[:, gsl], in0=W1[:], in1=W3[:], op=ALU.add)
            psY = self.next_ps()
            for gl in range(8):
                g = g0 + gl
                o_ = psY[:, 64 * gl:64 * gl + 64]
                self.mm(psY, o_, Toep[:, g, :], U[:, g, csl], [Toep, U], start=True, stop=False)
                self.mm(psY, o_, Cpow[:, g, :], V1[:, gl, 0:64], [Cpow, V1], start=False, stop=False)
                self.mm(psY, o_, Cpow_sw[:, g, :], V2[:, gl, 0:64], [Cpow_sw, V2], start=False, stop=True, signal=(gl == 7))
            yield
            V("tensor_tensor", [tmpY, psY], [tmpY], out=tmpY[:], in0=tmpY[:], in1=s3(psY), op=ALU.add)
            yield
            A([tmpY], [Gall], out=Gall[:, gsl, csl], in_=tmpY[:], func=AF.Gelu_apprx_tanh)

        for bi in range(nblk - 1):
            for pair in ((0, 1), (2, 3)):
                gens = [lvl_b(bi, pair[0], slots[0]), lvl_b(bi, pair[1], slots[1])]
                while gens:
                    for g_ in list(gens):
                        try:
                            next(g_)
                        except StopIteration:
                            gens.remove(g_)
            V("tensor_copy", [Hnew], [Hin], out=Hin[:], in_=Hnew[:])
        ps = self.next_ps()
        self.tr(ps, ps[0:G, 0:128], Hin[:, :], self.identf[:], [Hin, self.identf])
        hp = sb("s_hp", [G, 128])
        A([ps], [hp], out=hp[:], in_=ps[0:G, 0:128], func=AF.Copy)
        self.out_events.append(sp.dma(O["hrp"][:, :], hp[:, 0:64], [hp], []))
        self.out_events.append(sp.dma(O["hip"][:, :], hp[:, 64:128], [hp], []))
        sp.dma(scrG, Gall[:], [Gall], [])
        for (src_, dst) in ((hs, H0), (hs_sw, H0sw)):
            for j in range(4):
                ps = self.next_ps()
                self.tr(ps, ps[:, 0:128], src_[:, j, :], self.identf[:], [src_, self.identf])
                A([ps], [dst], out=dst[:, 4 * j:4 * j + 4, :], in_=ps[:, 0:128].rearrange("p (s g) -> p s g", g=G), func=AF.Copy)
        Hm = sb("s_Hm", [128, NSEQ, G]); Hp = sb("s_Hp", [128, NSEQ, G]); tq = sb("s_tq", [128, NSEQ, G])
        Hm_bf = sb("s_Hmbf", [128, G, NSEQ], BF16)
        bc = lambda b: b[:].unsqueeze(1).to_broadcast([128, NSEQ, G])
        V("tensor_tensor", [H0, ARm4], [Hm], out=Hm[:], in0=H0[:], in1=bc(ARm4), op=ALU.mult)
        V("tensor_tensor", [H0sw, AIm4], [tq], out=tq[:], in0=H0sw[:], in1=bc(AIm4), op=ALU.mult)
        V("tensor_tensor", [Hm, tq], [Hm_bf], out=Hm_bf[:].rearrange("p g s -> p s g"), in0=Hm[:], in1=tq[:], op=ALU.add)
        V("tensor_tensor", [H0, AR4], [Hp], out=Hp[:], in0=H0[:], in1=bc(AR4), op=ALU.mult)
        V("tensor_tensor", [H0sw, AI4], [tq], out=tq[:], in0=H0sw[:], in1=bc(AI4), op=ALU.mult)
        V("tensor_tensor", [Hp, tq], [Hp], out=Hp[:], in0=Hp[:], in1=tq[:], op=ALU.add)
        Hout = sb("s_Hout", [128, NSEQ, G])
        Gs = sb("s_Gs", [128, G, NSEQ], BF16)
        tmps = sb("s_tmps", [128, G, NSEQ])
        psS = self.next_ps(); psY = self.next_ps()
        for g in range(G):
            self.mm(psS, psS[:, NSEQ * g:NSEQ * g + NSEQ], Wend[:, g, :], Us[:, g, :], [Wend, Us], signal=(g == G - 1))
        V("tensor_tensor", [psS, Hp], [Hout], out=Hout[:], in0=psS[:, 0:512].rearrange("p (g s) -> p s g", s=NSEQ),
          in1=Hp[:], op=ALU.add)
        for g in range(G):
            o_ = psY[:, NSEQ * g:NSEQ * g + NSEQ]
            self.mm(psY, o_, Toep[:, g, :], Us[:, g, :], [Toep, Us], start=True, stop=False)
            self.mm(psY, o_, Cpow[:, g, :], Hm_bf[:, g, :], [Cpow, Hm_bf], start=False, stop=True, signal=(g == G - 1))
        V("tensor_tensor", [Us, Drep], [tmps], out=tmps[:], in0=Us[:], in1=Drep[:].unsqueeze(2).to_broadcast([128, G, NSEQ]), op=ALU.mult)
        V("tensor_tensor", [tmps, psY], [tmps], out=tmps[:], in0=tmps[:], in1=psY[:, 0:512].rearrange("p (g s) -> p g s", s=NSEQ), op=ALU.add)
        A([tmps], [Gs], out=Gs[:], in_=tmps[:], func=AF.Gelu_apprx_tanh)
        sp.dma(scrGs, Gs[64:128, :, :], [Gs], [])
        ho = sb("s_ho", [128, 4, 128])
        for j in range(4):
            ps = self.next_ps()
            self.tr(ps, ps[:, 0:128], Hout[:, 4 * j:4 * j + 4, :].rearrange("p s g -> p (s g)"), self.identf[:], [Hout, self.identf])
            A([ps], [ho], out=ho[:, j, :], in_=ps[:, 0:128], func=AF.Copy)
            self.out_events.append(sp.dma(O["hrs"][128 * j:128 * j + 128, :], ho[:, j, 0:64], [ho], []))
            self.out_events.append(sp.dma(O["his"][128 * j:128 * j + 128, :], ho[:, j, 64:128], [ho], []))
        self.scrG, self.scrGs = scrG, scrGs

    def load_tokens(self, src, bi, dst):
        t0_, nt = BLOCKS[bi]
        ntile = (nt + 127) // 128
        tp = min(nt, 128)
        for i in range(ntile):
            self.pool.dma(dst[0:tp, i, :], src[t0_ + 128 * i:t0_ + 128 * i + tp, :], [], [dst], par=True)

    def transposes(self, xb, xt, nt):
        ntile = (nt + 127) // 128
        tp = min(nt, 128)
        for k in range(8):
            ps = self.next_ps()
            pv = ps[:].bitcast(BF16)
            for i in range(ntile):
                self.tr(ps, pv[:, 128 * i:128 * i + tp], xb[0:tp, i, 128 * k:128 * k + 128], self.ident[0:tp, 0:tp],
                        [xb, self.ident], signal=(i == ntile - 1))
            self.A([ps], [xt], out=xt[:, k, 0:nt], in_=pv[:, 0:nt], func=AF.Copy)

    def halves(self, buf):
        return (Buf(buf.t, buf.name + "_lo"), Buf(buf.t, buf.name + "_hi"))

    def layer_norm(self, ps_pair, x_tok, r, xh, g_bc, b_bc, st6, mv, sd, tp):
        V, A, Pl = self.V, self.A, self.Pl
        cs = [slice(0, 512), slice(512, D)]
        for h in range(2):
            V("scalar_tensor_tensor", [x_tok, ps_pair[h]], [r[h]], out=r[h][0:tp, cs[h]],
              in0=x_tok[0:tp, cs[h]], scalar=ALPHA, in1=ps_pair[h][0:tp, :], op0=ALU.mult, op1=ALU.add)
        for h in range(2):
            V("bn_stats", [r[h]], [st6], out=st6[0:tp, h, :], in_=r[h][0:tp, cs[h]])
        V("bn_aggr", [st6], [mv], out=mv[0:tp, :], in_=st6[0:tp, :, :].rearrange("p a b -> p (a b)"))
        V("tensor_scalar", [mv], [sd], out=sd[0:tp, 0:1], in0=mv[0:tp, 1:2], scalar1=LN_EPS, scalar2=None, op0=ALU.add)
        Pl("tensor_tensor", [sd, self.mhalf], [sd], out=sd[0:tp, 1:2], in0=sd[0:tp, 0:1], in1=self.mhalf[0:tp, 0:1], op=ALU.pow)
        V("scalar_tensor_tensor", [mv, sd], [sd], out=sd[0:tp, 2:3], in0=mv[0:tp, 0:1], scalar=-1.0, in1=sd[0:tp, 1:2],
          op0=ALU.mult, op1=ALU.mult)
        V("tensor_scalar", [r[0], sd], [xh[0]], out=xh[0][0:tp, cs[0]], in0=r[0][0:tp, cs[0]], scalar1=sd[0:tp, 1:2], scalar2=sd[0:tp, 2:3], op0=ALU.mult, op1=ALU.add)
        Pl("tensor_scalar", [r[1], sd], [xh[1]], out=xh[1][0:tp, cs[1]], in0=r[1][0:tp, cs[1]], scalar1=sd[0:tp, 1:2], scalar2=sd[0:tp, 2:3], op0=ALU.mult, op1=ALU.add)
        Pl("tensor_tensor", [xh[1], g_bc], [xh[1]], out=xh[1][0:tp, cs[1]], in0=xh[1][0:tp, cs[1]], in1=g_bc[0:tp, cs[1]], op=ALU.mult)
        V("tensor_tensor", [xh[0], g_bc], [xh[0]], out=xh[0][0:tp, cs[0]], in0=xh[0][0:tp, cs[0]], in1=g_bc[0:tp, cs[0]], op=ALU.mult)
        Pl("tensor_tensor", [xh[1], b_bc], [xh[1]], out=xh[1][0:tp, cs[1]], in0=xh[1][0:tp, cs[1]], in1=b_bc[0:tp, cs[1]], op=ALU.add)
        V("tensor_tensor", [xh[0], b_bc], [xh[0]], out=xh[0][0:tp, cs[0]], in0=xh[0][0:tp, cs[0]], in1=b_bc[0:tp, cs[0]], op=ALU.add)

    def pass_a(self):
        nc = self.nc
        I, O = self.ins, self.outs
        V, A, Pl = self.V, self.A, self.Pl
        sp, pool = self.sp, self.pool
        NC = L // T
        with ExitStack() as es:
            sb = lambda n, s, d=F32: self.sb(es, n, s, d)
            gT = sb("gT", [128, 4, NTOK], BF16)
            w_a = sb("w_a", [128, 8, 2816], BF16)
            w_glu = sb("w_glu", [128, 4, 2048], BF16)
            w_att = sb("w_att", [128, 4, D], BF16)
            w_o = sb("w_o", [128, 8, D], BF16)
            g_bc = sb("g1_bc", [128, D]); b_bc = sb("b1_bc", [128, D])
            sp.dma(g_bc[:], I["ln1_g"][0:1, :].partition_broadcast(128), [], [g_bc])
            sp.dma(b_bc[:], I["ln1_b"][0:1, :].partition_broadcast(128), [], [b_bc])
            maskD = sb("maskD", [128, 512], BF16); maskP = sb("maskP", [128, 512], BF16)
            maskC = sb("maskC", [128, 256], BF16); maskN = sb("maskN", [64, 256], BF16)
            es8 = sb("es8", [128, 8]); ES = sb("ES", [128, 2, 4, 128])
            vext = sb("vext", [128, 5, 2, 128], BF16)
            vc_ext = sb("vc_ext", [128, NSEQ, 2, 128], BF16)
            es_m = ExitStack()
            zer = self.sb(es_m, "zer", [128, 512]); mtmp = self.sb(es_m, "mtmp", [128, 512]); one_t = self.sb(es_m, "one_t", [128, 256])
            Pl("memset", [], [one_t], ap=one_t[:], constant=1.0)
            Pl("memset", [], [zer], ap=zer[:], constant=0.0)
            Pl("memset", [], [vext], ap=vext[:], constant=1.0)
            Pl("memset", [], [vc_ext], ap=vc_ext[:], constant=1.0)
            self.barrier()
            sp.dma(es8[:], I["sinks"][0:1, :].partition_broadcast(128), [], [es8])
            A([es8], [es8], out=es8[:], in_=es8[:], func=AF.Exp)
            V("tensor_copy", [es8], [ES], out=ES[:].rearrange("p a h q -> p (a h) q"), in_=es8[:].unsqueeze(2).to_broadcast([128, 8, 128]))
            z3 = zer[:].rearrange("p (h q) -> p h q", q=128)
            Pl("affine_select", [zer], [mtmp], out=mtmp[:].rearrange("p (h q) -> p h q", q=128), in_=z3, pattern=[[0, 4], [1, 128]],
               compare_op=ALU.is_ge, fill=NEG, base=0, channel_multiplier=-1)
            Pl("tensor_copy", [mtmp], [maskD], out=maskD[:], in_=mtmp[:])
            Pl("affine_select", [zer], [mtmp], out=mtmp[:].rearrange("p (h q) -> p h q", q=128), in_=z3, pattern=[[0, 4], [-1, 128]],
               compare_op=ALU.is_ge, fill=NEG, base=-1, channel_multiplier=1)
            Pl("tensor_copy", [mtmp], [maskP], out=maskP[:], in_=mtmp[:])
            Pl("affine_select", [one_t], [mtmp], out=mtmp[:, 0:256].rearrange("p (s h t) -> p s h t", h=4, t=TS),
               in_=one_t[:, 0:256].rearrange("p (s h t) -> p s h t", h=4, t=TS), pattern=[[0, NSEQ], [0, 4], [-1, TS]],
               compare_op=ALU.is_ge, fill=0.0, base=-1, channel_multiplier=1)
            Pl("tensor_copy", [mtmp], [maskC], out=maskC[:], in_=mtmp[:, 0:256])
            Pl("affine_select", [one_t], [mtmp], out=mtmp[0:64, 0:256].rearrange("p (h s t) -> p h s t", s=NSEQ, t=TS),
               in_=one_t[0:64, 0:256].rearrange("p (h s t) -> p h s t", s=NSEQ, t=TS), pattern=[[0, 4], [-4, NSEQ], [0, TS]],
               compare_op=ALU.is_ge, fill=0.0, base=0, channel_multiplier=1)
            Pl("affine_select", [mtmp], [mtmp], out=mtmp[0:64, 0:256].rearrange("p (h s t) -> p h s t", s=NSEQ, t=TS),
               in_=mtmp[0:64, 0:256].rearrange("p (h s t) -> p h s t", s=NSEQ, t=TS), pattern=[[0, 4], [4, NSEQ], [1, TS]],
               compare_op=ALU.is_ge, fill=0.0, base=0, channel_multiplier=-1)
            Pl("tensor_copy", [mtmp], [maskN], out=maskN[:], in_=mtmp[0:64, 0:256])
            self.barrier()
            es_m.close()
            x_bf = [sb("a_xbf", [128, 4, D], BF16)] * 2
            xT = sb("a_xT", [128, 8, 512], BF16)
            qT = sb("a_qT", [128, 4, 512], BF16)
            kT = sb("a_kT", [128, 640], BF16)
            kvf = sb("a_kvf", [128, 256])
            PT = [sb("a_PT%d" % i, [128, 512], BF16) for i in range(4)]
            oT = sb("a_oT", [128, 4, 512], BF16)
            den = sb("a_den", [64, 512]); rec = sb("a_rec", [64, 512])
            dens = [den, rec]
            osc = sb("a_osc", [128, 256]); osum = sb("a_osum", [128, 256])
            sig = [sb("a_sig%d" % i, [128, 512]) for i in range(2)]
            gsb = [sb("a_gs%d" % i, [128, 512], BF16) for i in range(2)]
            gab = [sb("a_ga%d" % i, [128, 512], BF16) for i in range(2)]
            bsb = [sb("a_bs%d" % i, [128, 512], BF16) for i in range(2)]
            t1 = sb("a_t1", [128, 512]); t2 = sb("a_t2", [128, 512])
            mT = sb("a_mT", [128, 8, 512], BF16)
            x_tok = [sb("a_xtok", [128, D])] * 2
            rr = [self.halves(sb("a_r", [128, D]))] * 2
            xh = [self.halves(sb("a_xh", [128, D]))] * 2
            st6 = sb("a_st6", [128, 2, 6]); mv = sb("a_mv", [128, 2]); sd = sb("a_sd", [128, 3])
            ckb = sb("a_ckb", [128, NSEQ, 128], BF16); kcT = sb("a_kcT", [128, NSEQ, 128], BF16)
            if "A_d2d" not in SKIP:
                self.out_events.append(sp.dma(O["kws"][:, 0:124, :], I["ck"][:, 4:128, :], [], []))
                self.out_events.append(sp.dma(O["vws"][:, 0:124, :], I["cv"][:, 4:128, :], [], []))

            xall = I["x"]
            nblk = len(BLOCKS)
            self.load_tokens(xall, 0, x_bf[0])
            wv = I["w_in"].rearrange("(k p) n -> p k n", p=128)
            for k in range(8 if "A_w" not in SKIP else 0):
                pool.dma(w_a[:, k, 0:768], wv[:, k, 512:1280], [], [w_a], par=True)
            wg = I["w_glu"].rearrange("(k p) n -> p k n", p=128)
            for k in range(4 if "A_w" not in SKIP else 0):
                for c in range(2):
                    pool.dma(w_glu[:, k, 1024 * c:1024 * c + 1024], wg[:, k, 1024 * c:1024 * c + 1024], [], [w_glu], par=True)
            wa = I["w_attn"].rearrange("(k p) n -> p k n", p=128)
            for k in range(4 if "A_w" not in SKIP else 0):
                pool.dma(w_att[:, k, :], wa[:, k, :], [], [w_att], par=True)
            for k in range(8 if "A_w" not in SKIP else 0):
                for c in range(2):
                    pool.dma(w_a[:, k, 768 + 1024 * c:1792 + 1024 * c], wv[:, k, 1280 + 1024 * c:2304 + 1024 * c], [], [w_a], par=True)
            wo = I["w_o"].rearrange("(k p) n -> p k n", p=128)
            for k in range(8 if "A_w" not in SKIP else 0):
                pool.dma(w_o[:, k, :], wo[:, k, :], [], [w_o], par=True)
            for g in range(G):
                j, gl = divmod(g, 8)
                src = bass.AP(tensor=self.scrG.tensor, offset=g * NC, ap=[[G * NC, CH], [CH * G * NC, T], [1, NC]])
                sp.dma(gT[16 * gl:16 * gl + 16, j, 0:L].rearrange("c (r n) -> c r n", n=NC), src, [], [gT], par=True)
                srcs = bass.AP(tensor=self.scrGs.tensor, offset=g * NSEQ, ap=[[G * NSEQ, CH], [CH * G * NSEQ, TS], [1, NSEQ]])
                sp.dma(gT[16 * gl:16 * gl + 16, j, L:L + NSAMP].rearrange("c (r n) -> c r n", n=NSEQ), srcs, [], [gT], par=True)
            if "A_cache" not in SKIP:
                pool.dma(ckb[:], I["ck"].rearrange("s w d -> w s d"), [], [ckb])
            for a_ in range(2 if "A_cache" not in SKIP else 0):
                pool.dma(vc_ext[:, :, a_, 0:64], I["cv"][:, :, 64 * a_:64 * a_ + 64].rearrange("s w d -> w s d"), [], [vc_ext])
            def blk(bi):
                if "A_blk" in SKIP or ("A_samp" in SKIP and bi == nblk - 1) or ("A_prompt" in SKIP and bi < nblk - 1):
                    return
                t0_, nt = BLOCKS[bi]
                ntile = (nt + 127) // 128
                tp = min(nt, 128)
                samp = (bi == nblk - 1)
                xb = x_bf[bi % 2]
                self.transposes(xb, xT, nt)
                if bi + 1 < nblk:
                    self.load_tokens(xall, bi + 1, x_bf[(bi + 1) % 2])
                if "A_proj" in SKIP:
                    return
                for j in range(4):
                    ps = self.next_ps()
                    for k in range(8):
                        self.mm(ps, ps[:, 0:nt], w_a[:, k, 128 * j:128 * j + 128], xT[:, k, 0:nt], [w_a, xT], start=(k == 0), stop=(k == 7))
                    A([ps], [qT], out=qT[:, j, 0:nt], in_=ps[:, 0:nt], func=AF.Copy)
                ps = self.next_ps()
                for k in range(8):
                    self.mm(ps, ps[:, 0:nt], w_a[:, k, 512:640], xT[:, k, 0:nt], [w_a, xT], start=(k == 0), stop=(k == 7))
                A([ps], [kT], out=kT[:, 128:128 + nt], in_=ps[:, 0:nt], func=AF.Copy)
                for i in range(ntile if "A_kvt" not in SKIP else 0):
                    ps = self.next_ps()
                    for k in range(8):
                        self.mm(ps, ps[0:tp, 0:256], xT[:, k, 128 * i:128 * i + tp], w_a[:, k, 512:768], [w_a, xT], start=(k == 0), stop=(k == 7))
                    if "A_kv_act" not in SKIP:
                        A([ps], [vext], out=vext[0:tp, 1 + i, :, 0:64], in_=ps[0:tp, 128:256].rearrange("p (a d) -> p a d", a=2), func=AF.Copy)
                    if ((bi == nblk - 2 and i == ntile - 1) or samp) and "A_kv_out" not in SKIP:
                        A([ps], [kvf], out=kvf[0:tp, :], in_=ps[0:tp, 0:256], func=AF.Copy)
                        if samp:
                            self.out_events.append(sp.dma(O["kws"][:, 124:128, :], kvf[0:NSAMP, 0:128], [kvf], []))
                            self.out_events.append(sp.dma(O["vws"][:, 124:128, :], kvf[0:NSAMP, 128:256], [kvf], []))
                        elif "A_kv_dma" not in SKIP:
                            self.out_events.append(sp.dma(O["kwp"][:, :], kvf[:, 0:128], [kvf], []))
                            self.out_events.append(sp.dma(O["vwp"][:, :], kvf[:, 128:256], [kvf], []))
                if "A_attn" in SKIP:
                    pass
                elif not samp:
                    def attn_unit(i, kv, slot):
                        qs = slice(128 * i, 128 * i + 128)
                        hp_ = slice(64 * kv, 64 * kv + 64)
                        has_prev = not (bi == 0 and i == 0)
                        rhs_q = qT[hp_, :, qs]
                        den_ = dens[slot]
                        PTd, PTp = PT[2 * slot], PT[2 * slot + 1]
                        psD = self.next_ps()
                        self.mm(psD, psD[:, :], self.ident[:], maskD[:], [self.ident, maskD], start=True, stop=False)
                        self.mm(psD, psD[:, :].rearrange("p (h q) -> p h q", q=128), kT[hp_, 128 + 128 * i:256 + 128 * i], rhs_q, [kT, qT], start=False, stop=True)
                        if has_prev:
                            psP = self.next_ps()
                            self.mm(psP, psP[:, :], self.ident[:], maskP[:], [self.ident, maskP], start=True, stop=False)
                            self.mm(psP, psP[:, :].rearrange("p (h q) -> p h q", q=128), kT[hp_, 128 * i:128 + 128 * i], rhs_q, [kT, qT], start=False, stop=True)
                        yield
                        A([psD], [PTd], out=PTd[:], in_=psD[:, :], func=AF.Exp, scale=0.125)
                        if has_prev:
                            A([psP], [PTp], out=PTp[:], in_=psP[:, :], func=AF.Exp, scale=0.125)
                        yield
                        psO = self.next_ps()
                        self.mm(psO, psO[:, :], vext[:, 1 + i, kv, :], PTd[:], [vext, PTd], start=True, stop=not has_prev)
                        if has_prev:
                            self.mm(psO, psO[:, :], vext[:, i, kv, :], PTp[:], [vext, PTp], start=False, stop=True)
                        yield
                        V("tensor_tensor", [psO, ES], [den_], out=den_[:], in0=psO[64:128, :], in1=ES[64:128, kv, :, :].rearrange("p h q -> p (h q)"), op=ALU.add)
                        yield
                        A([den_], [den_], out=den_[:], in_=den_[:], func=AF.Ln)
                        A([den_], [den_], out=den_[:], in_=den_[:], func=AF.Exp, scale=-1.0)
                        yield
                        V("tensor_tensor", [psO, den_], [oT], out=oT[hp_, :, qs], in0=psO[0:64, :].rearrange("p (h q) -> p h q", q=128),
                          in1=den_[:].rearrange("p (h q) -> p h q", q=128), op=ALU.mult)

                    units = [(i, kv) for i in range(ntile) for kv in range(2)]
                    for u0 in range(0, len(units), 2):
                        gens = [attn_unit(units[u0][0], units[u0][1], 0), attn_unit(units[u0 + 1][0], units[u0 + 1][1], 1)]
                        while gens:
                            for g_ in list(gens):
                                try:
                                    next(g_)
                                except StopIteration:
                                    gens.remove(g_)
                else:
                    for s_ in range(NSEQ):
                        ps = self.next_ps()
                        pv = ps[:].bitcast(BF16)
                        self.tr(ps, pv[:, 0:128], ckb[:, s_, :], self.ident[:], [ckb, self.ident])
                        A([ps], [kcT], out=kcT[:, s_, :], in_=pv[:, 0:128], func=AF.Copy)
                    for kv in range(2):
                        hp_ = slice(64 * kv, 64 * kv + 64)
                        psC = self.next_ps()
                        for s_ in range(NSEQ):
                            self.mm(psC, psC[:, 16 * s_:16 * s_ + 16].rearrange("p (h t) -> p h t", t=TS), kcT[hp_, s_, :], qT[hp_, :, TS * s_:TS * s_ + TS], [kcT, qT], start=True, stop=True, signal=(s_ == NSEQ - 1))
                        PTc = PT[0]
                        A([psC], [PTc], out=PTc[:, 0:256], in_=psC[:, 0:256], func=AF.Exp, scale=0.125)
                        V("tensor_tensor", [PTc, maskC], [PTc], out=PTc[:, 0:256], in0=PTc[:, 0:256], in1=maskC[:], op=ALU.mult)
                        psN = self.next_ps()
                        self.mm(psN, psN[0:64, 0:256].rearrange("p (h q) -> p h q", q=NSAMP), kT[hp_, 128:128 + NSAMP], qT[hp_, :, 0:NSAMP], [kT, qT], start=True, stop=True)
                        PTn = PT[1]
                        A([psN], [PTn], out=PTn[0:64, 0:256], in_=psN[0:64, 0:256], func=AF.Exp, scale=0.125)
                        V("tensor_tensor", [PTn, maskN], [PTn], out=PTn[0:64, 0:256], in0=PTn[0:64, 0:256], in1=maskN[:], op=ALU.mult)
                        psOc = self.next_ps()
                        for s_ in range(NSEQ):
                            self.mm(psOc, psOc[:, 16 * s_:16 * s_ + 16], vc_ext[:, s_, kv, :], PTc[:, 16 * s_:16 * s_ + 16], [vc_ext, PTc], start=True, stop=True, signal=(s_ == NSEQ - 1))
                        psOn = self.next_ps()
                        self.mm(psOn, psOn[:, 0:256], vext[0:64, 1, kv, :], PTn[0:64, 0:256], [vext, PTn], start=True, stop=True)
                        A([psOc], [osc], out=osc[:, 0:256], in_=psOc[:, 0:256], func=AF.Copy)
                        V("tensor_tensor", [psOn, osc], [osum], out=osum[:, 0:256].rearrange("p (h s t) -> p h s t", s=NSEQ, t=TS),
                          in0=psOn[:, 0:256].rearrange("p (h s t) -> p h s t", s=NSEQ, t=TS),
                          in1=osc[:, 0:256].rearrange("p (s h t) -> p h s t", h=4, t=TS), op=ALU.add)
                        V("tensor_tensor", [osum, ES], [den], out=den[:, 0:256].rearrange("p (h q) -> p h q", q=NSAMP), in0=osum[64:128, 0:256].rearrange("p (h q) -> p h q", q=NSAMP),
                          in1=ES[64:128, kv, :, 0:NSAMP], op=ALU.add)
                        A([den], [den], out=den[:, 0:256], in_=den[:, 0:256], func=AF.Ln)
                        A([den], [rec], out=rec[:, 0:256], in_=den[:, 0:256], func=AF.Exp, scale=-1.0)
                        V("tensor_tensor", [osum, rec], [oT], out=oT[hp_, :, 0:NSAMP], in0=osum[0:64, 0:256].rearrange("p (h q) -> p h q", q=NSAMP),
                          in1=rec[:, 0:256].rearrange("p (h q) -> p h q", q=NSAMP), op=ALU.mult)
                if not samp and bi + 1 < nblk - 1 and "A_carry" not in SKIP:
                    A([kT], [kT], out=kT[:, 0:128], in_=kT[:, 512:640], func=AF.Copy)
                    A([vext], [vext], out=vext[:, 0, :, 0:64], in_=vext[:, 4, :, 0:64], func=AF.Copy)
                yield
                for jf in range(8 if "A_merge" not in SKIP else 0):
                    psA = self.next_ps(); psB = self.next_ps()
                    for (psx, c0) in ((psA, 128 * jf), (psB, 1024 + 128 * jf)):
                        for k in range(4):
                            if not samp:
                                rhs = gT[:, k, 0:L].rearrange("p (r c) -> p r c", c=NC)[:, :, 64 * bi:64 * bi + 64]
                                o_ = psx[:, :].rearrange("p (r c) -> p r c", c=64)
                            else:
                                rhs = gT[:, k, L:L + NSAMP]
                                o_ = psx[:, 0:NSAMP]
                            self.mm(psx, o_, w_glu[:, k, c0:c0 + 128], rhs, [w_glu, gT], start=(k == 0), stop=(k == 3))
                    sg = sig[jf % 2]
                    bs = bsb[jf % 2]
                    A([psB], [sg], out=sg[:, 0:nt], in_=psB[:, 0:nt], func=AF.Sigmoid)
                    if not samp:
                        V("tensor_tensor", [psA, sg], [bs], out=bs[:, :].rearrange("p (c r) -> p r c", r=T),
                          in0=psA[:, :].rearrange("p (r c) -> p r c", c=64), in1=sg[:].rearrange("p (r c) -> p r c", c=64), op=ALU.mult)
                    else:
                        V("tensor_tensor", [psA, sg], [bs], out=bs[:, 0:NSAMP].rearrange("p (s t) -> p t s", t=TS),
                          in0=psA[:, 0:NSAMP].rearrange("p (t s) -> p t s", s=NSEQ), in1=sg[:, 0:NSAMP].rearrange("p (t s) -> p t s", s=NSEQ), op=ALU.mult)
                    psBA = self.next_ps()
                    for k in range(4):
                        self.mm(psBA, psBA[:, 0:nt], w_att[:, k, 128 * jf:128 * jf + 128], oT[:, k, 0:nt], [w_att, oT], start=(k == 0), stop=(k == 3))
                    psGS = self.next_ps()
                    for k in range(8):
                        self.mm(psGS, psGS[:, 0:nt], w_a[:, k, 768 + 128 * jf:896 + 128 * jf], xT[:, k, 0:nt], [w_a, xT], start=(k == 0), stop=(k == 7))
                    psGA = self.next_ps()
                    for k in range(8):
                        self.mm(psGA, psGA[:, 0:nt], w_a[:, k, 1792 + 128 * jf:1920 + 128 * jf], xT[:, k, 0:nt], [w_a, xT], start=(k == 0), stop=(k == 7))
                    gs_, ga_ = gsb[jf % 2], gab[jf % 2]
                    A([psGS], [gs_], out=gs_[:, 0:nt], in_=psGS[:, 0:nt], func=AF.Sigmoid)
                    A([psGA], [ga_], out=ga_[:, 0:nt], in_=psGA[:, 0:nt], func=AF.Sigmoid)
                    Pl("tensor_tensor", [gs_, bs], [t1], out=t1[:, 0:nt], in0=gs_[:, 0:nt], in1=bs[:, 0:nt], op=ALU.mult)
                    V("tensor_tensor", [ga_, psBA], [t2], out=t2[:, 0:nt], in0=ga_[:, 0:nt], in1=psBA[:, 0:nt], op=ALU.mult)
                    V("tensor_tensor", [t1, t2], [mT], out=mT[:, jf, 0:nt], in0=t1[:, 0:nt], in1=t2[:, 0:nt], op=ALU.add)
                yield
                for i in range(ntile if "A_ln" not in SKIP else 0):
                    tsl = slice(t0_ + 128 * i, t0_ + 128 * i + tp)
                    xt_ = x_tok[i % 2]; r_ = rr[i % 2]; xh_ = xh[i % 2]
                    sp.dma(xt_[0:tp, :], xall[tsl, :], [], [xt_])
                    pp = [self.next_ps(), self.next_ps()]
                    for h in range(2):
                        for k in range(8):
                            self.mm(pp[h], pp[h][0:tp, :], mT[:, k, 128 * i:128 * i + tp], w_o[:, k, 512 * h:512 * h + 512], [mT, w_o], start=(k == 0), stop=(k == 7))
                    self.layer_norm(pp, xt_, r_, xh_, g_bc, b_bc, st6, mv, sd, tp)
                    sp.dma(self.x1d[tsl, :], xh_[0][0:tp, :], [xh_[0], xh_[1]], [])

            def finish(g_):
                for _ in g_:
                    pass

            gens = [blk(bi) for bi in range(nblk)]
            next(gens[0], None)
            next(gens[0], None)
            for b_ in range(1, nblk):
                next(gens[b_], None)
                finish(gens[b_ - 1])
                next(gens[b_], None)
            finish(gens[nblk - 1])

    def pass_b(self):
        nc = self.nc
        I, O = self.ins, self.outs
        V, A, Pl = self.V, self.A, self.Pl
        sp, pool = self.sp, self.pool
        with ExitStack() as es:
            sb = lambda n, s, d=F32: self.sb(es, n, s, d)
            w_up = sb("w_up", [128, 8, 2 * DFF], BF16)
            w_dn = sb("w_dn", [128, NF, D], BF16)
            g_bc = sb("g2_bc", [128, D]); b_bc = sb("b2_bc", [128, D])
            sp.dma(g_bc[:], I["ln2_g"][0:1, :].partition_broadcast(128), [], [g_bc])
            sp.dma(b_bc[:], I["ln2_b"][0:1, :].partition_broadcast(128), [], [b_bc])
            cw = sb("cw", [128, NF, 3]); cb = sb("cb", [128, NF])
            for j in range(3):
                sp.dma(cw[:, :, j], I["conv_w"][j:j + 1, :].rearrange("o (f p) -> p (o f)", p=128), [], [cw], allow_slow_non_contiguous=True)
            sp.dma(cb[:], I["conv_b"][0:1, :].rearrange("o (f p) -> p (o f)", p=128), [], [cb], allow_slow_non_contiguous=True)
            a_carry = sb("a_carry", [128, NF, 2])
            Pl("memset", [], [a_carry], ap=a_carry[:], constant=0.0)
            self.barrier()
            scT = sb("scT", [128, NF, 2 * NSEQ]); csT = sb("csT", [128, NF, 2 * NSEQ])
            stg = [sb("b_stg%d" % i, [32, 512]) for i in range(2)]
            for c in range(6):
                w_ = min(512, DFF - 512 * c)
                st_ = stg[c % 2]
                sp.dma(st_[:, 0:w_], I["sconv"][:, 512 * c:512 * c + w_], [], [st_])
                for q in range(w_ // 128):
                    f = 4 * c + q
                    ps = self.next_ps()
                    self.tr(ps, ps[:, 0:32], st_[:, 128 * q:128 * q + 128], self.identf[0:32, 0:32], [st_, self.identf])
                    A([ps], [scT], out=scT[:, f, :], in_=ps[:, 0:32], func=AF.Copy)
            x_bf = sb("b_xbf", [128, 4, D], BF16)
            xT = sb("b_xT", [128, 8, 512], BF16)
            a_ext = [sb("b_aext", [128, 514])] * 2
            c1 = [sb("b_c1%d" % i, [128, 512]) for i in range(2)]
            ge = [sb("b_ge%d" % i, [128, 512], BF16) for i in range(2)]
            hT = sb("b_hT", [128, NF, 512], BF16)
            x_tok = [sb("b_xtok", [128, D])] * 2
            rr = self.halves(sb("b_r", [128, D])); xh = rr
            st6 = sb("b_st6", [128, 2, 6]); mv = sb("b_mv", [128, 2]); sd = sb("b_sd", [128, 3])
            nblk = len(BLOCKS)
            self.load_tokens(self.x1d, 0, x_bf)
            wu = I["w_up"].rearrange("(k p) n -> p k n", p=128)
            for c in (0, 2, 1, 3):
                for k in range(8):
                    pool.dma(w_up[:, k, 1408 * c:1408 * c + 1408], wu[:, k, 1408 * c:1408 * c + 1408], [], [w_up], par=True)
            wd = I["w_down"].rearrange("(k p) n -> p k n", p=128)
            for k in range(NF):
                pool.dma(w_dn[:, k, :], wd[:, k, :], [], [w_dn], par=True)
            for bi in range(nblk):
                t0_, nt = BLOCKS[bi]
                ntile = (nt + 127) // 128
                tp = min(nt, 128)
                samp = (bi == nblk - 1)
                self.transposes(x_bf, xT, nt)
                if bi + 1 < nblk:
                    self.load_tokens(self.x1d, bi + 1, x_bf)
                GF = 2
                for f0 in range(0, NF, GF):
                    fs = list(range(f0, min(NF, f0 + GF)))
                    pA, pG = {}, {}
                    for f in fs:
                        pA[f] = self.next_ps(); pG[f] = self.next_ps()
                        for (psx, c0) in ((pA[f], 128 * f), (pG[f], DFF + 128 * f)):
                            for k in range(8):
                                self.mm(psx, psx[:, 0:nt], w_up[:, k, c0:c0 + 128], xT[:, k, 0:nt], [w_up, xT], start=(k == 0), stop=(k == 7))
                    if not samp:
                        for f in fs:
                            c_ = c1[f % 2]
                            V("tensor_scalar", [a_carry, cw, cb], [c_], out=c_[:, 0:2], in0=a_carry[:, f, :], scalar1=cw[:, f, 0:1], scalar2=cb[:, f:f + 1], op0=ALU.mult, op1=ALU.add)
                        for f in fs:
                            c_ = c1[f % 2]
                            V("tensor_scalar", [pA[f], cw, cb], [c_], out=c_[:, 2:nt], in0=pA[f][:, 0:nt - 2], scalar1=cw[:, f, 0:1], scalar2=cb[:, f:f + 1], op0=ALU.mult, op1=ALU.add)
                        for f in fs:
                            c_ = c1[f % 2]
                            V("scalar_tensor_tensor", [a_carry, cw, c_], [c_], out=c_[:, 0:1], in0=a_carry[:, f, 1:2], scalar=cw[:, f, 1:2], in1=c_[:, 0:1], op0=ALU.mult, op1=ALU.add)
                        for f in fs:
                            c_ = c1[f % 2]
                            V("scalar_tensor_tensor", [pA[f], cw, c_], [c_], out=c_[:, 1:nt], in0=pA[f][:, 0:nt - 1], scalar=cw[:, f, 1:2], in1=c_[:, 1:nt], op0=ALU.mult, op1=ALU.add)
                        for f in fs:
                            c_ = c1[f % 2]
                            V("scalar_tensor_tensor", [pA[f], cw, c_], [c_], out=c_[:, 0:nt], in0=pA[f][:, 0:nt], scalar=cw[:, f, 2:3], in1=c_[:, 0:nt], op0=ALU.mult, op1=ALU.add)
                        for f in fs:
                            A([pA[f]], [a_carry], out=a_carry[:, f, :], in_=pA[f][:, nt - 2:nt], func=AF.Copy)
                        for f in fs:
                            A([c1[f % 2]], [ge[f % 2]], out=ge[f % 2][:, 0:nt], in_=c1[f % 2][:, 0:nt], func=AF.Gelu_apprx_tanh)
                        for f in fs:
                            V("tensor_tensor", [ge[f % 2], pG[f]], [hT], out=hT[:, f, 0:nt], in0=ge[f % 2][:, 0:nt], in1=pG[f][:, 0:nt], op=ALU.mult)
                    else:
                        for f in fs:
                            psA, psG = pA[f], pG[f]
                            ae = a_ext[0]; c_ = c1[f % 2]; g_ = ge[f % 2]
                            a3 = ae[:, 0:6 * NSEQ].rearrange("p (s j) -> p s j", j=6)
                            c3 = c_[:, 0:NSAMP].rearrange("p (s t) -> p s t", t=TS)
                            A([scT], [ae], out=a3[:, :, 0:2], in_=scT[:, f, :].rearrange("p (s j) -> p s j", j=2), func=AF.Copy)
                            A([psA], [ae], out=a3[:, :, 2:6], in_=psA[:, 0:NSAMP].rearrange("p (s t) -> p s t", t=TS), func=AF.Copy)
                            A([ae], [csT], out=csT[:, f, :].rearrange("p (s j) -> p s j", j=2), in_=a3[:, :, 4:6], func=AF.Copy)
                            V("tensor_scalar", [ae, cw, cb], [c_], out=c3, in0=a3[:, :, 0:4], scalar1=cw[:, f, 0:1], scalar2=cb[:, f:f + 1], op0=ALU.mult, op1=ALU.add)
                            V("scalar_tensor_tensor", [ae, cw, c_], [c_], out=c3, in0=a3[:, :, 1:5], scalar=cw[:, f, 1:2], in1=c3, op0=ALU.mult, op1=ALU.add)
                            V("scalar_tensor_tensor", [ae, cw, c_], [c_], out=c3, in0=a3[:, :, 2:6], scalar=cw[:, f, 2:3], in1=c3, op0=ALU.mult, op1=ALU.add)
                            A([c_], [g_], out=g_[:, 0:nt], in_=c_[:, 0:nt], func=AF.Gelu_apprx_tanh)
                            V("tensor_tensor", [g_, psG], [hT], out=hT[:, f, 0:nt], in0=g_[:, 0:nt], in1=psG[:, 0:nt], op=ALU.mult)
                for i in range(ntile):
                    tsl = slice(t0_ + 128 * i, t0_ + 128 * i + tp)
                    xt_ = x_tok[i % 2]
                    sp.dma(xt_[0:tp, :], self.x1d[tsl, :], [], [xt_])
                    pp = [self.next_ps(), self.next_ps()]
                    for h in range(2):
                        for f in range(NF):
                            self.mm(pp[h], pp[h][0:tp, :], hT[:, f, 128 * i:128 * i + tp], w_dn[:, f, 512 * h:512 * h + 512], [hT, w_dn], start=(f == 0), stop=(f == NF - 1))
                    self.layer_norm(pp, xt_, rr, xh, g_bc, b_bc, st6, mv, sd, tp)
                    self.out_events.append(sp.dma(O["y"][tsl, :], xh[0][0:tp, :], [xh[0], xh[1]], []))
            for c in range(6):
                w_ = min(512, DFF - 512 * c)
                nq = w_ // 128
                ps = self.next_ps(); ps2 = self.next_ps()
                for q in range(nq):
                    f = 4 * c + q
                    self.tr(ps, ps[0:2, 128 * q:128 * q + 128], a_carry[:, f, :], self.identf[:], [a_carry, self.identf], signal=(q == nq - 1))
                for q in range(nq):
                    f = 4 * c + q
                    self.tr(ps2, ps2[0:32, 128 * q:128 * q + 128], csT[:, f, :], self.identf[:], [csT, self.identf], signal=(q == nq - 1))
                s0, s1 = stg[0], stg[1]
                A([ps], [s0], out=s0[0:2, 0:w_], in_=ps[0:2, 0:w_], func=AF.Copy)
                A([ps2], [s1], out=s1[0:32, 0:w_], in_=ps2[0:32, 0:w_], func=AF.Copy)
                self.out_events.append(sp.dma(O["cp"][:, 512 * c:512 * c + w_], s0[0:2, 0:w_], [s0], []))
                self.out_events.append(sp.dma(O["cs"][:, 512 * c:512 * c + w_], s1[0:32, 0:w_], [s1], []))


def _host_inputs(inp):
    f = lambda a: np.ascontiguousarray(np.asarray(a, dtype=np.float32))
    w_in = f(inp["w_in"][0])
    qcols = np.concatenate([np.r_[512 + 64 * j:512 + 64 * j + 64, 512 + 64 * (4 + j):512 + 64 * (4 + j) + 64] for j in range(4)])
    perm = np.r_[0:512, qcols, 1024:DIN]
    w_in = np.ascontiguousarray(w_in[:, perm])
    w_attn = f(inp["w_attn_br"][0])
    rows = np.concatenate([np.r_[64 * j:64 * j + 64, 64 * (4 + j):64 * (4 + j) + 64] for j in range(4)])
    w_attn = np.ascontiguousarray(w_attn[rows, :])
    shared = {
        "w_in": w_in, "lam_re": f(inp["ssm_lam_re"][0]), "lam_im": f(inp["ssm_lam_im"][0]),
        "log_dt": f(inp["ssm_log_dt"][0]).reshape(1, G), "b_re": f(inp["ssm_b_re"][0]), "b_im": f(inp["ssm_b_im"][0]),
        "c_re": f(inp["ssm_c_re"][0]).reshape(G * CH, P), "c_im": f(inp["ssm_c_im"][0]).reshape(G * CH, P),
        "d": f(inp["ssm_d"][0]).reshape(G, CH), "w_glu": f(inp["w_glu"][0]), "sinks": f(inp["attn_sinks"][0]).reshape(1, 8),
        "w_attn": w_attn, "w_o": f(inp["w_o"][0]), "ln1_g": f(inp["ln1_g"][0]).reshape(1, D), "ln1_b": f(inp["ln1_b"][0]).reshape(1, D),
        "w_up": f(inp["w_up"][0]), "conv_w": f(inp["conv_w"][0]), "conv_b": f(inp["conv_b"][0]).reshape(1, DFF),
        "w_down": f(inp["w_down"][0]), "ln2_g": f(inp["ln2_g"][0]).reshape(1, D), "ln2_b": f(inp["ln2_b"][0]).reshape(1, D),
    }
    maps = []
    for c in range(8):
        s = slice(NSEQ * c, NSEQ * c + NSEQ)
        m = dict(shared)
        m["x"] = np.ascontiguousarray(np.concatenate([f(inp["x_prompt"][c]), f(inp["x_sample"][s]).reshape(NSAMP, D)], 0))
        m["ck"] = f(inp["cache_k_win"][0, s]).reshape(NSEQ, 128, 128)
        m["cv"] = f(inp["cache_v_win"][0, s]).reshape(NSEQ, 128, 128)
        m["sre"] = f(inp["state_ssm_re"][0, s]).reshape(NSEQ * G, P)
        m["sim"] = f(inp["state_ssm_im"][0, s]).reshape(NSEQ * G, P)
        m["sconv"] = f(inp["state_ffn_conv"][0, s]).reshape(NSEQ * 2, DFF)
        maps.append(m)
    return maps


_NC_CACHE = {}


def _run(inp):
    if "nc" not in _NC_CACHE:
        _NC_CACHE["nc"] = KB().build()
    nc = _NC_CACHE["nc"]
    maps = _host_inputs(inp)
    res = run_bass_kernel_spmd(nc, maps, core_ids=list(range(8)))
    return res.results


def kernel(**inp):
    rs = _run(inp)
    cat = lambda k: np.stack([np.asarray(r[k]) for r in rs], 0)
    y = cat("y")
    yp = y[:, :L, :]
    ys = y[:, L:, :].reshape(8 * NSEQ, TS, D)
    kwp = cat("kwp").reshape(1, 8, 128, 2, 64)
    vwp = cat("vwp").reshape(1, 8, 128, 2, 64)
    kws = cat("kws").reshape(1, 8 * NSEQ, 128, 2, 64)
    vws = cat("vws").reshape(1, 8 * NSEQ, 128, 2, 64)
    hrp = cat("hrp").reshape(1, 8, G, P)
    hip = cat("hip").reshape(1, 8, G, P)
    hrs = cat("hrs").reshape(1, 8 * NSEQ, G, P)
    his = cat("his").reshape(1, 8 * NSEQ, G, P)
    cp = cat("cp").reshape(1, 8, 2, DFF)
    cs = cat("cs").reshape(1, 8 * NSEQ, 2, DFF)
    return (np.ascontiguousarray(yp), np.ascontiguousarray(ys), kwp, vwp, kws, vws, hrp, hip, hrs, his, cp, cs)
```

```python
import math
from contextlib import ExitStack

import numpy as np
import concourse.bass as bass
import concourse.mybir as mybir
from concourse.bass_utils import run_bass_kernel_spmd

F32 = mybir.dt.float32
BF16 = mybir.dt.bfloat16
I32 = mybir.dt.int32
AF = mybir.ActivationFunctionType
ALU = mybir.AluOpType

D = 1024
L = 2048
NSEQ = 16
TS = 4
NSAMP = NSEQ * TS
NTOK = L + NSAMP
G = 32
P = 64
CH = 16
T = 8
DFF = 2816
NF = DFF // 128
DIN = 3328
ALPHA = 2.0 ** 0.25
LN_EPS = 1e-5
TWO_PI = 2.0 * math.pi
C1 = 6.28125
C2 = TWO_PI - C1
MAGIC = 12582912.0
NEG = -30000.0
BLOCKS = [(0, 512), (512, 512), (1024, 512), (1536, 512), (2048, 64)]
DEBUG = {}
TAPS = set()
SKIP = set()


def psplit(ap, c):
    (pstep, npart), (estep, n) = ap.ap
    return bass.AP(tensor=ap.tensor, offset=ap.offset, ap=[[pstep, c], [pstep * c, npart // c], [estep, n]])


class Buf:
    __slots__ = ("t", "wr", "rd", "name", "psum")

    def __init__(self, t, name):
        self.t = t
        self.wr = []
        self.rd = {}
        self.name = name
        self.psum = False

    def __getitem__(self, k):
        return self.t[k]


class Eng:
    def __init__(self, name, eng, sem, dma_sems=None):
        self.name = name
        self.eng = eng
        self.sem = sem
        self.count = 0
        self.waited = {}
        self.pool = [[s, 0] for s in (dma_sems or [])]
        self.pidx = 0

    def wait_ev(self, ev, force=False):
        if ev is None:
            return
        sem, val = ev
        if sem is self.sem and not force and self.name == "pe":
            return
        k = id(sem)
        if self.waited.get(k, 0) >= val:
            return
        self.eng.wait_ge(sem, val)
        self.waited[k] = val

    def deps(self, reads, writes, par=False):
        for b in reads:
            for ev in b.wr:
                self.wait_ev(ev)
            if b.psum:
                for ev in b.rd.values():
                    self.wait_ev(ev)
        for b in writes:
            if not par:
                for ev in b.wr:
                    self.wait_ev(ev)
            for ev in b.rd.values():
                self.wait_ev(ev)

    def record(self, ev, reads, writes, par=False):
        k = id(ev[0])
        for b in reads:
            b.rd[k] = ev
        for b in writes:
            if par:
                b.wr = [e for e in b.wr if id(e[0]) != k] + [ev]
            else:
                b.wr = [ev]
            b.rd = {}

    def op(self, method, reads, writes, signal=True, **kw):
        self.deps(reads, writes)
        ins = getattr(self.eng, method)(**kw)
        if signal:
            self.count += 1
            ins.then_inc(self.sem, 1)
            ev = (self.sem, self.count)
        else:
            ev = (self.sem, self.count + 1)
        self.record(ev, reads, writes)
        return ev

    def dma(self, out, in_, reads, writes, par=False, **kw):
        self.deps(reads, writes, par)
        slot = self.pool[self.pidx % len(self.pool)]
        self.pidx += 1
        sem, cnt = slot
        if cnt > 0:
            self.wait_ev((sem, cnt))
        self.eng.dma_start(out=out, in_=in_, **kw).then_inc(sem, 16)
        slot[1] = cnt + 16
        ev = (sem, cnt + 16)
        self.record(ev, reads, writes, par)
        return ev


class KB:
    def __init__(self):
        self.nc = bass.Bass("TRN2", target_bir_lowering=False)
        self.out_events = []

    def dram_in(self, name, shape, dt=F32):
        return self.nc.dram_tensor(name, list(shape), dt, kind="ExternalInput").ap()

    def dram_out(self, name, shape, dt=F32):
        return self.nc.dram_tensor(name, list(shape), dt, kind="ExternalOutput").ap()

    def sb(self, es, name, shape, dt=F32):
        return Buf(es.enter_context(self.nc.sbuf_tensor("sb_" + name, list(shape), dt)), name)

    def barrier(self):
        engs = [self.pe, self.act, self.dve, self.pool, self.sp]
        evs = [(e.sem, e.count) for e in engs if e.sem is not None and e.count > 0]
        for e in engs:
            for s, c in e.pool:
                if c > 0:
                    evs.append((s, c))
        for e in engs:
            for ev in evs:
                e.wait_ev(ev, force=True)

    def tap(self, name, buf, ap=None):
        if name not in TAPS:
            return
        ap = buf[:] if ap is None else ap
        shp = list(ap.shape)
        o = self.dram_out("tap_" + name, shp, ap.dtype)
        self.out_events.append(self.sp.dma(o, ap, [buf], []))

    def next_ps(self):
        b = self.psb[self.psi % 8]
        self.psi += 1
        return b

    def V(self, method, reads, writes, **kw):
        return self.dve.op(method, reads, writes, **kw)

    def A(self, reads, writes, **kw):
        return self.act.op("activation", reads, writes, **kw)

    def Pl(self, method, reads, writes, **kw):
        return self.pool.op(method, reads, writes, **kw)

    def mm(self, ps, out, lhsT, rhs, reads, start=True, stop=True, signal=None):
        if signal is None:
            signal = stop
        return self.pe.op("matmul", reads, [ps], signal=signal, out=out, lhsT=lhsT, rhs=rhs,
                          start=start, stop=stop)

    def tr(self, ps, out, in_, ident, reads, signal=True):
        return self.pe.op("transpose", reads, [ps], signal=signal, out=out, in_=in_, identity=ident)

    def sin_rr(self, out_b, out_ap, in_b, in_ap, shift, tmp_bs, tmp_aps):
        (ty, tn, tr_) = tmp_aps
        (by, bn, br) = tmp_bs
        V = self.V
        if shift != 0.0:
            V("tensor_scalar", [in_b], [by], out=ty, in0=in_ap, scalar1=float(shift), scalar2=None, op0=ALU.add)
            yb, y = by, ty
        else:
            yb, y = in_b, in_ap
        V("tensor_scalar", [yb], [bn], out=tn, in0=y, scalar1=1.0 / TWO_PI, scalar2=MAGIC, op0=ALU.mult, op1=ALU.add)
        V("tensor_scalar", [bn], [bn], out=tn, in0=tn, scalar1=MAGIC, scalar2=None, op0=ALU.subtract)
        V("scalar_tensor_tensor", [bn, yb], [br], out=tr_, in0=tn, scalar=-C1, in1=y, op0=ALU.mult, op1=ALU.add)
        V("scalar_tensor_tensor", [bn, br], [br], out=tr_, in0=tn, scalar=-C2, in1=tr_, op0=ALU.mult, op1=ALU.add)
        V("tensor_scalar", [br], [br], out=tr_, in0=tr_, scalar1=-math.pi, scalar2=math.pi, op0=ALU.max, op1=ALU.min)
        self.A([br], [out_b], out=out_ap, in_=tr_, func=AF.Sin)

    def build(self):
        nc = self.nc
        self.ins = {}
        I = self.ins
        I["x"] = self.dram_in("x", [NTOK, D])
        I["ck"] = self.dram_in("ck", [NSEQ, 128, 128])
        I["cv"] = self.dram_in("cv", [NSEQ, 128, 128])
        I["sre"] = self.dram_in("sre", [NSEQ * G, P])
        I["sim"] = self.dram_in("sim", [NSEQ * G, P])
        I["sconv"] = self.dram_in("sconv", [NSEQ * 2, DFF])
        I["w_in"] = self.dram_in("w_in", [D, DIN])
        I["lam_re"] = self.dram_in("lam_re", [G, P])
        I["lam_im"] = self.dram_in("lam_im", [G, P])
        I["log_dt"] = self.dram_in("log_dt", [1, G])
        I["b_re"] = self.dram_in("b_re", [G, P, CH])
        I["b_im"] = self.dram_in("b_im", [G, P, CH])
        I["c_re"] = self.dram_in("c_re", [G * CH, P])
        I["c_im"] = self.dram_in("c_im", [G * CH, P])
        I["d"] = self.dram_in("d", [G, CH])
        I["w_glu"] = self.dram_in("w_glu", [512, 2048])
        I["sinks"] = self.dram_in("sinks", [1, 8])
        I["w_attn"] = self.dram_in("w_attn", [512, D])
        I["w_o"] = self.dram_in("w_o", [D, D])
        I["ln1_g"] = self.dram_in("ln1_g", [1, D])
        I["ln1_b"] = self.dram_in("ln1_b", [1, D])
        I["w_up"] = self.dram_in("w_up", [D, 2 * DFF])
        I["conv_w"] = self.dram_in("conv_w", [3, DFF])
        I["conv_b"] = self.dram_in("conv_b", [1, DFF])
        I["w_down"] = self.dram_in("w_down", [DFF, D])
        I["ln2_g"] = self.dram_in("ln2_g", [1, D])
        I["ln2_b"] = self.dram_in("ln2_b", [1, D])
        self.outs = {}
        O = self.outs
        O["y"] = self.dram_out("y", [NTOK, D])
        O["kwp"] = self.dram_out("kwp", [128, 128])
        O["vwp"] = self.dram_out("vwp", [128, 128])
        O["kws"] = self.dram_out("kws", [NSEQ, 128, 128])
        O["vws"] = self.dram_out("vws", [NSEQ, 128, 128])
        O["hrp"] = self.dram_out("hrp", [G, P])
        O["hip"] = self.dram_out("hip", [G, P])
        O["hrs"] = self.dram_out("hrs", [NSEQ * G, P])
        O["his"] = self.dram_out("his", [NSEQ * G, P])
        O["cp"] = self.dram_out("cp", [2, DFF])
        O["cs"] = self.dram_out("cs", [NSEQ * 2, DFF])
        self.x1d = nc.dram_tensor("x1d", [NTOK, D], F32, kind="Internal").ap()
        for k, (shp, dt_) in DEBUG.items():
            O[k] = self.dram_out(k, shp, dt_)

        with ExitStack() as es0:
            E = es0.enter_context
            sems = [E(nc.semaphore("s%d" % i)) for i in range(4)]
            dsem_sp = [E(nc.semaphore("dsp%d" % i)) for i in range(40)]
            dsem_pl = [E(nc.semaphore("dpl%d" % i)) for i in range(40)]
            self.psb = [Buf(E(nc.psum_tensor("ps%d" % i, [128, 512], F32)), "ps%d" % i) for i in range(8)]
            self.psi = 0
            for b_ in self.psb:
                b_.psum = True
            block = E(nc.Block())
            self.pe = Eng("pe", nc.tensor, sems[0])
            self.act = Eng("act", nc.scalar, sems[1])
            self.dve = Eng("dve", nc.vector, sems[2])
            self.pool = Eng("pool", nc.gpsimd, sems[3], dsem_pl)
            self.sp = Eng("sp", nc.sync, None, dsem_sp)
            self.ident = self.sb(es0, "ident", [128, 128], BF16)
            self.identf = self.sb(es0, "identf", [128, 128], F32)
            self.mhalf = self.sb(es0, "mhalf", [128, 1], F32)
            self.consts()
            with ExitStack() as esSA:
                self.pass_s()
                self.barrier()
                if "A" not in SKIP:
                    self.pass_a()
            self.barrier()
            if "dbg_x1" in DEBUG:
                self.out_events.append(self.sp.dma(O["dbg_x1"], self.x1d, [], []))
            if "B" not in SKIP:
                self.pass_b()
            for ev in self.out_events:
                self.sp.wait_ev(ev)
            self.barrier()
        return nc

    def consts(self):
        Pl = self.Pl
        Pl("memset", [], [self.identf], ap=self.identf[:], constant=0.0)
        Pl("memset", [], [self.mhalf], ap=self.mhalf[:], constant=-0.5)
        self.barrier()
        Pl("affine_select", [self.identf], [self.identf], out=self.identf[:], in_=self.identf[:], pattern=[[-1, 128]],
           compare_op=ALU.not_equal, fill=1.0, base=0, channel_multiplier=1)
        Pl("tensor_copy", [self.identf], [self.ident], out=self.ident[:], in_=self.identf[:])

    def pass_s(self):
        nc = self.nc
        I, O = self.ins, self.outs
        V, A, Pl = self.V, self.A, self.Pl
        sp, pool = self.sp, self.pool
        with ExitStack() as es:
            sb = lambda n, s, d=F32: self.sb(es, n, s, d)
            Toep = sb("Toep", [128, G, 128], BF16)
            Wend = sb("Wend", [128, G, 128], BF16)
            Wend_sw = sb("Wend_sw", [128, G, 128], BF16)
            Cpow = sb("Cpow", [128, G, 128], BF16)
            Cpow_sw = sb("Cpow_sw", [128, G, 128], BF16)
            CT = sb("CT", [128, G, 64])
            ST = sb("ST", [128, G, 64])
            RHOT = sb("RHOT", [128, G, 64])
            RHO1 = sb("RHO1", [128, G])
            AR4 = sb("AR4", [128, G]); AI4 = sb("AI4", [128, G]); ARm4 = sb("ARm4", [128, G]); AIm4 = sb("AIm4", [128, G])
            Drep = sb("Drep", [128, G])
            w_u = sb("w_u", [128, 8, 512], BF16)
            SGN = sb("SGN", [128, 1]); NSGN = sb("NSGN", [128, 1])
            Hin = sb("Hin", [128, G])
            self._w_u = w_u
            NC_ = L // T
            U = sb("s_U", [128, G, NC_], BF16)
            Us = sb("s_Us", [128, G, NSEQ], BF16)
            Pl("memset", [], [Us], ap=Us[:], constant=0.0)
            self._U, self._Us = U, Us
            with ExitStack() as esp:
                gens = [self.ssm_prep(esp, Toep, Wend, Wend_sw, Cpow, Cpow_sw, CT, ST, RHOT, RHO1, AR4, AI4, ARm4, AIm4,
                                      Drep, SGN, NSGN, Hin), self.ssm_stage1(esp, w_u)]
                first = True
                while gens:
                    for g_ in list(gens):
                        try:
                            next(g_)
                        except StopIteration:
                            gens.remove(g_)
                        if first:
                            first = False
                ev1 = sp.dma(self.scrU.rearrange("j p n -> p j n"), self._uT[:], [self._uT], [])
                ev2 = sp.dma(self.scrUs.rearrange("j p n -> p j n"), self._uTs[:], [self._uTs], [])
                sp.wait_ev(ev1); sp.wait_ev(ev2)
                for g in range(G):
                    j, gl = divmod(g, 8)
                    src = bass.AP(tensor=self.scrU.tensor, offset=(j * 128 + gl * 16) * L, ap=[[NC_, T], [L, CH], [1, NC_]])
                    sp.dma(U[:, g, :], src, [], [U], par=True)
                    srcs = bass.AP(tensor=self.scrUs.tensor, offset=(j * 128 + gl * 16) * NSAMP, ap=[[NSEQ, TS], [NSAMP, CH], [1, NSEQ]])
                    sp.dma(Us[64:128, g, :], srcs, [], [Us], par=True)
                for gh in range(self._prep_nparts):
                    self._prep_part2(gh)
                self.ssm_taps(Toep, Wend, Wend_sw, Cpow, Cpow_sw)
                self.barrier()
            self.ssm_main(es, Toep, Wend, Wend_sw, Cpow, Cpow_sw, CT, ST, RHOT, RHO1, AR4, AI4, ARm4, AIm4,
                          Drep, w_u, Hin)

    def ssm_prep(self, es, Toep, Wend, Wend_sw, Cpow, Cpow_sw, CT, ST, RHOT, RHO1, AR4, AI4, ARm4, AIm4, Drep,
                 SGN, NSGN, Hin):
        I = self.ins
        V, A, Pl = self.V, self.A, self.Pl
        sp = self.sp
        sb = lambda n, s, d=F32: self.sb(es, n, s, d)
        lamre2 = sb("lamre2", [128, G]); lamim2 = sb("lamim2", [128, G]); logdt2 = sb("logdt2", [128, G])
        BA = sb("BA", [128, G, CH]); BB = sb("BB", [128, G, CH])
        Ct_re = sb("Ct_re", [128, 4, 128]); Ct_im = sb("Ct_im", [128, 4, 128])
        CRE2 = sb("CRE2", [128, G, CH]); CIM2 = sb("CIM2", [128, G, CH])
        lt_re = sb("lt_re", [G, 128]); lt_im = sb("lt_im", [G, 128]); lt_d = sb("lt_d", [G, 128])
        def _loads():
            w_u = self._w_u
            wv = I["w_in"].rearrange("(k p) n -> p k n", p=128)
            for k in range(8):
                self.pool.dma(w_u[:, k, :], wv[:, k, 0:512], [], [w_u], par=True)
            for h in range(2):
                sl = slice(64 * h, 64 * h + 64)
                sp.dma(lt_re[:, sl], I["lam_re"][:, :], [], [lt_re], par=True)
                sp.dma(lt_im[:, sl], I["lam_im"][:, :], [], [lt_im], par=True)
            for r in range(T):
                sp.dma(lt_d[:, 16 * r:16 * r + 16], I["d"][:, :], [], [lt_d], par=True)
            sp.dma(logdt2[:], I["log_dt"][0:1, :].partition_broadcast(128), [], [logdt2])
            sp.dma(BA[0:64, :, :], I["b_re"].rearrange("g p c -> p g c"), [], [BA], par=True)
            sp.dma(BA[64:128, :, :], I["b_im"].rearrange("g p c -> p g c"), [], [BA], par=True)
            sp.dma(BB[0:64, :, :], I["b_im"].rearrange("g p c -> p g c"), [], [BB], par=True)
            sp.dma(BB[64:128, :, :], I["b_re"].rearrange("g p c -> p g c"), [], [BB], par=True)
            for j in range(4):
                for h in range(2):
                    sp.dma(Ct_re[:, j, 64 * h:64 * h + 64], I["c_re"][128 * j:128 * j + 128, :], [], [Ct_re], par=True)
                    sp.dma(Ct_im[:, j, 64 * h:64 * h + 64], I["c_im"][128 * j:128 * j + 128, :], [], [Ct_im], par=True)
        self.issue_prep_loads = _loads
        Pl("memset", [], [SGN], ap=SGN[0:64, :], constant=1.0)
        Pl("memset", [], [SGN], ap=SGN[64:128, :], constant=-1.0)
        Pl("memset", [], [NSGN], ap=NSGN[0:64, :], constant=-1.0)
        Pl("memset", [], [NSGN], ap=NSGN[64:128, :], constant=1.0)
        Pl("memset", [], [Hin], ap=Hin[:], constant=0.0)
        JVi = sb("JVi", [128, 64], I32); JV = sb("JV", [128, 64])
        KVi = sb("KVi", [128, 32], I32); KV = sb("KV", [128, 32])
        Pl("iota", [], [JVi], out=JVi[:], pattern=[[1, 64]], base=1, channel_multiplier=0)
        Pl("iota", [], [KVi], out=KVi[:, 0:16], pattern=[[1, 16]], base=-7, channel_multiplier=0)
        Pl("iota", [], [KVi], out=KVi[:, 16:32], pattern=[[-1, 16]], base=8, channel_multiplier=0)
        ones = sb("ones", [128, 128]); tmask = sb("tmask", [128, 128])
        Pl("memset", [], [ones], ap=ones[:], constant=1.0)
        self.barrier()
        self.issue_prep_loads()
        for (lt_, dst_) in ((lt_re, lamre2), (lt_im, lamim2), (lt_d, Drep)):
            ps = self.next_ps()
            self.tr(ps, ps[:, 0:G], lt_[0:G, :], self.identf[0:G, 0:G], [lt_, self.identf])
            A([ps], [dst_], out=dst_[:], in_=ps[:, 0:G], func=AF.Copy)
        Pl("tensor_copy", [JVi], [JV], out=JV[:], in_=JVi[:])
        Pl("tensor_copy", [KVi], [KV], out=KV[:], in_=KVi[:])
        dt = sb("dt", [128, G]); ell = sb("ell", [128, G]); th = sb("th", [128, G])
        den = sb("den", [128, G]); t0 = sb("t0", [128, G]); wr = sb("wr", [128, G]); wi = sb("wi", [128, G])
        A([logdt2], [dt], out=dt[:], in_=logdt2[:], func=AF.Exp)
        yield
        V("tensor_tensor", [lamre2, dt], [ell], out=ell[:], in0=lamre2[:], in1=dt[:], op=ALU.mult)
        V("tensor_tensor", [lamim2, dt], [th], out=th[:], in0=lamim2[:], in1=dt[:], op=ALU.mult)
        V("tensor_tensor", [lamre2], [den], out=den[:], in0=lamre2[:], in1=lamre2[:], op=ALU.mult)
        V("tensor_tensor", [lamim2], [t0], out=t0[:], in0=lamim2[:], in1=lamim2[:], op=ALU.mult)
        V("tensor_tensor", [den, t0], [den], out=den[:], in0=den[:], in1=t0[:], op=ALU.add)
        V("reciprocal", [den], [den], out=den[:], in_=den[:])
        V("tensor_tensor", [lamre2, den], [wr], out=wr[:], in0=lamre2[:], in1=den[:], op=ALU.mult)
        V("scalar_tensor_tensor", [lamim2, den], [wi], out=wi[:], in0=lamim2[:], scalar=-1.0, in1=den[:], op0=ALU.mult, op1=ALU.mult)
        kvals = list(range(-7, 9)) + list(range(8, -8, -1))
        ELLK = sb("ELLK", [128, G, 32]); ANGK = sb("ANGK", [128, G, 32])
        V("tensor_tensor", [ell, KV], [ELLK], out=ELLK[:], in0=ell[:].unsqueeze(2).to_broadcast([128, G, 32]),
          in1=KV[:].unsqueeze(1).to_broadcast([128, G, 32]), op=ALU.mult)
        V("tensor_tensor", [th, KV], [ANGK], out=ANGK[:], in0=th[:].unsqueeze(2).to_broadcast([128, G, 32]),
          in1=KV[:].unsqueeze(1).to_broadcast([128, G, 32]), op=ALU.mult)
        MAGK = ELLK; RR = sb("RR", [128, G, 32]); II = sb("II", [128, G, 32])
        ty = sb("ty", [128, G * 32]); tn = sb("tn", [128, G * 32]); trr = sb("trr", [128, G * 32])
        f1 = lambda b, n=G * 32: b[:, 0:n]
        fl = lambda b: b[:].rearrange("p g k -> p (g k)")
        A([ELLK], [MAGK], out=fl(MAGK), in_=fl(ELLK), func=AF.Exp)
        yield
        self.sin_rr(II, fl(II), ANGK, fl(ANGK), 0.0, (ty, tn, trr), (f1(ty), f1(tn), f1(trr)))
        yield
        self.sin_rr(RR, fl(RR), ANGK, fl(ANGK), math.pi / 2, (ty, tn, trr), (f1(ty), f1(tn), f1(trr)))
        yield
        V("tensor_tensor", [RR, MAGK], [RR], out=fl(RR), in0=fl(RR), in1=fl(MAGK), op=ALU.mult)
        V("tensor_tensor", [II, MAGK], [II], out=fl(II), in0=fl(II), in1=fl(MAGK), op=ALU.mult)
        self.tap("RR", RR); self.tap("II", II); self.tap("th", th); self.tap("ell", ell); self.tap("CRE2", CRE2); self.tap("BA", BA)
        V("tensor_copy", [RR], [AR4], out=AR4[:], in_=RR[:, :, 11])
        V("tensor_scalar", [II, NSGN], [AI4], out=AI4[:], in0=II[:, :, 11], scalar1=NSGN[:, 0:1], scalar2=None, op0=ALU.mult)
        V("tensor_copy", [RR], [ARm4], out=ARm4[:], in_=RR[:, :, 3])
        V("tensor_scalar", [II, NSGN], [AIm4], out=AIm4[:], in0=II[:, :, 3], scalar1=NSGN[:, 0:1], scalar2=None, op0=ALU.mult)
        ph = sb("ph", [128, G]); ANGJ = sb("ANGJ", [128, 16, 64])
        V("tensor_scalar", [th], [ph], out=ph[:], in0=th[:], scalar1=8.0, scalar2=None, op0=ALU.mult)
        V("tensor_scalar", [ph], [t0], out=t0[:], in0=ph[:], scalar1=1.0 / TWO_PI, scalar2=MAGIC, op0=ALU.mult, op1=ALU.add)
        V("tensor_scalar", [t0], [t0], out=t0[:], in0=t0[:], scalar1=MAGIC, scalar2=None, op0=ALU.subtract)
        V("scalar_tensor_tensor", [t0, ph], [ph], out=ph[:], in0=t0[:], scalar=-C1, in1=ph[:], op0=ALU.mult, op1=ALU.add)
        V("scalar_tensor_tensor", [t0, ph], [ph], out=ph[:], in0=t0[:], scalar=-C2, in1=ph[:], op0=ALU.mult, op1=ALU.add)
        f2 = lambda a: a.rearrange("p g j -> p (g j)")
        for gh in range(2):
            hs_ = slice(16 * gh, 16 * gh + 16)
            V("tensor_tensor", [ph, JV], [ANGJ], out=ANGJ[:], in0=ph[:, hs_].unsqueeze(2).to_broadcast([128, 16, 64]),
              in1=JV[:].unsqueeze(1).to_broadcast([128, 16, 64]), op=ALU.mult)
            self.sin_rr(CT, f2(CT[:, hs_, :]), ANGJ, f2(ANGJ[:]), math.pi / 2, (ty, tn, trr), (ty[:], tn[:], trr[:]))
            yield
            self.sin_rr(ST, f2(ST[:, hs_, :]), ANGJ, f2(ANGJ[:]), 0.0, (ty, tn, trr), (ty[:], tn[:], trr[:]))
            yield
        V("tensor_scalar", [ST, SGN], [ST], out=f2(ST[:]), in0=f2(ST[:]), scalar1=SGN[:, 0:1], scalar2=None, op0=ALU.mult)
        A([ell], [RHO1], out=RHO1[:], in_=ell[:], func=AF.Exp, scale=8.0)
        yield
        V("tensor_copy", [RHO1], [RHOT], out=RHOT[:], in_=RHO1[:].unsqueeze(2).to_broadcast([128, G, 64]))
        V("tensor_scalar", [RHOT], [RHOT], out=RHOT[:, :, 0], in0=RHOT[:, :, 0], scalar1=0.0, scalar2=None, op0=ALU.mult)
        self.tap("CT", CT); self.tap("ST", ST); self.tap("RHOT", RHOT); self.tap("ph", ph); self.tap("JV", JV); self.tap("ANGJ", ANGJ)
        for (Ct, Cdst) in ((Ct_re, CRE2), (Ct_im, CIM2)):
            for j in range(4):
                ps = self.next_ps()
                self.tr(ps, ps[:, 0:128], Ct[:, j, :], self.identf[:], [Ct, self.identf])
                A([ps], [Cdst], out=Cdst[:, 8 * j:8 * j + 8, :], in_=ps[:, 0:128].rearrange("p (g c) -> p g c", c=CH), func=AF.Copy)
                yield
        NRR = sb("NRR", [128, G, 16]); NII = sb("NII", [128, G, 16])
        V("tensor_scalar", [RR], [NRR], out=NRR[:], in0=RR[:, :, 0:16], scalar1=-1.0, scalar2=None, op0=ALU.mult)
        V("tensor_scalar", [II], [NII], out=NII[:], in0=II[:, :, 0:16], scalar1=-1.0, scalar2=None, op0=ALU.mult)
        X = [sb("X%d" % i, [128, G, 16]) for i in range(4)]
        lo, hi = slice(0, 64), slice(64, 128)
        srcs = [(RR, NII), (NII, NRR), (NII, RR), (NRR, NII)]
        for i in range(4):
            bl, bh = srcs[i]
            V("tensor_copy", [bl], [X[i]], out=X[i][lo, :, :], in_=bl[lo, :, 0:16])
            V("tensor_copy", [bh], [X[i]], out=X[i][hi, :, :], in_=bh[hi, :, 0:16])
        DR = sb("DR", [128, G, 16]); DI = sb("DI", [128, G, 16])
        V("tensor_tensor", [RR], [DR], out=DR[:, :, 1:16], in0=RR[:, :, 16:31], in1=RR[:, :, 17:32], op=ALU.subtract)
        V("tensor_tensor", [II], [DI], out=DI[:, :, 1:16], in0=II[:, :, 16:31], in1=II[:, :, 17:32], op=ALU.subtract)
        QWR = sb("QWR", [128, G, 16]); QWI = sb("QWI", [128, G, 16]); q1 = sb("q1", [128, G, 16])
        wrb = wr[:].unsqueeze(2).to_broadcast([128, G, 15]); wib = wi[:].unsqueeze(2).to_broadcast([128, G, 15])
        s15 = (slice(None), slice(None), slice(1, 16))
        V("tensor_tensor", [DR, wr], [QWR], out=QWR[s15], in0=DR[s15], in1=wrb, op=ALU.mult)
        V("tensor_tensor", [DI, wi], [q1], out=q1[s15], in0=DI[s15], in1=wib, op=ALU.mult)
        V("tensor_tensor", [QWR, q1], [QWR], out=QWR[s15], in0=QWR[s15], in1=q1[s15], op=ALU.subtract)
        V("tensor_tensor", [DR, wi], [QWI], out=QWI[s15], in0=DR[s15], in1=wib, op=ALU.mult)
        V("tensor_tensor", [DI, wr], [q1], out=q1[s15], in0=DI[s15], in1=wrb, op=ALU.mult)
        V("tensor_tensor", [QWI, q1], [QWI], out=QWI[s15], in0=QWI[s15], in1=q1[s15], op=ALU.add)
        V("tensor_scalar", [QWI, NSGN], [QWI], out=QWI[s15], in0=QWI[s15], scalar1=NSGN[:, 0:1], scalar2=None, op0=ALU.mult)
        Pl("affine_select", [ones], [tmask], out=tmask[:].rearrange("p (r c) -> p r c", c=CH),
           in_=ones[:].rearrange("p (r c) -> p r c", c=CH), pattern=[[16, 8], [0, 16]], compare_op=ALU.is_ge, fill=0.0,
           base=15, channel_multiplier=-1)
        GH = 8
        class _View:
            def __init__(self, buf):
                self.buf = buf
        tA, tB, tC, tD = ty, tn, trr, ANGJ
        v4t = {id(ty): lambda: ty[:, 0:1024].rearrange("p (g r c) -> p g r c", g=GH, r=T),
               id(tn): lambda: tn[:, 0:1024].rearrange("p (g r c) -> p g r c", g=GH, r=T),
               id(trr): lambda: trr[:, 0:1024].rearrange("p (g r c) -> p g r c", g=GH, r=T),
               id(ANGJ): lambda: ANGJ[:].rearrange("p a (b c) -> p (a b) c", c=CH).rearrange("p (g r) c -> p g r c", r=T)}
        Cp0 = sb("Cp0", [128, GH, 128]); Bp0 = sb("Bp0", [128, GH, 128]); Bp7 = sb("Bp7", [128, GH, 128])
        v4 = lambda a: a.rearrange("p g (r c) -> p g r c", c=CH)
        def part2(gh):
            hs_ = slice(GH * gh, GH * gh + GH)

            def cmat(outb, out4, Xa, Xb, ki0, E=V, ta=tA, tb=tB):
                c_re_b = CRE2[:, hs_, :].unsqueeze(2).to_broadcast([128, GH, T, CH])
                c_im_b = CIM2[:, hs_, :].unsqueeze(2).to_broadcast([128, GH, T, CH])
                xa = Xa[:, hs_, ki0:ki0 + T].unsqueeze(3).to_broadcast([128, GH, T, CH])
                xb = Xb[:, hs_, ki0:ki0 + T].unsqueeze(3).to_broadcast([128, GH, T, CH])
                E("tensor_tensor", [CRE2, Xa], [ta], out=v4t[id(ta)](), in0=c_re_b, in1=xa, op=ALU.mult)
                E("tensor_tensor", [CIM2, Xb], [tb], out=v4t[id(tb)](), in0=c_im_b, in1=xb, op=ALU.mult)
                E("tensor_tensor", [ta, tb], [outb], out=out4, in0=v4t[id(ta)](), in1=v4t[id(tb)](), op=ALU.add)

            def bmat(outb, i0, E=V, ta=tA, tb=tB):
                ba = BA[:, hs_, :].unsqueeze(2).to_broadcast([128, GH, T, CH])
                bb = BB[:, hs_, :].unsqueeze(2).to_broadcast([128, GH, T, CH])
                qa = QWR[:, hs_, i0:i0 + T].unsqueeze(3).to_broadcast([128, GH, T, CH])
                qb = QWI[:, hs_, i0:i0 + T].unsqueeze(3).to_broadcast([128, GH, T, CH])
                E("tensor_tensor", [BA, QWR], [ta], out=v4t[id(ta)](), in0=ba, in1=qa, op=ALU.mult)
                E("tensor_tensor", [BB, QWI], [tb], out=v4t[id(tb)](), in0=bb, in1=qb, op=ALU.mult)
                E("tensor_tensor", [ta, tb], [outb], out=v4(outb[:]), in0=v4t[id(ta)](), in1=v4t[id(tb)](), op=ALU.add)

            bmat(Bp7, 1, Pl, tC, tD)
            cmat(Cp0, v4(Cp0[:]), X[0], X[1], 7)
            bmat(Bp0, 8)
            cmat(Cpow, v4(Cpow[:, hs_, :]), X[0], X[1], 8)
            cmat(Cpow_sw, v4(Cpow_sw[:, hs_, :]), X[2], X[3], 8)
            for gl in range(GH):
                g = GH * gh + gl
                ps = self.next_ps()
                self.mm(ps, ps[:, 0:128], Bp0[:, gl, :], Cp0[:, gl, :], [Bp0, Cp0])
                V("tensor_tensor", [ps, tmask], [Toep], out=Toep[:, g, :], in0=ps[:, 0:128], in1=tmask[:], op=ALU.mult)
                ps2 = self.next_ps()
                self.tr(ps2, ps2[:, 0:128], Bp7[:, gl, :], self.identf[:], [Bp7, self.identf])
                A([ps2], [Wend], out=Wend[:, g, :], in_=ps2[:, 0:128], func=AF.Copy)
                A([ps2], [Wend_sw], out=Wend_sw[:, g, 0:64], in_=ps2[:, 64:128], func=AF.Copy)
                A([ps2], [Wend_sw], out=Wend_sw[:, g, 64:128], in_=ps2[:, 0:64], func=AF.Copy)

        self._prep_part2 = part2
        self._prep_nparts = G // GH

    def ssm_taps(self, Toep, Wend, Wend_sw, Cpow, Cpow_sw):
        self.tap("Toep", Toep); self.tap("Wend", Wend); self.tap("Wend_sw", Wend_sw); self.tap("Cpow", Cpow); self.tap("Cpow_sw", Cpow_sw)

    def ssm_stage1(self, es, w_u):
        nc = self.nc
        I = self.ins
        A, pool, sp = self.A, self.pool, self.sp
        sb = lambda n, s, d=F32: self.sb(es, n, s, d)
        NC = L // T
        self.scrU = nc.dram_tensor("scrU", [4, 128, L], BF16, kind="Internal").ap()
        self.scrUs = nc.dram_tensor("scrUs", [4, 128, NSAMP], BF16, kind="Internal").ap()
        x_bf = sb("s_xbf", [128, 4, D], BF16)
        xT = sb("s_xT", [128, 8, 512], BF16)
        uT = sb("s_uT", [128, 4, L], BF16)
        uTs = sb("s_uTs", [128, 4, NSAMP], BF16)
        xall = I["x"]
        nblk = len(BLOCKS)
        self.load_tokens(xall, 0, x_bf)
        for bi in range(nblk):
            t0_, nt = BLOCKS[bi]
            samp = (bi == nblk - 1)
            self.transposes(x_bf, xT, nt)
            if bi + 1 < nblk:
                self.load_tokens(xall, bi + 1, x_bf)
            yield
            for j in range(4):
                ps = self.next_ps()
                for k in range(8):
                    self.mm(ps, ps[:, 0:nt], w_u[:, k, 128 * j:128 * j + 128], xT[:, k, 0:nt], [w_u, xT],
                            start=(k == 0), stop=(k == 7))
                if not samp:
                    A([ps], [uT], out=uT[:, j, :].rearrange("p (r c) -> p r c", c=NC)[:, :, 64 * bi:64 * bi + 64],
                      in_=ps[:, 0:512].rearrange("p (c r) -> p r c", r=T), func=AF.Copy)
                else:
                    A([ps], [uTs], out=uTs[:, j, :].rearrange("p (t s) -> p t s", s=NSEQ),
                      in_=ps[:, 0:64].rearrange("p (s t) -> p t s", t=TS), func=AF.Copy)
                yield
        self._uT, self._uTs = uT, uTs

    def ssm_main(self, es, Toep, Wend, Wend_sw, Cpow, Cpow_sw, CT, ST, RHOT, RHO1, AR4, AI4, ARm4, AIm4, Drep, w_u, Hin):
        nc = self.nc
        I, O = self.ins, self.outs
        V, A, Pl = self.V, self.A, self.Pl
        sp, pool = self.sp, self.pool
        sb = lambda n, s, d=F32: self.sb(es, n, s, d)
        NC = L // T
        scrG = nc.dram_tensor("scrG", [128, G, NC], BF16, kind="Internal").ap()
        scrGs = nc.dram_tensor("scrGs", [64, G, NSEQ], BF16, kind="Internal").ap()
        U, Us = self._U, self._Us
        Gall = sb("s_G", [128, G, NC], BF16)
        V1e = [sb("s_V1e%d" % i, [128, 8, 65], BF16) for i in range(2)]
        V2e = [sb("s_V2e%d" % i, [128, 8, 65], BF16) for i in range(2)]
        T1 = sb("s_T1", [128, 8, 64]); T2 = sb("s_T2", [128, 8, 64]); Z = sb("s_Z", [128, 8, 64])
        W1 = sb("s_W1", [128, 8]); W2 = sb("s_W2", [128, 8]); W3 = sb("s_W3", [128, 8])
        Hnew = sb("s_Hnew", [128, G])
        tmpY = sb("s_tmpY", [128, 8, 64])
        for i in range(2):
            Pl("memset", [], [V2e[i]], ap=V2e[i][:], constant=0.0)
        hs = sb("s_hs", [128, 4, 128]); hs_sw = sb("s_hs_sw", [128, 4, 128])
        H0 = sb("s_H0", [128, NSEQ, G]); H0sw = sb("s_H0sw", [128, NSEQ, G])
        self.barrier()
        for j in range(4):
            sp.dma(hs[:, j, 0:64], I["sre"][128 * j:128 * j + 128, :], [], [hs], par=True)
            sp.dma(hs[:, j, 64:128], I["sim"][128 * j:128 * j + 128, :], [], [hs], par=True)
            sp.dma(hs_sw[:, j, 0:64], I["sim"][128 * j:128 * j + 128, :], [], [hs_sw], par=True)
            sp.dma(hs_sw[:, j, 64:128], I["sre"][128 * j:128 * j + 128, :], [], [hs_sw], par=True)

        nblk = len(BLOCKS)
        scrU, scrUs = self.scrU, self.scrUs
        self.tap("U0", U)
        s3 = lambda b: b[:, 0:512].rearrange("p (g c) -> p g c", c=64)
        slots = [dict(T1=T1, T2=T2, Z=Z, W1=W1, W2=W2, W3=W3, tmpY=tmpY, V1=V1e[0], V2=V2e[0]),
                 dict(T1=sb("s_T1b", [128, 8, 64]), T2=sb("s_T2b", [128, 8, 64]), Z=sb("s_Zb", [128, 8, 64]),
                      W1=sb("s_W1b", [128, 8]), W2=sb("s_W2b", [128, 8]), W3=sb("s_W3b", [128, 8]),
                      tmpY=sb("s_tmpYb", [128, 8, 64]), V1=V1e[1], V2=V2e[1])]

        def lvl_b(bi, gs, B):
            T1, T2, Z, W1, W2, W3, tmpY, V1, V2 = (B[k] for k in ("T1", "T2", "Z", "W1", "W2", "W3", "tmpY", "V1", "V2"))
            csl = slice(64 * bi, 64 * bi + 64)
            g0 = 8 * gs
            gsl = slice(g0, g0 + 8)
            psS = self.next_ps(); psW = self.next_ps()
            for gl in range(8):
                g = g0 + gl
                self.mm(psS, psS[:, 64 * gl:64 * gl + 64], Wend[:, g, :], U[:, g, csl], [Wend, U], signal=(gl == 7))
            for gl in range(8):
                g = g0 + gl
                self.mm(psW, psW[:, 64 * gl:64 * gl + 64], Wend_sw[:, g, :], U[:, g, csl], [Wend_sw, U], signal=(gl == 7))
            yield
            V("tensor_tensor", [psS, CT], [T1], out=T1[:], in0=s3(psS), in1=CT[:, gsl, :], op=ALU.mult)
            V("tensor_tensor", [psW, ST], [T2], out=T2[:], in0=s3(psW), in1=ST[:, gsl, :], op=ALU.mult)
            V("tensor_tensor", [RHO1, Hin], [W1], out=W1[:], in0=RHO1[:, gsl], in1=Hin[:, gsl], op=ALU.mult)
            yield
            Pl("tensor_tensor", [T1, T2], [T1], out=T1[:], in0=T1[:], in1=T2[:], op=ALU.add)
            yield
            V("tensor_tensor", [T1, W1], [T1], out=T1[:, :, 0], in0=T1[:, :, 0], in1=W1[:], op=ALU.add)
            V("tensor_tensor_scan", [RHOT, T1], [Z], out=Z[:].rearrange("p g c -> p (g c)"),
              data0=RHOT[:, gsl, :].rearrange("p g c -> p (g c)"), data1=T1[:].rearrange("p g c -> p (g c)"),
              initial=0.0, op0=ALU.mult, op1=ALU.add)
            yield
            V("tensor_tensor", [Z, CT], [V1], out=V1[:, :, 1:65], in0=Z[:], in1=CT[:, gsl, :], op=ALU.mult)
            Pl("tensor_tensor", [Z, ST], [V2], out=V2[:, :, 1:65], in0=Z[:], in1=ST[:, gsl, :], op=ALU.mult)
            V("tensor_copy", [Hin], [V1], out=V1[:, :, 0], in_=Hin[:, gsl])
            Pl("tensor_tensor", [U, Drep], [tmpY], out=tmpY[:], in0=U[:, gsl, csl],
               in1=Drep[:, gsl].unsqueeze(2).to_broadcast([128, 8, 64]), op=ALU.mult)
            yield
            V("tensor_tensor", [Z, CT], [W1], out=W1[:], in0=Z[:, :, 63], in1=CT[:, gsl, 63], op=ALU.mult)
            V("tensor_tensor", [Z, ST], [W2], out=W2[:], in0=Z[:, :, 63], in1=ST[:, gsl, 63], op=ALU.mult)
            yield
            V("tensor_copy", [W2], [W3], out=W3[0:64, :], in_=W2[64:128, :])
            V("tensor_copy", [W2], [W3], out=W3[64:128, :], in_=W2[0:64, :])
            yield
            V("tensor_tensor", [W1, W3], [Hnew], out=Hnew[:, gsl], in0=W1[:], in1=W3[:], op=ALU.add)
            psY = self.next_ps()
            for gl in range(8):
                g = g0 + gl
                o_ = psY[:, 64 * gl:64 * gl + 64]
                self.mm(psY, o_, Toep[:, g, :], U[:, g, csl], [Toep, U], start=True, stop=False)
                self.mm(psY, o_, Cpow[:, g, :], V1[:, gl, 0:64], [Cpow, V1], start=False, stop=False)
                self.mm(psY, o_, Cpow_sw[:, g, :], V2[:, gl, 0:64], [Cpow_sw, V2], start=False, stop=True, signal=(gl == 7))
            yield
            V("tensor_tensor", [tmpY, psY], [tmpY], out=tmpY[:], in0=tmpY[:], in1=s3(psY), op=ALU.add)
            yield
            A([tmpY], [Gall], out=Gall[:, gsl, csl], in_=tmpY[:], func=AF.Gelu_apprx_tanh)

        for bi in range(nblk - 1):
            for pair in ((0, 1), (2, 3)):
                gens = [lvl_b(bi, pair[0], slots[0]), lvl_b(bi, pair[1], slots[1])]
                while gens:
                    for g_ in list(gens):
                        try:
                            next(g_)
                        except StopIteration:
                            gens.remove(g_)
            V("tensor_copy", [Hnew], [Hin], out=Hin[:], in_=Hnew[:])
        ps = self.next_ps()
        self.tr(ps, ps[0:G, 0:128], Hin[:, :], self.identf[:], [Hin, self.identf])
        hp = sb("s_hp", [G, 128])
        A([ps], [hp], out=hp[:], in_=ps[0:G, 0:128], func=AF.Copy)
        self.out_events.append(sp.dma(O["hrp"][:, :], hp[:, 0:64], [hp], []))
        self.out_events.append(sp.dma(O["hip"][:, :], hp[:, 64:128], [hp], []))
        sp.dma(scrG, Gall[:], [Gall], [])
        for (src_, dst) in ((hs, H0), (hs_sw, H0sw)):
            for j in range(4):
                ps = self.next_ps()
                self.tr(ps, ps[:, 0:128], src_[:, j, :], self.identf[:], [src_, self.identf])
                A([ps], [dst], out=dst[:, 4 * j:4 * j + 4, :], in_=ps[:, 0:128].rearrange("p (s g) -> p s g", g=G), func=AF.Copy)
        Hm = sb("s_Hm", [128, NSEQ, G]); Hp = sb("s_Hp", [128, NSEQ, G]); tq = sb("s_tq", [128, NSEQ, G])
        Hm_bf = sb("s_Hmbf", [128, G, NSEQ], BF16)
        bc = lambda b: b[:].unsqueeze(1).to_broadcast([128, NSEQ, G])
        V("tensor_tensor", [H0, ARm4], [Hm], out=Hm[:], in0=H0[:], in1=bc(ARm4), op=ALU.mult)
        V("tensor_tensor", [H0sw, AIm4], [tq], out=tq[:], in0=H0sw[:], in1=bc(AIm4), op=ALU.mult)
        V("tensor_tensor", [Hm, tq], [Hm_bf], out=Hm_bf[:].rearrange("p g s -> p s g"), in0=Hm[:], in1=tq[:], op=ALU.add)
        V("tensor_tensor", [H0, AR4], [Hp], out=Hp[:], in0=H0[:], in1=bc(AR4), op=ALU.mult)
        V("tensor_tensor", [H0sw, AI4], [tq], out=tq[:], in0=H0sw[:], in1=bc(AI4), op=ALU.mult)
        V("tensor_tensor", [Hp, tq], [Hp], out=Hp[:], in0=Hp[:], in1=tq[:], op=ALU.add)
        Hout = sb("s_Hout", [128, NSEQ, G])
        Gs = sb("s_Gs", [128, G, NSEQ], BF16)
        tmps = sb("s_tmps", [128, G, NSEQ])
        psS = self.next_ps(); psY = self.next_ps()
        for g in range(G):
            self.mm(psS, psS[:, NSEQ * g:NSEQ * g + NSEQ], Wend[:, g, :], Us[:, g, :], [Wend, Us], signal=(g == G - 1))
        V("tensor_tensor", [psS, Hp], [Hout], out=Hout[:], in0=psS[:, 0:512].rearrange("p (g s) -> p s g", s=NSEQ),
          in1=Hp[:], op=ALU.add)
        for g in range(G):
            o_ = psY[:, NSEQ * g:NSEQ * g + NSEQ]
            self.mm(psY, o_, Toep[:, g, :], Us[:, g, :], [Toep, Us], start=True, stop=False)
            self.mm(psY, o_, Cpow[:, g, :], Hm_bf[:, g, :], [Cpow, Hm_bf], start=False, stop=True, signal=(g == G - 1))
        V("tensor_tensor", [Us, Drep], [tmps], out=tmps[:], in0=Us[:], in1=Drep[:].unsqueeze(2).to_broadcast([128, G, NSEQ]), op=ALU.mult)
        V("tensor_tensor", [tmps, psY], [tmps], out=tmps[:], in0=tmps[:], in1=psY[:, 0:512].rearrange("p (g s) -> p g s", s=NSEQ), op=ALU.add)
        A([tmps], [Gs], out=Gs[:], in_=tmps[:], func=AF.Gelu_apprx_tanh)
        sp.dma(scrGs, Gs[64:128, :, :], [Gs], [])
        ho = sb("s_ho", [128, 4, 128])
        for j in range(4):
            ps = self.next_ps()
            self.tr(ps, ps[:, 0:128], Hout[:, 4 * j:4 * j + 4, :].rearrange("p s g -> p (s g)"), self.identf[:], [Hout, self.identf])
            A([ps], [ho], out=ho[:, j, :], in_=ps[:, 0:128], func=AF.Copy)
            self.out_events.append(sp.dma(O["hrs"][128 * j:128 * j + 128, :], ho[:, j, 0:64], [ho], []))
            self.out_events.append(sp.dma(O["his"][128 * j:128 * j + 128, :], ho[:, j, 64:128], [ho], []))
        self.scrG, self.scrGs = scrG, scrGs

    def load_tokens(self, src, bi, dst):
        t0_, nt = BLOCKS[bi]
        ntile = (nt + 127) // 128
        tp = min(nt, 128)
        for i in range(ntile):
            self.pool.dma(dst[0:tp, i, :], src[t0_ + 128 * i:t0_ + 128 * i + tp, :], [], [dst], par=True)

    def transposes(self, xb, xt, nt):
        ntile = (nt + 127) // 128
        tp = min(nt, 128)
        for k in range(8):
            ps = self.next_ps()
            pv = ps[:].bitcast(BF16)
            for i in range(ntile):
                self.tr(ps, pv[:, 128 * i:128 * i + tp], xb[0:tp, i, 128 * k:128 * k + 128], self.ident[0:tp, 0:tp],
                        [xb, self.ident], signal=(i == ntile - 1))
            self.A([ps], [xt], out=xt[:, k, 0:nt], in_=pv[:, 0:nt], func=AF.Copy)

    def halves(self, buf):
        return (Buf(buf.t, buf.name + "_lo"), Buf(buf.t, buf.name + "_hi"))

    def layer_norm(self, ps_pair, x_tok, r, xh, g_bc, b_bc, st6, mv, sd, tp):
        V, A, Pl = self.V, self.A, self.Pl
        cs = [slice(0, 512), slice(512, D)]
        for h in range(2):
            V("scalar_tensor_tensor", [x_tok, ps_pair[h]], [r[h]], out=r[h][0:tp, cs[h]],
              in0=x_tok[0:tp, cs[h]], scalar=ALPHA, in1=ps_pair[h][0:tp, :], op0=ALU.mult, op1=ALU.add)
        for h in range(2):
            V("bn_stats", [r[h]], [st6], out=st6[0:tp, h, :], in_=r[h][0:tp, cs[h]])
        V("bn_aggr", [st6], [mv], out=mv[0:tp, :], in_=st6[0:tp, :, :].rearrange("p a b -> p (a b)"))
        V("tensor_scalar", [mv], [sd], out=sd[0:tp, 0:1], in0=mv[0:tp, 1:2], scalar1=LN_EPS, scalar2=None, op0=ALU.add)
        Pl("tensor_tensor", [sd, self.mhalf], [sd], out=sd[0:tp, 1:2], in0=sd[0:tp, 0:1], in1=self.mhalf[0:tp, 0:1], op=ALU.pow)
        V("scalar_tensor_tensor", [mv, sd], [sd], out=sd[0:tp, 2:3], in0=mv[0:tp, 0:1], scalar=-1.0, in1=sd[0:tp, 1:2],
          op0=ALU.mult, op1=ALU.mult)
        V("tensor_scalar", [r[0], sd], [xh[0]], out=xh[0][0:tp, cs[0]], in0=r[0][0:tp, cs[0]], scalar1=sd[0:tp, 1:2], scalar2=sd[0:tp, 2:3], op0=ALU.mult, op1=ALU.add)
        Pl("tensor_scalar", [r[1], sd], [xh[1]], out=xh[1][0:tp, cs[1]], in0=r[1][0:tp, cs[1]], scalar1=sd[0:tp, 1:2], scalar2=sd[0:tp, 2:3], op0=ALU.mult, op1=ALU.add)
        Pl("tensor_tensor", [xh[1], g_bc], [xh[1]], out=xh[1][0:tp, cs[1]], in0=xh[1][0:tp, cs[1]], in1=g_bc[0:tp, cs[1]], op=ALU.mult)
        V("tensor_tensor", [xh[0], g_bc], [xh[0]], out=xh[0][0:tp, cs[0]], in0=xh[0][0:tp, cs[0]], in1=g_bc[0:tp, cs[0]], op=ALU.mult)
        Pl("tensor_tensor", [xh[1], b_bc], [xh[1]], out=xh[1][0:tp, cs[1]], in0=xh[1][0:tp, cs[1]], in1=b_bc[0:tp, cs[1]], op=ALU.add)
        V("tensor_tensor", [xh[0], b_bc], [xh[0]], out=xh[0][0:tp, cs[0]], in0=xh[0][0:tp, cs[0]], in1=b_bc[0:tp, cs[0]], op=ALU.add)

    def pass_a(self):
        nc = self.nc
        I, O = self.ins, self.outs
        V, A, Pl = self.V, self.A, self.Pl
        sp, pool = self.sp, self.pool
        NC = L // T
        with ExitStack() as es:
            sb = lambda n, s, d=F32: self.sb(es, n, s, d)
            gT = sb("gT", [128, 4, NTOK], BF16)
            w_a = sb("w_a", [128, 8, 2816], BF16)
            w_glu = sb("w_glu", [128, 4, 2048], BF16)
            w_att = sb("w_att", [128, 4, D], BF16)
            w_o = sb("w_o", [128, 8, D], BF16)
            g_bc = sb("g1_bc", [128, D]); b_bc = sb("b1_bc", [128, D])
            sp.dma(g_bc[:], I["ln1_g"][0:1, :].partition_broadcast(128), [], [g_bc])
            sp.dma(b_bc[:], I["ln1_b"][0:1, :].partition_broadcast(128), [], [b_bc])
            maskD = sb("maskD", [128, 512], BF16); maskP = sb("maskP", [128, 512], BF16)
            maskC = sb("maskC", [128, 256], BF16); maskN = sb("maskN", [64, 256], BF16)
            es8 = sb("es8", [128, 8]); ES = sb("ES", [128, 2, 4, 128])
            vext = sb("vext", [128, 5, 2, 128], BF16)
            vc_ext = sb("vc_ext", [128, NSEQ, 2, 128], BF16)
            es_m = ExitStack()
            zer = self.sb(es_m, "zer", [128, 512]); mtmp = self.sb(es_m, "mtmp", [128, 512]); one_t = self.sb(es_m, "one_t", [128, 256])
            Pl("memset", [], [one_t], ap=one_t[:], constant=1.0)
            Pl("memset", [], [zer], ap=zer[:], constant=0.0)
            Pl("memset", [], [vext], ap=vext[:], constant=1.0)
            Pl("memset", [], [vc_ext], ap=vc_ext[:], constant=1.0)
            self.barrier()
            sp.dma(es8[:], I["sinks"][0:1, :].partition_broadcast(128), [], [es8])
            A([es8], [es8], out=es8[:], in_=es8[:], func=AF.Exp)
            V("tensor_copy", [es8], [ES], out=ES[:].rearrange("p a h q -> p (a h) q"), in_=es8[:].unsqueeze(2).to_broadcast([128, 8, 128]))
            z3 = zer[:].rearrange("p (h q) -> p h q", q=128)
            Pl("affine_select", [zer], [mtmp], out=mtmp[:].rearrange("p (h q) -> p h q", q=128), in_=z3, pattern=[[0, 4], [1, 128]],
               compare_op=ALU.is_ge, fill=NEG, base=0, channel_multiplier=-1)
            Pl("tensor_copy", [mtmp], [maskD], out=maskD[:], in_=mtmp[:])
            Pl("affine_select", [zer], [mtmp], out=mtmp[:].rearrange("p (h q) -> p h q", q=128), in_=z3, pattern=[[0, 4], [-1, 128]],
               compare_op=ALU.is_ge, fill=NEG, base=-1, channel_multiplier=1)
            Pl("tensor_copy", [mtmp], [maskP], out=maskP[:], in_=mtmp[:])
            Pl("affine_select", [one_t], [mtmp], out=mtmp[:, 0:256].rearrange("p (s h t) -> p s h t", h=4, t=TS),
               in_=one_t[:, 0:256].rearrange("p (s h t) -> p s h t", h=4, t=TS), pattern=[[0, NSEQ], [0, 4], [-1, TS]],
               compare_op=ALU.is_ge, fill=0.0, base=-1, channel_multiplier=1)
            Pl("tensor_copy", [mtmp], [maskC], out=maskC[:], in_=mtmp[:, 0:256])
            Pl("affine_select", [one_t], [mtmp], out=mtmp[0:64, 0:256].rearrange("p (h s t) -> p h s t", s=NSEQ, t=TS),
               in_=one_t[0:64, 0:256].rearrange("p (h s t) -> p h s t", s=NSEQ, t=TS), pattern=[[0, 4], [-4, NSEQ], [0, TS]],
               compare_op=ALU.is_ge, fill=0.0, base=0, channel_multiplier=1)
            Pl("affine_select", [mtmp], [mtmp], out=mtmp[0:64, 0:256].rearrange("p (h s t) -> p h s t", s=NSEQ, t=TS),
               in_=mtmp[0:64, 0:256].rearrange("p (h s t) -> p h s t", s=NSEQ, t=TS), pattern=[[0, 4], [4, NSEQ], [1, TS]],
               compare_op=ALU.is_ge, fill=0.0, base=0, channel_multiplier=-1)
            Pl("tensor_copy", [mtmp], [maskN], out=maskN[:], in_=mtmp[0:64, 0:256])
            self.barrier()
            es_m.close()
            x_bf = [sb("a_xbf", [128, 4, D], BF16)] * 2
            xT = sb("a_xT", [128, 8, 512], BF16)
            qT = sb("a_qT", [128, 4, 512], BF16)
            kT = sb("a_kT", [128, 640], BF16)
            kvf = sb("a_kvf", [128, 256])
            PT = [sb("a_PT%d" % i, [128, 512], BF16) for i in range(4)]
            oT = sb("a_oT", [128, 4, 512], BF16)
            den = sb("a_den", [64, 512]); rec = sb("a_rec", [64, 512])
            dens = [den, rec]
            osc = sb("a_osc", [128, 256]); osum = sb("a_osum", [128, 256])
            sig = [sb("a_sig%d" % i, [128, 512]) for i in range(2)]
            gsb = [sb("a_gs%d" % i, [128, 512], BF16) for i in range(2)]
            gab = [sb("a_ga%d" % i, [128, 512], BF16) for i in range(2)]
            bsb = [sb("a_bs%d" % i, [128, 512], BF16) for i in range(2)]
            t1 = sb("a_t1", [128, 512]); t2 = sb("a_t2", [128, 512])
            mT = sb("a_mT", [128, 8, 512], BF16)
            x_tok = [sb("a_xtok", [128, D])] * 2
            rr = [self.halves(sb("a_r", [128, D]))] * 2
            xh = [self.halves(sb("a_xh", [128, D]))] * 2
            st6 = sb("a_st6", [128, 2, 6]); mv = sb("a_mv", [128, 2]); sd = sb("a_sd", [128, 3])
            ckb = sb("a_ckb", [128, NSEQ, 128], BF16); kcT = sb("a_kcT", [128, NSEQ, 128], BF16)
            if "A_d2d" not in SKIP:
                self.out_events.append(sp.dma(O["kws"][:, 0:124, :], I["ck"][:, 4:128, :], [], []))
                self.out_events.append(sp.dma(O["vws"][:, 0:124, :], I["cv"][:, 4:128, :], [], []))

            xall = I["x"]
            nblk = len(BLOCKS)
            self.load_tokens(xall, 0, x_bf[0])
            wv = I["w_in"].rearrange("(k p) n -> p k n", p=128)
            for k in range(8 if "A_w" not in SKIP else 0):
                pool.dma(w_a[:, k, 0:768], wv[:, k, 512:1280], [], [w_a], par=True)
            wg = I["w_glu"].rearrange("(k p) n -> p k n", p=128)
            for k in range(4 if "A_w" not in SKIP else 0):
                for c in range(2):
                    pool.dma(w_glu[:, k, 1024 * c:1024 * c + 1024], wg[:, k, 1024 * c:1024 * c + 1024], [], [w_glu], par=True)
            wa = I["w_attn"].rearrange("(k p) n -> p k n", p=128)
            for k in range(4 if "A_w" not in SKIP else 0):
                pool.dma(w_att[:, k, :], wa[:, k, :], [], [w_att], par=True)
            for k in range(8 if "A_w" not in SKIP else 0):
                for c in range(2):
                    pool.dma(w_a[:, k, 768 + 1024 * c:1792 + 1024 * c], wv[:, k, 1280 + 1024 * c:2304 + 1024 * c], [], [w_a], par=True)
            wo = I["w_o"].rearrange("(k p) n -> p k n", p=128)
            for k in range(8 if "A_w" not in SKIP else 0):
                pool.dma(w_o[:, k, :], wo[:, k, :], [], [w_o], par=True)
            for g in range(G):
                j, gl = divmod(g, 8)
                src = bass.AP(tensor=self.scrG.tensor, offset=g * NC, ap=[[G * NC, CH], [CH * G * NC, T], [1, NC]])
                sp.dma(gT[16 * gl:16 * gl + 16, j, 0:L].rearrange("c (r n) -> c r n", n=NC), src, [], [gT], par=True)
                srcs = bass.AP(tensor=self.scrGs.tensor, offset=g * NSEQ, ap=[[G * NSEQ, CH], [CH * G * NSEQ, TS], [1, NSEQ]])
                sp.dma(gT[16 * gl:16 * gl + 16, j, L:L + NSAMP].rearrange("c (r n) -> c r n", n=NSEQ), srcs, [], [gT], par=True)
            if "A_cache" not in SKIP:
                pool.dma(ckb[:], I["ck"].rearrange("s w d -> w s d"), [], [ckb])
            for a_ in range(2 if "A_cache" not in SKIP else 0):
                pool.dma(vc_ext[:, :, a_, 0:64], I["cv"][:, :, 64 * a_:64 * a_ + 64].rearrange("s w d -> w s d"), [], [vc_ext])
            def blk(bi):
                if "A_blk" in SKIP or ("A_samp" in SKIP and bi == nblk - 1) or ("A_prompt" in SKIP and bi < nblk - 1):
                    return
                t0_, nt = BLOCKS[bi]
                ntile = (nt + 127) // 128
                tp = min(nt, 128)
                samp = (bi == nblk - 1)
                xb = x_bf[bi % 2]
                self.transposes(xb, xT, nt)
                if bi + 1 < nblk:
                    self.load_tokens(xall, bi + 1, x_bf[(bi + 1) % 2])
                if "A_proj" in SKIP:
                    return
                for j in range(4):
                    ps = self.next_ps()
                    for k in range(8):
                        self.mm(ps, ps[:, 0:nt], w_a[:, k, 128 * j:128 * j + 128], xT[:, k, 0:nt], [w_a, xT], start=(k == 0), stop=(k == 7))
                    A([ps], [qT], out=qT[:, j, 0:nt], in_=ps[:, 0:nt], func=AF.Copy)
                ps = self.next_ps()
                for k in range(8):
                    self.mm(ps, ps[:, 0:nt], w_a[:, k, 512:640], xT[:, k, 0:nt], [w_a, xT], start=(k == 0), stop=(k == 7))
                A([ps], [kT], out=kT[:, 128:128 + nt], in_=ps[:, 0:nt], func=AF.Copy)
                for i in range(ntile if "A_kvt" not in SKIP else 0):
                    ps = self.next_ps()
                    for k in range(8):
                        self.mm(ps, ps[0:tp, 0:256], xT[:, k, 128 * i:128 * i + tp], w_a[:, k, 512:768], [w_a, xT], start=(k == 0), stop=(k == 7))
                    if "A_kv_act" not in SKIP:
                        A([ps], [vext], out=vext[0:tp, 1 + i, :, 0:64], in_=ps[0:tp, 128:256].rearrange("p (a d) -> p a d", a=2), func=AF.Copy)
                    if ((bi == nblk - 2 and i == ntile - 1) or samp) and "A_kv_out" not in SKIP:
                        A([ps], [kvf], out=kvf[0:tp, :], in_=ps[0:tp, 0:256], func=AF.Copy)
                        if samp:
                            self.out_events.append(sp.dma(O["kws"][:, 124:128, :], kvf[0:NSAMP, 0:128], [kvf], []))
                            self.out_events.append(sp.dma(O["vws"][:, 124:128, :], kvf[0:NSAMP, 128:256], [kvf], []))
                        elif "A_kv_dma" not in SKIP:
                            self.out_events.append(sp.dma(O["kwp"][:, :], kvf[:, 0:128], [kvf], []))
                            self.out_events.append(sp.dma(O["vwp"][:, :], kvf[:, 128:256], [kvf], []))
                if "A_attn" in SKIP:
                    pass
                elif not samp:
                    def attn_unit(i, kv, slot):
                        qs = slice(128 * i, 128 * i + 128)
                        hp_ = slice(64 * kv, 64 * kv + 64)
                        has_prev = not (bi == 0 and i == 0)
                        rhs_q = qT[hp_, :, qs]
                        den_ = dens[slot]
                        PTd, PTp = PT[2 * slot], PT[2 * slot + 1]
                        psD = self.next_ps()
                        self.mm(psD, psD[:, :], self.ident[:], maskD[:], [self.ident, maskD], start=True, stop=False)
                        self.mm(psD, psD[:, :].rearrange("p (h q) -> p h q", q=128), kT[hp_, 128 + 128 * i:256 + 128 * i], rhs_q, [kT, qT], start=False, stop=True)
                        if has_prev:
                            psP = self.next_ps()
                            self.mm(psP, psP[:, :], self.ident[:], maskP[:], [self.ident, maskP], start=True, stop=False)
                            self.mm(psP, psP[:, :].rearrange("p (h q) -> p h q", q=128), kT[hp_, 128 * i:128 + 128 * i], rhs_q, [kT, qT], start=False, stop=True)
                        yield
                        A([psD], [PTd], out=PTd[:], in_=psD[:, :], func=AF.Exp, scale=0.125)
                        if has_prev:
                            A([psP], [PTp], out=PTp[:], in_=psP[:, :], func=AF.Exp, scale=0.125)
                        yield
                        psO = self.next_ps()
                        self.mm(psO, psO[:, :], vext[:, 1 + i, kv, :], PTd[:], [vext, PTd], start=True, stop=not has_prev)
                        if has_prev:
                            self.mm(psO, psO[:, :], vext[:, i, kv, :], PTp[:], [vext, PTp], start=False, stop=True)
                        yield
                        V("tensor_tensor", [psO, ES], [den_], out=den_[:], in0=psO[64:128, :], in1=ES[64:128, kv, :, :].rearrange("p h q -> p (h q)"), op=ALU.add)
                        yield
                        A([den_], [den_], out=den_[:], in_=den_[:], func=AF.Ln)
                        A([den_], [den_], out=den_[:], in_=den_[:], func=AF.Exp, scale=-1.0)
                        yield
                        V("tensor_tensor", [psO, den_], [oT], out=oT[hp_, :, qs], in0=psO[0:64, :].rearrange("p (h q) -> p h q", q=128),
                          in1=den_[:].rearrange("p (h q) -> p h q", q=128), op=ALU.mult)

                    units = [(i, kv) for i in range(ntile) for kv in range(2)]
                    for u0 in range(0, len(units), 2):
                        gens = [attn_unit(units[u0][0], units[u0][1], 0), attn_unit(units[u0 + 1][0], units[u0 + 1][1], 1)]
                        while gens:
                            for g_ in list(gens):
                                try:
                                    next(g_)
                                except StopIteration:
                                    gens.remove(g_)
                else:
                    for s_ in range(NSEQ):
                        ps = self.next_ps()
                        pv = ps[:].bitcast(BF16)
                        self.tr(ps, pv[:, 0:128], ckb[:, s_, :], self.ident[:], [ckb, self.ident])
                        A([ps], [kcT], out=kcT[:, s_, :], in_=pv[:, 0:128], func=AF.Copy)
                    for kv in range(2):
                        hp_ = slice(64 * kv, 64 * kv + 64)
                        psC = self.next_ps()
                        for s_ in range(NSEQ):
                            self.mm(psC, psC[:, 16 * s_:16 * s_ + 16].rearrange("p (h t) -> p h t", t=TS), kcT[hp_, s_, :], qT[hp_, :, TS * s_:TS * s_ + TS], [kcT, qT], start=True, stop=True, signal=(s_ == NSEQ - 1))
                        PTc = PT[0]
                        A([psC], [PTc], out=PTc[:, 0:256], in_=psC[:, 0:256], func=AF.Exp, scale=0.125)
                        V("tensor_tensor", [PTc, maskC], [PTc], out=PTc[:, 0:256], in0=PTc[:, 0:256], in1=maskC[:], op=ALU.mult)
                        psN = self.next_ps()
                        self.mm(psN, psN[0:64, 0:256].rearrange("p (h q) -> p h q", q=NSAMP), kT[hp_, 128:128 + NSAMP], qT[hp_, :, 0:NSAMP], [kT, qT], start=True, stop=True)
                        PTn = PT[1]
                        A([psN], [PTn], out=PTn[0:64, 0:256], in_=psN[0:64, 0:256], func=AF.Exp, scale=0.125)
                        V("tensor_tensor", [PTn, maskN], [PTn], out=PTn[0:64, 0:256], in0=PTn[0:64, 0:256], in1=maskN[:], op=ALU.mult)
                        psOc = self.next_ps()
                        for s_ in range(NSEQ):
                            self.mm(psOc, psOc[:, 16 * s_:16 * s_ + 16], vc_ext[:, s_, kv, :], PTc[:, 16 * s_:16 * s_ + 16], [vc_ext, PTc], start=True, stop=True, signal=(s_ == NSEQ - 1))
                        psOn = self.next_ps()
                        self.mm(psOn, psOn[:, 0:256], vext[0:64, 1, kv, :], PTn[0:64, 0:256], [vext, PTn], start=True, stop=True)
                        A([psOc], [osc], out=osc[:, 0:256], in_=psOc[:, 0:256], func=AF.Copy)
                        V("tensor_tensor", [psOn, osc], [osum], out=osum[:, 0:256].rearrange("p (h s t) -> p h s t", s=NSEQ, t=TS),
                          in0=psOn[:, 0:256].rearrange("p (h s t) -> p h s t", s=NSEQ, t=TS),
                          in1=osc[:, 0:256].rearrange("p (s h t) -> p h s t", h=4, t=TS), op=ALU.add)
                        V("tensor_tensor", [osum, ES], [den], out=den[:, 0:256].rearrange("p (h q) -> p h q", q=NSAMP), in0=osum[64:128, 0:256].rearrange("p (h q) -> p h q", q=NSAMP),
                          in1=ES[64:128, kv, :, 0:NSAMP], op=ALU.add)
                        A([den], [den], out=den[:, 0:256], in_=den[:, 0:256], func=AF.Ln)
                        A([den], [rec], out=rec[:, 0:256], in_=den[:, 0:256], func=AF.Exp, scale=-1.0)
                        V("tensor_tensor", [osum, rec], [oT], out=oT[hp_, :, 0:NSAMP], in0=osum[0:64, 0:256].rearrange("p (h q) -> p h q", q=NSAMP),
                          in1=rec[:, 0:256].rearrange("p (h q) -> p h q", q=NSAMP), op=ALU.mult)
                if not samp and bi + 1 < nblk - 1 and "A_carry" not in SKIP:
                    A([kT], [kT], out=kT[:, 0:128], in_=kT[:, 512:640], func=AF.Copy)
                    A([vext], [vext], out=vext[:, 0, :, 0:64], in_=vext[:, 4, :, 0:64], func=AF.Copy)
                yield
                for jf in range(8 if "A_merge" not in SKIP else 0):
                    psA = self.next_ps(); psB = self.next_ps()
                    for (psx, c0) in ((psA, 128 * jf), (psB, 1024 + 128 * jf)):
                        for k in range(4):
                            if not samp:
                                rhs = gT[:, k, 0:L].rearrange("p (r c) -> p r c", c=NC)[:, :, 64 * bi:64 * bi + 64]
                                o_ = psx[:, :].rearrange("p (r c) -> p r c", c=64)
                            else:
                                rhs = gT[:, k, L:L + NSAMP]
                                o_ = psx[:, 0:NSAMP]
                            self.mm(psx, o_, w_glu[:, k, c0:c0 + 128], rhs, [w_glu, gT], start=(k == 0), stop=(k == 3))
                    sg = sig[jf % 2]
                    bs = bsb[jf % 2]
                    A([psB], [sg], out=sg[:, 0:nt], in_=psB[:, 0:nt], func=AF.Sigmoid)
                    if not samp:
                        V("tensor_tensor", [psA, sg], [bs], out=bs[:, :].rearrange("p (c r) -> p r c", r=T),
                          in0=psA[:, :].rearrange("p (r c) -> p r c", c=64), in1=sg[:].rearrange("p (r c) -> p r c", c=64), op=ALU.mult)
                    else:
                        V("tensor_tensor", [psA, sg], [bs], out=bs[:, 0:NSAMP].rearrange("p (s t) -> p t s", t=TS),
                          in0=psA[:, 0:NSAMP].rearrange("p (t s) -> p t s", s=NSEQ), in1=sg[:, 0:NSAMP].rearrange("p (t s) -> p t s", s=NSEQ), op=ALU.mult)
                    psBA = self.next_ps()
                    for k in range(4):
                        self.mm(psBA, psBA[:, 0:nt], w_att[:, k, 128 * jf:128 * jf + 128], oT[:, k, 0:nt], [w_att, oT], start=(k == 0), stop=(k == 3))
                    psGS = self.next_ps()
                    for k in range(8):
                        self.mm(psGS, psGS[:, 0:nt], w_a[:, k, 768 + 128 * jf:896 + 128 * jf], xT[:, k, 0:nt], [w_a, xT], start=(k == 0), stop=(k == 7))
                    psGA = self.next_ps()
                    for k in range(8):
                        self.mm(psGA, psGA[:, 0:nt], w_a[:, k, 1792 + 128 * jf:1920 + 128 * jf], xT[:, k, 0:nt], [w_a, xT], start=(k == 0), stop=(k == 7))
                    gs_, ga_ = gsb[jf % 2], gab[jf % 2]
                    A([psGS], [gs_], out=gs_[:, 0:nt], in_=psGS[:, 0:nt], func=AF.Sigmoid)
                    A([psGA], [ga_], out=ga_[:, 0:nt], in_=psGA[:, 0:nt], func=AF.Sigmoid)
                    Pl("tensor_tensor", [gs_, bs], [t1], out=t1[:, 0:nt], in0=gs_[:, 0:nt], in1=bs[:, 0:nt], op=ALU.mult)
                    V("tensor_tensor", [ga_, psBA], [t2], out=t2[:, 0:nt], in0=ga_[:, 0:nt], in1=psBA[:, 0:nt], op=ALU.mult)
                    V("tensor_tensor", [t1, t2], [mT], out=mT[:, jf, 0:nt], in0=t1[:, 0:nt], in1=t2[:, 0:nt], op=ALU.add)
                yield
                for i in range(ntile if "A_ln" not in SKIP else 0):
                    tsl = slice(t0_ + 128 * i, t0_ + 128 * i + tp)
                    xt_ = x_tok[i % 2]; r_ = rr[i % 2]; xh_ = xh[i % 2]
                    sp.dma(xt_[0:tp, :], xall[tsl, :], [], [xt_])
                    pp = [self.next_ps(), self.next_ps()]
                    for h in range(2):
                        for k in range(8):
                            self.mm(pp[h], pp[h][0:tp, :], mT[:, k, 128 * i:128 * i + tp], w_o[:, k, 512 * h:512 * h + 512], [mT, w_o], start=(k == 0), stop=(k == 7))
                    self.layer_norm(pp, xt_, r_, xh_, g_bc, b_bc, st6, mv, sd, tp)
                    sp.dma(self.x1d[tsl, :], xh_[0][0:tp, :], [xh_[0], xh_[1]], [])

            def finish(g_):
                for _ in g_:
                    pass

            gens = [blk(bi) for bi in range(nblk)]
            next(gens[0], None)
            next(gens[0], None)
            for b_ in range(1, nblk):
                next(gens[b_], None)
                finish(gens[b_ - 1])
                next(gens[b_], None)
            finish(gens[nblk - 1])

    def pass_b(self):
        nc = self.nc
        I, O = self.ins, self.outs
        V, A, Pl = self.V, self.A, self.Pl
        sp, pool = self.sp, self.pool
        with ExitStack() as es:
            sb = lambda n, s, d=F32: self.sb(es, n, s, d)
            w_up = sb("w_up", [128, 8, 2 * DFF], BF16)
            w_dn = sb("w_dn", [128, NF, D], BF16)
            g_bc = sb("g2_bc", [128, D]); b_bc = sb("b2_bc", [128, D])
            sp.dma(g_bc[:], I["ln2_g"][0:1, :].partition_broadcast(128), [], [g_bc])
            sp.dma(b_bc[:], I["ln2_b"][0:1, :].partition_broadcast(128), [], [b_bc])
            cw = sb("cw", [128, NF, 3]); cb = sb("cb", [128, NF])
            for j in range(3):
                sp.dma(cw[:, :, j], I["conv_w"][j:j + 1, :].rearrange("o (f p) -> p (o f)", p=128), [], [cw], allow_slow_non_contiguous=True)
            sp.dma(cb[:], I["conv_b"][0:1, :].rearrange("o (f p) -> p (o f)", p=128), [], [cb], allow_slow_non_contiguous=True)
            a_carry = sb("a_carry", [128, NF, 2])
            Pl("memset", [], [a_carry], ap=a_carry[:], constant=0.0)
            self.barrier()
            scT = sb("scT", [128, NF, 2 * NSEQ]); csT = sb("csT", [128, NF, 2 * NSEQ])
            stg = [sb("b_stg%d" % i, [32, 512]) for i in range(2)]
            for c in range(6):
                w_ = min(512, DFF - 512 * c)
                st_ = stg[c % 2]
                sp.dma(st_[:, 0:w_], I["sconv"][:, 512 * c:512 * c + w_], [], [st_])
                for q in range(w_ // 128):
                    f = 4 * c + q
                    ps = self.next_ps()
                    self.tr(ps, ps[:, 0:32], st_[:, 128 * q:128 * q + 128], self.identf[0:32, 0:32], [st_, self.identf])
                    A([ps], [scT], out=scT[:, f, :], in_=ps[:, 0:32], func=AF.Copy)
            x_bf = sb("b_xbf", [128, 4, D], BF16)
            xT = sb("b_xT", [128, 8, 512], BF16)
            a_ext = [sb("b_aext", [128, 514])] * 2
            c1 = [sb("b_c1%d" % i, [128, 512]) for i in range(2)]
            ge = [sb("b_ge%d" % i, [128, 512], BF16) for i in range(2)]
            hT = sb("b_hT", [128, NF, 512], BF16)
            x_tok = [sb("b_xtok", [128, D])] * 2
            rr = self.halves(sb("b_r", [128, D])); xh = rr
            st6 = sb("b_st6", [128, 2, 6]); mv = sb("b_mv", [128, 2]); sd = sb("b_sd", [128, 3])
            nblk = len(BLOCKS)
            self.load_tokens(self.x1d, 0, x_bf)
            wu = I["w_up"].rearrange("(k p) n -> p k n", p=128)
            for c in (0, 2, 1, 3):
                for k in range(8):
                    pool.dma(w_up[:, k, 1408 * c:1408 * c + 1408], wu[:, k, 1408 * c:1408 * c + 1408], [], [w_up], par=True)
            wd = I["w_down"].rearrange("(k p) n -> p k n", p=128)
            for k in range(NF):
                pool.dma(w_dn[:, k, :], wd[:, k, :], [], [w_dn], par=True)
            for bi in range(nblk):
                t0_, nt = BLOCKS[bi]
                ntile = (nt + 127) // 128
                tp = min(nt, 128)
                samp = (bi == nblk - 1)
                self.transposes(x_bf, xT, nt)
                if bi + 1 < nblk:
                    self.load_tokens(self.x1d, bi + 1, x_bf)
                GF = 2
                for f0 in range(0, NF, GF):
                    fs = list(range(f0, min(NF, f0 + GF)))
                    pA, pG = {}, {}
                    for f in fs:
                        pA[f] = self.next_ps(); pG[f] = self.next_ps()
                        for (psx, c0) in ((pA[f], 128 * f), (pG[f], DFF + 128 * f)):
                            for k in range(8):
                                self.mm(psx, psx[:, 0:nt], w_up[:, k, c0:c0 + 128], xT[:, k, 0:nt], [w_up, xT], start=(k == 0), stop=(k == 7))
                    if not samp:
                        for f in fs:
                            c_ = c1[f % 2]
                            V("tensor_scalar", [a_carry, cw, cb], [c_], out=c_[:, 0:2], in0=a_carry[:, f, :], scalar1=cw[:, f, 0:1], scalar2=cb[:, f:f + 1], op0=ALU.mult, op1=ALU.add)
                        for f in fs:
                            c_ = c1[f % 2]
                            V("tensor_scalar", [pA[f], cw, cb], [c_], out=c_[:, 2:nt], in0=pA[f][:, 0:nt - 2], scalar1=cw[:, f, 0:1], scalar2=cb[:, f:f + 1], op0=ALU.mult, op1=ALU.add)
                        for f in fs:
                            c_ = c1[f % 2]
                            V("scalar_tensor_tensor", [a_carry, cw, c_], [c_], out=c_[:, 0:1], in0=a_carry[:, f, 1:2], scalar=cw[:, f, 1:2], in1=c_[:, 0:1], op0=ALU.mult, op1=ALU.add)
                        for f in fs:
                            c_ = c1[f % 2]
                            V("scalar_tensor_tensor", [pA[f], cw, c_], [c_], out=c_[:, 1:nt], in0=pA[f][:, 0:nt - 1], scalar=cw[:, f, 1:2], in1=c_[:, 1:nt], op0=ALU.mult, op1=ALU.add)
                        for f in fs:
                            c_ = c1[f % 2]
                            V("scalar_tensor_tensor", [pA[f], cw, c_], [c_], out=c_[:, 0:nt], in0=pA[f][:, 0:nt], scalar=cw[:, f, 2:3], in1=c_[:, 0:nt], op0=ALU.mult, op1=ALU.add)
                        for f in fs:
                            A([pA[f]], [a_carry], out=a_carry[:, f, :], in_=pA[f][:, nt - 2:nt], func=AF.Copy)
                        for f in fs:
                            A([c1[f % 2]], [ge[f % 2]], out=ge[f % 2][:, 0:nt], in_=c1[f % 2][:, 0:nt], func=AF.Gelu_apprx_tanh)
                        for f in fs:
                            V("tensor_tensor", [ge[f % 2], pG[f]], [hT], out=hT[:, f, 0:nt], in0=ge[f % 2][:, 0:nt], in1=pG[f][:, 0:nt], op=ALU.mult)
                    else:
                        for f in fs:
                            psA, psG = pA[f], pG[f]
                            ae = a_ext[0]; c_ = c1[f % 2]; g_ = ge[f % 2]
                            a3 = ae[:, 0:6 * NSEQ].rearrange("p (s j) -> p s j", j=6)
                            c3 = c_[:, 0:NSAMP].rearrange("p (s t) -> p s t", t=TS)
                            A([scT], [ae], out=a3[:, :, 0:2], in_=scT[:, f, :].rearrange("p (s j) -> p s j", j=2), func=AF.Copy)
                            A([psA], [ae], out=a3[:, :, 2:6], in_=psA[:, 0:NSAMP].rearrange("p (s t) -> p s t", t=TS), func=AF.Copy)
                            A([ae], [csT], out=csT[:, f, :].rearrange("p (s j) -> p s j", j=2), in_=a3[:, :, 4:6], func=AF.Copy)
                            V("tensor_scalar", [ae, cw, cb], [c_], out=c3, in0=a3[:, :, 0:4], scalar1=cw[:, f, 0:1], scalar2=cb[:, f:f + 1], op0=ALU.mult, op1=ALU.add)
                            V("scalar_tensor_tensor", [ae, cw, c_], [c_], out=c3, in0=a3[:, :, 1:5], scalar=cw[:, f, 1:2], in1=c3, op0=ALU.mult, op1=ALU.add)
                            V("scalar_tensor_tensor", [ae, cw, c_], [c_], out=c3, in0=a3[:, :, 2:6], scalar=cw[:, f, 2:3], in1=c3, op0=ALU.mult, op1=ALU.add)
                            A([c_], [g_], out=g_[:, 0:nt], in_=c_[:, 0:nt], func=AF.Gelu_apprx_tanh)
                            V("tensor_tensor", [g_, psG], [hT], out=hT[:, f, 0:nt], in0=g_[:, 0:nt], in1=psG[:, 0:nt], op=ALU.mult)
                for i in range(ntile):
                    tsl = slice(t0_ + 128 * i, t0_ + 128 * i + tp)
                    xt_ = x_tok[i % 2]
                    sp.dma(xt_[0:tp, :], self.x1d[tsl, :], [], [xt_])
                    pp = [self.next_ps(), self.next_ps()]
                    for h in range(2):
                        for f in range(NF):
                            self.mm(pp[h], pp[h][0:tp, :], hT[:, f, 128 * i:128 * i + tp], w_dn[:, f, 512 * h:512 * h + 512], [hT, w_dn], start=(f == 0), stop=(f == NF - 1))
                    self.layer_norm(pp, xt_, rr, xh, g_bc, b_bc, st6, mv, sd, tp)
                    self.out_events.append(sp.dma(O["y"][tsl, :], xh[0][0:tp, :], [xh[0], xh[1]], []))
            for c in range(6):
                w_ = min(512, DFF - 512 * c)
                nq = w_ // 128
                ps = self.next_ps(); ps2 = self.next_ps()
                for q in range(nq):
                    f = 4 * c + q
                    self.tr(ps, ps[0:2, 128 * q:128 * q + 128], a_carry[:, f, :], self.identf[:], [a_carry, self.identf], signal=(q == nq - 1))
                for q in range(nq):
                    f = 4 * c + q
                    self.tr(ps2, ps2[0:32, 128 * q:128 * q + 128], csT[:, f, :], self.identf[:], [csT, self.identf], signal=(q == nq - 1))
                s0, s1 = stg[0], stg[1]
                A([ps], [s0], out=s0[0:2, 0:w_], in_=ps[0:2, 0:w_], func=AF.Copy)
                A([ps2], [s1], out=s1[0:32, 0:w_], in_=ps2[0:32, 0:w_], func=AF.Copy)
                self.out_events.append(sp.dma(O["cp"][:, 512 * c:512 * c + w_], s0[0:2, 0:w_], [s0], []))
                self.out_events.append(sp.dma(O["cs"][:, 512 * c:512 * c + w_], s1[0:32, 0:w_], [s1], []))


def _host_inputs(inp):
    f = lambda a: np.ascontiguousarray(np.asarray(a, dtype=np.float32))
    w_in = f(inp["w_in"][0])
    qcols = np.concatenate([np.r_[512 + 64 * j:512 + 64 * j + 64, 512 + 64 * (4 + j):512 + 64 * (4 + j) + 64] for j in range(4)])
    perm = np.r_[0:512, qcols, 1024:DIN]
    w_in = np.ascontiguousarray(w_in[:, perm])
    w_attn = f(inp["w_attn_br"][0])
    rows = np.concatenate([np.r_[64 * j:64 * j + 64, 64 * (4 + j):64 * (4 + j) + 64] for j in range(4)])
    w_attn = np.ascontiguousarray(w_attn[rows, :])
    shared = {
        "w_in": w_in, "lam_re": f(inp["ssm_lam_re"][0]), "lam_im": f(inp["ssm_lam_im"][0]),
        "log_dt": f(inp["ssm_log_dt"][0]).reshape(1, G), "b_re": f(inp["ssm_b_re"][0]), "b_im": f(inp["ssm_b_im"][0]),
        "c_re": f(inp["ssm_c_re"][0]).reshape(G * CH, P), "c_im": f(inp["ssm_c_im"][0]).reshape(G * CH, P),
        "d": f(inp["ssm_d"][0]).reshape(G, CH), "w_glu": f(inp["w_glu"][0]), "sinks": f(inp["attn_sinks"][0]).reshape(1, 8),
        "w_attn": w_attn, "w_o": f(inp["w_o"][0]), "ln1_g": f(inp["ln1_g"][0]).reshape(1, D), "ln1_b": f(inp["ln1_b"][0]).reshape(1, D),
        "w_up": f(inp["w_up"][0]), "conv_w": f(inp["conv_w"][0]), "conv_b": f(inp["conv_b"][0]).reshape(1, DFF),
        "w_down": f(inp["w_down"][0]), "ln2_g": f(inp["ln2_g"][0]).reshape(1, D), "ln2_b": f(inp["ln2_b"][0]).reshape(1, D),
    }
    maps = []
    for c in range(8):
        s = slice(NSEQ * c, NSEQ * c + NSEQ)
        m = dict(shared)
        m["x"] = np.ascontiguousarray(np.concatenate([f(inp["x_prompt"][c]), f(inp["x_sample"][s]).reshape(NSAMP, D)], 0))
        m["ck"] = f(inp["cache_k_win"][0, s]).reshape(NSEQ, 128, 128)
        m["cv"] = f(inp["cache_v_win"][0, s]).reshape(NSEQ, 128, 128)
        m["sre"] = f(inp["state_ssm_re"][0, s]).reshape(NSEQ * G, P)
        m["sim"] = f(inp["state_ssm_im"][0, s]).reshape(NSEQ * G, P)
        m["sconv"] = f(inp["state_ffn_conv"][0, s]).reshape(NSEQ * 2, DFF)
        maps.append(m)
    return maps


_NC_CACHE = {}


def _run(inp):
    if "nc" not in _NC_CACHE:
        _NC_CACHE["nc"] = KB().build()
    nc = _NC_CACHE["nc"]
    maps = _host_inputs(inp)
    res = run_bass_kernel_spmd(nc, maps, core_ids=list(range(8)))
    return res.results


def kernel(**inp):
    rs = _run(inp)
    cat = lambda k: np.stack([np.asarray(r[k]) for r in rs], 0)
    y = cat("y")
    yp = y[:, :L, :]
    ys = y[:, L:, :].reshape(8 * NSEQ, TS, D)
    kwp = cat("kwp").reshape(1, 8, 128, 2, 64)
    vwp = cat("vwp").reshape(1, 8, 128, 2, 64)
    kws = cat("kws").reshape(1, 8 * NSEQ, 128, 2, 64)
    vws = cat("vws").reshape(1, 8 * NSEQ, 128, 2, 64)
    hrp = cat("hrp").reshape(1, 8, G, P)
    hip = cat("hip").reshape(1, 8, G, P)
    hrs = cat("hrs").reshape(1, 8 * NSEQ, G, P)
    his = cat("his").reshape(1, 8 * NSEQ, G, P)
    cp = cat("cp").reshape(1, 8, 2, DFF)
    cs = cat("cs").reshape(1, 8 * NSEQ, 2, DFF)
    return (np.ascontiguousarray(yp), np.ascontiguousarray(ys), kwp, vwp, kws, vws, hrp, hip, hrs, his, cp, cs)
```

```python
import math
from contextlib import ExitStack

import numpy as np
import concourse.bass as bass
import concourse.mybir as mybir
from concourse.bass_utils import run_bass_kernel_spmd

F32 = mybir.dt.float32
BF16 = mybir.dt.bfloat16
I32 = mybir.dt.int32
AF = mybir.ActivationFunctionType
ALU = mybir.AluOpType

D = 1024
L = 2048
NSEQ = 16
TS = 4
NSAMP = NSEQ * TS
NTOK = L + NSAMP
G = 32
P = 64
CH = 16
T = 8
DFF = 2816
NF = DFF // 128
DIN = 3328
ALPHA = 2.0 ** 0.25
LN_EPS = 1e-5
TWO_PI = 2.0 * math.pi
C1 = 6.28125
C2 = TWO_PI - C1
MAGIC = 12582912.0
NEG = -30000.0
BLOCKS = [(0, 512), (512, 512), (1024, 512), (1536, 512), (2048, 64)]
DEBUG = {}
TAPS = set()
SKIP = set()


def psplit(ap, c):
    (pstep, npart), (estep, n) = ap.ap
    return bass.AP(tensor=ap.tensor, offset=ap.offset, ap=[[pstep, c], [pstep * c, npart // c], [estep, n]])


class Buf:
    __slots__ = ("t", "wr", "rd", "name", "psum")

    def __init__(self, t, name):
        self.t = t
        self.wr = []
        self.rd = {}
        self.name = name
        self.psum = False

    def __getitem__(self, k):
        return self.t[k]


class Eng:
    def __init__(self, name, eng, sem, dma_sems=None):
        self.name = name
        self.eng = eng
        self.sem = sem
        self.count = 0
        self.waited = {}
        self.pool = [[s, 0] for s in (dma_sems or [])]
        self.pidx = 0

    def wait_ev(self, ev, force=False):
        if ev is None:
            return
        sem, val = ev
        if sem is self.sem and not force and self.name == "pe":
            return
        k = id(sem)
        if self.waited.get(k, 0) >= val:
            return
        self.eng.wait_ge(sem, val)
        self.waited[k] = val

    def deps(self, reads, writes, par=False):
        for b in reads:
            for ev in b.wr:
                self.wait_ev(ev)
            if b.psum:
                for ev in b.rd.values():
                    self.wait_ev(ev)
        for b in writes:
            if not par:
                for ev in b.wr:
                    self.wait_ev(ev)
            for ev in b.rd.values():
                self.wait_ev(ev)

    def record(self, ev, reads, writes, par=False):
        k = id(ev[0])
        for b in reads:
            b.rd[k] = ev
        for b in writes:
            if par:
                b.wr = [e for e in b.wr if id(e[0]) != k] + [ev]
            else:
                b.wr = [ev]
            b.rd = {}

    def op(self, method, reads, writes, signal=True, **kw):
        self.deps(reads, writes)
        ins = getattr(self.eng, method)(**kw)
        if signal:
            self.count += 1
            ins.then_inc(self.sem, 1)
            ev = (self.sem, self.count)
        else:
            ev = (self.sem, self.count + 1)
        self.record(ev, reads, writes)
        return ev

    def dma(self, out, in_, reads, writes, par=False, **kw):
        self.deps(reads, writes, par)
        slot = self.pool[self.pidx % len(self.pool)]
        self.pidx += 1
        sem, cnt = slot
        if cnt > 0:
            self.wait_ev((sem, cnt))
        self.eng.dma_start(out=out, in_=in_, **kw).then_inc(sem, 16)
        slot[1] = cnt + 16
        ev = (sem, cnt + 16)
        self.record(ev, reads, writes, par)
        return ev


class KB:
    def __init__(self):
        self.nc = bass.Bass("TRN2", target_bir_lowering=False)
        self.out_events = []

    def dram_in(self, name, shape, dt=F32):
        return self.nc.dram_tensor(name, list(shape), dt, kind="ExternalInput").ap()

    def dram_out(self, name, shape, dt=F32):
        return self.nc.dram_tensor(name, list(shape), dt, kind="ExternalOutput").ap()

    def sb(self, es, name, shape, dt=F32):
        return Buf(es.enter_context(self.nc.sbuf_tensor("sb_" + name, list(shape), dt)), name)

    def barrier(self):
        engs = [self.pe, self.act, self.dve, self.pool, self.sp]
        evs = [(e.sem, e.count) for e in engs if e.sem is not None and e.count > 0]
        for e in engs:
            for s, c in e.pool:
                if c > 0:
                    evs.append((s, c))
        for e in engs:
            for ev in evs:
                e.wait_ev(ev, force=True)

    def tap(self, name, buf, ap=None):
        if name not in TAPS:
            return
        ap = buf[:] if ap is None else ap
        shp = list(ap.shape)
        o = self.dram_out("tap_" + name, shp, ap.dtype)
        self.out_events.append(self.sp.dma(o, ap, [buf], []))

    def next_ps(self):
        b = self.psb[self.psi % 8]
        self.psi += 1
        return b

    def V(self, method, reads, writes, **kw):
        return self.dve.op(method, reads, writes, **kw)

    def A(self, reads, writes, **kw):
        return self.act.op("activation", reads, writes, **kw)

    def Pl(self, method, reads, writes, **kw):
        return self.pool.op(method, reads, writes, **kw)

    def mm(self, ps, out, lhsT, rhs, reads, start=True, stop=True, signal=None):
        if signal is None:
            signal = stop
        return self.pe.op("matmul", reads, [ps], signal=signal, out=out, lhsT=lhsT, rhs=rhs,
                          start=start, stop=stop)

    def tr(self, ps, out, in_, ident, reads, signal=True):
        return self.pe.op("transpose", reads, [ps], signal=signal, out=out, in_=in_, identity=ident)

    def sin_rr(self, out_b, out_ap, in_b, in_ap, shift, tmp_bs, tmp_aps):
        (ty, tn, tr_) = tmp_aps
        (by, bn, br) = tmp_bs
        V = self.V
        if shift != 0.0:
            V("tensor_scalar", [in_b], [by], out=ty, in0=in_ap, scalar1=float(shift), scalar2=None, op0=ALU.add)
            yb, y = by, ty
        else:
            yb, y = in_b, in_ap
        V("tensor_scalar", [yb], [bn], out=tn, in0=y, scalar1=1.0 / TWO_PI, scalar2=MAGIC, op0=ALU.mult, op1=ALU.add)
        V("tensor_scalar", [bn], [bn], out=tn, in0=tn, scalar1=MAGIC, scalar2=None, op0=ALU.subtract)
        V("scalar_tensor_tensor", [bn, yb], [br], out=tr_, in0=tn, scalar=-C1, in1=y, op0=ALU.mult, op1=ALU.add)
        V("scalar_tensor_tensor", [bn, br], [br], out=tr_, in0=tn, scalar=-C2, in1=tr_, op0=ALU.mult, op1=ALU.add)
        V("tensor_scalar", [br], [br], out=tr_, in0=tr_, scalar1=-math.pi, scalar2=math.pi, op0=ALU.max, op1=ALU.min)
        self.A([br], [out_b], out=out_ap, in_=tr_, func=AF.Sin)

    def build(self):
        nc = self.nc
        self.ins = {}
        I = self.ins
        I["x"] = self.dram_in("x", [NTOK, D])
        I["ck"] = self.dram_in("ck", [NSEQ, 128, 128])
        I["cv"] = self.dram_in("cv", [NSEQ, 128, 128])
        I["sre"] = self.dram_in("sre", [NSEQ * G, P])
        I["sim"] = self.dram_in("sim", [NSEQ * G, P])
        I["sconv"] = self.dram_in("sconv", [NSEQ * 2, DFF])
        I["w_in"] = self.dram_in("w_in", [D, DIN])
        I["lam_re"] = self.dram_in("lam_re", [G, P])
        I["lam_im"] = self.dram_in("lam_im", [G, P])
        I["log_dt"] = self.dram_in("log_dt", [1, G])
        I["b_re"] = self.dram_in("b_re", [G, P, CH])
        I["b_im"] = self.dram_in("b_im", [G, P, CH])
        I["c_re"] = self.dram_in("c_re", [G * CH, P])
        I["c_im"] = self.dram_in("c_im", [G * CH, P])
        I["d"] = self.dram_in("d", [G, CH])
        I["w_glu"] = self.dram_in("w_glu", [512, 2048])
        I["sinks"] = self.dram_in("sinks", [1, 8])
        I["w_attn"] = self.dram_in("w_attn", [512, D])
        I["w_o"] = self.dram_in("w_o", [D, D])
        I["ln1_g"] = self.dram_in("ln1_g", [1, D])
        I["ln1_b"] = self.dram_in("ln1_b", [1, D])
        I["w_up"] = self.dram_in("w_up", [D, 2 * DFF])
        I["conv_w"] = self.dram_in("conv_w", [3, DFF])
        I["conv_b"] = self.dram_in("conv_b", [1, DFF])
        I["w_down"] = self.dram_in("w_down", [DFF, D])
        I["ln2_g"] = self.dram_in("ln2_g", [1, D])
        I["ln2_b"] = self.dram_in("ln2_b", [1, D])
        self.outs = {}
        O = self.outs
        O["y"] = self.dram_out("y", [NTOK, D])
        O["kwp"] = self.dram_out("kwp", [128, 128])
        O["vwp"] = self.dram_out("vwp", [128, 128])
        O["kws"] = self.dram_out("kws", [NSEQ, 128, 128])
        O["vws"] = self.dram_out("vws", [NSEQ, 128, 128])
        O["hrp"] = self.dram_out("hrp", [G, P])
        O["hip"] = self.dram_out("hip", [G, P])
        O["hrs"] = self.dram_out("hrs", [NSEQ * G, P])
        O["his"] = self.dram_out("his", [NSEQ * G, P])
        O["cp"] = self.dram_out("cp", [2, DFF])
        O["cs"] = self.dram_out("cs", [NSEQ * 2, DFF])
        self.x1d = nc.dram_tensor("x1d", [NTOK, D], F32, kind="Internal").ap()
        for k, (shp, dt_) in DEBUG.items():
            O[k] = self.dram_out(k, shp, dt_)

        with ExitStack() as es0:
            E = es0.enter_context
            sems = [E(nc.semaphore("s%d" % i)) for i in range(4)]
            dsem_sp = [E(nc.semaphore("dsp%d" % i)) for i in range(40)]
            dsem_pl = [E(nc.semaphore("dpl%d" % i)) for i in range(40)]
            self.psb = [Buf(E(nc.psum_tensor("ps%d" % i, [128, 512], F32)), "ps%d" % i) for i in range(8)]
            self.psi = 0
            for b_ in self.psb:
                b_.psum = True
            block = E(nc.Block())
            self.pe = Eng("pe", nc.tensor, sems[0])
            self.act = Eng("act", nc.scalar, sems[1])
            self.dve = Eng("dve", nc.vector, sems[2])
            self.pool = Eng("pool", nc.gpsimd, sems[3], dsem_pl)
            self.sp = Eng("sp", nc.sync, None, dsem_sp)
            self.ident = self.sb(es0, "ident", [128, 128], BF16)
            self.identf = self.sb(es0, "identf", [128, 128], F32)
            self.mhalf = self.sb(es0, "mhalf", [128, 1], F32)
            self.consts()
            with ExitStack() as esSA:
                self.pass_s()
                self.barrier()
                if "A" not in SKIP:
                    self.pass_a()
            self.barrier()
            if "dbg_x1" in DEBUG:
                self.out_events.append(self.sp.dma(O["dbg_x1"], self.x1d, [], []))
            if "B" not in SKIP:
                self.pass_b()
            for ev in self.out_events:
                self.sp.wait_ev(ev)
            self.barrier()
        return nc

    def consts(self):
        Pl = self.Pl
        Pl("memset", [], [self.identf], ap=self.identf[:], constant=0.0)
        Pl("memset", [], [self.mhalf], ap=self.mhalf[:], constant=-0.5)
        self.barrier()
        Pl("affine_select", [self.identf], [self.identf], out=self.identf[:], in_=self.identf[:], pattern=[[-1, 128]],
           compare_op=ALU.not_equal, fill=1.0, base=0, channel_multiplier=1)
        Pl("tensor_copy", [self.identf], [self.ident], out=self.ident[:], in_=self.identf[:])

    def pass_s(self):
        nc = self.nc
        I, O = self.ins, self.outs
        V, A, Pl = self.V, self.A, self.Pl
        sp, pool = self.sp, self.pool
        with ExitStack() as es:
            sb = lambda n, s, d=F32: self.sb(es, n, s, d)
            Toep = sb("Toep", [128, G, 128], BF16)
            Wend = sb("Wend", [128, G, 128], BF16)
            Wend_sw = sb("Wend_sw", [128, G, 128], BF16)
            Cpow = sb("Cpow", [128, G, 128], BF16)
            Cpow_sw = sb("Cpow_sw", [128, G, 128], BF16)
            CT = sb("CT", [128, G, 64])
            ST = sb("ST", [128, G, 64])
            RHOT = sb("RHOT", [128, G, 64])
            RHO1 = sb("RHO1", [128, G])
            AR4 = sb("AR4", [128, G]); AI4 = sb("AI4", [128, G]); ARm4 = sb("ARm4", [128, G]); AIm4 = sb("AIm4", [128, G])
            Drep = sb("Drep", [128, G])
            w_u = sb("w_u", [128, 8, 512], BF16)
            SGN = sb("SGN", [128, 1]); NSGN = sb("NSGN", [128, 1])
            Hin = sb("Hin", [128, G])
            self._w_u = w_u
            NC_ = L // T
            U = sb("s_U", [128, G, NC_], BF16)
            Us = sb("s_Us", [128, G, NSEQ], BF16)
            Pl("memset", [], [Us], ap=Us[:], constant=0.0)
            self._U, self._Us = U, Us
            with ExitStack() as esp:
                gens = [self.ssm_prep(esp, Toep, Wend, Wend_sw, Cpow, Cpow_sw, CT, ST, RHOT, RHO1, AR4, AI4, ARm4, AIm4,
                                      Drep, SGN, NSGN, Hin), self.ssm_stage1(esp, w_u)]
                first = True
                while gens:
                    for g_ in list(gens):
                        try:
                            next(g_)
                        except StopIteration:
                            gens.remove(g_)
                        if first:
                            first = False
                ev1 = sp.dma(self.scrU.rearrange("j p n -> p j n"), self._uT[:], [self._uT], [])
                ev2 = sp.dma(self.scrUs.rearrange("j p n -> p j n"), self._uTs[:], [self._uTs], [])
                sp.wait_ev(ev1); sp.wait_ev(ev2)
                for g in range(G):
                    j, gl = divmod(g, 8)
                    src = bass.AP(tensor=self.scrU.tensor, offset=(j * 128 + gl * 16) * L, ap=[[NC_, T], [L, CH], [1, NC_]])
                    sp.dma(U[:, g, :], src, [], [U], par=True)
                    srcs = bass.AP(tensor=self.scrUs.tensor, offset=(j * 128 + gl * 16) * NSAMP, ap=[[NSEQ, TS], [NSAMP, CH], [1, NSEQ]])
                    sp.dma(Us[64:128, g, :], srcs, [], [Us], par=True)
                for gh in range(self._prep_nparts):
                    self._prep_part2(gh)
                self.ssm_taps(Toep, Wend, Wend_sw, Cpow, Cpow_sw)
                self.barrier()
            self.ssm_main(es, Toep, Wend, Wend_sw, Cpow, Cpow_sw, CT, ST, RHOT, RHO1, AR4, AI4, ARm4, AIm4,
                          Drep, w_u, Hin)

    def ssm_prep(self, es, Toep, Wend, Wend_sw, Cpow, Cpow_sw, CT, ST, RHOT, RHO1, AR4, AI4, ARm4, AIm4, Drep,
                 SGN, NSGN, Hin):
        I = self.ins
        V, A, Pl = self.V, self.A, self.Pl
        sp = self.sp
        sb = lambda n, s, d=F32: self.sb(es, n, s, d)
        lamre2 = sb("lamre2", [128, G]); lamim2 = sb("lamim2", [128, G]); logdt2 = sb("logdt2", [128, G])
        BA = sb("BA", [128, G, CH]); BB = sb("BB", [128, G, CH])
        Ct_re = sb("Ct_re", [128, 4, 128]); Ct_im = sb("Ct_im", [128, 4, 128])
        CRE2 = sb("CRE2", [128, G, CH]); CIM2 = sb("CIM2", [128, G, CH])
        lt_re = sb("lt_re", [G, 128]); lt_im = sb("lt_im", [G, 128]); lt_d = sb("lt_d", [G, 128])
        def _loads():
            w_u = self._w_u
            wv = I["w_in"].rearrange("(k p) n -> p k n", p=128)
            for k in range(8):
                self.pool.dma(w_u[:, k, :], wv[:, k, 0:512], [], [w_u], par=True)
            for h in range(2):
                sl = slice(64 * h, 64 * h + 64)
                sp.dma(lt_re[:, sl], I["lam_re"][:, :], [], [lt_re], par=True)
                sp.dma(lt_im[:, sl], I["lam_im"][:, :], [], [lt_im], par=True)
            for r in range(T):
                sp.dma(lt_d[:, 16 * r:16 * r + 16], I["d"][:, :], [], [lt_d], par=True)
            sp.dma(logdt2[:], I["log_dt"][0:1, :].partition_broadcast(128), [], [logdt2])
            sp.dma(BA[0:64, :, :], I["b_re"].rearrange("g p c -> p g c"), [], [BA], par=True)
            sp.dma(BA[64:128, :, :], I["b_im"].rearrange("g p c -> p g c"), [], [BA], par=True)
            sp.dma(BB[0:64, :, :], I["b_im"].rearrange("g p c -> p g c"), [], [BB], par=True)
            sp.dma(BB[64:128, :, :], I["b_re"].rearrange("g p c -> p g c"), [], [BB], par=True)
            for j in range(4):
                for h in range(2):
                    sp.dma(Ct_re[:, j, 64 * h:64 * h + 64], I["c_re"][128 * j:128 * j + 128, :], [], [Ct_re], par=True)
                    sp.dma(Ct_im[:, j, 64 * h:64 * h + 64], I["c_im"][128 * j:128 * j + 128, :], [], [Ct_im], par=True)
        self.issue_prep_loads = _loads
        Pl("memset", [], [SGN], ap=SGN[0:64, :], constant=1.0)
        Pl("memset", [], [SGN], ap=SGN[64:128, :], constant=-1.0)
        Pl("memset", [], [NSGN], ap=NSGN[0:64, :], constant=-1.0)
        Pl("memset", [], [NSGN], ap=NSGN[64:128, :], constant=1.0)
        Pl("memset", [], [Hin], ap=Hin[:], constant=0.0)
        JVi = sb("JVi", [128, 64], I32); JV = sb("JV", [128, 64])
        KVi = sb("KVi", [128, 32], I32); KV = sb("KV", [128, 32])
        Pl("iota", [], [JVi], out=JVi[:], pattern=[[1, 64]], base=1, channel_multiplier=0)
        Pl("iota", [], [KVi], out=KVi[:, 0:16], pattern=[[1, 16]], base=-7, channel_multiplier=0)
        Pl("iota", [], [KVi], out=KVi[:, 16:32], pattern=[[-1, 16]], base=8, channel_multiplier=0)
        ones = sb("ones", [128, 128]); tmask = sb("tmask", [128, 128])
        Pl("memset", [], [ones], ap=ones[:], constant=1.0)
        self.barrier()
        self.issue_prep_loads()
        for (lt_, dst_) in ((lt_re, lamre2), (lt_im, lamim2), (lt_d, Drep)):
            ps = self.next_ps()
            self.tr(ps, ps[:, 0:G], lt_[0:G, :], self.identf[0:G, 0:G], [lt_, self.identf])
            A([ps], [dst_], out=dst_[:], in_=ps[:, 0:G], func=AF.Copy)
        Pl("tensor_copy", [JVi], [JV], out=JV[:], in_=JVi[:])
        Pl("tensor_copy", [KVi], [KV], out=KV[:], in_=KVi[:])
        dt = sb("dt", [128, G]); ell = sb("ell", [128, G]); th = sb("th", [128, G])
        den = sb("den", [128, G]); t0 = sb("t0", [128, G]); wr = sb("wr", [128, G]); wi = sb("wi", [128, G])
        A([logdt2], [dt], out=dt[:], in_=logdt2[:], func=AF.Exp)
        yield
        V("tensor_tensor", [lamre2, dt], [ell], out=ell[:], in0=lamre2[:], in1=dt[:], op=ALU.mult)
        V("tensor_tensor", [lamim2, dt], [th], out=th[:], in0=lamim2[:], in1=dt[:], op=ALU.mult)
        V("tensor_tensor", [lamre2], [den], out=den[:], in0=lamre2[:], in1=lamre2[:], op=ALU.mult)
        V("tensor_tensor", [lamim2], [t0], out=t0[:], in0=lamim2[:], in1=lamim2[:], op=ALU.mult)
        V("tensor_tensor", [den, t0], [den], out=den[:], in0=den[:], in1=t0[:], op=ALU.add)
        V("reciprocal", [den], [den], out=den[:], in_=den[:])
        V("tensor_tensor", [lamre2, den], [wr], out=wr[:], in0=lamre2[:], in1=den[:], op=ALU.mult)
        V("scalar_tensor_tensor", [lamim2, den], [wi], out=wi[:], in0=lamim2[:], scalar=-1.0, in1=den[:], op0=ALU.mult, op1=ALU.mult)
        kvals = list(range(-7, 9)) + list(range(8, -8, -1))
        ELLK = sb("ELLK", [128, G, 32]); ANGK = sb("ANGK", [128, G, 32])
        V("tensor_tensor", [ell, KV], [ELLK], out=ELLK[:], in0=ell[:].unsqueeze(2).to_broadcast([128, G, 32]),
          in1=KV[:].unsqueeze(1).to_broadcast([128, G, 32]), op=ALU.mult)
        V("tensor_tensor", [th, KV], [ANGK], out=ANGK[:], in0=th[:].unsqueeze(2).to_broadcast([128, G, 32]),
          in1=KV[:].unsqueeze(1).to_broadcast([128, G, 32]), op=ALU.mult)
        MAGK = ELLK; RR = sb("RR", [128, G, 32]); II = sb("II", [128, G, 32])
        ty = sb("ty", [128, G * 32]); tn = sb("tn", [128, G * 32]); trr = sb("trr", [128, G * 32])
        f1 = lambda b, n=G * 32: b[:, 0:n]
        fl = lambda b: b[:].rearrange("p g k -> p (g k)")
        A([ELLK], [MAGK], out=fl(MAGK), in_=fl(ELLK), func=AF.Exp)
        yield
        self.sin_rr(II, fl(II), ANGK, fl(ANGK), 0.0, (ty, tn, trr), (f1(ty), f1(tn), f1(trr)))
        yield
        self.sin_rr(RR, fl(RR), ANGK, fl(ANGK), math.pi / 2, (ty, tn, trr), (f1(ty), f1(tn), f1(trr)))
        yield
        V("tensor_tensor", [RR, MAGK], [RR], out=fl(RR), in0=fl(RR), in1=fl(MAGK), op=ALU.mult)
        V("tensor_tensor", [II, MAGK], [II], out=fl(II), in0=fl(II), in1=fl(MAGK), op=ALU.mult)
        self.tap("RR", RR); self.tap("II", II); self.tap("th", th); self.tap("ell", ell); self.tap("CRE2", CRE2); self.tap("BA", BA)
        V("tensor_copy", [RR], [AR4], out=AR4[:], in_=RR[:, :, 11])
        V("tensor_scalar", [II, NSGN], [AI4], out=AI4[:], in0=II[:, :, 11], scalar1=NSGN[:, 0:1], scalar2=None, op0=ALU.mult)
        V("tensor_copy", [RR], [ARm4], out=ARm4[:], in_=RR[:, :, 3])
        V("tensor_scalar", [II, NSGN], [AIm4], out=AIm4[:], in0=II[:, :, 3], scalar1=NSGN[:, 0:1], scalar2=None, op0=ALU.mult)
        ph = sb("ph", [128, G]); ANGJ = sb("ANGJ", [128, 16, 64])
        V("tensor_scalar", [th], [ph], out=ph[:], in0=th[:], scalar1=8.0, scalar2=None, op0=ALU.mult)
        V("tensor_scalar", [ph], [t0], out=t0[:], in0=ph[:], scalar1=1.0 / TWO_PI, scalar2=MAGIC, op0=ALU.mult, op1=ALU.add)
        V("tensor_scalar", [t0], [t0], out=t0[:], in0=t0[:], scalar1=MAGIC, scalar2=None, op0=ALU.subtract)
        V("scalar_tensor_tensor", [t0, ph], [ph], out=ph[:], in0=t0[:], scalar=-C1, in1=ph[:], op0=ALU.mult, op1=ALU.add)
        V("scalar_tensor_tensor", [t0, ph], [ph], out=ph[:], in0=t0[:], scalar=-C2, in1=ph[:], op0=ALU.mult, op1=ALU.add)
        f2 = lambda a: a.rearrange("p g j -> p (g j)")
        for gh in range(2):
            hs_ = slice(16 * gh, 16 * gh + 16)
            V("tensor_tensor", [ph, JV], [ANGJ], out=ANGJ[:], in0=ph[:, hs_].unsqueeze(2).to_broadcast([128, 16, 64]),
              in1=JV[:].unsqueeze(1).to_broadcast([128, 16, 64]), op=ALU.mult)
            self.sin_rr(CT, f2(CT[:, hs_, :]), ANGJ, f2(ANGJ[:]), math.pi / 2, (ty, tn, trr), (ty[:], tn[:], trr[:]))
            yield
            self.sin_rr(ST, f2(ST[:, hs_, :]), ANGJ, f2(ANGJ[:]), 0.0, (ty, tn, trr), (ty[:], tn[:], trr[:]))
            yield
        V("tensor_scalar", [ST, SGN], [ST], out=f2(ST[:]), in0=f2(ST[:]), scalar1=SGN[:, 0:1], scalar2=None, op0=ALU.mult)
        A([ell], [RHO1], out=RHO1[:], in_=ell[:], func=AF.Exp, scale=8.0)
        yield
        V("tensor_copy", [RHO1], [RHOT], out=RHOT[:], in_=RHO1[:].unsqueeze(2).to_broadcast([128, G, 64]))
        V("tensor_scalar", [RHOT], [RHOT], out=RHOT[:, :, 0], in0=RHOT[:, :, 0], scalar1=0.0, scalar2=None, op0=ALU.mult)
        self.tap("CT", CT); self.tap("ST", ST); self.tap("RHOT", RHOT); self.tap("ph", ph); self.tap("JV", JV); self.tap("ANGJ", ANGJ)
        for (Ct, Cdst) in ((Ct_re, CRE2), (Ct_im, CIM2)):
            for j in range(4):
                ps = self.next_ps()
                self.tr(ps, ps[:, 0:128], Ct[:, j, :], self.identf[:], [Ct, self.identf])
                A([ps], [Cdst], out=Cdst[:, 8 * j:8 * j + 8, :], in_=ps[:, 0:128].rearrange("p (g c) -> p g c", c=CH), func=AF.Copy)
                yield
        NRR = sb("NRR", [128, G, 16]); NII = sb("NII", [128, G, 16])
        V("tensor_scalar", [RR], [NRR], out=NRR[:], in0=RR[:, :, 0:16], scalar1=-1.0, scalar2=None, op0=ALU.mult)
        V("tensor_scalar", [II], [NII], out=NII[:], in0=II[:, :, 0:16], scalar1=-1.0, scalar2=None, op0=ALU.mult)
        X = [sb("X%d" % i, [128, G, 16]) for i in range(4)]
        lo, hi = slice(0, 64), slice(64, 128)
        srcs = [(RR, NII), (NII, NRR), (NII, RR), (NRR, NII)]
        for i in range(4):
            bl, bh = srcs[i]
            V("tensor_copy", [bl], [X[i]], out=X[i][lo, :, :], in_=bl[lo, :, 0:16])
            V("tensor_copy", [bh], [X[i]], out=X[i][hi, :, :], in_=bh[hi, :, 0:16])
        DR = sb("DR", [128, G, 16]); DI = sb("DI", [128, G, 16])
        V("tensor_tensor", [RR], [DR], out=DR[:, :, 1:16], in0=RR[:, :, 16:31], in1=RR[:, :, 17:32], op=ALU.subtract)
        V("tensor_tensor", [II], [DI], out=DI[:, :, 1:16], in0=II[:, :, 16:31], in1=II[:, :, 17:32], op=ALU.subtract)
        QWR = sb("QWR", [128, G, 16]); QWI = sb("QWI", [128, G, 16]); q1 = sb("q1", [128, G, 16])
        wrb = wr[:].unsqueeze(2).to_broadcast([128, G, 15]); wib = wi[:].unsqueeze(2).to_broadcast([128, G, 15])
        s15 = (slice(None), slice(None), slice(1, 16))
        V("tensor_tensor", [DR, wr], [QWR], out=QWR[s15], in0=DR[s15], in1=wrb, op=ALU.mult)
        V("tensor_tensor", [DI, wi], [q1], out=q1[s15], in0=DI[s15], in1=wib, op=ALU.mult)
        V("tensor_tensor", [QWR, q1], [QWR], out=QWR[s15], in0=QWR[s15], in1=q1[s15], op=ALU.subtract)
        V("tensor_tensor", [DR, wi], [QWI], out=QWI[s15], in0=DR[s15], in1=wib, op=ALU.mult)
        V("tensor_tensor", [DI, wr], [q1], out=q1[s15], in0=DI[s15], in1=wrb, op=ALU.mult)
        V("tensor_tensor", [QWI, q1], [QWI], out=QWI[s15], in0=QWI[s15], in1=q1[s15], op=ALU.add)
        V("tensor_scalar", [QWI, NSGN], [QWI], out=QWI[s15], in0=QWI[s15], scalar1=NSGN[:, 0:1], scalar2=None, op0=ALU.mult)
        Pl("affine_select", [ones], [tmask], out=tmask[:].rearrange("p (r c) -> p r c", c=CH),
           in_=ones[:].rearrange("p (r c) -> p r c", c=CH), pattern=[[16, 8], [0, 16]], compare_op=ALU.is_ge, fill=0.0,
           base=15, channel_multiplier=-1)
        GH = 8
        class _View:
            def __init__(self, buf):
                self.buf = buf
        tA, tB, tC, tD = ty, tn, trr, ANGJ
        v4t = {id(ty): lambda: ty[:, 0:1024].rearrange("p (g r c) -> p g r c", g=GH, r=T),
               id(tn): lambda: tn[:, 0:1024].rearrange("p (g r c) -> p g r c", g=GH, r=T),
               id(trr): lambda: trr[:, 0:1024].rearrange("p (g r c) -> p g r c", g=GH, r=T),
               id(ANGJ): lambda: ANGJ[:].rearrange("p a (b c) -> p (a b) c", c=CH).rearrange("p (g r) c -> p g r c", r=T)}
        Cp0 = sb("Cp0", [128, GH, 128]); Bp0 = sb("Bp0", [128, GH, 128]); Bp7 = sb("Bp7", [128, GH, 128])
        v4 = lambda a: a.rearrange("p g (r c) -> p g r c", c=CH)
        def part2(gh):
            hs_ = slice(GH * gh, GH * gh + GH)

            def cmat(outb, out4, Xa, Xb, ki0, E=V, ta=tA, tb=tB):
                c_re_b = CRE2[:, hs_, :].unsqueeze(2).to_broadcast([128, GH, T, CH])
                c_im_b = CIM2[:, hs_, :].unsqueeze(2).to_broadcast([128, GH, T, CH])
                xa = Xa[:, hs_, ki0:ki0 + T].unsqueeze(3).to_broadcast([128, GH, T, CH])
                xb = Xb[:, hs_, ki0:ki0 + T].unsqueeze(3).to_broadcast([128, GH, T, CH])
                E("tensor_tensor", [CRE2, Xa], [ta], out=v4t[id(ta)](), in0=c_re_b, in1=xa, op=ALU.mult)
                E("tensor_tensor", [CIM2, Xb], [tb], out=v4t[id(tb)](), in0=c_im_b, in1=xb, op=ALU.mult)
                E("tensor_tensor", [ta, tb], [outb], out=out4, in0=v4t[id(ta)](), in1=v4t[id(tb)](), op=ALU.add)

            def bmat(outb, i0, E=V, ta=tA, tb=tB):
                ba = BA[:, hs_, :].unsqueeze(2).to_broadcast([128, GH, T, CH])
                bb = BB[:, hs_, :].unsqueeze(2).to_broadcast([128, GH, T, CH])
                qa = QWR[:, hs_, i0:i0 + T].unsqueeze(3).to_broadcast([128, GH, T, CH])
                qb = QWI[:, hs_, i0:i0 + T].unsqueeze(3).to_broadcast([128, GH, T, CH])
                E("tensor_tensor", [BA, QWR], [ta], out=v4t[id(ta)](), in0=ba, in1=qa, op=ALU.mult)
                E("tensor_tensor", [BB, QWI], [tb], out=v4t[id(tb)](), in0=bb, in1=qb, op=ALU.mult)
                E("tensor_tensor", [ta, tb], [outb], out=v4(outb[:]), in0=v4t[id(ta)](), in1=v4t[id(tb)](), op=ALU.add)

            bmat(Bp7, 1, Pl, tC, tD)
            cmat(Cp0, v4(Cp0[:]), X[0], X[1], 7)
            bmat(Bp0, 8)
            cmat(Cpow, v4(Cpow[:, hs_, :]), X[0], X[1], 8)
            cmat(Cpow_sw, v4(Cpow_sw[:, hs_, :]), X[2], X[3], 8)
            for gl in range(GH):
                g = GH * gh + gl
                ps = self.next_ps()
                self.mm(ps, ps[:, 0:128], Bp0[:, gl, :], Cp0[:, gl, :], [Bp0, Cp0])
                V("tensor_tensor", [ps, tmask], [Toep], out=Toep[:, g, :], in0=ps[:, 0:128], in1=tmask[:], op=ALU.mult)
                ps2 = self.next_ps()
                self.tr(ps2, ps2[:, 0:128], Bp7[:, gl, :], self.identf[:], [Bp7, self.identf])
                A([ps2], [Wend], out=Wend[:, g, :], in_=ps2[:, 0:128], func=AF.Copy)
                A([ps2], [Wend_sw], out=Wend_sw[:, g, 0:64], in_=ps2[:, 64:128], func=AF.Copy)
                A([ps2], [Wend_sw], out=Wend_sw[:, g, 64:128], in_=ps2[:, 0:64], func=AF.Copy)

        self._prep_part2 = part2
        self._prep_nparts = G // GH

    def ssm_taps(self, Toep, Wend, Wend_sw, Cpow, Cpow_sw):
        self.tap("Toep", Toep); self.tap("Wend", Wend); self.tap("Wend_sw", Wend_sw); self.tap("Cpow", Cpow); self.tap("Cpow_sw", Cpow_sw)

    def ssm_stage1(self, es, w_u):
        nc = self.nc
        I = self.ins
        A, pool, sp = self.A, self.pool, self.sp
        sb = lambda n, s, d=F32: self.sb(es, n, s, d)
        NC = L // T
        self.scrU = nc.dram_tensor("scrU", [4, 128, L], BF16, kind="Internal").ap()
        self.scrUs = nc.dram_tensor("scrUs", [4, 128, NSAMP], BF16, kind="Internal").ap()
        x_bf = sb("s_xbf", [128, 4, D], BF16)
        xT = sb("s_xT", [128, 8, 512], BF16)
        uT = sb("s_uT", [128, 4, L], BF16)
        uTs = sb("s_uTs", [128, 4, NSAMP], BF16)
        xall = I["x"]
        nblk = len(BLOCKS)
        self.load_tokens(xall, 0, x_bf)
        for bi in range(nblk):
            t0_, nt = BLOCKS[bi]
            samp = (bi == nblk - 1)
            self.transposes(x_bf, xT, nt)
            if bi + 1 < nblk:
                self.load_tokens(xall, bi + 1, x_bf)
            yield
            for j in range(4):
                ps = self.next_ps()
                for k in range(8):
                    self.mm(ps, ps[:, 0:nt], w_u[:, k, 128 * j:128 * j + 128], xT[:, k, 0:nt], [w_u, xT],
                            start=(k == 0), stop=(k == 7))
                if not samp:
                    A([ps], [uT], out=uT[:, j, :].rearrange("p (r c) -> p r c", c=NC)[:, :, 64 * bi:64 * bi + 64],
                      in_=ps[:, 0:512].rearrange("p (c r) -> p r c", r=T), func=AF.Copy)
                else:
                    A([ps], [uTs], out=uTs[:, j, :].rearrange("p (t s) -> p t s", s=NSEQ),
                      in_=ps[:, 0:64].rearrange("p (s t) -> p t s", t=TS), func=AF.Copy)
                yield
        self._uT, self._uTs = uT, uTs

    def ssm_main(self, es, Toep, Wend, Wend_sw, Cpow, Cpow_sw, CT, ST, RHOT, RHO1, AR4, AI4, ARm4, AIm4, Drep, w_u, Hin):
        nc = self.nc
        I, O = self.ins, self.outs
        V, A, Pl = self.V, self.A, self.Pl
        sp, pool = self.sp, self.pool
        sb = lambda n, s, d=F32: self.sb(es, n, s, d)
        NC = L // T
        scrG = nc.dram_tensor("scrG", [128, G, NC], BF16, kind="Internal").ap()
        scrGs = nc.dram_tensor("scrGs", [64, G, NSEQ], BF16, kind="Internal").ap()
        U, Us = self._U, self._Us
        Gall = sb("s_G", [128, G, NC], BF16)
        V1e = [sb("s_V1e%d" % i, [128, 8, 65], BF16) for i in range(2)]
        V2e = [sb("s_V2e%d" % i, [128, 8, 65], BF16) for i in range(2)]
        T1 = sb("s_T1", [128, 8, 64]); T2 = sb("s_T2", [128, 8, 64]); Z = sb("s_Z", [128, 8, 64])
        W1 = sb("s_W1", [128, 8]); W2 = sb("s_W2", [128, 8]); W3 = sb("s_W3", [128, 8])
        Hnew = sb("s_Hnew", [128, G])
        tmpY = sb("s_tmpY", [128, 8, 64])
        for i in range(2):
            Pl("memset", [], [V2e[i]], ap=V2e[i][:], constant=0.0)
        hs = sb("s_hs", [128, 4, 128]); hs_sw = sb("s_hs_sw", [128, 4, 128])
        H0 = sb("s_H0", [128, NSEQ, G]); H0sw = sb("s_H0sw", [128, NSEQ, G])
        self.barrier()
        for j in range(4):
            sp.dma(hs[:, j, 0:64], I["sre"][128 * j:128 * j + 128, :], [], [hs], par=True)
            sp.dma(hs[:, j, 64:128], I["sim"][128 * j:128 * j + 128, :], [], [hs], par=True)
            sp.dma(hs_sw[:, j, 0:64], I["sim"][128 * j:128 * j + 128, :], [], [hs_sw], par=True)
            sp.dma(hs_sw[:, j, 64:128], I["sre"][128 * j:128 * j + 128, :], [], [hs_sw], par=True)

        nblk = len(BLOCKS)
        scrU, scrUs = self.scrU, self.scrUs
        self.tap("U0", U)
        s3 = lambda b: b[:, 0:512].rearrange("p (g c) -> p g c", c=64)
        slots = [dict(T1=T1, T2=T2, Z=Z, W1=W1, W2=W2, W3=W3, tmpY=tmpY, V1=V1e[0], V2=V2e[0]),
                 dict(T1=sb("s_T1b", [128, 8, 64]), T2=sb("s_T2b", [128, 8, 64]), Z=sb("s_Zb", [128, 8, 64]),
                      W1=sb("s_W1b", [128, 8]), W2=sb("s_W2b", [128, 8]), W3=sb("s_W3b", [128, 8]),
                      tmpY=sb("s_tmpYb", [128, 8, 64]), V1=V1e[1], V2=V2e[1])]

        def lvl_b(bi, gs, B):
            T1, T2, Z, W1, W2, W3, tmpY, V1, V2 = (B[k] for k in ("T1", "T2", "Z", "W1", "W2", "W3", "tmpY", "V1", "V2"))
            csl = slice(64 * bi, 64 * bi + 64)
            g0 = 8 * gs
            gsl = slice(g0, g0 + 8)
            psS = self.next_ps(); psW = self.next_ps()
            for gl in range(8):
                g = g0 + gl
                self.mm(psS, psS[:, 64 * gl:64 * gl + 64], Wend[:, g, :], U[:, g, csl], [Wend, U], signal=(gl == 7))
            for gl in range(8):
                g = g0 + gl
                self.mm(psW, psW[:, 64 * gl:64 * gl + 64], Wend_sw[:, g, :], U[:, g, csl], [Wend_sw, U], signal=(gl == 7))
            yield
            V("tensor_tensor", [psS, CT], [T1], out=T1[:], in0=s3(psS), in1=CT[:, gsl, :], op=ALU.mult)
            V("tensor_tensor", [psW, ST], [T2], out=T2[:], in0=s3(psW), in1=ST[:, gsl, :], op=ALU.mult)
            V("tensor_tensor", [RHO1, Hin], [W1], out=W1[:], in0=RHO1[:, gsl], in1=Hin[:, gsl], op=ALU.mult)
            yield
            Pl("tensor_tensor", [T1, T2], [T1], out=T1[:], in0=T1[:], in1=T2[:], op=ALU.add)
            yield
            V("tensor_tensor", [T1, W1], [T1], out=T1[:, :, 0], in0=T1[:, :, 0], in1=W1[:], op=ALU.add)
            V("tensor_tensor_scan", [RHOT, T1], [Z], out=Z[:].rearrange("p g c -> p (g c)"),
              data0=RHOT[:, gsl, :].rearrange("p g c -> p (g c)"), data1=T1[:].rearrange("p g c -> p (g c)"),
              initial=0.0, op0=ALU.mult, op1=ALU.add)
            yield
            V("tensor_tensor", [Z, CT], [V1], out=V1[:, :, 1:65], in0=Z[:], in1=CT[:, gsl, :], op=ALU.mult)
            Pl("tensor_tensor", [Z, ST], [V2], out=V2[:, :, 1:65], in0=Z[:], in1=ST[:, gsl, :], op=ALU.mult)
            V("tensor_copy", [Hin], [V1], out=V1[:, :, 0], in_=Hin[:, gsl])
            Pl("tensor_tensor", [U, Drep], [tmpY], out=tmpY[:], in0=U[:, gsl, csl],
               in1=Drep[:, gsl].unsqueeze(2).to_broadcast([128, 8, 64]), op=ALU.mult)
            yield
            V("tensor_tensor", [Z, CT], [W1], out=W1[:], in0=Z[:, :, 63], in1=CT[:, gsl, 63], op=ALU.mult)
            V("tensor_tensor", [Z, ST], [W2], out=W2[:], in0=Z[:, :, 63], in1=ST[:, gsl, 63], op=ALU.mult)
            yield
            V("tensor_copy", [W2], [W3], out=W3[0:64, :], in_=W2[64:128, :])
            V("tensor_copy", [W2], [W3], out=W3[64:128, :], in_=W2[0:64, :])
            yield
            V("tensor_tensor", [W1, W3], [Hnew], out=Hnew[:, gsl], in0=W1[:], in1=W3[:], op=ALU.add)
            psY = self.next_ps()
            for gl in range(8):
                g = g0 + gl
                o_ = psY[:, 64 * gl:64 * gl + 64]
                self.mm(psY, o_, Toep[:, g, :], U[:, g, csl], [Toep, U], start=True, stop=False)
                self.mm(psY, o_, Cpow[:, g, :], V1[:, gl, 0:64], [Cpow, V1], start=False, stop=False)
                self.mm(psY, o_, Cpow_sw[:, g, :], V2[:, gl, 0:64], [Cpow_sw, V2], start=False, stop=True, signal=(gl == 7))
            yield
            V("tensor_tensor", [tmpY, psY], [tmpY], out=tmpY[:], in0=tmpY[:], in1=s3(psY), op=ALU.add)
            yield
            A([tmpY], [Gall], out=Gall[:, gsl, csl], in_=tmpY[:], func=AF.Gelu_apprx_tanh)

        for bi in range(nblk - 1):
            for pair in ((0, 1), (2, 3)):
                gens = [lvl_b(bi, pair[0], slots[0]), lvl_b(bi, pair[1], slots[1])]
                while gens:
                    for g_ in list(gens):
                        try:
                            next(g_)
                        except StopIteration:
                            gens.remove(g_)
            V("tensor_copy", [Hnew], [Hin], out=Hin[:], in_=Hnew[:])
        ps = self.next_ps()
        self.tr(ps, ps[0:G, 0:128], Hin[:, :], self.identf[:], [Hin, self.identf])
        hp = sb("s_hp", [G, 128])
        A([ps], [hp], out=hp[:], in_=ps[0:G, 0:128], func=AF.Copy)
        self.out_events.append(sp.dma(O["hrp"][:, :], hp[:, 0:64], [hp], []))
        self.out_events.append(sp.dma(O["hip"][:, :], hp[:, 64:128], [hp], []))
        sp.dma(scrG, Gall[:], [Gall], [])
        for (src_, dst) in ((hs, H0), (hs_sw, H0sw)):
            for j in range(4):
                ps = self.next_ps()
                self.tr(ps, ps[:, 0:128], src_[:, j, :], self.identf[:], [src_, self.identf])
                A([ps], [dst], out=dst[:, 4 * j:4 * j + 4, :], in_=ps[:, 0:128].rearrange("p (s g) -> p s g", g=G), func=AF.Copy)
        Hm = sb("s_Hm", [128, NSEQ, G]); Hp = sb("s_Hp", [128, NSEQ, G]); tq = sb("s_tq", [128, NSEQ, G])
        Hm_bf = sb("s_Hmbf", [128, G, NSEQ], BF16)
        bc = lambda b: b[:].unsqueeze(1).to_broadcast([128, NSEQ, G])
        V("tensor_tensor", [H0, ARm4], [Hm], out=Hm[:], in0=H0[:], in1=bc(ARm4), op=ALU.mult)
        V("tensor_tensor", [H0sw, AIm4], [tq], out=tq[:], in0=H0sw[:], in1=bc(AIm4), op=ALU.mult)
        V("tensor_tensor", [Hm, tq], [Hm_bf], out=Hm_bf[:].rearrange("p g s -> p s g"), in0=Hm[:], in1=tq[:], op=ALU.add)
        V("tensor_tensor", [H0, AR4], [Hp], out=Hp[:], in0=H0[:], in1=bc(AR4), op=ALU.mult)
        V("tensor_tensor", [H0sw, AI4], [tq], out=tq[:], in0=H0sw[:], in1=bc(AI4), op=ALU.mult)
        V("tensor_tensor", [Hp, tq], [Hp], out=Hp[:], in0=Hp[:], in1=tq[:], op=ALU.add)
        Hout = sb("s_Hout", [128, NSEQ, G])
        Gs = sb("s_Gs", [128, G, NSEQ], BF16)
        tmps = sb("s_tmps", [128, G, NSEQ])
        psS = self.next_ps(); psY = self.next_ps()
        for g in range(G):
            self.mm(psS, psS[:, NSEQ * g:NSEQ * g + NSEQ], Wend[:, g, :], Us[:, g, :], [Wend, Us], signal=(g == G - 1))
        V("tensor_tensor", [psS, Hp], [Hout], out=Hout[:], in0=psS[:, 0:512].rearrange("p (g s) -> p s g", s=NSEQ),
          in1=Hp[:], op=ALU.add)
        for g in range(G):
            o_ = psY[:, NSEQ * g:NSEQ * g + NSEQ]
            self.mm(psY, o_, Toep[:, g, :], Us[:, g, :], [Toep, Us], start=True, stop=False)
            self.mm(psY, o_, Cpow[:, g, :], Hm_bf[:, g, :], [Cpow, Hm_bf], start=False, stop=True, signal=(g == G - 1))
        V("tensor_tensor", [Us, Drep], [tmps], out=tmps[:], in0=Us[:], in1=Drep[:].unsqueeze(2).to_broadcast([128, G, NSEQ]), op=ALU.mult)
        V("tensor_tensor", [tmps, psY], [tmps], out=tmps[:], in0=tmps[:], in1=psY[:, 0:512].rearrange("p (g s) -> p g s", s=NSEQ), op=ALU.add)
        A([tmps], [Gs], out=Gs[:], in_=tmps[:], func=AF.Gelu_apprx_tanh)
        sp.dma(scrGs, Gs[64:128, :, :], [Gs], [])
        ho = sb("s_ho", [128, 4, 128])
        for j in range(4):
            ps = self.next_ps()
            self.tr(ps, ps[:, 0:128], Hout[:, 4 * j:4 * j + 4, :].rearrange("p s g -> p (s g)"), self.identf[:], [Hout, self.identf])
            A([ps], [ho], out=ho[:, j, :], in_=ps[:, 0:128], func=AF.Copy)
            self.out_events.append(sp.dma(O["hrs"][128 * j:128 * j + 128, :], ho[:, j, 0:64], [ho], []))
            self.out_events.append(sp.dma(O["his"][128 * j:128 * j + 128, :], ho[:, j, 64:128], [ho], []))
        self.scrG, self.scrGs = scrG, scrGs

    def load_tokens(self, src, bi, dst):
        t0_, nt = BLOCKS[bi]
        ntile = (nt + 127) // 128
        tp = min(nt, 128)
        for i in range(ntile):
            self.pool.dma(dst[0:tp, i, :], src[t0_ + 128 * i:t0_ + 128 * i + tp, :], [], [dst], par=True)

    def transposes(self, xb, xt, nt):
        ntile = (nt + 127) // 128
        tp = min(nt, 128)
        for k in range(8):
            ps = self.next_ps()
            pv = ps[:].bitcast(BF16)
            for i in range(ntile):
                self.tr(ps, pv[:, 128 * i:128 * i + tp], xb[0:tp, i, 128 * k:128 * k + 128], self.ident[0:tp, 0:tp],
                        [xb, self.ident], signal=(i == ntile - 1))
            self.A([ps], [xt], out=xt[:, k, 0:nt], in_=pv[:, 0:nt], func=AF.Copy)

    def halves(self, buf):
        return (Buf(buf.t, buf.name + "_lo"), Buf(buf.t, buf.name + "_hi"))

    def layer_norm(self, ps_pair, x_tok, r, xh, g_bc, b_bc, st6, mv, sd, tp):
        V, A, Pl = self.V, self.A, self.Pl
        cs = [slice(0, 512), slice(512, D)]
        for h in range(2):
            V("scalar_tensor_tensor", [x_tok, ps_pair[h]], [r[h]], out=r[h][0:tp, cs[h]],
              in0=x_tok[0:tp, cs[h]], scalar=ALPHA, in1=ps_pair[h][0:tp, :], op0=ALU.mult, op1=ALU.add)
        for h in range(2):
            V("bn_stats", [r[h]], [st6], out=st6[0:tp, h, :], in_=r[h][0:tp, cs[h]])
        V("bn_aggr", [st6], [mv], out=mv[0:tp, :], in_=st6[0:tp, :, :].rearrange("p a b -> p (a b)"))
        V("tensor_scalar", [mv], [sd], out=sd[0:tp, 0:1], in0=mv[0:tp, 1:2], scalar1=LN_EPS, scalar2=None, op0=ALU.add)
        Pl("tensor_tensor", [sd, self.mhalf], [sd], out=sd[0:tp, 1:2], in0=sd[0:tp, 0:1], in1=self.mhalf[0:tp, 0:1], op=ALU.pow)
        V("scalar_tensor_tensor", [mv, sd], [sd], out=sd[0:tp, 2:3], in0=mv[0:tp, 0:1], scalar=-1.0, in1=sd[0:tp, 1:2],
          op0=ALU.mult, op1=ALU.mult)
        V("tensor_scalar", [r[0], sd], [xh[0]], out=xh[0][0:tp, cs[0]], in0=r[0][0:tp, cs[0]], scalar1=sd[0:tp, 1:2], scalar2=sd[0:tp, 2:3], op0=ALU.mult, op1=ALU.add)
        Pl("tensor_scalar", [r[1], sd], [xh[1]], out=xh[1][0:tp, cs[1]], in0=r[1][0:tp, cs[1]], scalar1=sd[0:tp, 1:2], scalar2=sd[0:tp, 2:3], op0=ALU.mult, op1=ALU.add)
        Pl("tensor_tensor", [xh[1], g_bc], [xh[1]], out=xh[1][0:tp, cs[1]], in0=xh[1][0:tp, cs[1]], in1=g_bc[0:tp, cs[1]], op=ALU.mult)
        V("tensor_tensor", [xh[0], g_bc], [xh[0]], out=xh[0][0:tp, cs[0]], in0=xh[0][0:tp, cs[0]], in1=g_bc[0:tp, cs[0]], op=ALU.mult)
        Pl("tensor_tensor", [xh[1], b_bc], [xh[1]], out=xh[1][0:tp, cs[1]], in0=xh[1][0:tp, cs[1]], in1=b_bc[0:tp, cs[1]], op=ALU.add)
        V("tensor_tensor", [xh[0], b_bc], [xh[0]], out=xh[0][0:tp, cs[0]], in0=xh[0][0:tp, cs[0]], in1=b_bc[0:tp, cs[0]], op=ALU.add)

    def pass_a(self):
        nc = self.nc
        I, O = self.ins, self.outs
        V, A, Pl = self.V, self.A, self.Pl
        sp, pool = self.sp, self.pool
        NC = L // T
        with ExitStack() as es:
            sb = lambda n, s, d=F32: self.sb(es, n, s, d)
            gT = sb("gT", [128, 4, NTOK], BF16)
            w_a = sb("w_a", [128, 8, 2816], BF16)
            w_qkv = Buf(w_a.t, "w_qkv"); w_gate = Buf(w_a.t, "w_gate")
            w_glu = sb("w_glu", [128, 4, 2048], BF16)
            w_att = sb("w_att", [128, 4, D], BF16)
            w_o = sb("w_o", [128, 8, D], BF16)
            g_bc = sb("g1_bc", [128, D]); b_bc = sb("b1_bc", [128, D])
            sp.dma(g_bc[:], I["ln1_g"][0:1, :].partition_broadcast(128), [], [g_bc])
            sp.dma(b_bc[:], I["ln1_b"][0:1, :].partition_broadcast(128), [], [b_bc])
            maskD = sb("maskD", [128, 512], BF16); maskP = sb("maskP", [128, 512], BF16)
            maskC = sb("maskC", [128, 256], BF16); maskN = sb("maskN", [64, 256], BF16)
            es8 = sb("es8", [128, 8]); ES = sb("ES", [128, 2, 4, 128])
            vext = sb("vext", [128, 5, 2, 128], BF16)
            vc_ext = sb("vc_ext", [128, NSEQ, 2, 128], BF16)
            es_m = ExitStack()
            zer = self.sb(es_m, "zer", [128, 512]); mtmp = self.sb(es_m, "mtmp", [128, 512]); one_t = self.sb(es_m, "one_t", [128, 256])
            Pl("memset", [], [one_t], ap=one_t[:], constant=1.0)
            Pl("memset", [], [zer], ap=zer[:], constant=0.0)
            Pl("memset", [], [vext], ap=vext[:], constant=1.0)
            Pl("memset", [], [vc_ext], ap=vc_ext[:], constant=1.0)
            self.barrier()
            sp.dma(es8[:], I["sinks"][0:1, :].partition_broadcast(128), [], [es8])
            A([es8], [es8], out=es8[:], in_=es8[:], func=AF.Exp)
            V("tensor_copy", [es8], [ES], out=ES[:].rearrange("p a h q -> p (a h) q"), in_=es8[:].unsqueeze(2).to_broadcast([128, 8, 128]))
            z3 = zer[:].rearrange("p (h q) -> p h q", q=128)
            Pl("affine_select", [zer], [mtmp], out=mtmp[:].rearrange("p (h q) -> p h q", q=128), in_=z3, pattern=[[0, 4], [1, 128]],
               compare_op=ALU.is_ge, fill=NEG, base=0, channel_multiplier=-1)
            Pl("tensor_copy", [mtmp], [maskD], out=maskD[:], in_=mtmp[:])
            Pl("affine_select", [zer], [mtmp], out=mtmp[:].rearrange("p (h q) -> p h q", q=128), in_=z3, pattern=[[0, 4], [-1, 128]],
               compare_op=ALU.is_ge, fill=NEG, base=-1, channel_multiplier=1)
            Pl("tensor_copy", [mtmp], [maskP], out=maskP[:], in_=mtmp[:])
            Pl("affine_select", [one_t], [mtmp], out=mtmp[:, 0:256].rearrange("p (s h t) -> p s h t", h=4, t=TS),
               in_=one_t[:, 0:256].rearrange("p (s h t) -> p s h t", h=4, t=TS), pattern=[[0, NSEQ], [0, 4], [-1, TS]],
               compare_op=ALU.is_ge, fill=0.0, base=-1, channel_multiplier=1)
            Pl("tensor_copy", [mtmp], [maskC], out=maskC[:], in_=mtmp[:, 0:256])
            Pl("affine_select", [one_t], [mtmp], out=mtmp[0:64, 0:256].rearrange("p (h s t) -> p h s t", s=NSEQ, t=TS),
               in_=one_t[0:64, 0:256].rearrange("p (h s t) -> p h s t", s=NSEQ, t=TS), pattern=[[0, 4], [-4, NSEQ], [0, TS]],
               compare_op=ALU.is_ge, fill=0.0, base=0, channel_multiplier=1)
            Pl("affine_select", [mtmp], [mtmp], out=mtmp[0:64, 0:256].rearrange("p (h s t) -> p h s t", s=NSEQ, t=TS),
               in_=mtmp[0:64, 0:256].rearrange("p (h s t) -> p h s t", s=NSEQ, t=TS), pattern=[[0, 4], [4, NSEQ], [1, TS]],
               compare_op=ALU.is_ge, fill=0.0, base=0, channel_multiplier=-1)
            Pl("tensor_copy", [mtmp], [maskN], out=maskN[:], in_=mtmp[0:64, 0:256])
            self.barrier()
            es_m.close()
            x_bf = [sb("a_xbf", [128, 4, D], BF16)] * 2
            xT = sb("a_xT", [128, 8, 512], BF16)
            qT = sb("a_qT", [128, 4, 512], BF16)
            kT = sb("a_kT", [128, 640], BF16)
            kvf = sb("a_kvf", [128, 256])
            PT = [sb("a_PT%d" % i, [128, 512], BF16) for i in range(4)]
            oT = sb("a_oT", [128, 4, 512], BF16)
            den = sb("a_den", [64, 512]); rec = sb("a_rec", [64, 512])
            dens = [den, rec]
            osc = sb("a_osc", [128, 256]); osum = sb("a_osum", [128, 256])
            sig = [sb("a_sig%d" % i, [128, 512]) for i in range(2)]
            gsb = [sb("a_gs%d" % i, [128, 512], BF16) for i in range(2)]
            gab = [sb("a_ga%d" % i, [128, 512], BF16) for i in range(2)]
            bsb = [sb("a_bs%d" % i, [128, 512], BF16) for i in range(2)]
            t1 = sb("a_t1", [128, 512]); t2 = sb("a_t2", [128, 512])
            mT = sb("a_mT", [128, 8, 512], BF16)
            x_tok = [sb("a_xtok", [128, D])] * 2
            rr = [self.halves(sb("a_r", [128, D]))] * 2
            xh = [self.halves(sb("a_xh", [128, D]))] * 2
            st6 = sb("a_st6", [128, 2, 6]); mv = sb("a_mv", [128, 2]); sd = sb("a_sd", [128, 3])
            ckb = sb("a_ckb", [128, NSEQ, 128], BF16); kcT = sb("a_kcT", [128, NSEQ, 128], BF16)
            if "A_d2d" not in SKIP:
                self.out_events.append(sp.dma(O["kws"][:, 0:124, :], I["ck"][:, 4:128, :], [], []))
                self.out_events.append(sp.dma(O["vws"][:, 0:124, :], I["cv"][:, 4:128, :], [], []))

            xall = I["x"]
            nblk = len(BLOCKS)
            self.load_tokens(xall, 0, x_bf[0])
            wv = I["w_in"].rearrange("(k p) n -> p k n", p=128)
            for k in range(8 if "A_w" not in SKIP else 0):
                pool.dma(w_a[:, k, 0:768], wv[:, k, 512:1280], [], [w_qkv], par=True)
            wg = I["w_glu"].rearrange("(k p) n -> p k n", p=128)
            for k in range(4 if "A_w" not in SKIP else 0):
                for c in range(2):
                    pool.dma(w_glu[:, k, 1024 * c:1024 * c + 1024], wg[:, k, 1024 * c:1024 * c + 1024], [], [w_glu], par=True)
            wa = I["w_attn"].rearrange("(k p) n -> p k n", p=128)
            for k in range(4 if "A_w" not in SKIP else 0):
                pool.dma(w_att[:, k, :], wa[:, k, :], [], [w_att], par=True)
            for k in range(8 if "A_w" not in SKIP else 0):
                for c in range(2):
                    pool.dma(w_a[:, k, 768 + 1024 * c:1792 + 1024 * c], wv[:, k, 1280 + 1024 * c:2304 + 1024 * c], [], [w_gate], par=True)
            wo = I["w_o"].rearrange("(k p) n -> p k n", p=128)
            for k in range(8 if "A_w" not in SKIP else 0):
                pool.dma(w_o[:, k, :], wo[:, k, :], [], [w_o], par=True)
            for g in range(G):
                j, gl = divmod(g, 8)
                src = bass.AP(tensor=self.scrG.tensor, offset=g * NC, ap=[[G * NC, CH], [CH * G * NC, T], [1, NC]])
                sp.dma(gT[16 * gl:16 * gl + 16, j, 0:L].rearrange("c (r n) -> c r n", n=NC), src, [], [gT], par=True)
                srcs = bass.AP(tensor=self.scrGs.tensor, offset=g * NSEQ, ap=[[G * NSEQ, CH], [CH * G * NSEQ, TS], [1, NSEQ]])
                sp.dma(gT[16 * gl:16 * gl + 16, j, L:L + NSAMP].rearrange("c (r n) -> c r n", n=NSEQ), srcs, [], [gT], par=True)
            if "A_cache" not in SKIP:
                pool.dma(ckb[:], I["ck"].rearrange("s w d -> w s d"), [], [ckb])
            for a_ in range(2 if "A_cache" not in SKIP else 0):
                pool.dma(vc_ext[:, :, a_, 0:64], I["cv"][:, :, 64 * a_:64 * a_ + 64].rearrange("s w d -> w s d"), [], [vc_ext])
            def blk(bi):
                if "A_blk" in SKIP or ("A_samp" in SKIP and bi == nblk - 1) or ("A_prompt" in SKIP and bi < nblk - 1):
                    return
                t0_, nt = BLOCKS[bi]
                ntile = (nt + 127) // 128
                tp = min(nt, 128)
                samp = (bi == nblk - 1)
                xb = x_bf[bi % 2]
                self.transposes(xb, xT, nt)
                if bi + 1 < nblk:
                    self.load_tokens(xall, bi + 1, x_bf[(bi + 1) % 2])
                if "A_proj" in SKIP:
                    return
                for j in range(4):
                    ps = self.next_ps()
                    for k in range(8):
                        self.mm(ps, ps[:, 0:nt], w_a[:, k, 128 * j:128 * j + 128], xT[:, k, 0:nt], [w_qkv, xT], start=(k == 0), stop=(k == 7))
                    A([ps], [qT], out=qT[:, j, 0:nt], in_=ps[:, 0:nt], func=AF.Copy)
                ps = self.next_ps()
                for k in range(8):
                    self.mm(ps, ps[:, 0:nt], w_a[:, k, 512:640], xT[:, k, 0:nt], [w_qkv, xT], start=(k == 0), stop=(k == 7))
                A([ps], [kT], out=kT[:, 128:128 + nt], in_=ps[:, 0:nt], func=AF.Copy)
                for i in range(ntile if "A_kvt" not in SKIP else 0):
                    ps = self.next_ps()
                    for k in range(8):
                        self.mm(ps, ps[0:tp, 0:256], xT[:, k, 128 * i:128 * i + tp], w_a[:, k, 512:768], [w_qkv, xT], start=(k == 0), stop=(k == 7))
                    if "A_kv_act" not in SKIP:
                        A([ps], [vext], out=vext[0:tp, 1 + i, :, 0:64], in_=ps[0:tp, 128:256].rearrange("p (a d) -> p a d", a=2), func=AF.Copy)
                    if ((bi == nblk - 2 and i == ntile - 1) or samp) and "A_kv_out" not in SKIP:
                        A([ps], [kvf], out=kvf[0:tp, :], in_=ps[0:tp, 0:256], func=AF.Copy)
                        if samp:
                            self.out_events.append(sp.dma(O["kws"][:, 124:128, :], kvf[0:NSAMP, 0:128], [kvf], []))
                            self.out_events.append(sp.dma(O["vws"][:, 124:128, :], kvf[0:NSAMP, 128:256], [kvf], []))
                        elif "A_kv_dma" not in SKIP:
                            self.out_events.append(sp.dma(O["kwp"][:, :], kvf[:, 0:128], [kvf], []))
                            self.out_events.append(sp.dma(O["vwp"][:, :], kvf[:, 128:256], [kvf], []))
                if "A_attn" in SKIP:
                    pass
                elif not samp:
                    def attn_unit(i, kv, slot):
                        qs = slice(128 * i, 128 * i + 128)
                        hp_ = slice(64 * kv, 64 * kv + 64)
                        has_prev = not (bi == 0 and i == 0)
                        rhs_q = qT[hp_, :, qs]
                        den_ = dens[slot]
                        PTd, PTp = PT[2 * slot], PT[2 * slot + 1]
                        psD = self.next_ps()
                        self.mm(psD, psD[:, :], self.ident[:], maskD[:], [self.ident, maskD], start=True, stop=False)
                        self.mm(psD, psD[:, :].rearrange("p (h q) -> p h q", q=128), kT[hp_, 128 + 128 * i:256 + 128 * i], rhs_q, [kT, qT], start=False, stop=True)
                        if has_prev:
                            psP = self.next_ps()
                            self.mm(psP, psP[:, :], self.ident[:], maskP[:], [self.ident, maskP], start=True, stop=False)
                            self.mm(psP, psP[:, :].rearrange("p (h q) -> p h q", q=128), kT[hp_, 128 * i:128 + 128 * i], rhs_q, [kT, qT], start=False, stop=True)
                        yield
                        A([psD], [PTd], out=PTd[:], in_=psD[:, :], func=AF.Exp, scale=0.125)
                        if has_prev:
                            A([psP], [PTp], out=PTp[:], in_=psP[:, :], func=AF.Exp, scale=0.125)
                        yield
                        psO = self.next_ps()
                        self.mm(psO, psO[:, :], vext[:, 1 + i, kv, :], PTd[:], [vext, PTd], start=True, stop=not has_prev)
                        if has_prev:
                            self.mm(psO, psO[:, :], vext[:, i, kv, :], PTp[:], [vext, PTp], start=False, stop=True)
                        yield
                        V("tensor_tensor", [psO, ES], [den_], out=den_[:], in0=psO[64:128, :], in1=ES[64:128, kv, :, :].rearrange("p h q -> p (h q)"), op=ALU.add)
                        yield
                        A([den_], [den_], out=den_[:], in_=den_[:], func=AF.Ln)
                        A([den_], [den_], out=den_[:], in_=den_[:], func=AF.Exp, scale=-1.0)
                        yield
                        V("tensor_tensor", [psO, den_], [oT], out=oT[hp_, :, qs], in0=psO[0:64, :].rearrange("p (h q) -> p h q", q=128),
                          in1=den_[:].rearrange("p (h q) -> p h q", q=128), op=ALU.mult)

                    units = [(i, kv) for i in range(ntile) for kv in range(2)]
                    for u0 in range(0, len(units), 2):
                        gens = [attn_unit(units[u0][0], units[u0][1], 0), attn_unit(units[u0 + 1][0], units[u0 + 1][1], 1)]
                        while gens:
                            for g_ in list(gens):
                                try:
                                    next(g_)
                                except StopIteration:
                                    gens.remove(g_)
                else:
                    for s_ in range(NSEQ):
                        ps = self.next_ps()
                        pv = ps[:].bitcast(BF16)
                        self.tr(ps, pv[:, 0:128], ckb[:, s_, :], self.ident[:], [ckb, self.ident])
                        A([ps], [kcT], out=kcT[:, s_, :], in_=pv[:, 0:128], func=AF.Copy)
                    for kv in range(2):
                        hp_ = slice(64 * kv, 64 * kv + 64)
                        psC = self.next_ps()
                        for s_ in range(NSEQ):
                            self.mm(psC, psC[:, 16 * s_:16 * s_ + 16].rearrange("p (h t) -> p h t", t=TS), kcT[hp_, s_, :], qT[hp_, :, TS * s_:TS * s_ + TS], [kcT, qT], start=True, stop=True, signal=(s_ == NSEQ - 1))
                        PTc = PT[0]
                        A([psC], [PTc], out=PTc[:, 0:256], in_=psC[:, 0:256], func=AF.Exp, scale=0.125)
                        V("tensor_tensor", [PTc, maskC], [PTc], out=PTc[:, 0:256], in0=PTc[:, 0:256], in1=maskC[:], op=ALU.mult)
                        psN = self.next_ps()
                        self.mm(psN, psN[0:64, 0:256].rearrange("p (h q) -> p h q", q=NSAMP), kT[hp_, 128:128 + NSAMP], qT[hp_, :, 0:NSAMP], [kT, qT], start=True, stop=True)
                        PTn = PT[1]
                        A([psN], [PTn], out=PTn[0:64, 0:256], in_=psN[0:64, 0:256], func=AF.Exp, scale=0.125)
                        V("tensor_tensor", [PTn, maskN], [PTn], out=PTn[0:64, 0:256], in0=PTn[0:64, 0:256], in1=maskN[:], op=ALU.mult)
                        psOc = self.next_ps()
                        for s_ in range(NSEQ):
                            self.mm(psOc, psOc[:, 16 * s_:16 * s_ + 16], vc_ext[:, s_, kv, :], PTc[:, 16 * s_:16 * s_ + 16], [vc_ext, PTc], start=True, stop=True, signal=(s_ == NSEQ - 1))
                        psOn = self.next_ps()
                        self.mm(psOn, psOn[:, 0:256], vext[0:64, 1, kv, :], PTn[0:64, 0:256], [vext, PTn], start=True, stop=True)
                        A([psOc], [osc], out=osc[:, 0:256], in_=psOc[:, 0:256], func=AF.Copy)
                        V("tensor_tensor", [psOn, osc], [osum], out=osum[:, 0:256].rearrange("p (h s t) -> p h s t", s=NSEQ, t=TS),
                          in0=psOn[:, 0:256].rearrange("p (h s t) -> p h s t", s=NSEQ, t=TS),
                          in1=osc[:, 0:256].rearrange("p (s h t) -> p h s t", h=4, t=TS), op=ALU.add)
                        V("tensor_tensor", [osum, ES], [den], out=den[:, 0:256].rearrange("p (h q) -> p h q", q=NSAMP), in0=osum[64:128, 0:256].rearrange("p (h q) -> p h q", q=NSAMP),
                          in1=ES[64:128, kv, :, 0:NSAMP], op=ALU.add)
                        A([den], [den], out=den[:, 0:256], in_=den[:, 0:256], func=AF.Ln)
                        A([den], [rec], out=rec[:, 0:256], in_=den[:, 0:256], func=AF.Exp, scale=-1.0)
                        V("tensor_tensor", [osum, rec], [oT], out=oT[hp_, :, 0:NSAMP], in0=osum[0:64, 0:256].rearrange("p (h q) -> p h q", q=NSAMP),
                          in1=rec[:, 0:256].rearrange("p (h q) -> p h q", q=NSAMP), op=ALU.mult)
                if not samp and bi + 1 < nblk - 1 and "A_carry" not in SKIP:
                    A([kT], [kT], out=kT[:, 0:128], in_=kT[:, 512:640], func=AF.Copy)
                    A([vext], [vext], out=vext[:, 0, :, 0:64], in_=vext[:, 4, :, 0:64], func=AF.Copy)
                yield
                for jf in range(8 if "A_merge" not in SKIP else 0):
                    psA = self.next_ps(); psB = self.next_ps()
                    for (psx, c0) in ((psA, 128 * jf), (psB, 1024 + 128 * jf)):
                        for k in range(4):
                            if not samp:
                                rhs = gT[:, k, 0:L].rearrange("p (r c) -> p r c", c=NC)[:, :, 64 * bi:64 * bi + 64]
                                o_ = psx[:, :].rearrange("p (r c) -> p r c", c=64)
                            else:
                                rhs = gT[:, k, L:L + NSAMP]
                                o_ = psx[:, 0:NSAMP]
                            self.mm(psx, o_, w_glu[:, k, c0:c0 + 128], rhs, [w_glu, gT], start=(k == 0), stop=(k == 3))
                    sg = sig[jf % 2]
                    bs = bsb[jf % 2]
                    A([psB], [sg], out=sg[:, 0:nt], in_=psB[:, 0:nt], func=AF.Sigmoid)
                    if not samp:
                        V("tensor_tensor", [psA, sg], [bs], out=bs[:, :].rearrange("p (c r) -> p r c", r=T),
                          in0=psA[:, :].rearrange("p (r c) -> p r c", c=64), in1=sg[:].rearrange("p (r c) -> p r c", c=64), op=ALU.mult)
                    else:
                        V("tensor_tensor", [psA, sg], [bs], out=bs[:, 0:NSAMP].rearrange("p (s t) -> p t s", t=TS),
                          in0=psA[:, 0:NSAMP].rearrange("p (t s) -> p t s", s=NSEQ), in1=sg[:, 0:NSAMP].rearrange("p (t s) -> p t s", s=NSEQ), op=ALU.mult)
                    psBA = self.next_ps()
                    for k in range(4):
                        self.mm(psBA, psBA[:, 0:nt], w_att[:, k, 128 * jf:128 * jf + 128], oT[:, k, 0:nt], [w_att, oT], start=(k == 0), stop=(k == 3))
                    psGS = self.next_ps()
                    for k in range(8):
                        self.mm(psGS, psGS[:, 0:nt], w_a[:, k, 768 + 128 * jf:896 + 128 * jf], xT[:, k, 0:nt], [w_gate, xT], start=(k == 0), stop=(k == 7))
                    psGA = self.next_ps()
                    for k in range(8):
                        self.mm(psGA, psGA[:, 0:nt], w_a[:, k, 1792 + 128 * jf:1920 + 128 * jf], xT[:, k, 0:nt], [w_gate, xT], start=(k == 0), stop=(k == 7))
                    gs_, ga_ = gsb[jf % 2], gab[jf % 2]
                    A([psGS], [gs_], out=gs_[:, 0:nt], in_=psGS[:, 0:nt], func=AF.Sigmoid)
                    A([psGA], [ga_], out=ga_[:, 0:nt], in_=psGA[:, 0:nt], func=AF.Sigmoid)
                    Pl("tensor_tensor", [gs_, bs], [t1], out=t1[:, 0:nt], in0=gs_[:, 0:nt], in1=bs[:, 0:nt], op=ALU.mult)
                    V("tensor_tensor", [ga_, psBA], [t2], out=t2[:, 0:nt], in0=ga_[:, 0:nt], in1=psBA[:, 0:nt], op=ALU.mult)
                    V("tensor_tensor", [t1, t2], [mT], out=mT[:, jf, 0:nt], in0=t1[:, 0:nt], in1=t2[:, 0:nt], op=ALU.add)
                yield
                for i in range(ntile if "A_ln" not in SKIP else 0):
                    tsl = slice(t0_ + 128 * i, t0_ + 128 * i + tp)
                    xt_ = x_tok[i % 2]; r_ = rr[i % 2]; xh_ = xh[i % 2]
                    sp.dma(xt_[0:tp, :], xall[tsl, :], [], [xt_])
                    pp = [self.next_ps(), self.next_ps()]
                    for h in range(2):
                        for k in range(8):
                            self.mm(pp[h], pp[h][0:tp, :], mT[:, k, 128 * i:128 * i + tp], w_o[:, k, 512 * h:512 * h + 512], [mT, w_o], start=(k == 0), stop=(k == 7))
                    self.layer_norm(pp, xt_, r_, xh_, g_bc, b_bc, st6, mv, sd, tp)
                    sp.dma(self.x1d[tsl, :], xh_[0][0:tp, :], [xh_[0], xh_[1]], [])

            def finish(g_):
                for _ in g_:
                    pass

            gens = [blk(bi) for bi in range(nblk)]
            next(gens[0], None)
            next(gens[0], None)
            for b_ in range(1, nblk):
                next(gens[b_], None)
                finish(gens[b_ - 1])
                next(gens[b_], None)
            finish(gens[nblk - 1])

    def pass_b(self):
        nc = self.nc
        I, O = self.ins, self.outs
        V, A, Pl = self.V, self.A, self.Pl
        sp, pool = self.sp, self.pool
        with ExitStack() as es:
            sb = lambda n, s, d=F32: self.sb(es, n, s, d)
            w_up = sb("w_up", [128, 8, 2 * DFF], BF16)
            w_upq = [Buf(w_up.t, "w_upq%d" % c) for c in range(4)]
            w_dn = sb("w_dn", [128, NF, D], BF16)
            g_bc = sb("g2_bc", [128, D]); b_bc = sb("b2_bc", [128, D])
            sp.dma(g_bc[:], I["ln2_g"][0:1, :].partition_broadcast(128), [], [g_bc])
            sp.dma(b_bc[:], I["ln2_b"][0:1, :].partition_broadcast(128), [], [b_bc])
            cw = sb("cw", [128, NF, 3]); cb = sb("cb", [128, NF])
            for j in range(3):
                sp.dma(cw[:, :, j], I["conv_w"][j:j + 1, :].rearrange("o (f p) -> p (o f)", p=128), [], [cw], allow_slow_non_contiguous=True)
            sp.dma(cb[:], I["conv_b"][0:1, :].rearrange("o (f p) -> p (o f)", p=128), [], [cb], allow_slow_non_contiguous=True)
            a_carry = sb("a_carry", [128, NF, 2])
            Pl("memset", [], [a_carry], ap=a_carry[:], constant=0.0)
            self.barrier()
            scT = sb("scT", [128, NF, 2 * NSEQ]); csT = sb("csT", [128, NF, 2 * NSEQ])
            stg = [sb("b_stg%d" % i, [32, 512]) for i in range(2)]
            for c in range(6):
                w_ = min(512, DFF - 512 * c)
                st_ = stg[c % 2]
                sp.dma(st_[:, 0:w_], I["sconv"][:, 512 * c:512 * c + w_], [], [st_])
                for q in range(w_ // 128):
                    f = 4 * c + q
                    ps = self.next_ps()
                    self.tr(ps, ps[:, 0:32], st_[:, 128 * q:128 * q + 128], self.identf[0:32, 0:32], [st_, self.identf])
                    A([ps], [scT], out=scT[:, f, :], in_=ps[:, 0:32], func=AF.Copy)
            x_bf = sb("b_xbf", [128, 4, D], BF16)
            xT = sb("b_xT", [128, 8, 512], BF16)
            a_ext = [sb("b_aext", [128, 514])] * 2
            c1 = [sb("b_c1%d" % i, [128, 512]) for i in range(2)]
            ge = [sb("b_ge%d" % i, [128, 512], BF16) for i in range(2)]
            hT = sb("b_hT", [128, NF, 512], BF16)
            x_tok = [sb("b_xtok", [128, D])] * 2
            rr = self.halves(sb("b_r", [128, D])); xh = rr
            st6 = sb("b_st6", [128, 2, 6]); mv = sb("b_mv", [128, 2]); sd = sb("b_sd", [128, 3])
            nblk = len(BLOCKS)
            self.load_tokens(self.x1d, 0, x_bf)
            wu = I["w_up"].rearrange("(k p) n -> p k n", p=128)
            for c in (0, 2, 1, 3):
                for k in range(8):
                    pool.dma(w_up[:, k, 1408 * c:1408 * c + 1408], wu[:, k, 1408 * c:1408 * c + 1408], [], [w_upq[c]], par=True)
            wd = I["w_down"].rearrange("(k p) n -> p k n", p=128)
            for k in range(NF):
                pool.dma(w_dn[:, k, :], wd[:, k, :], [], [w_dn], par=True)
            for bi in range(nblk):
                t0_, nt = BLOCKS[bi]
                ntile = (nt + 127) // 128
                tp = min(nt, 128)
                samp = (bi == nblk - 1)
                self.transposes(x_bf, xT, nt)
                if bi + 1 < nblk:
                    self.load_tokens(self.x1d, bi + 1, x_bf)
                GF = 2
                for f0 in range(0, NF, GF):
                    fs = list(range(f0, min(NF, f0 + GF)))
                    pA, pG = {}, {}
                    for f in fs:
                        pA[f] = self.next_ps(); pG[f] = self.next_ps()
                        for (psx, c0) in ((pA[f], 128 * f), (pG[f], DFF + 128 * f)):
                            for k in range(8):
                                self.mm(psx, psx[:, 0:nt], w_up[:, k, c0:c0 + 128], xT[:, k, 0:nt], [w_upq[c0 // 1408], xT], start=(k == 0), stop=(k == 7))
                    if not samp:
                        for f in fs:
                            c_ = c1[f % 2]
                            V("tensor_scalar", [a_carry, cw, cb], [c_], out=c_[:, 0:2], in0=a_carry[:, f, :], scalar1=cw[:, f, 0:1], scalar2=cb[:, f:f + 1], op0=ALU.mult, op1=ALU.add)
                        for f in fs:
                            c_ = c1[f % 2]
                            V("tensor_scalar", [pA[f], cw, cb], [c_], out=c_[:, 2:nt], in0=pA[f][:, 0:nt - 2], scalar1=cw[:, f, 0:1], scalar2=cb[:, f:f + 1], op0=ALU.mult, op1=ALU.add)
                        for f in fs:
                            c_ = c1[f % 2]
                            V("scalar_tensor_tensor", [a_carry, cw, c_], [c_], out=c_[:, 0:1], in0=a_carry[:, f, 1:2], scalar=cw[:, f, 1:2], in1=c_[:, 0:1], op0=ALU.mult, op1=ALU.add)
                        for f in fs:
                            c_ = c1[f % 2]
                            V("scalar_tensor_tensor", [pA[f], cw, c_], [c_], out=c_[:, 1:nt], in0=pA[f][:, 0:nt - 1], scalar=cw[:, f, 1:2], in1=c_[:, 1:nt], op0=ALU.mult, op1=ALU.add)
                        for f in fs:
                            c_ = c1[f % 2]
                            V("scalar_tensor_tensor", [pA[f], cw, c_], [c_], out=c_[:, 0:nt], in0=pA[f][:, 0:nt], scalar=cw[:, f, 2:3], in1=c_[:, 0:nt], op0=ALU.mult, op1=ALU.add)
                        for f in fs:
                            A([pA[f]], [a_carry], out=a_carry[:, f, :], in_=pA[f][:, nt - 2:nt], func=AF.Copy)
                        for f in fs:
                            A([c1[f % 2]], [ge[f % 2]], out=ge[f % 2][:, 0:nt], in_=c1[f % 2][:, 0:nt], func=AF.Gelu_apprx_tanh)
                        for f in fs:
                            V("tensor_tensor", [ge[f % 2], pG[f]], [hT], out=hT[:, f, 0:nt], in0=ge[f % 2][:, 0:nt], in1=pG[f][:, 0:nt], op=ALU.mult)
                    else:
                        for f in fs:
                            psA, psG = pA[f], pG[f]
                            ae = a_ext[0]; c_ = c1[f % 2]; g_ = ge[f % 2]
                            a3 = ae[:, 0:6 * NSEQ].rearrange("p (s j) -> p s j", j=6)
                            c3 = c_[:, 0:NSAMP].rearrange("p (s t) -> p s t", t=TS)
                            A([scT], [ae], out=a3[:, :, 0:2], in_=scT[:, f, :].rearrange("p (s j) -> p s j", j=2), func=AF.Copy)
                            A([psA], [ae], out=a3[:, :, 2:6], in_=psA[:, 0:NSAMP].rearrange("p (s t) -> p s t", t=TS), func=AF.Copy)
                            A([ae], [csT], out=csT[:, f, :].rearrange("p (s j) -> p s j", j=2), in_=a3[:, :, 4:6], func=AF.Copy)
                            V("tensor_scalar", [ae, cw, cb], [c_], out=c3, in0=a3[:, :, 0:4], scalar1=cw[:, f, 0:1], scalar2=cb[:, f:f + 1], op0=ALU.mult, op1=ALU.add)
                            V("scalar_tensor_tensor", [ae, cw, c_], [c_], out=c3, in0=a3[:, :, 1:5], scalar=cw[:, f, 1:2], in1=c3, op0=ALU.mult, op1=ALU.add)
                            V("scalar_tensor_tensor", [ae, cw, c_], [c_], out=c3, in0=a3[:, :, 2:6], scalar=cw[:, f, 2:3], in1=c3, op0=ALU.mult, op1=ALU.add)
                            A([c_], [g_], out=g_[:, 0:nt], in_=c_[:, 0:nt], func=AF.Gelu_apprx_tanh)
                            V("tensor_tensor", [g_, psG], [hT], out=hT[:, f, 0:nt], in0=g_[:, 0:nt], in1=psG[:, 0:nt], op=ALU.mult)
                for i in range(ntile):
                    tsl = slice(t0_ + 128 * i, t0_ + 128 * i + tp)
                    xt_ = x_tok[i % 2]
                    sp.dma(xt_[0:tp, :], self.x1d[tsl, :], [], [xt_])
                    pp = [self.next_ps(), self.next_ps()]
                    for h in range(2):
                        for f in range(NF):
                            self.mm(pp[h], pp[h][0:tp, :], hT[:, f, 128 * i:128 * i + tp], w_dn[:, f, 512 * h:512 * h + 512], [hT, w_dn], start=(f == 0), stop=(f == NF - 1))
                    self.layer_norm(pp, xt_, rr, xh, g_bc, b_bc, st6, mv, sd, tp)
                    self.out_events.append(sp.dma(O["y"][tsl, :], xh[0][0:tp, :], [xh[0], xh[1]], []))
            for c in range(6):
                w_ = min(512, DFF - 512 * c)
                nq = w_ // 128
                ps = self.next_ps(); ps2 = self.next_ps()
                for q in range(nq):
                    f = 4 * c + q
                    self.tr(ps, ps[0:2, 128 * q:128 * q + 128], a_carry[:, f, :], self.identf[:], [a_carry, self.identf], signal=(q == nq - 1))
                for q in range(nq):
                    f = 4 * c + q
                    self.tr(ps2, ps2[0:32, 128 * q:128 * q + 128], csT[:, f, :], self.identf[:], [csT, self.identf], signal=(q == nq - 1))
                s0, s1 = stg[0], stg[1]
                A([ps], [s0], out=s0[0:2, 0:w_], in_=ps[0:2, 0:w_], func=AF.Copy)
                A([ps2], [s1], out=s1[0:32, 0:w_], in_=ps2[0:32, 0:w_], func=AF.Copy)
                self.out_events.append(sp.dma(O["cp"][:, 512 * c:512 * c + w_], s0[0:2, 0:w_], [s0], []))
                self.out_events.append(sp.dma(O["cs"][:, 512 * c:512 * c + w_], s1[0:32, 0:w_], [s1], []))


def _host_inputs(inp):
    f = lambda a: np.ascontiguousarray(np.asarray(a, dtype=np.float32))
    w_in = f(inp["w_in"][0])
    qcols = np.concatenate([np.r_[512 + 64 * j:512 + 64 * j + 64, 512 + 64 * (4 + j):512 + 64 * (4 + j) + 64] for j in range(4)])
    perm = np.r_[0:512, qcols, 1024:DIN]
    w_in = np.ascontiguousarray(w_in[:, perm])
    w_attn = f(inp["w_attn_br"][0])
    rows = np.concatenate([np.r_[64 * j:64 * j + 64, 64 * (4 + j):64 * (4 + j) + 64] for j in range(4)])
    w_attn = np.ascontiguousarray(w_attn[rows, :])
    shared = {
        "w_in": w_in, "lam_re": f(inp["ssm_lam_re"][0]), "lam_im": f(inp["ssm_lam_im"][0]),
        "log_dt": f(inp["ssm_log_dt"][0]).reshape(1, G), "b_re": f(inp["ssm_b_re"][0]), "b_im": f(inp["ssm_b_im"][0]),
        "c_re": f(inp["ssm_c_re"][0]).reshape(G * CH, P), "c_im": f(inp["ssm_c_im"][0]).reshape(G * CH, P),
        "d": f(inp["ssm_d"][0]).reshape(G, CH), "w_glu": f(inp["w_glu"][0]), "sinks": f(inp["attn_sinks"][0]).reshape(1, 8),
        "w_attn": w_attn, "w_o": f(inp["w_o"][0]), "ln1_g": f(inp["ln1_g"][0]).reshape(1, D), "ln1_b": f(inp["ln1_b"][0]).reshape(1, D),
        "w_up": f(inp["w_up"][0]), "conv_w": f(inp["conv_w"][0]), "conv_b": f(inp["conv_b"][0]).reshape(1, DFF),
        "w_down": f(inp["w_down"][0]), "ln2_g": f(inp["ln2_g"][0]).reshape(1, D), "ln2_b": f(inp["ln2_b"][0]).reshape(1, D),
    }
    maps = []
    for c in range(8):
        s = slice(NSEQ * c, NSEQ * c + NSEQ)
        m = dict(shared)
        m["x"] = np.ascontiguousarray(np.concatenate([f(inp["x_prompt"][c]), f(inp["x_sample"][s]).reshape(NSAMP, D)], 0))
        m["ck"] = f(inp["cache_k_win"][0, s]).reshape(NSEQ, 128, 128)
        m["cv"] = f(inp["cache_v_win"][0, s]).reshape(NSEQ, 128, 128)
        m["sre"] = f(inp["state_ssm_re"][0, s]).reshape(NSEQ * G, P)
        m["sim"] = f(inp["state_ssm_im"][0, s]).reshape(NSEQ * G, P)
        m["sconv"] = f(inp["state_ffn_conv"][0, s]).reshape(NSEQ * 2, DFF)
        maps.append(m)
    return maps


_NC_CACHE = {}


def _run(inp):
    if "nc" not in _NC_CACHE:
        _NC_CACHE["nc"] = KB().build()
    nc = _NC_CACHE["nc"]
    maps = _host_inputs(inp)
    res = run_bass_kernel_spmd(nc, maps, core_ids=list(range(8)))
    return res.results


def kernel(**inp):
    rs = _run(inp)
    cat = lambda k: np.stack([np.asarray(r[k]) for r in rs], 0)
    y = cat("y")
    yp = y[:, :L, :]
    ys = y[:, L:, :].reshape(8 * NSEQ, TS, D)
    kwp = cat("kwp").reshape(1, 8, 128, 2, 64)
    vwp = cat("vwp").reshape(1, 8, 128, 2, 64)
    kws = cat("kws").reshape(1, 8 * NSEQ, 128, 2, 64)
    vws = cat("vws").reshape(1, 8 * NSEQ, 128, 2, 64)
    hrp = cat("hrp").reshape(1, 8, G, P)
    hip = cat("hip").reshape(1, 8, G, P)
    hrs = cat("hrs").reshape(1, 8 * NSEQ, G, P)
    his = cat("his").reshape(1, 8 * NSEQ, G, P)
    cp = cat("cp").reshape(1, 8, 2, DFF)
    cs = cat("cs").reshape(1, 8 * NSEQ, 2, DFF)
    return (np.ascontiguousarray(yp), np.ascontiguousarray(ys), kwp, vwp, kws, vws, hrp, hip, hrs, his, cp, cs)
```
